# Optimizing a Trainium2 kernel written in Bass

```python
import math
import jax, jax.numpy as jnp
from jax import lax
import numpy as np

D_MODEL = 1024
BATCH = 8
SEQ = 2048
DEPTH = 1
DEC_BATCH = 128
DEC_SEQ = 1
PAST_LEN = 16384
PAGE_SIZE = 128

D_MIX = D_MODEL
C_CONV = D_MIX // 2
CONV_WIDTH = 31
DN_HEADS = 4
DN_DK = (D_MIX - C_CONV) // DN_HEADS
DN_DV = DN_DK
DN_QK = DN_HEADS * DN_DK
DN_V = DN_HEADS * DN_DV
QKV_COLS = 2 * DN_QK + DN_V
SHORT_CONV = 4
DN_CHUNK = 64
N_MEM = 256
MEM_HEADS = 4
MEM_HD = D_MODEL // MEM_HEADS
D_FF = -(-8 * D_MODEL // (3 * 256)) * 256

O_GLU_A = 0
O_GLU_B = C_CONV
O_QKV = 2 * C_CONV
O_Z = O_QKV + QKV_COLS
O_BETA = O_Z + DN_V
O_DECAY = O_BETA + DN_HEADS
IN_COLS = O_DECAY + DN_HEADS

kernel_name = 'hymba_conformer_gdn_memxattn_step'


def rms_norm(x, g, eps=1e-6):
    xf = x.astype(jnp.float32)
    y = xf * lax.rsqrt(jnp.mean(xf * xf, axis=-1, keepdims=True) + eps)
    return (y * g.astype(jnp.float32)).astype(x.dtype)


def layer_norm(x, g, b, eps=1e-5):
    xf = x.astype(jnp.float32)
    mu = jnp.mean(xf, axis=-1, keepdims=True)
    xc = xf - mu
    y = xc * lax.rsqrt(jnp.mean(xc * xc, axis=-1, keepdims=True) + eps)
    return (y * g.astype(jnp.float32) + b.astype(jnp.float32)).astype(x.dtype)


def l2_norm(x, eps=1e-6):
    return x * lax.rsqrt(jnp.sum(x * x, axis=-1, keepdims=True) + eps)


def causal_depthwise_conv(x, buf, w):
    xp = jnp.concatenate([buf.astype(x.dtype), x], axis=1)
    y = lax.conv_general_dilated(xp, w[:, None, :].astype(x.dtype), window_strides=(1,), padding='VALID',
                                 dimension_numbers=('NWC', 'WIO', 'NWC'), feature_group_count=x.shape[-1])
    return y, xp[:, xp.shape[1] - (w.shape[0] - 1):]


def gated_delta_chunked(q, k, v, g, beta, s0):
    bsz, seq = q.shape[0], q.shape[1]
    c = min(DN_CHUNK, seq)
    n = -(-seq // c)
    pad = n * c - seq
    if pad:
        padw = lambda t: jnp.pad(t, [(0, 0), (0, pad)] + [(0, 0)] * (t.ndim - 2))
        q, k, v, g, beta = padw(q), padw(k), padw(v), padw(g), padw(beta)

    def chunks(t):
        t = t.reshape((bsz, n, c) + t.shape[2:])
        return jnp.swapaxes(jnp.swapaxes(t, 0, 1), 2, 3)

    q, k, v, g, beta = chunks(q), chunks(k), chunks(v), chunks(g), chunks(beta)
    gc = jnp.cumsum(g, axis=-1)
    incl = jnp.tril(jnp.ones((c, c), dtype=bool))
    strict = jnp.tril(jnp.ones((c, c), dtype=bool), k=-1)
    decay = jnp.exp(jnp.where(incl, gc[..., :, None] - gc[..., None, :], -jnp.inf))
    kb = k * beta[..., None]
    lower = jnp.where(strict, jnp.einsum('nbhid,nbhjd->nbhij', kb, k) * decay, 0.0)
    a_mat = lower + jnp.eye(c, dtype=lower.dtype)
    u = lax.linalg.triangular_solve(a_mat, v * beta[..., None], left_side=True, lower=True, unit_diagonal=True)
    w = lax.linalg.triangular_solve(a_mat, kb * jnp.exp(gc)[..., None], left_side=True, lower=True,
                                    unit_diagonal=True)
    qk = jnp.where(incl, jnp.einsum('nbhid,nbhjd->nbhij', q, k) * decay, 0.0)

    def step(s, xs):
        q_i, k_i, u_i, w_i, g_i, qk_i = xs
        v_new = u_i - jnp.einsum('bhcd,bhde->bhce', w_i, s)
        o_i = (jnp.einsum('bhcd,bhde->bhce', q_i * jnp.exp(g_i)[..., None], s)
               + jnp.einsum('bhij,bhje->bhie', qk_i, v_new))
        g_last = g_i[..., -1]
        k_dec = k_i * jnp.exp(g_last[..., None] - g_i)[..., None]
        s = s * jnp.exp(g_last)[..., None, None] + jnp.einsum('bhcd,bhce->bhde', k_dec, v_new)
        return s, o_i

    s_fin, o = lax.scan(step, s0, (q, k, u, w, gc, qk))
    o = jnp.swapaxes(jnp.swapaxes(o, 2, 3), 0, 1).reshape(bsz, n * c, o.shape[2], o.shape[-1])[:, :seq]
    return o, s_fin


def parallel_mixer(h, conv_buf, sc_buf, s0, w_in, conv_w, conv_b, conv_ln_g, conv_ln_b, sc_w, a_log, dt_bias,
                   dn_norm, w_out):
    bsz, seq, _ = h.shape
    p = h @ w_in
    u = p[..., O_GLU_A:O_GLU_B] * jax.nn.sigmoid(p[..., O_GLU_B:O_QKV])
    c, new_conv_buf = causal_depthwise_conv(u, conv_buf, conv_w)
    c = jax.nn.silu(layer_norm(c + conv_b, conv_ln_g, conv_ln_b))
    qkv, new_sc_buf = causal_depthwise_conv(p[..., O_QKV:O_Z], sc_buf, sc_w)
    qkv = jax.nn.silu(qkv).astype(jnp.float32)
    q = l2_norm(qkv[..., :DN_QK].reshape(bsz, seq, DN_HEADS, DN_DK)) * (DN_DK ** -0.5)
    k = l2_norm(qkv[..., DN_QK:2 * DN_QK].reshape(bsz, seq, DN_HEADS, DN_DK))
    v = qkv[..., 2 * DN_QK:].reshape(bsz, seq, DN_HEADS, DN_DV)
    beta = jax.nn.sigmoid(p[..., O_BETA:O_DECAY].astype(jnp.float32))
    g = -jnp.exp(a_log.astype(jnp.float32)) * jax.nn.softplus(
        p[..., O_DECAY:IN_COLS].astype(jnp.float32) + dt_bias.astype(jnp.float32))
    o, s_new = gated_delta_chunked(q, k, v, g, beta, s0.astype(jnp.float32))
    z = p[..., O_Z:O_BETA].reshape(bsz, seq, DN_HEADS, DN_DV).astype(jnp.float32)
    o = rms_norm(o, dn_norm) * jax.nn.silu(z)
    d = o.reshape(bsz, seq, DN_V).astype(h.dtype)
    y = jnp.concatenate([c, d], axis=-1) @ w_out
    return y, new_conv_buf, new_sc_buf, s_new.astype(h.dtype)


def memory_kv(mem, norm_mem_kv, w_mk, w_mv):
    bsz, n_mem, _ = mem.shape
    m = rms_norm(mem, norm_mem_kv)
    k = (m @ w_mk).reshape(bsz, n_mem, MEM_HEADS, MEM_HD)
    v = (m @ w_mv).reshape(bsz, n_mem, MEM_HEADS, MEM_HD)
    return k, v


def memory_attend(h, mem_k, mem_v, w_mq, w_mo):
    bsz, seq, _ = h.shape
    q = (h @ w_mq).reshape(bsz, seq, MEM_HEADS, MEM_HD)
    s = jnp.einsum('blhd,bmhd->bhlm', q.astype(jnp.float32), mem_k.astype(jnp.float32)) * (MEM_HD ** -0.5)
    pr = jax.nn.softmax(s, axis=-1)
    o = jnp.einsum('bhlm,bmhd->blhd', pr, mem_v.astype(jnp.float32)).astype(h.dtype)
    return o.reshape(bsz, seq, MEM_HEADS * MEM_HD) @ w_mo


def swiglu(h, w_gate, w_up, w_down):
    return (jax.nn.silu(h @ w_gate) * (h @ w_up)) @ w_down


def decoder_layer(x, conv_buf, sc_buf, s0, mem_k, mem_v, norm_mix, w_in, conv_w, conv_b, conv_ln_g, conv_ln_b,
                  sc_w, a_log, dt_bias, dn_norm, w_out, norm_mem_q, w_mq, w_mo, norm_ffn, w_gate, w_up, w_down):
    y, conv_buf, sc_buf, s = parallel_mixer(rms_norm(x, norm_mix), conv_buf, sc_buf, s0, w_in, conv_w, conv_b,
                                            conv_ln_g, conv_ln_b, sc_w, a_log, dt_bias, dn_norm, w_out)
    x = x + y
    x = x + memory_attend(rms_norm(x, norm_mem_q), mem_k, mem_v, w_mq, w_mo)
    x = x + swiglu(rms_norm(x, norm_ffn), w_gate, w_up, w_down)
    return x, conv_buf, sc_buf, s


def setup_inputs(seed: int = 0) -> dict:
    key = jax.random.key(seed)
    ks = jax.random.split(key, 32)

    def nrm(k, shape, scale):
        return jax.random.normal(k, shape, jnp.float32) * scale

    def gain(k, shape):
        return 1.0 + 0.05 * jax.random.normal(k, shape, jnp.float32)

    dt = jnp.exp(jax.random.uniform(ks[10], (DEPTH, DN_HEADS), jnp.float32, math.log(1e-3), math.log(1e-1)))
    return {
        'x_prompt': nrm(ks[0], (BATCH, SEQ, D_MODEL), 1.0),
        'x_sample': nrm(ks[1], (DEC_BATCH, DEC_SEQ, D_MODEL), 1.0),
        'mem_prompt': nrm(ks[2], (BATCH, N_MEM, D_MODEL), 1.0),
        'cache_conv': nrm(ks[3], (DEPTH, DEC_BATCH, CONV_WIDTH - 1, C_CONV), 0.5),
        'state_short_conv': nrm(ks[4], (DEPTH, DEC_BATCH, SHORT_CONV - 1, QKV_COLS), 1.0),
        'state_delta': nrm(ks[5], (DEPTH, DEC_BATCH, DN_HEADS, DN_DK, DN_DV), 0.1),
        'cache_mem_k': nrm(ks[6], (DEPTH, DEC_BATCH, N_MEM, MEM_HEADS, MEM_HD), 1.0),
        'cache_mem_v': nrm(ks[7], (DEPTH, DEC_BATCH, N_MEM, MEM_HEADS, MEM_HD), 1.0),
        'norm_mix': gain(ks[8], (DEPTH, D_MODEL)),
        'w_in': nrm(ks[9], (DEPTH, D_MODEL, IN_COLS), D_MODEL ** -0.5),
        'conv_w': nrm(ks[11], (DEPTH, CONV_WIDTH, C_CONV), CONV_WIDTH ** -0.5),
        'conv_b': nrm(ks[12], (DEPTH, C_CONV), 0.02),
        'conv_ln_g': gain(ks[13], (DEPTH, C_CONV)),
        'conv_ln_b': nrm(ks[14], (DEPTH, C_CONV), 0.02),
        'sc_w': nrm(ks[15], (DEPTH, SHORT_CONV, QKV_COLS), SHORT_CONV ** -0.5),
        'a_log': jnp.log(jax.random.uniform(ks[16], (DEPTH, DN_HEADS), jnp.float32, 1.0, 16.0)),
        'dt_bias': dt + jnp.log(-jnp.expm1(-dt)),
        'dn_norm': gain(ks[17], (DEPTH, DN_DV)),
        'w_out': nrm(ks[18], (DEPTH, D_MIX, D_MODEL), D_MIX ** -0.5),
        'norm_mem_q': gain(ks[19], (DEPTH, D_MODEL)),
        'norm_mem_kv': gain(ks[20], (DEPTH, D_MODEL)),
        'w_mq': nrm(ks[21], (DEPTH, D_MODEL, MEM_HEADS * MEM_HD), D_MODEL ** -0.5),
        'w_mk': nrm(ks[22], (DEPTH, D_MODEL, MEM_HEADS * MEM_HD), D_MODEL ** -0.5),
        'w_mv': nrm(ks[23], (DEPTH, D_MODEL, MEM_HEADS * MEM_HD), D_MODEL ** -0.5),
        'w_mo': nrm(ks[24], (DEPTH, MEM_HEADS * MEM_HD, D_MODEL), (MEM_HEADS * MEM_HD) ** -0.5),
        'norm_ffn': gain(ks[25], (DEPTH, D_MODEL)),
        'w_gate': nrm(ks[26], (DEPTH, D_MODEL, D_FF), D_MODEL ** -0.5),
        'w_up': nrm(ks[27], (DEPTH, D_MODEL, D_FF), D_MODEL ** -0.5),
        'w_down': nrm(ks[28], (DEPTH, D_FF, D_MODEL), D_FF ** -0.5),
        'norm_f': gain(ks[29], (D_MODEL,)),
    }


def reference(x_prompt, x_sample, mem_prompt, cache_conv, state_short_conv, state_delta, cache_mem_k, cache_mem_v,
              norm_mix, w_in, conv_w, conv_b, conv_ln_g, conv_ln_b, sc_w, a_log, dt_bias, dn_norm, w_out,
              norm_mem_q, norm_mem_kv, w_mq, w_mk, w_mv, w_mo, norm_ffn, w_gate, w_up, w_down, norm_f):
    bp = x_prompt.shape[0]
    dt_ = x_prompt.dtype
    xp, xs = x_prompt, x_sample
    conv_p, sc_p, dl_p, mk_p, mv_p = [], [], [], [], []
    conv_s, sc_s, dl_s = [], [], []
    for l in range(DEPTH):
        lw = (norm_mix[l], w_in[l], conv_w[l], conv_b[l], conv_ln_g[l], conv_ln_b[l], sc_w[l], a_log[l],
              dt_bias[l], dn_norm[l], w_out[l], norm_mem_q[l], w_mq[l], w_mo[l], norm_ffn[l], w_gate[l],
              w_up[l], w_down[l])
        mk, mv = memory_kv(mem_prompt, norm_mem_kv[l], w_mk[l], w_mv[l])
        xp, cb, sb, st = decoder_layer(
            xp, jnp.zeros((bp, CONV_WIDTH - 1, C_CONV), dt_), jnp.zeros((bp, SHORT_CONV - 1, QKV_COLS), dt_),
            jnp.zeros((bp, DN_HEADS, DN_DK, DN_DV), dt_), mk, mv, *lw)
        conv_p.append(cb); sc_p.append(sb); dl_p.append(st); mk_p.append(mk); mv_p.append(mv)
        xs, cb, sb, st = decoder_layer(xs, cache_conv[l], state_short_conv[l], state_delta[l], cache_mem_k[l],
                                       cache_mem_v[l], *lw)
        conv_s.append(cb); sc_s.append(sb); dl_s.append(st)
    y_prompt = rms_norm(xp, norm_f)
    y_sample = rms_norm(xs, norm_f)
    return (y_prompt, y_sample, jnp.stack(conv_p), jnp.stack(sc_p), jnp.stack(dl_p), jnp.stack(mk_p),
            jnp.stack(mv_p), jnp.stack(conv_s), jnp.stack(sc_s), jnp.stack(dl_s))
```

```python
import numpy as np
import concourse.bass as bass
import concourse.mybir as mybir
from concourse.bass_utils import run_bass_kernel_spmd

F32 = mybir.dt.float32
BF16 = mybir.dt.bfloat16
AF = mybir.ActivationFunctionType
ALU = mybir.AluOpType
AX = mybir.AxisListType

T = 2048
NS = 16
TT = T + NS
NT = 16
D = 1024
DFF = 2816
NFC = 22


class Op:
    __slots__ = ("eng", "fn", "deps", "sig", "need", "dma", "tag", "cnt", "idx")


import os
STOP = float(os.environ.get("KSTOP", "99"))


class _Stop(Exception):
    pass


def stop_at(n):
    if STOP == n:
        raise _Stop()


class _Rec:
    def __init__(self):
        self.call = None

    def __getattr__(self, name):
        def f(*a, **k):
            self.call = (name, a, k)
            return self
        return f


class Prog:
    ENGS = ("sp", "act", "pool", "dve", "pe")

    def __init__(self, nc):
        self.nc = nc
        self.ops = {e: [] for e in self.ENGS}
        self.all = []
        self.lastw = {}
        self.readers = {}
        self.tagcnt = {}
        self.taggroup = {}
        self.base = []

    def _add(self, eng, fn, r, w, dma=False, tag=None, group=False):
        op = Op()
        rec = _Rec()
        fn(rec)
        name_, a_, k_ = rec.call
        fn = lambda e, name_=name_, a_=a_, k_=k_: getattr(e, name_)(*a_, **k_)
        op.eng, op.fn, op.dma, op.tag = eng, fn, dma, tag
        op.need = False
        op.sig = None
        psr = [k for k in r if isinstance(k, tuple) and k[0] == "ps"]
        if psr:
            r = [k for k in r if k not in psr]
            w = list(w) + psr
        deps = list(self.base)
        for k in r:
            if k in self.lastw:
                deps.append(self.lastw[k])
        for k in w:
            if k in self.lastw:
                deps.append(self.lastw[k])
            deps.extend(self.readers.get(k, ()))
        if dma:
            deps = [d for d in deps if not (d.dma and d.tag == tag)]
        if eng == "pe":
            deps = [d for d in deps if d.dma or d.eng != "pe"]
        op.deps = deps
        for k in r:
            self.readers.setdefault(k, []).append(op)
        for k in w:
            self.lastw[k] = op
            self.readers[k] = []
        if dma:
            self.tagcnt[tag] = self.tagcnt.get(tag, 0) + 1
            self.taggroup[tag] = group
            op.cnt = self.tagcnt[tag]
        op.idx = len(self.all)
        self.all.append(op)
        self.ops[eng].append(op)
        return op

    def pe(self, fn, r=(), w=()):
        return self._add("pe", fn, r, w)

    def dve(self, fn, r=(), w=()):
        return self._add("dve", fn, r, w)

    def act(self, fn, r=(), w=()):
        return self._add("act", fn, r, w)

    def pool(self, fn, r=(), w=()):
        return self._add("pool", fn, r, w)

    def dma(self, out, in_, r=(), w=(), tag="ld", group=False, q="sp"):
        return self._add(q, lambda e: e.dma_start(out=out, in_=in_), r, w, dma=True, tag=tag, group=group)

    def barrier(self):
        base = []
        for e in self.ENGS:
            last = None
            for op in reversed(self.ops[e]):
                if not op.dma:
                    last = op
                    break
            if last is not None:
                base.append(last)
        lastdma = {}
        for op in self.all:
            if op.dma:
                lastdma[op.tag] = op
        base.extend(lastdma.values())
        self.base = base
        self.lastw = {}
        self.readers = {}

    def emit(self, final_tags):
        nc = self.nc
        for op in self.all:
            for d in op.deps:
                if not d.dma:
                    d.need = True
        for e in self.ENGS:
            n = 0
            for op in self.ops[e]:
                if not op.dma and op.need:
                    n += 1
                    op.sig = n
        from contextlib import ExitStack
        with ExitStack() as st:
            esem = {e: st.enter_context(nc.semaphore("s_" + e)) for e in self.ENGS}
            tsem = {t: st.enter_context(nc.semaphore("t_%d" % i)) for i, t in enumerate(self.tagcnt)}
            block = st.enter_context(nc.Block())

            def run(e, eng):
                waited = {}
                for op in self.ops[e]:
                    need = {}
                    for d in op.deps:
                        if d.dma:
                            s = tsem[d.tag]
                            v = 16 * (self.tagcnt[d.tag] if self.taggroup[d.tag] else d.cnt)
                        else:
                            s = esem[d.eng]
                            v = d.sig
                        if v > need.get(s, (0, 0))[1] if s in need else True:
                            need[s] = (s, v)
                    for s, v in need.values():
                        if waited.get(s, 0) < v:
                            eng.wait_ge(s, v)
                            waited[s] = v
                    ins = op.fn(eng)
                    if op.dma:
                        ins.then_inc(tsem[op.tag], 16)
                    elif op.need:
                        ins.then_inc(esem[e], 1)
                if e == "sp":
                    for t in tsem:
                        eng.wait_ge(tsem[t], 16 * self.tagcnt[t])

            block.sync(lambda eng: run("sp", eng))
            block.scalar(lambda eng: run("act", eng))
            block.gpsimd(lambda eng: run("pool", eng))
            block.vector(lambda eng: run("dve", eng))
            block.tensor(lambda eng: run("pe", eng))


class Alloc:
    def __init__(self, nc):
        self.nc = nc
        self.off = (int(nc.sbuf_base) + 63) // 64 * 64
        self.top = int(nc.sbuf_top)
        self.n = 0

    def t(self, shape, dt, name=None):
        sz = 2 if dt == BF16 else 4
        nb = sz
        for s in shape[1:]:
            nb *= s
        nb = (nb + 63) // 64 * 64
        assert self.off + nb <= self.top, ("SBUF overflow", name, self.off, nb, self.top)
        self.n += 1
        h = self.nc.alloc_sbuf_tensor_at("%s_%d" % (name or "t", self.n), list(shape), dt, offset=self.off)
        self.off += nb
        return h

    def child(self, off, top):
        c = Alloc.__new__(Alloc)
        c.nc, c.off, c.top, c.n = self.nc, (off + 63) // 64 * 64, top, self.n + 1000 * (1 + off % 97)
        return c

    def mark(self):
        return self.off

    def reset(self, m):
        self.off = m


def make_consts():
    i = np.arange(128)
    ident = np.eye(128, dtype=np.float32)
    U = (i[:, None] <= i[None, :]).astype(np.float32)
    SL = (i[:, None] > i[None, :]).astype(np.float32)
    ones = np.ones((128, 128), np.float32)
    blk = (i[:, None] // 64) == (i[None, :] // 64)
    mBDneg = -(SL * blk).astype(np.float32)
    mOFF = (SL * (~blk)).astype(np.float32)
    mTin = U.copy()
    oh16 = np.zeros((128, 16), np.float32)
    oh16[:16, :16] = np.eye(16)
    parts = [ident, U, SL, ones, mBDneg, mOFF, mTin, oh16]
    return np.ascontiguousarray(np.concatenate(parts, axis=1))


C_ID, C_U, C_SL, C_ONE = 0, 128, 256, 384
C_BD, C_OFF, C_TIN = 512, 640, 768
C_OH16 = 896
NCST = C_OH16 + 16

PC_CONVW, PC_CONVB, PC_LNG, PC_LNB, PC_SCW, PC_DN, PC_ALOG, PC_DTB = 0, 124, 128, 132, 136, 184, 185, 186
NPC = 187
PR_GMIX, PR_GMQ, PR_GFFN, PR_GKV, PR_GF, PR_DN4, PR_ALOG, PR_DTB = 0, 1024, 2048, 3072, 4096, 5120, 5632, 5636
NPR = 5640


def make_params(inp):
    pc = np.zeros((128, NPC), np.float32)
    pc[:, PC_CONVW:PC_CONVW + 124] = inp["conv_w"][0].reshape(31, 4, 128).transpose(2, 1, 0).reshape(128, 124)
    pc[:, PC_CONVB:PC_CONVB + 4] = inp["conv_b"][0].reshape(4, 128).T
    pc[:, PC_LNG:PC_LNG + 4] = inp["conv_ln_g"][0].reshape(4, 128).T
    pc[:, PC_LNB:PC_LNB + 4] = inp["conv_ln_b"][0].reshape(4, 128).T
    pc[:, PC_SCW:PC_SCW + 48] = inp["sc_w"][0].reshape(4, 12, 128).transpose(2, 1, 0).reshape(128, 48)
    pc[:, PC_DN] = inp["dn_norm"][0]
    pc[:4, PC_ALOG] = inp["a_log"][0]
    pc[:4, PC_DTB] = inp["dt_bias"][0]
    pr = np.zeros((128, NPR), np.float32)
    bc = lambda v: np.broadcast_to(np.asarray(v, np.float32).reshape(1, -1), (128, np.asarray(v).size))
    pr[:, PR_GMIX:PR_GMIX + 1024] = bc(inp["norm_mix"][0])
    pr[:, PR_GMQ:PR_GMQ + 1024] = bc(inp["norm_mem_q"][0])
    pr[:, PR_GFFN:PR_GFFN + 1024] = bc(inp["norm_ffn"][0])
    pr[:, PR_GKV:PR_GKV + 1024] = bc(inp["norm_mem_kv"][0])
    pr[:, PR_GF:PR_GF + 1024] = bc(inp["norm_f"])
    pr[:, PR_DN4:PR_DN4 + 512] = bc(np.tile(inp["dn_norm"][0], 4))
    pr[:, PR_ALOG:PR_ALOG + 4] = bc(inp["a_log"][0])
    pr[:, PR_DTB:PR_DTB + 4] = bc(inp["dt_bias"][0])
    return pc, pr


def build_nc():
    nc = bass.Bass("TRN2", target_bir_lowering=False)
    P = Prog(nc)
    try:
        return _build_nc(nc, P)
    except _Stop:
        P.emit(["out"])
        return nc, P


def _build_nc(nc, P):
    A = Alloc(nc)

    def dr(name, shape, out=False):
        return nc.dram_tensor(name, list(shape), F32, kind="ExternalOutput" if out else "ExternalInput").ap()

    x_p = dr("x_p", [T, D]); x_s = dr("x_s", [NS, D]); mem = dr("mem", [256, D])
    cconv = dr("cconv", [NS * 30, 512]); ssc = dr("ssc", [NS * 3, 1536]); sdel = dr("sdel", [NS * 4, 128, 128])
    cmk = dr("cmk", [NS, 256, 1024]); cmv = dr("cmv", [NS, 256, 1024])
    w_in = dr("w_in", [D, 3080]); w_out = dr("w_out", [D, D]); w_mq = dr("w_mq", [D, D]); w_mk = dr("w_mk", [D, D])
    w_mv = dr("w_mv", [D, D]); w_mo = dr("w_mo", [D, D]); w_gate = dr("w_gate", [D, DFF]); w_up = dr("w_up", [D, DFF])
    w_down = dr("w_down", [DFF, D])
    cst_d = dr("cst", [128, NCST]); pc_d = dr("pc", [128, NPC]); pr_d = dr("pr", [128, NPR])
    y_p = dr("y_p", [T, D], True); y_s = dr("y_s", [NS, D], True)
    nconv_p = dr("nconv_p", [30, 512], True); nsc_p = dr("nsc_p", [3, 1536], True)
    ndel_p = dr("ndel_p", [4, 128, 128], True)
    mk_p = dr("mk_p", [256, D], True); mv_p = dr("mv_p", [256, D], True)
    nconv_s = dr("nconv_s", [NS, 30, 512], True); nsc_s = dr("nsc_s", [NS, 3, 1536], True)
    ndel_s = dr("ndel_s", [NS * 4, 128, 128], True)

    ps = [nc.alloc_psum_tensor("ps%d" % i, [128, 512], F32) for i in range(8)]
    PK = lambda i: ("ps", i)

    def psb(i):
        return ps[i][:].bitcast(BF16)

    cst = A.t([128, NCST], F32, "cst")
    pc = A.t([128, NPC], F32, "pc")
    idb = A.t([128, 128], BF16, "idb")
    oneb = A.t([128, 128], BF16, "oneb")
    onesc = A.t([128, 128], BF16, "onesc")
    gam = A.t([128, 1024], F32, "gam")
    P.dma(cst[:], cst_d[:, :], w=["cst"], tag="c0")
    P.dma(pc[:], pc_d[:, :], w=["pc"], tag="c1")
    P.dve(lambda e: e.tensor_copy(out=idb[:], in_=cst[:, C_ID:C_ID + 128]), r=["cst"], w=["idb"])
    P.dve(lambda e: e.tensor_copy(out=oneb[:], in_=cst[:, C_ONE:C_ONE + 128]), r=["cst"], w=["oneb"])
    P.dve(lambda e: e.tensor_scalar(out=onesc[:], in0=cst[:, C_ONE:C_ONE + 128], scalar1=1.0 / 512, scalar2=None,
                                    op0=ALU.mult), r=["cst"], w=["onesc"])
    ident_f = cst[:, C_ID:C_ID + 128]
    Uf = cst[:, C_U:C_U + 128]
    SLf = cst[:, C_SL:C_SL + 128]
    onef = cst[:, C_ONE:C_ONE + 128]

    def load_gam(off):
        P.dma(gam[:], pr_d[:, off:off + 1024], w=["gam"], tag="gam")

    nrm_junk = A.t([128, 1024], BF16, "njunk")
    nrm_xn3 = [A.t([128, 1024], BF16, "nxn%d" % i) for i in range(3)]
    nrm_s3 = [A.t([128, 8], F32, "nrs%d" % i) for i in range(3)]

    def norm_pass(items, dstT, dkey):
        L = len(items)

        def S1(i):
            src, rkeys, n, col0, pre = items[i]
            if pre is not None:
                pre()
            ns_, nsk = nrm_s3[i % 3], ("nrs", i % 3)
            P.act(lambda e: e.activation(out=nrm_junk[:n, :], in_=src, func=AF.Square, accum_out=ns_[:n, 0:1]),
                  r=rkeys, w=[nsk, "njunk"])
            P.dve(lambda e: e.tensor_scalar(out=ns_[:n, 1:2], in0=ns_[:n, 0:1], scalar1=1.0 / 1024, scalar2=1e-6,
                                            op0=ALU.mult, op1=ALU.add), r=[nsk], w=[nsk])

        def S2(i):
            src, rkeys, n, col0, pre = items[i]
            ns_, nsk = nrm_s3[i % 3], ("nrs", i % 3)
            xn_, nxk = nrm_xn3[i % 3], ("nxn", i % 3)
            P.act(lambda e: e.activation(out=ns_[:n, 2:3], in_=ns_[:n, 1:2], func=AF.Sqrt), r=[nsk], w=[nsk])
            P.dve(lambda e: e.reciprocal(out=ns_[:n, 3:4], in_=ns_[:n, 2:3]), r=[nsk], w=[nsk])
            P.dve(lambda e: e.scalar_tensor_tensor(out=xn_[:n, :], in0=src, scalar=ns_[:n, 3:4], in1=gam[:n, :],
                                                   op0=ALU.mult, op1=ALU.mult), r=rkeys + [nsk, "gam"], w=[nxk])

        def S3(i):
            src, rkeys, n, col0, pre = items[i]
            xn_, nxk = nrm_xn3[i % 3], ("nxn", i % 3)
            bank = i % 2
            pv = psb(bank)
            for kc in range(8):
                P.pe(lambda e, kc=kc: e.transpose(out=pv[:, kc * 128:kc * 128 + n],
                                                  in_=xn_[:n, kc * 128:(kc + 1) * 128], identity=idb[:n, :n]),
                     r=[nxk, "idb"], w=[PK(bank)])
            pv3 = pv.rearrange("p (c n) -> p c n", c=8)
            dk_ = (dkey, col0 // 512)
            if i % 2 == 0:
                P.act(lambda e: e.activation(out=dstT[:, :, col0:col0 + n], in_=pv3[:, :, 0:n], func=AF.Copy),
                      r=[PK(bank)], w=[dk_])
            else:
                P.dve(lambda e: e.tensor_copy(out=dstT[:, :, col0:col0 + n], in_=pv3[:, :, 0:n]),
                      r=[PK(bank)], w=[dk_])

        for step in range(L + 2):
            if step < L:
                S1(step)
            if 0 <= step - 1 < L:
                S2(step - 1)
            if 0 <= step - 2 < L:
                S3(step - 2)

    def load_w(dst, dkey, wd, r0, nk, c0, ncols):
        for k in range(nk):
            P.dma(dst[:, k, 0:ncols], wd[r0 + k * 128:r0 + (k + 1) * 128, c0:c0 + ncols], w=[dkey],
                  tag="w_" + str(dkey), q="pool")

    off_hT = A.mark()
    hT = A.t([128, 8, TT], BF16, "hT")
    off_cT = A.mark()
    cT = A.t([128, 4, TT], BF16, "cT")
    mX = A.mark()
    XSZ = 17 * 1024 * 4
    A.off += XSZ
    mXe = A.mark()
    AX_ = A.child(mX, mXe)
    zs = A.t([128, NT, 512], BF16, "zs")
    zsT = A.t([128, 4, NS], F32, "zsT")
    vTs = A.t([128, 4, NS], F32, "vTs")
    gtok = A.t([128, NT, 4], F32, "gtok")
    btok = A.t([128, NT, 4], F32, "btok")
    gbS = A.t([4, 2, NS], F32, "gbS")
    utail = A.t([128, 4, 32], F32, "utail")
    ptail = A.t([128, 12, 4], F32, "ptail")
    unew_s = A.t([128, 4, NS], F32, "unews")
    pnew_s = A.t([128, 12, NS], F32, "pnews")
    m_w = A.mark()
    wsl = [A.t([128, 8, 512], BF16, "wsl%d" % i) for i in range(3)]
    m_p1 = A.mark()
    A1 = A
    A = AX_

    xin = [A.t([128, 1024], F32, "xin%d" % i) for i in range(3)]
    load_gam(PR_GMIX)
    items = []
    for t in range(NT + 1):
        s = t % 3
        n = 128 if t < NT else NS
        src_d = x_p[t * 128:(t + 1) * 128, :] if t < NT else x_s[:, :]
        pre = (lambda s=s, n=n, src_d=src_d: P.dma(xin[s][:n, :], src_d, w=[("xin", s)], tag="xin%d" % s))
        items.append((xin[s][:n, :], [("xin", s)], n, t * 128, pre))
    norm_pass(items, hT, "hT")

    stop_at(1)
    BLK = [(i * 512, 512) for i in range(4)] + [(T, NS)]

    def mm8(bank, w_ap_fn, rhs_fn, ncols, rk, mrows=128):
        for kc in range(8):
            P.pe(lambda e, kc=kc: e.matmul(ps[bank][0:mrows, 0:ncols], lhsT=w_ap_fn(kc), rhs=rhs_fn(kc),
                                           start=(kc == 0), stop=(kc == 7)), r=rk, w=[PK(bank)])

    load_w(wsl[2], ("wsl", 2), w_in, 0, 8, 3072, 8)
    load_w(wsl[0], ("wsl", 0), w_in, 0, 8, 0, 512)
    load_w(wsl[1], ("wsl", 1), w_in, 0, 8, 512, 512)
    prb = A.t([128, 16], F32, "prb")
    P.dma(prb[:, 0:8], pr_d[:, PR_ALOG:PR_ALOG + 8], w=["prb"], tag="c2")
    P.act(lambda e: e.activation(out=prb[:, 8:12], in_=prb[:, 0:4], func=AF.Exp), r=["prb"], w=["prb"])
    P.dve(lambda e: e.tensor_scalar(out=prb[:, 8:12], in0=prb[:, 8:12], scalar1=-1.0, scalar2=None, op0=ALU.mult),
          r=["prb"], w=["prb"])
    negA_c = A.t([4, 1], F32, "negAc")
    P.act(lambda e: e.activation(out=negA_c[:], in_=pc[0:4, PC_ALOG:PC_ALOG + 1], func=AF.Exp), r=["pc"], w=["negAc"])
    P.dve(lambda e: e.tensor_scalar(out=negA_c[:], in0=negA_c[:], scalar1=-1.0, scalar2=None, op0=ALU.mult),
          r=["negAc"], w=["negAc"])
    gx = A.t([128, NT, 4], F32, "gx")
    dtb64 = A.t([128, NT, 4], F32, "dtb64")
    nga64 = A.t([128, NT, 4], F32, "nga64")
    P.dve(lambda e: e.tensor_copy(out=dtb64[:], in_=prb[:, 4:8].unsqueeze(1).to_broadcast([128, NT, 4])),
          r=["prb"], w=["dtb64"])
    P.dve(lambda e: e.tensor_copy(out=nga64[:], in_=prb[:, 8:12].unsqueeze(1).to_broadcast([128, NT, 4])),
          r=["prb"], w=["nga64"])
    for t in range(NT):
        for kc in range(8):
            P.pe(lambda e, t=t, kc=kc: e.matmul(ps[2][:, t * 8:(t + 1) * 8], lhsT=hT[:, kc, t * 128:(t + 1) * 128],
                                                rhs=wsl[2][:, kc, 0:8], start=(kc == 0), stop=(kc == 7)),
                 r=[("hT", t // 4), ("wsl", 2)], w=[PK(2)])
    ps3 = ps[2][:, 0:NT * 8].rearrange("p (t c) -> p t c", c=8)
    P.act(lambda e: e.activation(out=btok[:, :, :], in_=ps3[:, :, 0:4], func=AF.Exp, scale=-1.0), r=[PK(2)],
          w=["btok"])
    P.dve(lambda e: e.tensor_scalar(out=btok[:, :, :], in0=btok[:, :, :], scalar1=1.0, scalar2=None, op0=ALU.add),
          r=["btok"], w=["btok"])
    P.dve(lambda e: e.reciprocal(out=btok[:, :, :], in_=btok[:, :, :]), r=["btok"], w=["btok"])
    P.dve(lambda e: e.tensor_tensor(out=gx[:], in0=ps3[:, :, 4:8], in1=dtb64[:], op=ALU.add), r=[PK(2), "dtb64"],
          w=["gx"])
    P.act(lambda e: e.activation(out=gx[:], in_=gx[:], func=AF.Exp), r=["gx"], w=["gx"])
    P.act(lambda e: e.activation(out=gx[:], in_=gx[:], func=AF.Ln, bias=1.0), r=["gx"], w=["gx"])
    P.dve(lambda e: e.tensor_tensor(out=gtok[:, :, :], in0=gx[:], in1=nga64[:], op=ALU.mult), r=["gx", "nga64"],
          w=["gtok"])
    for half in range(2):
        b = 2 + half
        mm8(b, lambda kc: wsl[2][:, kc, half * 4:half * 4 + 4], lambda kc: hT[:, kc, T:TT], NS, [("hT", 4), ("wsl", 2)],
            mrows=4)
    P.act(lambda e: e.activation(out=gbS[:, 0, :], in_=ps[2][0:4, 0:NS], func=AF.Exp, scale=-1.0), r=[PK(2)],
          w=["gbS"])
    P.dve(lambda e: e.tensor_scalar(out=gbS[:, 0, :], in0=gbS[:, 0, :], scalar1=1.0, scalar2=None, op0=ALU.add),
          r=["gbS"], w=["gbS"])
    P.dve(lambda e: e.reciprocal(out=gbS[:, 0, :], in_=gbS[:, 0, :]), r=["gbS"], w=["gbS"])
    gts = A.t([4, 2, NS], F32, "gts")
    P.act(lambda e: e.activation(out=gts[:, 0, :], in_=ps[3][0:4, 0:NS], func=AF.Exp,
                                 bias=pc[0:4, PC_DTB:PC_DTB + 1]), r=[PK(3), "pc"], w=["gts"])
    P.act(lambda e: e.activation(out=gts[:, 1, :], in_=gts[:, 0, :], func=AF.Ln, bias=1.0), r=["gts"], w=["gts"])
    P.dve(lambda e: e.tensor_scalar(out=gbS[:, 1, :], in0=gts[:, 1, :], scalar1=negA_c[:, 0:1], scalar2=None,
                                    op0=ALU.mult), r=["gts", "negAc"], w=["gbS"])

    stop_at(2)
    upad = A.t([128, 4, 30 + T], BF16, "upad")
    us = A.t([128, 4, NS, 32], BF16, "us")
    sig = [A.t([128, 512], F32, "sig%d" % i) for i in range(2)]
    diag = A.t([128, 31, 128], BF16, "diag")
    P.dve(lambda e: e.memset(upad[:, :, 0:30], 0.0), w=["upad"])
    load_w(wsl[2], ("wsl", 2), w_in, 0, 8, 1024, 512)
    cc_in = A.t([120, 4, 512], F32, "ccin")
    for g4 in range(4):
        P.dma(cc_in[:, g4, :], cconv[g4 * 120:(g4 + 1) * 120, :], w=[("ccin", g4)], tag="cc", group=True)
    P.dma(nconv_s[:, 0:29, :], cconv.rearrange("(s j) c -> s j c", j=30)[:, 1:30, :], tag="out")
    for c in range(4):
        for g4 in range(4):
            b = 4 + (g4 % 2)
            P.pe(lambda e, c=c, g4=g4, b=b: e.transpose(out=ps[b][:, 0:120], in_=cc_in[:, g4, c * 128:(c + 1) * 128],
                                                        identity=ident_f[0:120, 0:120]), r=[("ccin", g4), "cst"], w=[PK(b)])
            P.act(lambda e, c=c, g4=g4, b=b: e.activation(
                out=us[:, c, g4 * 4:(g4 + 1) * 4, 0:30],
                in_=ps[b][:, 0:120].rearrange("p (s j) -> p s j", j=30), func=AF.Copy), r=[PK(b)], w=["us"])
    for c in range(4):
        for bi, (t0, n) in enumerate(BLK):
            ba, bb = (bi % 2) * 2, (bi % 2) * 2 + 1
            mm8(ba, lambda kc: wsl[0][:, kc, c * 128:(c + 1) * 128], lambda kc: hT[:, kc, t0:t0 + n], n,
                [("hT", t0 // 512), ("wsl", 0)])
            mm8(bb, lambda kc: wsl[1][:, kc, c * 128:(c + 1) * 128], lambda kc: hT[:, kc, t0:t0 + n], n,
                [("hT", t0 // 512), ("wsl", 1)])
            sg = sig[bi % 2]
            sk = ("sig", bi % 2)
            P.act(lambda e, sg=sg, bb=bb, n=n: e.activation(out=sg[:, 0:n], in_=ps[bb][:, 0:n], func=AF.Sigmoid),
                  r=[PK(bb)], w=[sk])
            if bi < 4:
                P.dve(lambda e, sg=sg, ba=ba, c=c, t0=t0: e.tensor_tensor(
                    out=upad[:, c, 30 + t0:30 + t0 + 512], in0=ps[ba][:, 0:512], in1=sg[:, 0:512], op=ALU.mult),
                    r=[PK(ba), sk], w=["upad"])
                if bi == 3:
                    P.dve(lambda e, sg=sg, ba=ba, c=c: e.tensor_tensor(
                        out=utail[:, c, 0:30], in0=ps[ba][:, 482:512], in1=sg[:, 482:512], op=ALU.mult),
                        r=[PK(ba), sk], w=["utail"])
            else:
                P.dve(lambda e, sg=sg, ba=ba, c=c: e.tensor_tensor(
                    out=unew_s[:, c, :], in0=ps[ba][:, 0:NS], in1=sg[:, 0:NS], op=ALU.mult),
                    r=[PK(ba), sk], w=["unews"])
                P.act(lambda e, c=c: e.activation(out=us[:, c, :, 30:31], in_=unew_s[:, c, :].unsqueeze(2),
                                                  func=AF.Copy), r=["unews"], w=["us"])
        for j in range(31):
            P.dve(lambda e, c=c, j=j: e.tensor_scalar(
                out=diag[:, j, :], in0=idb[:], scalar1=pc[:, PC_CONVW + c * 31 + j:PC_CONVW + c * 31 + j + 1],
                scalar2=None, op0=ALU.mult), r=["idb", "pc"], w=["diag"])
        for bi, (t0, n) in enumerate(BLK):
            b = 4 + bi % 2
            for j in range(31):
                if bi < 4:
                    rhs = upad[:, c, t0 + j:t0 + j + 512]
                else:
                    rhs = us[:, c, :, j]
                P.pe(lambda e, b=b, j=j, rhs=rhs, n=n: e.matmul(ps[b][:, 0:n], lhsT=diag[:, j, :], rhs=rhs,
                                                               start=(j == 0), stop=(j == 30)),
                     r=["diag", "upad", "us"], w=[PK(b)])
            P.act(lambda e, b=b, c=c, t0=t0, n=n: e.activation(
                out=cT[:, c, t0:t0 + n], in_=ps[b][:, 0:n], func=AF.Identity,
                bias=pc[:, PC_CONVB + c:PC_CONVB + c + 1]), r=[PK(b), "pc"], w=["cT"])
    csq = [A.t([128, 512], BF16, "csq%d" % i) for i in range(2)]
    lnm = A.t([128, 512], F32, "lnm")
    lnv = A.t([128, 512], F32, "lnv")
    lnt = [A.t([128, 512], F32, "lnt%d" % i) for i in range(2)]
    for bi, (t0, n) in enumerate(BLK):
        for c in range(4):
            P.pe(lambda e, c=c, t0=t0, n=n: e.matmul(ps[0][:, 0:n], lhsT=onesc[:], rhs=cT[:, c, t0:t0 + n],
                                                     start=(c == 0), stop=(c == 3)), r=["cT", "onesc"], w=[PK(0)])
        for c in range(4):
            q = csq[c % 2]
            P.dve(lambda e, q=q, c=c, t0=t0, n=n: e.tensor_tensor(out=q[:, 0:n], in0=cT[:, c, t0:t0 + n],
                                                                  in1=cT[:, c, t0:t0 + n], op=ALU.mult),
                  r=["cT"], w=[("csq", c % 2)])
            P.pe(lambda e, q=q, c=c, n=n: e.matmul(ps[1][:, 0:n], lhsT=onesc[:], rhs=q[:, 0:n],
                                                   start=(c == 0), stop=(c == 3)), r=[("csq", c % 2), "onesc"],
                 w=[PK(1)])
        P.act(lambda e, n=n: e.activation(out=lnm[:, 0:n], in_=ps[0][:, 0:n], func=AF.Copy), r=[PK(0)], w=["lnm"])
        P.dve(lambda e, n=n: e.tensor_tensor(out=lnv[:, 0:n], in0=lnm[:, 0:n], in1=lnm[:, 0:n], op=ALU.mult),
              r=["lnm"], w=["lnv"])
        P.dve(lambda e, n=n: e.tensor_tensor(out=lnv[:, 0:n], in0=ps[1][:, 0:n], in1=lnv[:, 0:n], op=ALU.subtract),
              r=[PK(1), "lnv"], w=["lnv"])
        P.dve(lambda e, n=n: e.tensor_scalar(out=lnv[:, 0:n], in0=lnv[:, 0:n], scalar1=0.0, scalar2=1e-5,
                                             op0=ALU.max, op1=ALU.add), r=["lnv"], w=["lnv"])
        P.act(lambda e, n=n: e.activation(out=lnv[:, 0:n], in_=lnv[:, 0:n], func=AF.Ln), r=["lnv"], w=["lnv"])
        P.act(lambda e, n=n: e.activation(out=lnv[:, 0:n], in_=lnv[:, 0:n], func=AF.Exp, scale=-0.5), r=["lnv"],
              w=["lnv"])
        for c in range(4):
            tt_ = lnt[c % 2]
            tk = ("lnt", c % 2)
            P.dve(lambda e, tt_=tt_, c=c, t0=t0, n=n: e.tensor_tensor(out=tt_[:, 0:n], in0=cT[:, c, t0:t0 + n],
                                                                      in1=lnm[:, 0:n], op=ALU.subtract),
                  r=["cT", "lnm"], w=[tk])
            P.dve(lambda e, tt_=tt_, n=n: e.tensor_tensor(out=tt_[:, 0:n], in0=tt_[:, 0:n], in1=lnv[:, 0:n],
                                                          op=ALU.mult), r=[tk, "lnv"], w=[tk])
            P.act(lambda e, tt_=tt_, c=c, t0=t0, n=n: e.activation(
                out=cT[:, c, t0:t0 + n], in_=tt_[:, 0:n], func=AF.Silu,
                scale=pc[:, PC_LNG + c:PC_LNG + c + 1], bias=pc[:, PC_LNB + c:PC_LNB + c + 1]),
                r=[tk, "pc"], w=["cT"])
    otl = A.t([32, 512], F32, "otl")
    for c in range(4):
        P.pe(lambda e, c=c: e.transpose(out=ps[2][0:30, c * 128:(c + 1) * 128], in_=utail[:, c, 0:30],
                                        identity=ident_f), r=["utail", "cst"], w=[PK(2)])
    P.act(lambda e: e.activation(out=otl[0:30, :], in_=ps[2][0:30, :], func=AF.Copy), r=[PK(2)], w=["otl"])
    P.dma(nconv_p[:, :], otl[0:30, :], r=["otl"], tag="out")
    otl2 = A.t([16, 512], F32, "otl2")
    for c in range(4):
        P.pe(lambda e, c=c: e.transpose(out=ps[3][0:NS, c * 128:(c + 1) * 128], in_=unew_s[:, c, :],
                                        identity=ident_f), r=["unews", "cst"], w=[PK(3)])
    P.act(lambda e: e.activation(out=otl2[:, :], in_=ps[3][0:NS, :], func=AF.Copy), r=[PK(3)], w=["otl2"])
    P.dma(nconv_s[:, 29, :], otl2[:, :], r=["otl2"], tag="out")

    stop_at(3)
    P.barrier()
    AX_ = A1.child(mX, mXe)
    qT = AX_.t([128, 4, TT], BF16, "qT")
    kT = AX_.t([128, 4, TT], BF16, "kT")
    ktok = AX_.t([128, NT, 512], BF16, "ktok")
    vb = AX_.t([128, NT, 512], BF16, "vb")
    A = A1
    scd2 = [A.t([128, 4, 128], BF16, "scd%d" % i) for i in range(2)]
    pre = [A.t([128, 3 + 512], BF16, "pre%d" % i) for i in range(2)]
    pres2 = [A.t([128, NS, 4], BF16, "pres%d" % i) for i in range(2)]
    sfl = [A.t([128, 512], F32, "sfl%d" % i) for i in range(2)]
    sqb = [A.t([128, 512], BF16, "sqb%d" % i) for i in range(2)]
    rnb = [A.t([128, 512], F32, "rnb%d" % i) for i in range(2)]
    vtmp = [A.t([128, 512], BF16, "vtmp%d" % i) for i in range(2)]
    ss_in = A.t([48, 1536], F32, "ssin")
    P.dma(ss_in[:, :], ssc[:, :], w=["ssin"], tag="ssin")
    P.dma(nsc_s[:, 0:2, :], ssc.rearrange("(s j) c -> s j c", j=3)[:, 1:3, :], tag="out")

    def run_skewed(iters):
        L = len(iters)
        S_ = max(len(x) for x in iters)
        for step in range(L + S_ - 1):
            for k in range(S_):
                i = step - k
                if 0 <= i < L and k < len(iters[i]):
                    iters[i][k]()

    iters = []
    it = [0]
    for grp in range(3):
        sl = (2, 0, 1)[grp]
        for hh in range(4):
            ch = grp * 4 + hh
            scd = scd2[ch % 2]
            sck = ("scd", ch % 2)
            pres = pres2[ch % 2]
            psk = ("pres", ch % 2)
            for bi, (t0, n) in enumerate(BLK):
                i = it[0]
                it[0] += 1

                def S0(grp=grp, sl=sl, hh=hh, ch=ch, scd=scd, sck=sck, pres=pres, psk=psk, bi=bi, t0=t0, n=n, i=i):
                    b = i % 2
                    pr_ = pre[i % 2]
                    prk = ("pre", i % 2)
                    if hh == 0 and bi == 0:
                        if grp < 2:
                            nsl = (2, 0, 1)[grp + 1]
                            load_w(wsl[nsl], ("wsl", nsl), w_in, 0, 8, 1024 + (grp + 1) * 512, 512)
                        else:
                            load_w(wsl[2], ("wsl", 2), w_in, 0, 8, 2560, 512)
                    if bi == 0:
                        for j in range(4):
                            P.dve(lambda e, j=j: e.tensor_scalar(
                                out=scd[:, j, :], in0=idb[:],
                                scalar1=pc[:, PC_SCW + ch * 4 + j:PC_SCW + ch * 4 + j + 1],
                                scalar2=None, op0=ALU.mult), r=["idb", "pc"], w=[sck])
                        P.pe(lambda e: e.transpose(out=ps[6][:, 0:48], in_=ss_in[:, ch * 128:(ch + 1) * 128],
                                                   identity=ident_f[0:48, 0:48]), r=["ssin", "cst"], w=[PK(6)])
                        P.act(lambda e: e.activation(out=pres[:, :, 0:3],
                                                     in_=ps[6][:, 0:48].rearrange("p (s j) -> p s j", j=3),
                                                     func=AF.Copy), r=[PK(6)], w=[psk])
                    mm8(b, lambda kc: wsl[sl][:, kc, hh * 128:(hh + 1) * 128], lambda kc: hT[:, kc, t0:t0 + n], n,
                        [("hT", t0 // 512), ("wsl", sl)])
                    if bi < 4:
                        if bi == 0:
                            P.dve(lambda e: e.memset(pr_[:, 0:3], 0.0), w=[prk])
                        else:
                            po = pre[(i - 1) % 2]
                            P.dve(lambda e: e.tensor_copy(out=pr_[:, 0:3], in_=po[:, 512:515]),
                                  r=[("pre", (i - 1) % 2)], w=[prk])
                        P.dve(lambda e: e.tensor_copy(out=pr_[:, 3:515], in_=ps[b][:, 0:512]),
                              r=[PK(b)], w=[prk])
                        if bi == 3:
                            P.dve(lambda e: e.tensor_copy(out=ptail[:, ch, 0:3], in_=ps[b][:, 509:512]),
                                  r=[PK(b)], w=["ptail"])
                    else:
                        P.act(lambda e: e.activation(out=pres[:, :, 3:4], in_=ps[b][:, 0:NS].unsqueeze(2),
                                                     func=AF.Copy), r=[PK(b)], w=[psk])
                        P.dve(lambda e: e.tensor_copy(out=pnew_s[:, ch, :], in_=ps[b][:, 0:NS]),
                              r=[PK(b)], w=["pnews"])

                def S1(grp=grp, hh=hh, scd=scd, sck=sck, pres=pres, psk=psk, bi=bi, n=n, i=i):
                    b2 = 2 + i % 2
                    pr_ = pre[i % 2]
                    prk = ("pre", i % 2)
                    for j in range(4):
                        rhs = pr_[:, j:j + 512] if bi < 4 else pres[:, :, j]
                        P.pe(lambda e, j=j, rhs=rhs: e.matmul(ps[b2][:, 0:n], lhsT=scd[:, j, :], rhs=rhs,
                                                              start=(j == 0), stop=(j == 3)),
                             r=[sck, prk if bi < 4 else psk], w=[PK(b2)])
                    if grp == 2:
                        if bi < 4:
                            vt = vtmp[i % 2]
                            P.act(lambda e: e.activation(out=vt[:, :], in_=ps[b2][:, 0:512], func=AF.Silu),
                                  r=[PK(b2)], w=[("vtmp", i % 2)])
                        else:
                            P.act(lambda e: e.activation(out=vTs[:, hh, :], in_=ps[b2][:, 0:NS], func=AF.Silu),
                                  r=[PK(b2)], w=["vTs"])
                        return
                    sf = sfl[i % 2]
                    sfk = ("sfl", i % 2)
                    P.act(lambda e: e.activation(out=sf[:, 0:n], in_=ps[b2][:, 0:n], func=AF.Exp, scale=-1.0),
                          r=[PK(b2)], w=[sfk])
                    P.act(lambda e: e.activation(out=sf[:, 0:n], in_=sf[:, 0:n], func=AF.Ln, bias=1.0),
                          r=[sfk], w=[sfk])
                    P.act(lambda e: e.activation(out=sf[:, 0:n], in_=sf[:, 0:n], func=AF.Exp, scale=-1.0),
                          r=[sfk], w=[sfk])
                    P.dve(lambda e: e.tensor_tensor(out=sf[:, 0:n], in0=ps[b2][:, 0:n], in1=sf[:, 0:n], op=ALU.mult),
                          r=[PK(b2), sfk], w=[sfk])
                    sq = sqb[i % 2]
                    P.dve(lambda e: e.tensor_tensor(out=sq[:, 0:n], in0=sf[:, 0:n], in1=sf[:, 0:n], op=ALU.mult),
                          r=[sfk], w=[("sqb", i % 2)])

                def S2(grp=grp, hh=hh, bi=bi, t0=t0, n=n, i=i):
                    pb = 4 + i % 2
                    if grp == 2:
                        if bi == 4:
                            return
                        vt = vtmp[i % 2]
                        vk = ("vtmp", i % 2)
                        pv = psb(pb)
                        for tl in range(4):
                            P.pe(lambda e, tl=tl: e.transpose(out=pv[:, tl * 128:(tl + 1) * 128],
                                                              in_=vt[:, tl * 128:(tl + 1) * 128], identity=idb[:]),
                                 r=[vk, "idb"], w=[PK(pb)])
                        for tl in range(4):
                            tg = bi * 4 + tl
                            P.dve(lambda e, tl=tl, tg=tg: e.tensor_scalar(
                                out=vb[:, tg, hh * 128:(hh + 1) * 128], in0=pv[:, tl * 128:(tl + 1) * 128],
                                scalar1=btok[:, tg, hh:hh + 1], scalar2=None, op0=ALU.mult),
                                r=[PK(pb), "btok"], w=["vb"])
                        return
                    dst = qT if grp == 0 else kT
                    dk = "qT" if grp == 0 else "kT"
                    sf = sfl[i % 2]
                    sfk = ("sfl", i % 2)
                    sq = sqb[i % 2]
                    P.pe(lambda e: e.matmul(ps[pb][:, 0:n], lhsT=oneb[:], rhs=sq[:, 0:n], start=True, stop=True),
                         r=[("sqb", i % 2), "oneb"], w=[PK(pb)])
                    rn = rnb[i % 2]
                    rk_ = ("rnb", i % 2)
                    sc_ = 128.0 if grp == 0 else 1.0
                    P.act(lambda e: e.activation(out=rn[:, 0:n], in_=ps[pb][:, 0:n], func=AF.Ln, scale=sc_,
                                                 bias=1e-6 * sc_), r=[PK(pb)], w=[rk_])
                    P.act(lambda e: e.activation(out=rn[:, 0:n], in_=rn[:, 0:n], func=AF.Exp, scale=-0.5), r=[rk_],
                          w=[rk_])
                    P.dve(lambda e: e.tensor_tensor(out=dst[:, hh, t0:t0 + n], in0=sf[:, 0:n], in1=rn[:, 0:n],
                                                    op=ALU.mult), r=[sfk, rk_], w=[dk])

                def S3(grp=grp, hh=hh, bi=bi, t0=t0, i=i):
                    if not (grp == 1 and bi < 4):
                        return
                    pb2 = 6 + i % 2
                    pv = psb(pb2)
                    for tl in range(4):
                        P.pe(lambda e, tl=tl: e.transpose(out=pv[:, tl * 128:(tl + 1) * 128],
                                                          in_=kT[:, hh, t0 + tl * 128:t0 + (tl + 1) * 128],
                                                          identity=idb[:]), r=["kT", "idb"], w=[PK(pb2)])
                    P.dve(lambda e: e.tensor_copy(out=ktok[:, bi * 4:(bi + 1) * 4, hh * 128:(hh + 1) * 128],
                                                  in_=pv[:, 0:512].rearrange("p (t d) -> p t d", t=4)),
                          r=[PK(pb2)], w=["ktok"])

                iters.append([S0, S1, S2, S3])
    run_skewed(iters)
    otp = A.t([16, 1536], F32, "otp")
    for ch in range(12):
        b = ch // 4
        P.pe(lambda e, ch=ch, b=b: e.transpose(out=ps[b][0:3, (ch % 4) * 128:(ch % 4 + 1) * 128],
                                               in_=ptail[:, ch, 0:3], identity=ident_f), r=["ptail", "cst"],
             w=[PK(b)])
    for b in range(3):
        P.act(lambda e, b=b: e.activation(out=otp[0:3, b * 512:(b + 1) * 512], in_=ps[b][0:3, :], func=AF.Copy),
              r=[PK(b)], w=["otp"])
    P.dma(nsc_p[:, :], otp[0:3, :], r=["otp"], tag="ootp")
    otp2 = otp
    for ch in range(12):
        b = 3 + ch // 4
        P.pe(lambda e, ch=ch, b=b: e.transpose(out=ps[b][0:NS, (ch % 4) * 128:(ch % 4 + 1) * 128],
                                               in_=pnew_s[:, ch, :], identity=ident_f), r=["pnews", "cst"],
             w=[PK(b)])
    for b in range(3):
        P.act(lambda e, b=b: e.activation(out=otp2[:, b * 512:(b + 1) * 512], in_=ps[3 + b][0:NS, :], func=AF.Copy),
              r=[PK(3 + b)], w=["otp"])
    P.dma(nsc_s[:, 2, :], otp2[:, :], r=["otp"], tag="ootp")

    for t in range(NT):
        b = t % 2
        mm8(b, lambda kc: hT[:, kc, t * 128:(t + 1) * 128], lambda kc: wsl[2][:, kc, 0:512], 512,
            [("hT", t // 4), ("wsl", 2)])
        P.act(lambda e, t=t, b=b: e.activation(out=zs[:, t, :], in_=ps[b][:, 0:512], func=AF.Silu), r=[PK(b)],
              w=["zs"])
    for hh in range(4):
        b = 2 + hh % 2
        mm8(b, lambda kc: wsl[2][:, kc, hh * 128:(hh + 1) * 128], lambda kc: hT[:, kc, T:TT], NS,
            [("hT", 4), ("wsl", 2)])
        P.act(lambda e, hh=hh, b=b: e.activation(out=zsT[:, hh, :], in_=ps[b][:, 0:NS], func=AF.Silu), r=[PK(b)],
              w=["zsT"])

    stop_at(4)
    build_rest(nc, P, A, locals())
    return nc, P


def build_rest(nc, P, A, L):
    g = dict(L)
    from types import SimpleNamespace
    V = SimpleNamespace(**g)
    ps, PK, psb, cst, pc, idb, oneb = V.ps, V.PK, V.psb, V.cst, V.pc, V.idb, V.oneb
    ident_f, Uf, SLf, onef = V.ident_f, V.Uf, V.SLf, V.onef
    hT, cT, qT, kT, ktok, vb, zs, zsT, vTs, gtok, btok, gbS = (V.hT, V.cT, V.qT, V.kT, V.ktok, V.vb, V.zs, V.zsT,
                                                                 V.vTs, V.gtok, V.btok, V.gbS)
    load_w, load_gam, mm8, gam = V.load_w, V.load_gam, V.mm8, V.gam
    pr_d = V.pr_d

    P.barrier()
    A.reset(V.m_w)
    dT = hT

    f32t = lambda name, shape=(128, 512): A.t(list(shape), F32, name)
    bft = lambda name, shape=(128, 512): A.t(list(shape), BF16, name)
    dn4 = f32t("dn4")
    P.dma(dn4[:], pr_d[:, PR_DN4:PR_DN4 + 512], w=["dn4"], tag="c3")
    S = f32t("S")
    Sb = bft("Sb")
    P.dve(lambda e: e.memset(S[:], 0.0), w=["S"])
    P.dve(lambda e: e.memset(Sb[:], 0.0), w=["Sb"])
    e3_2 = [f32t("e3_%d" % i, (128, 16)) for i in range(2)]
    gSL = f32t("gSL")
    E = f32t("E")
    ET = f32t("ET")
    EBbd = f32t("EBbd")
    EBoff = bft("EBoff")
    ETm = bft("ETm")
    Y = [bft("Y0"), bft("Y1")]
    YT = [bft("YT0"), bft("YT1")]
    PT = [bft("PT0"), bft("PT1")]
    Loff = bft("Loff")
    Tbd = bft("Tbd")
    Xb = bft("Xb")
    TTm = bft("TTm")
    kbg = bft("kbg")
    kdec_2 = [bft("kdec%d" % i) for i in range(2)]
    qkT_2 = [bft("qkT%d" % i) for i in range(2)]
    wT_2 = [bft("wT%d" % i) for i in range(2)]
    u_2 = [f32t("u_sb%d" % i) for i in range(2)]
    vnew = bft("vnew")
    o_sb = f32t("o_sb")
    qS_sb = f32t("qS_sb")
    bg_m = f32t("bg_m", (128, 8))
    bg_c = f32t("bg_c", (128, 8))
    dtok = bft("dtok")
    osq = V.nrm_junk
    B4 = lambda ap: ap.unsqueeze(1).to_broadcast([128, 4, 128])
    H4 = lambda ap: ap.rearrange("p (h n) -> p h n", h=4)
    mBD = B4(cst[:, C_BD:C_BD + 128])
    mOFF = B4(cst[:, C_OFF:C_OFF + 128])
    mTIN = B4(cst[:, C_TIN:C_TIN + 128])
    mBD4 = f32t("mBD4")
    P.pool(lambda e: e.tensor_copy(out=H4(mBD4[:]), in_=mBD), r=["cst"], w=["mBD4"])
    HS = [slice(h * 128, (h + 1) * 128) for h in range(4)]

    def make_tile(t):
        tk = slice(t * 128, (t + 1) * 128)
        p = t % 2
        e3, e3k = e3_2[p], ("e3", p)
        egc, erem, etot = e3[:, 0:4], e3[:, 4:8], e3[:, 8:12]
        qkT, qkk = qkT_2[p], ("qkT", p)
        wT, wTk = wT_2[p], ("wT", p)
        u_sb, uk = u_2[p], ("u_sb", p)
        kdec, kdk = kdec_2[p], ("kdec", p)
        gt = gtok[:, t, :]
        st = {"cur": 0}

        def A_():
            P.pe(lambda e: e.matmul(ps[0][:, 0:4], lhsT=Uf, rhs=gt, start=True, stop=True), r=["gtok", "cst"],
                 w=[PK(0)])
            P.pe(lambda e: e.matmul(ps[0][:, 4:8], lhsT=SLf, rhs=gt, start=True, stop=True), r=["gtok", "cst"],
                 w=[PK(0)])
            P.pe(lambda e: e.matmul(ps[0][:, 8:12], lhsT=onef, rhs=gt, start=True, stop=True), r=["gtok", "cst"],
                 w=[PK(0)])
            P.act(lambda e: e.activation(out=e3[:, 0:12], in_=ps[0][:, 0:12], func=AF.Exp), r=[PK(0)], w=[e3k])
            for h in range(4):
                P.dve(lambda e, h=h: e.tensor_scalar(out=gSL[:, HS[h]], in0=SLf, scalar1=gt[:, h:h + 1],
                                                     scalar2=None, op0=ALU.mult), r=["gtok", "cst"], w=["gSL"])
            P.pe(lambda e: e.matmul(ps[1][:, :], lhsT=Uf, rhs=gSL[:, :], start=True, stop=True), r=["gSL", "cst"],
                 w=[PK(1)])
            for h in range(4):
                P.pe(lambda e, h=h: e.matmul(ps[2][:, HS[h]], lhsT=gSL[:, HS[h]], rhs=Uf, start=True, stop=True),
                     r=["gSL", "cst"], w=[PK(2)])
            P.act(lambda e: e.activation(out=E[:], in_=ps[1][:, :], func=AF.Exp), r=[PK(1)], w=["E"])
            P.act(lambda e: e.activation(out=ET[:], in_=ps[2][:, :], func=AF.Exp), r=[PK(2)], w=["ET"])
            for h in range(4):
                P.dve(lambda e, h=h: e.tensor_scalar(out=E[:, HS[h]], in0=E[:, HS[h]], scalar1=btok[:, t, h:h + 1],
                                                     scalar2=None, op0=ALU.mult), r=["E", "btok"], w=["E"])
            P.dve(lambda e: e.tensor_tensor(out=EBbd[:], in0=E[:], in1=mBD4[:], op=ALU.mult), r=["E", "mBD4"],
                  w=["EBbd"])
            P.pool(lambda e: e.tensor_tensor(out=H4(EBoff[:]), in0=H4(E[:]), in1=mOFF, op=ALU.mult), r=["E", "cst"],
                   w=["EBoff"])
            P.pool(lambda e: e.tensor_tensor(out=H4(ETm[:]), in0=H4(ET[:]), in1=mTIN, op=ALU.mult), r=["ET", "cst"],
                   w=["ETm"])
            for h in range(4):
                P.pe(lambda e, h=h: e.matmul(ps[3][:, HS[h]], lhsT=kT[:, h, tk], rhs=kT[:, h, tk], start=True,
                                             stop=True), r=["kT"], w=[PK(3)])
            for h in range(4):
                P.pe(lambda e, h=h: e.matmul(ps[4][:, HS[h]], lhsT=kT[:, h, tk], rhs=qT[:, h, tk], start=True,
                                             stop=True), r=["kT", "qT"], w=[PK(4)])
            P.dve(lambda e: e.tensor_tensor(out=Y[0][:], in0=ps[3][:, :], in1=EBbd[:], op=ALU.mult),
                  r=[PK(3), "EBbd"], w=[("Y", 0)])
            pv = psb(5)
            for h in range(4):
                P.pe(lambda e, h=h: e.transpose(out=pv[:, HS[h]], in_=Y[0][:, HS[h]], identity=idb[:]),
                     r=[("Y", 0), "idb"], w=[PK(5)])
            P.act(lambda e: e.activation(out=YT[0][:], in_=pv[:, 0:512], func=AF.Copy), r=[PK(5)], w=[("YT", 0)])
            P.pool(lambda e: e.tensor_tensor(out=H4(PT[0][:]), in0=H4(YT[0][:]), in1=B4(ident_f), op=ALU.add),
                   r=[("YT", 0), "cst"], w=[("PT", 0)])
            P.dve(lambda e: e.tensor_tensor(out=Loff[:], in0=ps[3][:, :], in1=EBoff[:], op=ALU.mult),
                  r=[PK(3), "EBoff"], w=["Loff"])
            P.dve(lambda e: e.tensor_tensor(out=qkT[:], in0=ps[4][:, :], in1=ETm[:], op=ALU.mult),
                  r=[PK(4), "ETm"], w=[qkk])
            st["cur"] = 0

        def N_(m):
            cur = st["cur"]
            nx = 1 - cur
            for h in range(4):
                P.pe(lambda e, h=h: e.matmul(ps[6][:, HS[h]], lhsT=YT[cur][:, HS[h]], rhs=Y[cur][:, HS[h]],
                                             start=True, stop=True), r=[("Y", cur), ("YT", cur)], w=[PK(6)])
            if m < 5:
                for h in range(4):
                    P.pe(lambda e, h=h: e.matmul(ps[7][:, HS[h]], lhsT=Y[cur][:, HS[h]], rhs=YT[cur][:, HS[h]],
                                                 start=True, stop=True), r=[("Y", cur), ("YT", cur)], w=[PK(7)])
            P.act(lambda e: e.activation(out=Y[nx][:], in_=ps[6][:, :], func=AF.Copy), r=[PK(6)], w=[("Y", nx)])
            if m < 5:
                P.dve(lambda e: e.tensor_copy(out=YT[nx][:], in_=ps[7][:, :]), r=[PK(7)], w=[("YT", nx)])
            for h in range(4):
                P.pe(lambda e, h=h: e.matmul(ps[5][:, HS[h]], lhsT=Y[nx][:, HS[h]], rhs=PT[cur][:, HS[h]],
                                             start=True, stop=True), r=[("Y", nx), ("PT", cur)], w=[PK(5)])
            P.dve(lambda e: e.tensor_tensor(out=PT[nx][:], in0=ps[5][:, :], in1=PT[cur][:], op=ALU.add),
                  r=[PK(5), ("PT", cur)], w=[("PT", nx)])
            st["cur"] = nx

        def M_():
            cur = st["cur"]
            PTf, ptk = PT[cur], ("PT", cur)
            pv = psb(6)
            for h in range(4):
                P.pe(lambda e, h=h: e.transpose(out=pv[:, HS[h]], in_=PTf[:, HS[h]], identity=idb[:]),
                     r=[ptk, "idb"], w=[PK(6)])
            P.act(lambda e: e.activation(out=Tbd[:], in_=pv[:, 0:512], func=AF.Copy), r=[PK(6)], w=["Tbd"])
            for h in range(4):
                P.pe(lambda e, h=h: e.matmul(ps[7][:, HS[h]], lhsT=Loff[:, HS[h]], rhs=PTf[:, HS[h]], start=True,
                                             stop=True), r=["Loff", ptk], w=[PK(7)])
            P.act(lambda e: e.activation(out=Xb[:], in_=ps[7][:, :], func=AF.Copy), r=[PK(7)], w=["Xb"])
            for h in range(4):
                P.pe(lambda e, h=h: e.matmul(ps[5][:, HS[h]], lhsT=Tbd[:, HS[h]], rhs=Xb[:, HS[h]], start=True,
                                             stop=True), r=["Tbd", "Xb"], w=[PK(5)])
            P.dve(lambda e: e.tensor_tensor(out=TTm[:], in0=PTf[:], in1=ps[5][:, :], op=ALU.subtract),
                  r=[PK(5), ptk], w=["TTm"])
            P.dve(lambda e: e.tensor_tensor(out=bg_m[:, 0:4], in0=btok[:, t, :], in1=egc, op=ALU.mult),
                  r=["btok", e3k], w=["bg_m"])
            for h in range(4):
                P.dve(lambda e, h=h: e.tensor_scalar(out=kbg[:, HS[h]], in0=ktok[:, t, HS[h]],
                                                     scalar1=bg_m[:, h:h + 1], scalar2=None, op0=ALU.mult),
                      r=["ktok", "bg_m"], w=["kbg"])
                P.dve(lambda e, h=h: e.tensor_scalar(out=kdec[:, HS[h]], in0=ktok[:, t, HS[h]],
                                                     scalar1=erem[:, h:h + 1], scalar2=None, op0=ALU.mult),
                      r=["ktok", e3k], w=[kdk])
            for h in range(4):
                P.pe(lambda e, h=h: e.matmul(ps[0][:, HS[h]], lhsT=TTm[:, HS[h]], rhs=vb[:, t, HS[h]], start=True,
                                             stop=True), r=["TTm", "vb"], w=[PK(0)])
            for h in range(4):
                P.pe(lambda e, h=h: e.matmul(ps[1][:, HS[h]], lhsT=kbg[:, HS[h]], rhs=TTm[:, HS[h]], start=True,
                                             stop=True), r=["TTm", "kbg"], w=[PK(1)])
            P.act(lambda e: e.activation(out=u_sb[:], in_=ps[0][:, :], func=AF.Copy), r=[PK(0)], w=[uk])
            P.act(lambda e: e.activation(out=wT[:], in_=ps[1][:, :], func=AF.Copy), r=[PK(1)], w=[wTk])

        def C1():
            for h in range(4):
                P.pe(lambda e, h=h: e.matmul(ps[2][:, HS[h]], lhsT=wT[:, HS[h]], rhs=Sb[:, HS[h]], start=True,
                                             stop=True), r=[wTk, "Sb"], w=[PK(2)])
            for h in range(4):
                P.pe(lambda e, h=h: e.matmul(ps[3][:, HS[h]], lhsT=qT[:, h, tk], rhs=Sb[:, HS[h]], start=True,
                                             stop=True), r=["qT", "Sb"], w=[PK(3)])
            P.dve(lambda e: e.tensor_tensor(out=vnew[:], in0=u_sb[:], in1=ps[2][:, :], op=ALU.subtract),
                  r=[uk, PK(2)], w=["vnew"])

        def C2():
            for h in range(4):
                P.pe(lambda e, h=h: e.matmul(ps[4][:, HS[h]], lhsT=qkT[:, HS[h]], rhs=vnew[:, HS[h]], start=True,
                                             stop=True), r=[qkk, "vnew"], w=[PK(4)])
            for h in range(4):
                P.pe(lambda e, h=h: e.matmul(ps[2][:, HS[h]], lhsT=kdec[:, HS[h]], rhs=vnew[:, HS[h]], start=True,
                                             stop=True), r=[kdk, "vnew"], w=[PK(2)])
            for h in range(4):
                P.act(lambda e, h=h: e.activation(out=qS_sb[:, HS[h]], in_=ps[3][:, HS[h]], func=AF.Copy,
                                                  scale=egc[:, h:h + 1]), r=[PK(3), e3k], w=["qS_sb"])

        def C3():
            for h in range(4):
                P.dve(lambda e, h=h: e.scalar_tensor_tensor(out=S[:, HS[h]], in0=S[:, HS[h]],
                                                            scalar=etot[:, h:h + 1], in1=ps[2][:, HS[h]],
                                                            op0=ALU.mult, op1=ALU.add),
                      r=["S", e3k, PK(2)], w=["S"])
            P.act(lambda e: e.activation(out=Sb[:], in_=S[:], func=AF.Copy), r=["S"], w=["Sb"])
            P.dve(lambda e: e.tensor_tensor(out=o_sb[:], in0=qS_sb[:], in1=ps[4][:, :], op=ALU.add),
                  r=["qS_sb", PK(4)], w=["o_sb"])

        def C4():
            for h in range(4):
                P.act(lambda e, h=h: e.activation(out=osq[:, HS[h]], in_=o_sb[:, HS[h]], func=AF.Square,
                                                  accum_out=bg_c[:, 4 + h:5 + h]), r=["o_sb"], w=["njunk", "bg_c"])
            P.dve(lambda e: e.tensor_scalar(out=bg_c[:, 4:8], in0=bg_c[:, 4:8], scalar1=1.0 / 128, scalar2=1e-6,
                                            op0=ALU.mult, op1=ALU.add), r=["bg_c"], w=["bg_c"])
            P.act(lambda e: e.activation(out=bg_c[:, 4:8], in_=bg_c[:, 4:8], func=AF.Ln), r=["bg_c"], w=["bg_c"])
            P.act(lambda e: e.activation(out=bg_c[:, 4:8], in_=bg_c[:, 4:8], func=AF.Exp, scale=-0.5), r=["bg_c"],
                  w=["bg_c"])

        def C5():
            for h in range(4):
                P.dve(lambda e, h=h: e.scalar_tensor_tensor(out=o_sb[:, HS[h]], in0=o_sb[:, HS[h]],
                                                            scalar=bg_c[:, 4 + h:5 + h], in1=dn4[:, HS[h]],
                                                            op0=ALU.mult, op1=ALU.mult),
                      r=["o_sb", "bg_c", "dn4"], w=["o_sb"])
            P.dve(lambda e: e.tensor_tensor(out=dtok[:], in0=o_sb[:], in1=zs[:, t, :], op=ALU.mult),
                  r=["o_sb", "zs"], w=["dtok"])
            pv = psb(3)
            for h in range(4):
                P.pe(lambda e, h=h: e.transpose(out=pv[:, HS[h]], in_=dtok[:, HS[h]], identity=idb[:]),
                     r=["dtok", "idb"], w=[PK(3)])
            P.act(lambda e: e.activation(out=dT[:, 0:4, t * 128:(t + 1) * 128],
                                         in_=pv[:, 0:512].rearrange("p (h n) -> p h n", h=4), func=AF.Copy),
                  r=[PK(3)], w=["dT"])

        return A_, N_, M_, [C1, C2, C3, C4, C5]

    tiles = [make_tile(t) for t in range(NT)]
    A0, N0, M0, _ = tiles[0]
    A0()
    for m in range(1, 6):
        N0(m)
    M0()
    for t in range(NT):
        Cs = tiles[t][3]
        if t + 1 < NT:
            An, Nn, Mn, _ = tiles[t + 1]
            An()
            for m in range(1, 6):
                Nn(m)
                Cs[m - 1]()
            Mn()
        else:
            for c in Cs:
                c()
    P.dma(V.ndel_p.rearrange("h d e -> d h e"), H4(S[:]), r=["S"], tag="out")

    stop_at(5)
    P.barrier()
    A.reset(V.m_w)
    sel4f = A.t([4, 512], F32, "sel4f")
    P.pool(lambda e: e.tensor_copy(out=sel4f[:].rearrange("p (s m) -> p s m", s=4),
                                   in_=cst[0:4, C_OH16:C_OH16 + 4].unsqueeze(2).to_broadcast([4, 4, 128])),
           r=["cst"], w=["sel4f"])
    sel4 = sel4f[0:4, :]
    egS = f32t("egS", (4, NS))
    P.act(lambda e: e.activation(out=egS[:], in_=gbS[:, 1, :], func=AF.Exp), r=["gbS"], w=["egS"])
    Bbc = f32t("Bbc", (128, 4, NS))
    EGbc = f32t("EGbc", (128, 4, NS))
    for h in range(4):
        P.pe(lambda e, h=h: e.matmul(ps[0][:, h * NS:(h + 1) * NS], lhsT=sel4[:, h * 128:(h + 1) * 128],
                                     rhs=gbS[:, 0, :], start=True, stop=True), r=["gbS", "sel4f"], w=[PK(0)])
        P.pe(lambda e, h=h: e.matmul(ps[1][:, h * NS:(h + 1) * NS], lhsT=sel4[:, h * 128:(h + 1) * 128],
                                     rhs=egS[:, :], start=True, stop=True), r=["egS", "sel4f"], w=[PK(1)])
    P.act(lambda e: e.activation(out=Bbc[:].rearrange("p h s -> p (h s)"), in_=ps[0][:, 0:64], func=AF.Copy),
          r=[PK(0)], w=["Bbc"])
    P.act(lambda e: e.activation(out=EGbc[:].rearrange("p h s -> p (h s)"), in_=ps[1][:, 0:64], func=AF.Copy),
          r=[PK(1)], w=["EGbc"])
    qkf = f32t("qkf", (128, 4, 2, NS))
    P.dve(lambda e: e.tensor_copy(out=qkf[:, :, 0, :], in_=kT[:, :, T:TT]), r=["kT"], w=["qkf"])
    P.dve(lambda e: e.tensor_copy(out=qkf[:, :, 1, :], in_=qT[:, :, T:TT]), r=["qT"], w=["qkf"])
    S0all = f32t("S0all", (128, NS, 4, 128))
    for s in range(NS):
        P.dma(S0all[:, s, :, :], V.sdel[s * 4:(s + 1) * 4, :, :].rearrange("h d e -> d h e"), w=[("S0", s)],
              tag="S0g%d" % (s // 4), group=True)
    for s in range(NS):
        for h in range(4):
            P.pe(lambda e, s=s, h=h: e.matmul(ps[2][:, (s * 4 + h) * 2:(s * 4 + h) * 2 + 2],
                                              lhsT=S0all[:, s, h, :], rhs=qkf[:, h, :, s], start=True,
                                              stop=True), r=[("S0", s), "qkf"], w=[PK(2)])
    kqS = f32t("kqS", (128, NS, 4, 2))
    P.act(lambda e: e.activation(out=kqS[:].rearrange("p s h t -> p (s h t)"), in_=ps[2][:, 0:128], func=AF.Copy),
          r=[PK(2)], w=["kqS"])
    kSv = kqS[:, :, :, 0].rearrange("p s h -> p h s")
    qSv = kqS[:, :, :, 1].rearrange("p s h -> p h s")
    vn = f32t("vn", (128, 4, NS))
    tmpS = f32t("tmpS", (128, 4, NS))
    P.dve(lambda e: e.tensor_tensor(out=tmpS[:], in0=EGbc[:], in1=kSv, op=ALU.mult), r=["EGbc", "kqS"], w=["tmpS"])
    P.dve(lambda e: e.tensor_tensor(out=vn[:], in0=vTs[:], in1=tmpS[:], op=ALU.subtract), r=["vTs", "tmpS"],
          w=["vn"])
    P.dve(lambda e: e.tensor_tensor(out=vn[:], in0=vn[:], in1=Bbc[:], op=ALU.mult), r=["vn", "Bbc"], w=["vn"])
    prodS = f32t("prodS", (128, 4, NS))
    P.dve(lambda e: e.tensor_tensor(out=prodS[:], in0=qkf[:, :, 0, :], in1=qkf[:, :, 1, :], op=ALU.mult),
          r=["qkf"], w=["prodS"])
    P.pe(lambda e: e.matmul(ps[3][:, 0:64], lhsT=onef, rhs=prodS[:].rearrange("p h s -> p (h s)"), start=True,
                            stop=True), r=["prodS", "cst"], w=[PK(3)])
    oS = f32t("oS", (128, 4, NS))
    P.dve(lambda e: e.tensor_tensor(out=oS[:].rearrange("p h s -> p (h s)"), in0=ps[3][:, 0:64],
                                    in1=vn[:].rearrange("p h s -> p (h s)"), op=ALU.mult), r=[PK(3), "vn"],
          w=["oS"])
    P.dve(lambda e: e.tensor_tensor(out=tmpS[:], in0=EGbc[:], in1=qSv, op=ALU.mult), r=["EGbc", "kqS"], w=["tmpS"])
    P.dve(lambda e: e.tensor_tensor(out=oS[:], in0=oS[:], in1=tmpS[:], op=ALU.add), r=["oS", "tmpS"], w=["oS"])
    P.dve(lambda e: e.tensor_tensor(out=tmpS[:], in0=oS[:], in1=oS[:], op=ALU.mult), r=["oS"], w=["tmpS"])
    P.pe(lambda e: e.matmul(ps[4][:, 0:64], lhsT=onef, rhs=tmpS[:].rearrange("p h s -> p (h s)"), start=True,
                            stop=True), r=["tmpS", "cst"], w=[PK(4)])
    rS = f32t("rS", (128, 64))
    P.dve(lambda e: e.tensor_scalar(out=rS[:], in0=ps[4][:, 0:64], scalar1=1.0 / 128, scalar2=1e-6, op0=ALU.mult,
                                    op1=ALU.add), r=[PK(4)], w=["rS"])
    P.act(lambda e: e.activation(out=rS[:], in_=rS[:], func=AF.Sqrt), r=["rS"], w=["rS"])
    P.dve(lambda e: e.reciprocal(out=rS[:], in_=rS[:]), r=["rS"], w=["rS"])
    P.dve(lambda e: e.tensor_tensor(out=oS[:].rearrange("p h s -> p (h s)"), in0=oS[:].rearrange("p h s -> p (h s)"),
                                    in1=rS[:], op=ALU.mult), r=["oS", "rS"], w=["oS"])
    P.dve(lambda e: e.scalar_tensor_tensor(out=dT[:, 0:4, T:TT], in0=oS[:], scalar=pc[:, PC_DN:PC_DN + 1],
                                           in1=zsT[:], op0=ALU.mult, op1=ALU.mult), r=["oS", "pc", "zsT"],
          w=["dT"])
    ktS = f32t("ktS", (16, 512))
    vtS = f32t("vtS", (16, 512))
    for h in range(4):
        P.pe(lambda e, h=h: e.transpose(out=ps[5][0:NS, h * 128:(h + 1) * 128], in_=qkf[:, h, 0, :],
                                        identity=ident_f), r=["qkf", "cst"], w=[PK(5)])
        P.pe(lambda e, h=h: e.transpose(out=ps[6][0:NS, h * 128:(h + 1) * 128], in_=vn[:, h, :], identity=ident_f),
             r=["vn", "cst"], w=[PK(6)])
    P.act(lambda e: e.activation(out=ktS[:], in_=ps[5][0:NS, :], func=AF.Copy), r=[PK(5)], w=["ktS"])
    P.act(lambda e: e.activation(out=vtS[:], in_=ps[6][0:NS, :], func=AF.Copy), r=[PK(6)], w=["vtS"])
    vmask = [f32t("vmask%d" % i, (16, 512)) for i in range(2)]
    oh16 = cst[0:16, C_OH16:C_OH16 + 16]
    for s in range(NS):
        sl = s % 2
        P.dve(lambda e, s=s, sl=sl: e.tensor_scalar(out=vmask[sl][:], in0=vtS[:], scalar1=oh16[:, s:s + 1],
                                                    scalar2=None, op0=ALU.mult), r=["vtS", "cst"],
              w=[("vmask", sl)])
        b = 7 if sl else 0
        for h in range(4):
            hs = slice(h * 128, (h + 1) * 128)
            P.pe(lambda e, hs=hs, sl=sl, b=b: e.matmul(ps[b][:, hs], lhsT=ktS[:, hs], rhs=vmask[sl][:, hs],
                                                       start=True, stop=True), r=["ktS", ("vmask", sl)], w=[PK(b)])
        for h in range(4):
            hs = slice(h * 128, (h + 1) * 128)
            P.dve(lambda e, hs=hs, h=h, s=s, b=b: e.scalar_tensor_tensor(
                out=S0all[:, s, h, :], in0=S0all[:, s, h, :], scalar=EGbc[:, h, s:s + 1], in1=ps[b][:, hs],
                op0=ALU.mult, op1=ALU.add), r=["EGbc", PK(b)], w=[("S0", s)])
        P.dma(V.ndel_s[s * 4:(s + 1) * 4, :, :].rearrange("h d e -> d h e"), S0all[:, s, :, :], r=[("S0", s)],
              tag="out")

    stop_at(6)
    P.barrier()
    mX, mXe = V.mX, V.mXe
    x1 = A.child(mX, mXe).t([128, NT + 1, 1024], F32, "x1")
    A.reset(mXe)
    wo = A.t([128, 8, 512], BF16, "wo_a")
    wo2 = A.t([128, 8, 512], BF16, "wo_b")
    xin = [A.t([128, 1024], F32, "xin2_%d" % i) for i in range(2)]
    load_w(wo, "wo", V.w_out, 0, 8, 0, 512)
    load_w(wo2, "wo2", V.w_out, 0, 8, 512, 512)
    for t in range(NT + 1):
        n = 128 if t < NT else NS
        c0 = t * 128
        s = t % 2
        src_d = V.x_p[t * 128:(t + 1) * 128, :] if t < NT else V.x_s[:, :]
        P.dma(xin[s][:n, :], src_d, w=[("xin", s)], tag="xin%d" % s)
        for half, wv in ((0, wo), (1, wo2)):
            b = (t % 2) * 2 + half
            for kc in range(8):
                src = cT[:, kc, c0:c0 + n] if kc < 4 else dT[:, kc - 4, c0:c0 + n]
                P.pe(lambda e, b=b, kc=kc, src=src, wv=wv, n=n: e.matmul(ps[b][:n, :], lhsT=src, rhs=wv[:, kc, :],
                                                                        start=(kc == 0), stop=(kc == 7)),
                     r=["cT", "dT", "wo", "wo2"], w=[PK(b)])
            P.dve(lambda e, b=b, t=t, n=n, half=half, s=s: e.tensor_tensor(
                out=x1[:n, t, half * 512:(half + 1) * 512], in0=ps[b][:n, :],
                in1=xin[s][:n, half * 512:(half + 1) * 512], op=ALU.add), r=[PK(b), ("xin", s)], w=[("x1", t)])

    stop_at(7)
    P.barrier()
    A.reset(mXe)
    AH = A.child(V.off_cT, mX)
    kmT = AH.t([128, 8, 256], BF16, "kmT")
    vmem = AH.t([128, 2, 1024], BF16, "vmem")
    mT = AH.t([128, 8, 256], BF16, "mT")
    hT3 = hT
    wk2 = [A.t([128, 8, 512], BF16, "wk%d" % i) for i in range(2)]
    mo_f = A.t([128, 1024], F32, "mo_f")
    min_ = [A.t([128, 1024], F32, "min%d" % i) for i in range(2)]
    load_gam(PR_GKV)
    items = []
    for mt in range(2):
        pre = (lambda mt=mt: P.dma(min_[mt][:], V.mem[mt * 128:(mt + 1) * 128, :], w=[("min", mt)],
                                   tag="min%d" % mt))
        items.append((min_[mt][:, :], [("min", mt)], 128, mt * 128, pre))
    V.norm_pass(items, mT, "mT")
    kvl = [(0, V.w_mk, V.mk_p, 0), (0, V.w_mk, V.mk_p, 1), (1, V.w_mv, V.mv_p, 0), (1, V.w_mv, V.mv_p, 1)]
    load_w(wk2[0], ("wk", 0), V.w_mk, 0, 8, 0, 512)
    for li, (which, wd, od, half) in enumerate(kvl):
        if True:
            wk = wk2[li % 2]
            wkk = ("wk", li % 2)
            if li + 1 < 4:
                load_w(wk2[(li + 1) % 2], ("wk", (li + 1) % 2), kvl[li + 1][1], 0, 8, kvl[li + 1][3] * 512, 512)
            for mt in range(2):
                b = 2 + mt
                for kc in range(8):
                    P.pe(lambda e, b=b, kc=kc, mt=mt: e.matmul(ps[b][:, :], lhsT=mT[:, kc, mt * 128:(mt + 1) * 128],
                                                               rhs=wk[:, kc, :], start=(kc == 0), stop=(kc == 7)),
                         r=[("mT", 0), wkk], w=[PK(b)])
                P.act(lambda e, b=b, half=half: e.activation(out=mo_f[:, half * 512:(half + 1) * 512],
                                                             in_=ps[b][:, :], func=AF.Copy), r=[PK(b)], w=["mo_f"])
                P.dma(od[mt * 128:(mt + 1) * 128, half * 512:(half + 1) * 512],
                      mo_f[:, half * 512:(half + 1) * 512], r=["mo_f"], tag="omo")
                if which == 1:
                    P.dve(lambda e, b=b, half=half, mt=mt: e.tensor_copy(
                        out=vmem[:, mt, half * 512:(half + 1) * 512], in_=ps[b][:, :]), r=[PK(b)], w=["vmem"])
            if which == 0:
                for c4 in range(4):
                    b = 4 + c4 % 2
                    for kc in range(8):
                        P.pe(lambda e, b=b, kc=kc, c4=c4: e.matmul(ps[b][:, 0:256],
                                                                   lhsT=wk[:, kc, c4 * 128:(c4 + 1) * 128],
                                                                   rhs=mT[:, kc, :], start=(kc == 0), stop=(kc == 7)),
                             r=[("mT", 0), wkk], w=[PK(b)])
                    P.act(lambda e, b=b, c4=c4, half=half: e.activation(out=kmT[:, half * 4 + c4, :],
                                                                        in_=ps[b][:, 0:256], func=AF.Copy),
                          r=[PK(b)], w=["kmT"])
    stop_at(8)
    P.barrier()
    A.reset(mXe)
    qa = A.t([128, 8, TT], BF16, "qa")
    m_qa = A.mark()
    wq = A.t([128, 8, 512], BF16, "wq")
    wq2 = A.t([128, 8, 512], BF16, "wq2")
    load_gam(PR_GMQ)
    V.norm_pass([(x1[:(128 if t < NT else NS), t, :], [("x1", t)], (128 if t < NT else NS), t * 128, None)
                 for t in range(NT + 1)], hT3, "hT")
    load_w(wq, "wq", V.w_mq, 0, 8, 0, 512)
    load_w(wq2, "wq2", V.w_mq, 0, 8, 512, 512)
    BLK = [(i * 512, 512) for i in range(4)] + [(T, NS)]
    qi = 0
    for c in range(8):
        wv = wq if c < 4 else wq2
        cc = c % 4
        for (t0, n) in BLK:
            b = qi % 2
            qi += 1
            mm8(b, lambda kc: wv[:, kc, cc * 128:(cc + 1) * 128], lambda kc: hT3[:, kc, t0:t0 + n], n,
                [("hT", t0 // 512), "wq", "wq2"])
            P.act(lambda e, b=b, c=c, n=n, t0=t0: e.activation(out=qa[:, c, t0:t0 + n], in_=ps[b][:, 0:n],
                                                               func=AF.Copy, scale=1.0 / 16), r=[PK(b)], w=["qa"])
    stop_at(9)
    P.barrier()
    A.reset(m_qa)
    aoT = hT
    qs_s = AH.t([128, 8, NS], BF16, "qs_s")
    P.dve(lambda e: e.tensor_copy(out=qs_s[:], in_=qa[:, :, T:TT]), r=["qa"], w=["qs_s"])
    m_3c = A.mark()
    msm = [A.t([128, 16], F32, "msm%d" % i) for i in range(2)]
    ex = [A.t([128, 1024], F32, "ex%d" % i) for i in range(2)]
    pbf = [A.t([128, 1024], BF16, "pbf%d" % i) for i in range(2)]
    ptb = [A.t([128, 8, 128], BF16, "ptb%d" % i) for i in range(2)]
    iters = []
    for tg in range(NT):
        def S0(tg=tg):
            p = tg % 2
            lt = slice(tg * 128, (tg + 1) * 128)
            sb = (2 * p, 2 * p + 1)
            m_, mk_ = msm[p], ("msm", p)
            for h in range(4):
                b = sb[h // 2]
                cs = slice((h % 2) * 256, (h % 2) * 256 + 256)
                for hf in range(2):
                    P.pe(lambda e, b=b, cs=cs, h=h, hf=hf: e.matmul(ps[b][:, cs], lhsT=qa[:, h * 2 + hf, lt],
                                                                    rhs=kmT[:, h * 2 + hf, :], start=(hf == 0),
                                                                    stop=(hf == 1)), r=["qa", "kmT"], w=[PK(b)])
            for j in range(2):
                P.dve(lambda e, j=j: e.tensor_reduce(out=m_[:, 2 * j:2 * j + 2],
                                                     in_=ps[sb[j]][:, :].rearrange("p (h m) -> p h m", h=2),
                                                     axis=AX.X, op=ALU.max), r=[PK(sb[j])], w=[mk_])
            P.dve(lambda e: e.tensor_scalar(out=m_[:, 12:16], in0=m_[:, 0:4], scalar1=-1.0, scalar2=None,
                                            op0=ALU.mult), r=[mk_], w=[mk_])

        def S1(tg=tg):
            p = tg % 2
            sb = (2 * p, 2 * p + 1)
            m_, mk_ = msm[p], ("msm", p)
            ex_, exk = ex[p], ("ex", p)
            pb_, pbk = pbf[p], ("pbf", p)
            for h in range(4):
                P.act(lambda e, h=h: e.activation(out=ex_[:, h * 256:(h + 1) * 256],
                                                  in_=ps[sb[h // 2]][:, (h % 2) * 256:(h % 2) * 256 + 256],
                                                  func=AF.Exp, bias=m_[:, 12 + h:13 + h]),
                      r=[PK(sb[h // 2]), mk_], w=[exk])
            P.dve(lambda e: e.tensor_reduce(out=m_[:, 4:8], in_=ex_[:].rearrange("p (h m) -> p h m", h=4), axis=AX.X,
                                            op=ALU.add), r=[exk], w=[mk_])
            P.dve(lambda e: e.reciprocal(out=m_[:, 8:12], in_=m_[:, 4:8]), r=[mk_], w=[mk_])
            for h in range(4):
                P.dve(lambda e, h=h: e.tensor_scalar(out=pb_[:, h * 256:(h + 1) * 256],
                                                     in0=ex_[:, h * 256:(h + 1) * 256], scalar1=m_[:, 8 + h:9 + h],
                                                     scalar2=None, op0=ALU.mult), r=[exk, mk_], w=[pbk])

        def S2(tg=tg):
            p = tg % 2
            pb_, pbk = pbf[p], ("pbf", p)
            pt_, ptk = ptb[p], ("ptb", p)
            tb_ = 4 if p == 0 else 7
            pv = psb(tb_)
            for c8 in range(8):
                P.pe(lambda e, c8=c8: e.transpose(out=pv[:, c8 * 128:(c8 + 1) * 128],
                                                  in_=pb_[:, c8 * 128:(c8 + 1) * 128], identity=idb[:]),
                     r=[pbk, "idb"], w=[PK(tb_)])
            P.act(lambda e: e.activation(out=pt_[:].rearrange("p a b -> p (a b)"), in_=pv[:, 0:1024], func=AF.Copy),
                  r=[PK(tb_)], w=[ptk])

        def S3(tg=tg):
            p = tg % 2
            lt = slice(tg * 128, (tg + 1) * 128)
            pt_, ptk = ptb[p], ("ptb", p)
            for h in range(4):
                ob = 5 + h // 2
                for hf in range(2):
                    cs = slice(((h % 2) * 2 + hf) * 128, ((h % 2) * 2 + hf + 1) * 128)
                    for mt in range(2):
                        P.pe(lambda e, ob=ob, cs=cs, h=h, hf=hf, mt=mt: e.matmul(
                            ps[ob][:, cs], lhsT=vmem[:, mt, h * 256 + hf * 128:h * 256 + (hf + 1) * 128],
                            rhs=pt_[:, h * 2 + mt, :], start=(mt == 0), stop=(mt == 1)), r=["vmem", ptk],
                            w=[PK(ob)])
            P.act(lambda e: e.activation(out=aoT[:, 0:4, lt], in_=ps[5][:, :].rearrange("p (a n) -> p a n", a=4),
                                         func=AF.Copy), r=[PK(5)], w=["aoT"])
            P.dve(lambda e: e.tensor_copy(out=aoT[:, 4:8, lt], in_=ps[6][:, :].rearrange("p (a n) -> p a n", a=4)),
                  r=[PK(6)], w=["aoT"])

        iters.append([S0, S1, S2, S3])
    V.run_skewed(iters)
    P.barrier()
    A.reset(mXe)
    prod = A.t([128, 1024], F32, "prod")
    qtok = A.t([16, 1024], BF16, "qtok")
    sel16b = A.t([16, 2048], BF16, "sel16b")
    P.pool(lambda e: e.tensor_copy(out=sel16b[:].rearrange("p (s m) -> p s m", s=16),
                                   in_=cst[0:16, C_OH16:C_OH16 + 16].unsqueeze(2).to_broadcast([16, 16, 128])),
           r=["cst"], w=["sel16b"])
    NKS, NVS = 8, 6
    Kt2 = [A.t([128, 1024], BF16, "Kt%d" % i) for i in range(NKS)]
    Vt2 = [A.t([128, 2, 1024], BF16, "Vt%d" % i) for i in range(NVS)]
    scS = A.t([128, 2, 4], F32, "scS")
    sm4 = A.t([4, 256], F32, "sm4")
    sm4s = A.t([4, 8], F32, "sm4s")
    pS = A.t([128, 2, 4], BF16, "pS")
    aoS = A.t([128, 8, NS], F32, "aoS")
    pv = psb(2)
    for c in range(8):
        P.pe(lambda e, c=c: e.transpose(out=pv[0:NS, c * 128:(c + 1) * 128], in_=qs_s[:, c, :],
                                        identity=idb[:]), r=["qs_s", "idb"], w=[PK(2)])
    P.act(lambda e: e.activation(out=qtok[:], in_=pv[0:NS, 0:1024], func=AF.Copy), r=[PK(2)], w=["qtok"])
    Vt3 = Vt2
    scS2 = [scS, A.t([128, 2, 4], F32, "scSb")]
    sm42 = [sm4, A.t([4, 256], F32, "sm4b")]
    sm4s2 = [sm4s, A.t([4, 8], F32, "sm4sb")]
    iters = []
    for s in range(NS):
        def S0(s=s):
            Vt = Vt3[s % NVS]
            vk = ("Vt", s % NVS)
            sc_, sck = scS2[s % 2], ("scS", s % 2)
            for mt in range(2):
                P.dma(Vt[:, mt, :], V.cmv[s, mt * 128:(mt + 1) * 128, :], w=[vk], tag="Vt%d" % (s % NVS), q="pool")
            qb = (3, 4) if s % 2 == 0 else (0, 1)
            for half in range(2):
                P.pe(lambda e, half=half: e.matmul(ps[qb[half]][:, :], lhsT=sel16b[:, s * 128:(s + 1) * 128],
                                                   rhs=qtok[:, half * 512:(half + 1) * 512], start=True, stop=True),
                     r=["qtok", "sel16b"], w=[PK(qb[half])])
            for mt in range(2):
                kti = s * 2 + mt
                Kt = Kt2[kti % NKS]
                kk = ("Kt", kti % NKS)
                P.dma(Kt[:, :], V.cmk[s, mt * 128:(mt + 1) * 128, :], w=[kk], tag="Kt%d" % (kti % NKS), q="pool")
                for half in range(2):
                    P.dve(lambda e, half=half, Kt=Kt: e.tensor_tensor(out=prod[:, half * 512:(half + 1) * 512],
                                                                      in0=ps[qb[half]][:, :],
                                                                      in1=Kt[:, half * 512:(half + 1) * 512],
                                                                      op=ALU.mult),
                          r=[PK(qb[half]), kk], w=["prod"])
                P.dve(lambda e, mt=mt: e.tensor_reduce(out=sc_[:, mt, :],
                                                       in_=prod[:].rearrange("p (h d) -> p h d", h=4),
                                                       axis=AX.X, op=ALU.add), r=["prod"], w=[sck])

        def S1(s=s):
            sc_, sck = scS2[s % 2], ("scS", s % 2)
            m4, m4k = sm42[s % 2], ("sm4", s % 2)
            m4s, m4sk = sm4s2[s % 2], ("sm4s", s % 2)
            for mt in range(2):
                P.pe(lambda e, mt=mt: e.transpose(out=ps[5][0:4, mt * 128:(mt + 1) * 128], in_=sc_[:, mt, :],
                                                  identity=ident_f), r=[sck, "cst"], w=[PK(5)])
            P.dve(lambda e: e.tensor_reduce(out=m4s[:, 0:1], in_=ps[5][0:4, 0:256], axis=AX.X, op=ALU.max),
                  r=[PK(5)], w=[m4sk])
            P.dve(lambda e: e.tensor_scalar(out=m4s[:, 1:2], in0=m4s[:, 0:1], scalar1=-1.0, scalar2=None,
                                            op0=ALU.mult), r=[m4sk], w=[m4sk])
            P.act(lambda e: e.activation(out=m4[:], in_=ps[5][0:4, 0:256], func=AF.Exp, bias=m4s[:, 1:2],
                                         accum_out=m4s[:, 2:3]), r=[PK(5), m4sk], w=[m4k, m4sk])
            P.dve(lambda e: e.reciprocal(out=m4s[:, 3:4], in_=m4s[:, 2:3]), r=[m4sk], w=[m4sk])
            P.dve(lambda e: e.tensor_scalar(out=m4[:], in0=m4[:], scalar1=m4s[:, 3:4], scalar2=None, op0=ALU.mult),
                  r=[m4k, m4sk], w=[m4k])

        def S2(s=s):
            Vt = Vt3[s % NVS]
            vk = ("Vt", s % NVS)
            m4, m4k = sm42[s % 2], ("sm4", s % 2)
            for mt in range(2):
                P.pe(lambda e, mt=mt: e.transpose(out=ps[6][:, mt * 4:(mt + 1) * 4],
                                                  in_=m4[:, mt * 128:(mt + 1) * 128], identity=ident_f[0:4, 0:4]),
                     r=[m4k, "cst"], w=[PK(6)])
            P.act(lambda e: e.activation(out=pS[:].rearrange("p a h -> p (a h)"), in_=ps[6][:, 0:8], func=AF.Copy),
                  r=[PK(6)], w=["pS"])
            for h in range(4):
                for hf in range(2):
                    c = h * 2 + hf
                    for mt in range(2):
                        P.pe(lambda e, c=c, mt=mt, h=h: e.matmul(ps[7][:, c:c + 1],
                                                                 lhsT=Vt[:, mt, c * 128:(c + 1) * 128],
                                                                 rhs=pS[:, mt, h:h + 1], start=(mt == 0),
                                                                 stop=(mt == 1)),
                             r=[vk, "pS"], w=[PK(7)])
            P.act(lambda e: e.activation(out=aoS[:, :, s], in_=ps[7][:, 0:8], func=AF.Copy), r=[PK(7)], w=["aoS"])

        iters.append([S0, S1, S2])
    V.run_skewed(iters)
    P.dve(lambda e: e.tensor_copy(out=aoT[:, :, T:TT], in_=aoS[:]), r=["aoS"], w=["aoT"])
    P.barrier()
    A.reset(mXe)
    wmo = A.t([128, 8, 512], BF16, "wmo")
    wmo2 = A.t([128, 8, 512], BF16, "wmo2")
    load_w(wmo, "wmo", V.w_mo, 0, 8, 0, 512)
    load_w(wmo2, "wmo2", V.w_mo, 0, 8, 512, 512)
    for t in range(NT + 1):
        n = 128 if t < NT else NS
        c0 = t * 128
        for half, wv in ((0, wmo), (1, wmo2)):
            b = (t % 2) * 2 + half
            for kc in range(8):
                P.pe(lambda e, b=b, kc=kc, wv=wv, n=n, c0=c0: e.matmul(ps[b][:n, :], lhsT=aoT[:, kc, c0:c0 + n],
                                                                      rhs=wv[:, kc, :], start=(kc == 0),
                                                                      stop=(kc == 7)), r=["aoT", "wmo", "wmo2"],
                     w=[PK(b)])
            P.dve(lambda e, b=b, t=t, n=n, half=half: e.tensor_tensor(
                out=x1[:n, t, half * 512:(half + 1) * 512], in0=ps[b][:n, :],
                in1=x1[:n, t, half * 512:(half + 1) * 512], op=ALU.add), r=[PK(b), ("x1", t)], w=[("x1", t)])

    stop_at(11)
    P.barrier()
    A.reset(mXe)
    TH = 1024
    AHh = A.child(V.off_hT, V.off_cT)
    hT4 = AHh.t([128, 8, TH + NS], BF16, "hT4")
    wg = [AHh.t([128, 8, 256], BF16, "wg%d" % i) for i in range(2)]
    wu = [AHh.t([128, 8, 256], BF16, "wu%d" % i) for i in range(2)]
    AHc = A.child(V.off_cT, mX)
    gF = AHc.t([128, 1024], F32, "gF")
    sgf = [AHc.t([128, 512], F32, "sgf%d" % i) for i in range(2)]
    fs = AHc.t([128, 8], F32, "fs")
    aT = A.t([128, 11, TH + NS], BF16, "aT")
    wdn = [A.t([128, 11, 512], BF16, "wdn%d" % i) for i in range(2)]
    fj = V.nrm_junk
    P.dma(gF[:], pr_d[:, PR_GF:PR_GF + 1024], w=["gF"], tag="c4")
    load_gam(PR_GFFN)
    gu_list = [(fh, gi) for _th in range(2) for fh in range(2) for gi in range(6)]

    def gu_load(idx):
        fh, gi = gu_list[idx]
        ncg = 256 if gi < 5 else 128
        c0 = fh * 1408 + gi * 256
        sl = idx % 2
        load_w(wg[sl], ("wg", sl), V.w_gate, 0, 8, c0, ncg)
        load_w(wu[sl], ("wu", sl), V.w_up, 0, 8, c0, ncg)

    fsl = [AHc.t([128, 8], F32, "fs%d" % i) for i in range(2)]
    fcnt = [0]

    def final_norm(t):
        n = 128 if t < NT else NS
        i_ = fcnt[0] % 2
        fcnt[0] += 1
        fs_ = fsl[i_]
        fk = ("fs", i_)
        P.act(lambda e: e.activation(out=fj[:n, :], in_=x1[:n, t, :], func=AF.Square, accum_out=fs_[:n, 0:1]),
              r=[("x1", t)], w=["njunk", fk])
        P.dve(lambda e: e.tensor_scalar(out=fs_[:n, 1:2], in0=fs_[:n, 0:1], scalar1=1.0 / 1024, scalar2=1e-6,
                                        op0=ALU.mult, op1=ALU.add), r=[fk], w=[fk])
        P.act(lambda e: e.activation(out=fs_[:n, 2:3], in_=fs_[:n, 1:2], func=AF.Sqrt), r=[fk], w=[fk])
        P.dve(lambda e: e.reciprocal(out=fs_[:n, 3:4], in_=fs_[:n, 2:3]), r=[fk], w=[fk])
        P.dve(lambda e: e.scalar_tensor_tensor(out=x1[:n, t, :], in0=x1[:n, t, :], scalar=fs_[:n, 3:4],
                                               in1=gF[:n, :], op0=ALU.mult, op1=ALU.mult),
              r=[("x1", t), fk, "gF"], w=[("x1", t)])
        dst_ = V.y_p[t * 128:(t + 1) * 128, :] if t < NT else V.y_s[:, :]
        P.dma(dst_, x1[:n, t, :], r=[("x1", t)], tag="out")

    gu_load(0)
    gidx = 0
    for th in range(2):
        tiles = list(range(th * 8, th * 8 + 8)) + ([NT] if th == 1 else [])
        V.norm_pass([(x1[:(128 if t < NT else NS), t, :], [("x1", t)], (128 if t < NT else NS), ti * 128, None)
                     for ti, t in enumerate(tiles)], hT4, "hT4")
        blks = [(0, 512), (512, 512)] + ([(1024, NS)] if th == 1 else [])
        for fh in range(2):
            for gi in range(6):
                cur = gidx
                gidx += 1
                if cur + 1 < len(gu_list):
                    gu_load(cur + 1)
                if gi == 2 or gi == 4:
                    hf_ = 0 if gi == 2 else 1
                    load_w(wdn[hf_], ("wdn", hf_), V.w_down, fh * 1408, 11, hf_ * 512, 512)
                ncg = 256 if gi < 5 else 128
                sl = cur % 2
                for cc in range(ncg // 128):
                    fc = gi * 2 + cc
                    for bi, (t0, n) in enumerate(blks):
                        bg, bu = (bi % 2) * 2, (bi % 2) * 2 + 1
                        mm8(bg, lambda kc: wg[sl][:, kc, cc * 128:(cc + 1) * 128],
                            lambda kc: hT4[:, kc, t0:t0 + n], n, [("hT4", t0 // 512), ("wg", sl)])
                        mm8(bu, lambda kc: wu[sl][:, kc, cc * 128:(cc + 1) * 128],
                            lambda kc: hT4[:, kc, t0:t0 + n], n, [("hT4", t0 // 512), ("wu", sl)])
                        sg = sgf[bi % 2]
                        P.act(lambda e, sg=sg, bg=bg, n=n: e.activation(out=sg[:, 0:n], in_=ps[bg][:, 0:n],
                                                                        func=AF.Silu), r=[PK(bg)],
                              w=[("sgf", bi % 2)])
                        P.dve(lambda e, sg=sg, bu=bu, n=n, fc=fc, t0=t0: e.tensor_tensor(
                            out=aT[:, fc, t0:t0 + n], in0=ps[bu][:, 0:n], in1=sg[:, 0:n], op=ALU.mult),
                            r=[PK(bu), ("sgf", bi % 2)], w=["aT"])
            for half in range(2):
                wslot = half
                wdk = ("wdn", wslot)
                for ti, t in enumerate(tiles):
                    n = 128 if t < NT else NS
                    b = 4 + (ti % 4)
                    for k in range(11):
                        P.pe(lambda e, b=b, k=k, ti=ti, n=n, wslot=wslot: e.matmul(
                            ps[b][:n, :], lhsT=aT[:, k, ti * 128:ti * 128 + n], rhs=wdn[wslot][:, k, :],
                            start=(k == 0), stop=(k == 10)), r=["aT", wdk], w=[PK(b)])
                    P.dve(lambda e, b=b, t=t, n=n, half=half: e.tensor_tensor(
                        out=x1[:n, t, half * 512:(half + 1) * 512], in0=ps[b][:n, :],
                        in1=x1[:n, t, half * 512:(half + 1) * 512], op=ALU.add), r=[PK(b), ("x1", t)],
                        w=[("x1", t)])
                    if fh == 1 and half == 1:
                        final_norm(t)
        continue
        for t in tiles:
            n = 128 if t < NT else NS
            P.act(lambda e, t=t, n=n: e.activation(out=fj[:n, :], in_=x1[:n, t, :], func=AF.Square,
                                                   accum_out=fs[:n, 0:1]), r=[("x1", t)], w=["njunk", "fs"])
            P.dve(lambda e, n=n: e.tensor_scalar(out=fs[:n, 1:2], in0=fs[:n, 0:1], scalar1=1.0 / 1024, scalar2=1e-6,
                                                 op0=ALU.mult, op1=ALU.add), r=["fs"], w=["fs"])
            P.act(lambda e, n=n: e.activation(out=fs[:n, 2:3], in_=fs[:n, 1:2], func=AF.Sqrt), r=["fs"], w=["fs"])
            P.dve(lambda e, n=n: e.reciprocal(out=fs[:n, 3:4], in_=fs[:n, 2:3]), r=["fs"], w=["fs"])
            P.dve(lambda e, t=t, n=n: e.scalar_tensor_tensor(out=x1[:n, t, :], in0=x1[:n, t, :], scalar=fs[:n, 3:4],
                                                             in1=gF[:n, :], op0=ALU.mult, op1=ALU.add if False else ALU.mult),
                  r=[("x1", t), "fs", "gF"], w=[("x1", t)])
            dst = V.y_p[t * 128:(t + 1) * 128, :] if t < NT else V.y_s[:, :]
            P.dma(dst, x1[:n, t, :], r=[("x1", t)], tag="out")

    P.emit(["out"])


_CACHE = {}


def kernel(**inputs):
    inp = {k: np.asarray(v) for k, v in inputs.items()}
    if "nc" not in _CACHE:
        _CACHE["nc"] = build_nc()[0]
    nc = _CACHE["nc"]
    cst = make_consts()
    pc, pr = make_params(inp)
    f = lambda a: np.ascontiguousarray(a, dtype=np.float32)
    shared = {
        "w_in": f(inp["w_in"][0]), "w_out": f(inp["w_out"][0]), "w_mq": f(inp["w_mq"][0]), "w_mk": f(inp["w_mk"][0]),
        "w_mv": f(inp["w_mv"][0]), "w_mo": f(inp["w_mo"][0]), "w_gate": f(inp["w_gate"][0]),
        "w_up": f(inp["w_up"][0]), "w_down": f(inp["w_down"][0]), "cst": cst, "pc": pc, "pr": pr,
    }
    in_maps = []
    for c in range(8):
        sl = slice(c * NS, (c + 1) * NS)
        m = dict(shared)
        m["x_p"] = f(inp["x_prompt"][c])
        m["x_s"] = f(inp["x_sample"][sl, 0, :])
        m["mem"] = f(inp["mem_prompt"][c])
        m["cconv"] = f(inp["cache_conv"][0, sl].reshape(NS * 30, 512))
        m["ssc"] = f(inp["state_short_conv"][0, sl].reshape(NS * 3, 1536))
        m["sdel"] = f(inp["state_delta"][0, sl].reshape(NS * 4, 128, 128))
        m["cmk"] = f(inp["cache_mem_k"][0, sl].reshape(NS, 256, 1024))
        m["cmv"] = f(inp["cache_mem_v"][0, sl].reshape(NS, 256, 1024))
        in_maps.append(m)
    res = run_bass_kernel_spmd(nc, in_maps, core_ids=list(range(8)))
    R = res.results
    cat = lambda k: np.stack([np.asarray(R[c][k]) for c in range(8)])
    y_p = cat("y_p")
    y_s = cat("y_s").reshape(128, 1, D)
    nconv_p = cat("nconv_p")[None]
    nsc_p = cat("nsc_p")[None]
    ndel_p = cat("ndel_p")[None]
    mk = cat("mk_p").reshape(1, 8, 256, 4, 256)
    mv = cat("mv_p").reshape(1, 8, 256, 4, 256)
    nconv_s = cat("nconv_s").reshape(1, 128, 30, 512)
    nsc_s = cat("nsc_s").reshape(1, 128, 3, 1536)
    ndel_s = cat("ndel_s").reshape(1, 128, 4, 128, 128)
    return tuple(np.ascontiguousarray(a, dtype=np.float32) for a in
                 (y_p, y_s, nconv_p, nsc_p, ndel_p, mk, mv, nconv_s, nsc_s, ndel_s))
```

```python
import numpy as np
import concourse.bass as bass
import concourse.mybir as mybir
from concourse.bass_utils import run_bass_kernel_spmd

F32 = mybir.dt.float32
BF16 = mybir.dt.bfloat16
AF = mybir.ActivationFunctionType
ALU = mybir.AluOpType
AX = mybir.AxisListType

T = 2048
NS = 16
TT = T + NS
NT = 16
D = 1024
DFF = 2816
NFC = 22


class Op:
    __slots__ = ("eng", "fn", "deps", "sig", "need", "dma", "tag", "cnt", "idx")


import os
STOP = float(os.environ.get("KSTOP", "99"))


class _Stop(Exception):
    pass


def stop_at(n):
    if STOP == n:
        raise _Stop()


class _Rec:
    def __init__(self):
        self.call = None

    def __getattr__(self, name):
        def f(*a, **k):
            self.call = (name, a, k)
            return self
        return f


class Prog:
    ENGS = ("sp", "act", "pool", "dve", "pe")

    def __init__(self, nc):
        self.nc = nc
        self.ops = {e: [] for e in self.ENGS}
        self.all = []
        self.lastw = {}
        self.readers = {}
        self.tagcnt = {}
        self.taggroup = {}
        self.base = []

    def _add(self, eng, fn, r, w, dma=False, tag=None, group=False):
        op = Op()
        rec = _Rec()
        fn(rec)
        name_, a_, k_ = rec.call
        fn = lambda e, name_=name_, a_=a_, k_=k_: getattr(e, name_)(*a_, **k_)
        op.eng, op.fn, op.dma, op.tag = eng, fn, dma, tag
        op.need = False
        op.sig = None
        psr = [k for k in r if isinstance(k, tuple) and k[0] == "ps"]
        if psr:
            r = [k for k in r if k not in psr]
            w = list(w) + psr
        deps = list(self.base)
        for k in r:
            if k in self.lastw:
                deps.append(self.lastw[k])
        for k in w:
            if k in self.lastw:
                deps.append(self.lastw[k])
            deps.extend(self.readers.get(k, ()))
        if dma:
            deps = [d for d in deps if not (d.dma and d.tag == tag)]
        if eng == "pe":
            deps = [d for d in deps if d.dma or d.eng != "pe"]
        op.deps = deps
        for k in r:
            self.readers.setdefault(k, []).append(op)
        for k in w:
            self.lastw[k] = op
            self.readers[k] = []
        if dma:
            self.tagcnt[tag] = self.tagcnt.get(tag, 0) + 1
            self.taggroup[tag] = group
            op.cnt = self.tagcnt[tag]
        op.idx = len(self.all)
        self.all.append(op)
        self.ops[eng].append(op)
        return op

    def pe(self, fn, r=(), w=()):
        return self._add("pe", fn, r, w)

    def dve(self, fn, r=(), w=()):
        return self._add("dve", fn, r, w)

    def act(self, fn, r=(), w=()):
        return self._add("act", fn, r, w)

    def pool(self, fn, r=(), w=()):
        return self._add("pool", fn, r, w)

    def dma(self, out, in_, r=(), w=(), tag="ld", group=False, q="sp"):
        return self._add(q, lambda e: e.dma_start(out=out, in_=in_), r, w, dma=True, tag=tag, group=group)

    def barrier(self):
        base = []
        for e in self.ENGS:
            last = None
            for op in reversed(self.ops[e]):
                if not op.dma:
                    last = op
                    break
            if last is not None:
                base.append(last)
        lastdma = {}
        for op in self.all:
            if op.dma:
                lastdma[op.tag] = op
        base.extend(lastdma.values())
        self.base = base
        self.lastw = {}
        self.readers = {}

    def emit(self, final_tags):
        nc = self.nc
        for op in self.all:
            for d in op.deps:
                if not d.dma:
                    d.need = True
        for e in self.ENGS:
            n = 0
            for op in self.ops[e]:
                if not op.dma and op.need:
                    n += 1
                    op.sig = n
        from contextlib import ExitStack
        with ExitStack() as st:
            esem = {e: st.enter_context(nc.semaphore("s_" + e)) for e in self.ENGS}
            tsem = {t: st.enter_context(nc.semaphore("t_%d" % i)) for i, t in enumerate(self.tagcnt)}
            block = st.enter_context(nc.Block())

            def run(e, eng):
                waited = {}
                for op in self.ops[e]:
                    need = {}
                    for d in op.deps:
                        if d.dma:
                            s = tsem[d.tag]
                            v = 16 * (self.tagcnt[d.tag] if self.taggroup[d.tag] else d.cnt)
                        else:
                            s = esem[d.eng]
                            v = d.sig
                        if v > need.get(s, (0, 0))[1] if s in need else True:
                            need[s] = (s, v)
                    for s, v in need.values():
                        if waited.get(s, 0) < v:
                            eng.wait_ge(s, v)
                            waited[s] = v
                    ins = op.fn(eng)
                    if op.dma:
                        ins.then_inc(tsem[op.tag], 16)
                    elif op.need:
                        ins.then_inc(esem[e], 1)
                if e == "sp":
                    for t in tsem:
                        eng.wait_ge(tsem[t], 16 * self.tagcnt[t])

            block.sync(lambda eng: run("sp", eng))
            block.scalar(lambda eng: run("act", eng))
            block.gpsimd(lambda eng: run("pool", eng))
            block.vector(lambda eng: run("dve", eng))
            block.tensor(lambda eng: run("pe", eng))


class Alloc:
    def __init__(self, nc):
        self.nc = nc
        self.off = (int(nc.sbuf_base) + 63) // 64 * 64
        self.top = int(nc.sbuf_top)
        self.n = 0

    def t(self, shape, dt, name=None):
        sz = 2 if dt == BF16 else 4
        nb = sz
        for s in shape[1:]:
            nb *= s
        nb = (nb + 63) // 64 * 64
        assert self.off + nb <= self.top, ("SBUF overflow", name, self.off, nb, self.top)
        self.n += 1
        h = self.nc.alloc_sbuf_tensor_at("%s_%d" % (name or "t", self.n), list(shape), dt, offset=self.off)
        self.off += nb
        return h

    def child(self, off, top):
        c = Alloc.__new__(Alloc)
        c.nc, c.off, c.top, c.n = self.nc, (off + 63) // 64 * 64, top, self.n + 1000 * (1 + off % 97)
        return c

    def mark(self):
        return self.off

    def reset(self, m):
        self.off = m


def make_consts():
    i = np.arange(128)
    ident = np.eye(128, dtype=np.float32)
    U = (i[:, None] <= i[None, :]).astype(np.float32)
    SL = (i[:, None] > i[None, :]).astype(np.float32)
    ones = np.ones((128, 128), np.float32)
    blk = (i[:, None] // 64) == (i[None, :] // 64)
    mBDneg = -(SL * blk).astype(np.float32)
    mOFF = (SL * (~blk)).astype(np.float32)
    mTin = U.copy()
    oh16 = np.zeros((128, 16), np.float32)
    oh16[:16, :16] = np.eye(16)
    parts = [ident, U, SL, ones, mBDneg, mOFF, mTin, oh16]
    return np.ascontiguousarray(np.concatenate(parts, axis=1))


C_ID, C_U, C_SL, C_ONE = 0, 128, 256, 384
C_BD, C_OFF, C_TIN = 512, 640, 768
C_OH16 = 896
NCST = C_OH16 + 16

PC_CONVW, PC_CONVB, PC_LNG, PC_LNB, PC_SCW, PC_DN, PC_ALOG, PC_DTB = 0, 124, 128, 132, 136, 184, 185, 186
NPC = 187
PR_GMIX, PR_GMQ, PR_GFFN, PR_GKV, PR_GF, PR_DN4, PR_ALOG, PR_DTB = 0, 1024, 2048, 3072, 4096, 5120, 5632, 5636
NPR = 5640


def make_params(inp):
    pc = np.zeros((128, NPC), np.float32)
    pc[:, PC_CONVW:PC_CONVW + 124] = inp["conv_w"][0].reshape(31, 4, 128).transpose(2, 1, 0).reshape(128, 124)
    pc[:, PC_CONVB:PC_CONVB + 4] = inp["conv_b"][0].reshape(4, 128).T
    pc[:, PC_LNG:PC_LNG + 4] = inp["conv_ln_g"][0].reshape(4, 128).T
    pc[:, PC_LNB:PC_LNB + 4] = inp["conv_ln_b"][0].reshape(4, 128).T
    pc[:, PC_SCW:PC_SCW + 48] = inp["sc_w"][0].reshape(4, 12, 128).transpose(2, 1, 0).reshape(128, 48)
    pc[:, PC_DN] = inp["dn_norm"][0]
    pc[:4, PC_ALOG] = inp["a_log"][0]
    pc[:4, PC_DTB] = inp["dt_bias"][0]
    pr = np.zeros((128, NPR), np.float32)
    bc = lambda v: np.broadcast_to(np.asarray(v, np.float32).reshape(1, -1), (128, np.asarray(v).size))
    pr[:, PR_GMIX:PR_GMIX + 1024] = bc(inp["norm_mix"][0])
    pr[:, PR_GMQ:PR_GMQ + 1024] = bc(inp["norm_mem_q"][0])
    pr[:, PR_GFFN:PR_GFFN + 1024] = bc(inp["norm_ffn"][0])
    pr[:, PR_GKV:PR_GKV + 1024] = bc(inp["norm_mem_kv"][0])
    pr[:, PR_GF:PR_GF + 1024] = bc(inp["norm_f"])
    pr[:, PR_DN4:PR_DN4 + 512] = bc(np.tile(inp["dn_norm"][0], 4))
    pr[:, PR_ALOG:PR_ALOG + 4] = bc(inp["a_log"][0])
    pr[:, PR_DTB:PR_DTB + 4] = bc(inp["dt_bias"][0])
    return pc, pr


def build_nc():
    nc = bass.Bass("TRN2", target_bir_lowering=False)
    P = Prog(nc)
    try:
        return _build_nc(nc, P)
    except _Stop:
        P.emit(["out"])
        return nc, P


def _build_nc(nc, P):
    A = Alloc(nc)

    def dr(name, shape, out=False):
        return nc.dram_tensor(name, list(shape), F32, kind="ExternalOutput" if out else "ExternalInput").ap()

    x_p = dr("x_p", [T, D]); x_s = dr("x_s", [NS, D]); mem = dr("mem", [256, D])
    cconv = dr("cconv", [NS * 30, 512]); ssc = dr("ssc", [NS * 3, 1536]); sdel = dr("sdel", [NS * 4, 128, 128])
    cmk = dr("cmk", [NS, 256, 1024]); cmv = dr("cmv", [NS, 256, 1024])
    w_in = dr("w_in", [D, 3080]); w_out = dr("w_out", [D, D]); w_mq = dr("w_mq", [D, D]); w_mk = dr("w_mk", [D, D])
    w_mv = dr("w_mv", [D, D]); w_mo = dr("w_mo", [D, D]); w_gate = dr("w_gate", [D, DFF]); w_up = dr("w_up", [D, DFF])
    w_down = dr("w_down", [DFF, D])
    cst_d = dr("cst", [128, NCST]); pc_d = dr("pc", [128, NPC]); pr_d = dr("pr", [128, NPR])
    y_p = dr("y_p", [T, D], True); y_s = dr("y_s", [NS, D], True)
    nconv_p = dr("nconv_p", [30, 512], True); nsc_p = dr("nsc_p", [3, 1536], True)
    ndel_p = dr("ndel_p", [4, 128, 128], True)
    mk_p = dr("mk_p", [256, D], True); mv_p = dr("mv_p", [256, D], True)
    nconv_s = dr("nconv_s", [NS, 30, 512], True); nsc_s = dr("nsc_s", [NS, 3, 1536], True)
    ndel_s = dr("ndel_s", [NS * 4, 128, 128], True)

    ps = [nc.alloc_psum_tensor("ps%d" % i, [128, 512], F32) for i in range(8)]
    PK = lambda i: ("ps", i)

    def psb(i):
        return ps[i][:].bitcast(BF16)

    cst = A.t([128, NCST], F32, "cst")
    pc = A.t([128, NPC], F32, "pc")
    idb = A.t([128, 128], BF16, "idb")
    oneb = A.t([128, 128], BF16, "oneb")
    onesc = A.t([128, 128], BF16, "onesc")
    gam = A.t([128, 1024], F32, "gam")
    P.dma(cst[:], cst_d[:, :], w=["cst"], tag="c0")
    P.dma(pc[:], pc_d[:, :], w=["pc"], tag="c1")
    P.dve(lambda e: e.tensor_copy(out=idb[:], in_=cst[:, C_ID:C_ID + 128]), r=["cst"], w=["idb"])
    P.dve(lambda e: e.tensor_copy(out=oneb[:], in_=cst[:, C_ONE:C_ONE + 128]), r=["cst"], w=["oneb"])
    P.dve(lambda e: e.tensor_scalar(out=onesc[:], in0=cst[:, C_ONE:C_ONE + 128], scalar1=1.0 / 512, scalar2=None,
                                    op0=ALU.mult), r=["cst"], w=["onesc"])
    ident_f = cst[:, C_ID:C_ID + 128]
    Uf = cst[:, C_U:C_U + 128]
    SLf = cst[:, C_SL:C_SL + 128]
    onef = cst[:, C_ONE:C_ONE + 128]

    def load_gam(off):
        P.dma(gam[:], pr_d[:, off:off + 1024], w=["gam"], tag="gam")

    nrm_junk = A.t([128, 1024], BF16, "njunk")
    nrm_xn3 = [A.t([128, 1024], BF16, "nxn%d" % i) for i in range(3)]
    nrm_s3 = [A.t([128, 8], F32, "nrs%d" % i) for i in range(3)]

    def norm_pass(items, dstT, dkey):
        L = len(items)

        def S1(i):
            src, rkeys, n, col0, pre = items[i]
            if pre is not None:
                pre()
            ns_, nsk = nrm_s3[i % 3], ("nrs", i % 3)
            P.act(lambda e: e.activation(out=nrm_junk[:n, :], in_=src, func=AF.Square, accum_out=ns_[:n, 0:1]),
                  r=rkeys, w=[nsk, "njunk"])
            P.dve(lambda e: e.tensor_scalar(out=ns_[:n, 1:2], in0=ns_[:n, 0:1], scalar1=1.0 / 1024, scalar2=1e-6,
                                            op0=ALU.mult, op1=ALU.add), r=[nsk], w=[nsk])

        def S2(i):
            src, rkeys, n, col0, pre = items[i]
            ns_, nsk = nrm_s3[i % 3], ("nrs", i % 3)
            xn_, nxk = nrm_xn3[i % 3], ("nxn", i % 3)
            P.act(lambda e: e.activation(out=ns_[:n, 2:3], in_=ns_[:n, 1:2], func=AF.Sqrt), r=[nsk], w=[nsk])
            P.dve(lambda e: e.reciprocal(out=ns_[:n, 3:4], in_=ns_[:n, 2:3]), r=[nsk], w=[nsk])
            P.dve(lambda e: e.scalar_tensor_tensor(out=xn_[:n, :], in0=src, scalar=ns_[:n, 3:4], in1=gam[:n, :],
                                                   op0=ALU.mult, op1=ALU.mult), r=rkeys + [nsk, "gam"], w=[nxk])

        def S3(i):
            src, rkeys, n, col0, pre = items[i]
            xn_, nxk = nrm_xn3[i % 3], ("nxn", i % 3)
            bank = i % 2
            pv = psb(bank)
            for kc in range(8):
                P.pe(lambda e, kc=kc: e.transpose(out=pv[:, kc * 128:kc * 128 + n],
                                                  in_=xn_[:n, kc * 128:(kc + 1) * 128], identity=idb[:n, :n]),
                     r=[nxk, "idb"], w=[PK(bank)])
            pv3 = pv.rearrange("p (c n) -> p c n", c=8)
            dk_ = (dkey, col0 // 512)
            if i % 2 == 0:
                P.act(lambda e: e.activation(out=dstT[:, :, col0:col0 + n], in_=pv3[:, :, 0:n], func=AF.Copy),
                      r=[PK(bank)], w=[dk_])
            else:
                P.dve(lambda e: e.tensor_copy(out=dstT[:, :, col0:col0 + n], in_=pv3[:, :, 0:n]),
                      r=[PK(bank)], w=[dk_])

        for step in range(L + 2):
            if step < L:
                S1(step)
            if 0 <= step - 1 < L:
                S2(step - 1)
            if 0 <= step - 2 < L:
                S3(step - 2)

    def load_w(dst, dkey, wd, r0, nk, c0, ncols):
        for k in range(nk):
            P.dma(dst[:, k, 0:ncols], wd[r0 + k * 128:r0 + (k + 1) * 128, c0:c0 + ncols], w=[dkey],
                  tag="w_" + str(dkey), q="pool")

    off_hT = A.mark()
    hT = A.t([128, 8, TT], BF16, "hT")
    off_cT = A.mark()
    cT = A.t([128, 4, TT], BF16, "cT")
    mX = A.mark()
    XSZ = 17 * 1024 * 4
    A.off += XSZ
    mXe = A.mark()
    AX_ = A.child(mX, mXe)
    zs = A.t([128, NT, 512], BF16, "zs")
    zsT = A.t([128, 4, NS], F32, "zsT")
    vTs = A.t([128, 4, NS], F32, "vTs")
    gtok = A.t([128, NT, 4], F32, "gtok")
    btok = A.t([128, NT, 4], F32, "btok")
    gbS = A.t([4, 2, NS], F32, "gbS")
    utail = A.t([128, 4, 32], F32, "utail")
    ptail = A.t([128, 12, 4], F32, "ptail")
    unew_s = A.t([128, 4, NS], F32, "unews")
    pnew_s = A.t([128, 12, NS], F32, "pnews")
    m_w = A.mark()
    wsl = [A.t([128, 8, 512], BF16, "wsl%d" % i) for i in range(3)]
    m_p1 = A.mark()
    A1 = A
    A = AX_

    xin = [A.t([128, 1024], F32, "xin%d" % i) for i in range(3)]
    load_gam(PR_GMIX)
    items = []
    for t in range(NT + 1):
        s = t % 3
        n = 128 if t < NT else NS
        src_d = x_p[t * 128:(t + 1) * 128, :] if t < NT else x_s[:, :]
        pre = (lambda s=s, n=n, src_d=src_d: P.dma(xin[s][:n, :], src_d, w=[("xin", s)], tag="xin%d" % s))
        items.append((xin[s][:n, :], [("xin", s)], n, t * 128, pre))
    norm_pass(items, hT, "hT")

    stop_at(1)
    BLK = [(i * 512, 512) for i in range(4)] + [(T, NS)]

    def mm8(bank, w_ap_fn, rhs_fn, ncols, rk, mrows=128):
        for kc in range(8):
            P.pe(lambda e, kc=kc: e.matmul(ps[bank][0:mrows, 0:ncols], lhsT=w_ap_fn(kc), rhs=rhs_fn(kc),
                                           start=(kc == 0), stop=(kc == 7)), r=rk, w=[PK(bank)])

    load_w(wsl[2], ("wsl", 2), w_in, 0, 8, 3072, 8)
    load_w(wsl[0], ("wsl", 0), w_in, 0, 8, 0, 512)
    load_w(wsl[1], ("wsl", 1), w_in, 0, 8, 512, 512)
    prb = A.t([128, 16], F32, "prb")
    P.dma(prb[:, 0:8], pr_d[:, PR_ALOG:PR_ALOG + 8], w=["prb"], tag="c2")
    P.act(lambda e: e.activation(out=prb[:, 8:12], in_=prb[:, 0:4], func=AF.Exp), r=["prb"], w=["prb"])
    P.dve(lambda e: e.tensor_scalar(out=prb[:, 8:12], in0=prb[:, 8:12], scalar1=-1.0, scalar2=None, op0=ALU.mult),
          r=["prb"], w=["prb"])
    negA_c = A.t([4, 1], F32, "negAc")
    P.act(lambda e: e.activation(out=negA_c[:], in_=pc[0:4, PC_ALOG:PC_ALOG + 1], func=AF.Exp), r=["pc"], w=["negAc"])
    P.dve(lambda e: e.tensor_scalar(out=negA_c[:], in0=negA_c[:], scalar1=-1.0, scalar2=None, op0=ALU.mult),
          r=["negAc"], w=["negAc"])
    gx = A.t([128, NT, 4], F32, "gx")
    dtb64 = A.t([128, NT, 4], F32, "dtb64")
    nga64 = A.t([128, NT, 4], F32, "nga64")
    P.dve(lambda e: e.tensor_copy(out=dtb64[:], in_=prb[:, 4:8].unsqueeze(1).to_broadcast([128, NT, 4])),
          r=["prb"], w=["dtb64"])
    P.dve(lambda e: e.tensor_copy(out=nga64[:], in_=prb[:, 8:12].unsqueeze(1).to_broadcast([128, NT, 4])),
          r=["prb"], w=["nga64"])
    for t in range(NT):
        for kc in range(8):
            P.pe(lambda e, t=t, kc=kc: e.matmul(ps[2][:, t * 8:(t + 1) * 8], lhsT=hT[:, kc, t * 128:(t + 1) * 128],
                                                rhs=wsl[2][:, kc, 0:8], start=(kc == 0), stop=(kc == 7)),
                 r=[("hT", t // 4), ("wsl", 2)], w=[PK(2)])
    ps3 = ps[2][:, 0:NT * 8].rearrange("p (t c) -> p t c", c=8)
    P.act(lambda e: e.activation(out=btok[:, :, :], in_=ps3[:, :, 0:4], func=AF.Exp, scale=-1.0), r=[PK(2)],
          w=["btok"])
    P.dve(lambda e: e.tensor_scalar(out=btok[:, :, :], in0=btok[:, :, :], scalar1=1.0, scalar2=None, op0=ALU.add),
          r=["btok"], w=["btok"])
    P.dve(lambda e: e.reciprocal(out=btok[:, :, :], in_=btok[:, :, :]), r=["btok"], w=["btok"])
    P.dve(lambda e: e.tensor_tensor(out=gx[:], in0=ps3[:, :, 4:8], in1=dtb64[:], op=ALU.add), r=[PK(2), "dtb64"],
          w=["gx"])
    P.act(lambda e: e.activation(out=gx[:], in_=gx[:], func=AF.Exp), r=["gx"], w=["gx"])
    P.act(lambda e: e.activation(out=gx[:], in_=gx[:], func=AF.Ln, bias=1.0), r=["gx"], w=["gx"])
    P.dve(lambda e: e.tensor_tensor(out=gtok[:, :, :], in0=gx[:], in1=nga64[:], op=ALU.mult), r=["gx", "nga64"],
          w=["gtok"])
    for half in range(2):
        b = 2 + half
        mm8(b, lambda kc: wsl[2][:, kc, half * 4:half * 4 + 4], lambda kc: hT[:, kc, T:TT], NS, [("hT", 4), ("wsl", 2)],
            mrows=4)
    P.act(lambda e: e.activation(out=gbS[:, 0, :], in_=ps[2][0:4, 0:NS], func=AF.Exp, scale=-1.0), r=[PK(2)],
          w=["gbS"])
    P.dve(lambda e: e.tensor_scalar(out=gbS[:, 0, :], in0=gbS[:, 0, :], scalar1=1.0, scalar2=None, op0=ALU.add),
          r=["gbS"], w=["gbS"])
    P.dve(lambda e: e.reciprocal(out=gbS[:, 0, :], in_=gbS[:, 0, :]), r=["gbS"], w=["gbS"])
    gts = A.t([4, 2, NS], F32, "gts")
    P.act(lambda e: e.activation(out=gts[:, 0, :], in_=ps[3][0:4, 0:NS], func=AF.Exp,
                                 bias=pc[0:4, PC_DTB:PC_DTB + 1]), r=[PK(3), "pc"], w=["gts"])
    P.act(lambda e: e.activation(out=gts[:, 1, :], in_=gts[:, 0, :], func=AF.Ln, bias=1.0), r=["gts"], w=["gts"])
    P.dve(lambda e: e.tensor_scalar(out=gbS[:, 1, :], in0=gts[:, 1, :], scalar1=negA_c[:, 0:1], scalar2=None,
                                    op0=ALU.mult), r=["gts", "negAc"], w=["gbS"])

    stop_at(2)
    upad = A.t([128, 4, 30 + T], BF16, "upad")
    us = A.t([128, 4, NS, 32], BF16, "us")
    sig = [A.t([128, 512], F32, "sig%d" % i) for i in range(2)]
    diag = A.t([128, 31, 128], BF16, "diag")
    P.dve(lambda e: e.memset(upad[:, :, 0:30], 0.0), w=["upad"])
    load_w(wsl[2], ("wsl", 2), w_in, 0, 8, 1024, 512)
    cc_in = A.t([120, 4, 512], F32, "ccin")
    for g4 in range(4):
        P.dma(cc_in[:, g4, :], cconv[g4 * 120:(g4 + 1) * 120, :], w=[("ccin", g4)], tag="cc", group=True)
    P.dma(nconv_s[:, 0:29, :], cconv.rearrange("(s j) c -> s j c", j=30)[:, 1:30, :], tag="out")
    for c in range(4):
        for g4 in range(4):
            b = 4 + (g4 % 2)
            P.pe(lambda e, c=c, g4=g4, b=b: e.transpose(out=ps[b][:, 0:120], in_=cc_in[:, g4, c * 128:(c + 1) * 128],
                                                        identity=ident_f[0:120, 0:120]), r=[("ccin", g4), "cst"], w=[PK(b)])
            P.act(lambda e, c=c, g4=g4, b=b: e.activation(
                out=us[:, c, g4 * 4:(g4 + 1) * 4, 0:30],
                in_=ps[b][:, 0:120].rearrange("p (s j) -> p s j", j=30), func=AF.Copy), r=[PK(b)], w=["us"])
    for c in range(4):
        for bi, (t0, n) in enumerate(BLK):
            ba, bb = (bi % 2) * 2, (bi % 2) * 2 + 1
            mm8(ba, lambda kc: wsl[0][:, kc, c * 128:(c + 1) * 128], lambda kc: hT[:, kc, t0:t0 + n], n,
                [("hT", t0 // 512), ("wsl", 0)])
            mm8(bb, lambda kc: wsl[1][:, kc, c * 128:(c + 1) * 128], lambda kc: hT[:, kc, t0:t0 + n], n,
                [("hT", t0 // 512), ("wsl", 1)])
            sg = sig[bi % 2]
            sk = ("sig", bi % 2)
            P.act(lambda e, sg=sg, bb=bb, n=n: e.activation(out=sg[:, 0:n], in_=ps[bb][:, 0:n], func=AF.Sigmoid),
                  r=[PK(bb)], w=[sk])
            if bi < 4:
                P.dve(lambda e, sg=sg, ba=ba, c=c, t0=t0: e.tensor_tensor(
                    out=upad[:, c, 30 + t0:30 + t0 + 512], in0=ps[ba][:, 0:512], in1=sg[:, 0:512], op=ALU.mult),
                    r=[PK(ba), sk], w=["upad"])
                if bi == 3:
                    P.dve(lambda e, sg=sg, ba=ba, c=c: e.tensor_tensor(
                        out=utail[:, c, 0:30], in0=ps[ba][:, 482:512], in1=sg[:, 482:512], op=ALU.mult),
                        r=[PK(ba), sk], w=["utail"])
            else:
                P.dve(lambda e, sg=sg, ba=ba, c=c: e.tensor_tensor(
                    out=unew_s[:, c, :], in0=ps[ba][:, 0:NS], in1=sg[:, 0:NS], op=ALU.mult),
                    r=[PK(ba), sk], w=["unews"])
                P.act(lambda e, c=c: e.activation(out=us[:, c, :, 30:31], in_=unew_s[:, c, :].unsqueeze(2),
                                                  func=AF.Copy), r=["unews"], w=["us"])
        for j in range(31):
            P.dve(lambda e, c=c, j=j: e.tensor_scalar(
                out=diag[:, j, :], in0=idb[:], scalar1=pc[:, PC_CONVW + c * 31 + j:PC_CONVW + c * 31 + j + 1],
                scalar2=None, op0=ALU.mult), r=["idb", "pc"], w=["diag"])
        for bi, (t0, n) in enumerate(BLK):
            b = 4 + bi % 2
            for j in range(31):
                if bi < 4:
                    rhs = upad[:, c, t0 + j:t0 + j + 512]
                else:
                    rhs = us[:, c, :, j]
                P.pe(lambda e, b=b, j=j, rhs=rhs, n=n: e.matmul(ps[b][:, 0:n], lhsT=diag[:, j, :], rhs=rhs,
                                                               start=(j == 0), stop=(j == 30)),
                     r=["diag", "upad", "us"], w=[PK(b)])
            P.act(lambda e, b=b, c=c, t0=t0, n=n: e.activation(
                out=cT[:, c, t0:t0 + n], in_=ps[b][:, 0:n], func=AF.Identity,
                bias=pc[:, PC_CONVB + c:PC_CONVB + c + 1]), r=[PK(b), "pc"], w=["cT"])
    csq = [A.t([128, 512], BF16, "csq%d" % i) for i in range(2)]
    lnm = A.t([128, 512], F32, "lnm")
    lnv = A.t([128, 512], F32, "lnv")
    lnt = [A.t([128, 512], F32, "lnt%d" % i) for i in range(2)]
    for bi, (t0, n) in enumerate(BLK):
        for c in range(4):
            P.pe(lambda e, c=c, t0=t0, n=n: e.matmul(ps[0][:, 0:n], lhsT=onesc[:], rhs=cT[:, c, t0:t0 + n],
                                                     start=(c == 0), stop=(c == 3)), r=["cT", "onesc"], w=[PK(0)])
        for c in range(4):
            q = csq[c % 2]
            P.dve(lambda e, q=q, c=c, t0=t0, n=n: e.tensor_tensor(out=q[:, 0:n], in0=cT[:, c, t0:t0 + n],
                                                                  in1=cT[:, c, t0:t0 + n], op=ALU.mult),
                  r=["cT"], w=[("csq", c % 2)])
            P.pe(lambda e, q=q, c=c, n=n: e.matmul(ps[1][:, 0:n], lhsT=onesc[:], rhs=q[:, 0:n],
                                                   start=(c == 0), stop=(c == 3)), r=[("csq", c % 2), "onesc"],
                 w=[PK(1)])
        P.act(lambda e, n=n: e.activation(out=lnm[:, 0:n], in_=ps[0][:, 0:n], func=AF.Copy), r=[PK(0)], w=["lnm"])
        P.dve(lambda e, n=n: e.tensor_tensor(out=lnv[:, 0:n], in0=lnm[:, 0:n], in1=lnm[:, 0:n], op=ALU.mult),
              r=["lnm"], w=["lnv"])
        P.dve(lambda e, n=n: e.tensor_tensor(out=lnv[:, 0:n], in0=ps[1][:, 0:n], in1=lnv[:, 0:n], op=ALU.subtract),
              r=[PK(1), "lnv"], w=["lnv"])
        P.dve(lambda e, n=n: e.tensor_scalar(out=lnv[:, 0:n], in0=lnv[:, 0:n], scalar1=0.0, scalar2=1e-5,
                                             op0=ALU.max, op1=ALU.add), r=["lnv"], w=["lnv"])
        P.act(lambda e, n=n: e.activation(out=lnv[:, 0:n], in_=lnv[:, 0:n], func=AF.Ln), r=["lnv"], w=["lnv"])
        P.act(lambda e, n=n: e.activation(out=lnv[:, 0:n], in_=lnv[:, 0:n], func=AF.Exp, scale=-0.5), r=["lnv"],
              w=["lnv"])
        for c in range(4):
            tt_ = lnt[c % 2]
            tk = ("lnt", c % 2)
            P.dve(lambda e, tt_=tt_, c=c, t0=t0, n=n: e.tensor_tensor(out=tt_[:, 0:n], in0=cT[:, c, t0:t0 + n],
                                                                      in1=lnm[:, 0:n], op=ALU.subtract),
                  r=["cT", "lnm"], w=[tk])
            P.dve(lambda e, tt_=tt_, n=n: e.tensor_tensor(out=tt_[:, 0:n], in0=tt_[:, 0:n], in1=lnv[:, 0:n],
                                                          op=ALU.mult), r=[tk, "lnv"], w=[tk])
            P.act(lambda e, tt_=tt_, c=c, t0=t0, n=n: e.activation(
                out=cT[:, c, t0:t0 + n], in_=tt_[:, 0:n], func=AF.Silu,
                scale=pc[:, PC_LNG + c:PC_LNG + c + 1], bias=pc[:, PC_LNB + c:PC_LNB + c + 1]),
                r=[tk, "pc"], w=["cT"])
    otl = A.t([32, 512], F32, "otl")
    for c in range(4):
        P.pe(lambda e, c=c: e.transpose(out=ps[2][0:30, c * 128:(c + 1) * 128], in_=utail[:, c, 0:30],
                                        identity=ident_f), r=["utail", "cst"], w=[PK(2)])
    P.act(lambda e: e.activation(out=otl[0:30, :], in_=ps[2][0:30, :], func=AF.Copy), r=[PK(2)], w=["otl"])
    P.dma(nconv_p[:, :], otl[0:30, :], r=["otl"], tag="out")
    otl2 = A.t([16, 512], F32, "otl2")
    for c in range(4):
        P.pe(lambda e, c=c: e.transpose(out=ps[3][0:NS, c * 128:(c + 1) * 128], in_=unew_s[:, c, :],
                                        identity=ident_f), r=["unews", "cst"], w=[PK(3)])
    P.act(lambda e: e.activation(out=otl2[:, :], in_=ps[3][0:NS, :], func=AF.Copy), r=[PK(3)], w=["otl2"])
    P.dma(nconv_s[:, 29, :], otl2[:, :], r=["otl2"], tag="out")

    stop_at(3)
    P.barrier()
    AX_ = A1.child(mX, mXe)
    qT = AX_.t([128, 4, TT], BF16, "qT")
    kT = AX_.t([128, 4, TT], BF16, "kT")
    ktok = AX_.t([128, NT, 512], BF16, "ktok")
    vb = AX_.t([128, NT, 512], BF16, "vb")
    A = A1
    scd2 = [A.t([128, 4, 128], BF16, "scd%d" % i) for i in range(2)]
    pre = [A.t([128, 3 + 512], BF16, "pre%d" % i) for i in range(2)]
    pres2 = [A.t([128, NS, 4], BF16, "pres%d" % i) for i in range(2)]
    sfl = [A.t([128, 512], F32, "sfl%d" % i) for i in range(2)]
    sqb = [A.t([128, 512], BF16, "sqb%d" % i) for i in range(2)]
    rnb = [A.t([128, 512], F32, "rnb%d" % i) for i in range(2)]
    vtmp = [A.t([128, 512], BF16, "vtmp%d" % i) for i in range(2)]
    ss_in = A.t([48, 1536], F32, "ssin")
    P.dma(ss_in[:, :], ssc[:, :], w=["ssin"], tag="ssin")
    P.dma(nsc_s[:, 0:2, :], ssc.rearrange("(s j) c -> s j c", j=3)[:, 1:3, :], tag="out")

    def run_skewed(iters):
        L = len(iters)
        S_ = max(len(x) for x in iters)
        for step in range(L + S_ - 1):
            for k in range(S_):
                i = step - k
                if 0 <= i < L and k < len(iters[i]):
                    iters[i][k]()

    iters = []
    it = [0]
    for grp in range(3):
        sl = (2, 0, 1)[grp]
        for hh in range(4):
            ch = grp * 4 + hh
            scd = scd2[ch % 2]
            sck = ("scd", ch % 2)
            pres = pres2[ch % 2]
            psk = ("pres", ch % 2)
            for bi, (t0, n) in enumerate(BLK):
                i = it[0]
                it[0] += 1

                def S0(grp=grp, sl=sl, hh=hh, ch=ch, scd=scd, sck=sck, pres=pres, psk=psk, bi=bi, t0=t0, n=n, i=i):
                    b = i % 2
                    pr_ = pre[i % 2]
                    prk = ("pre", i % 2)
                    if hh == 0 and bi == 0:
                        if grp < 2:
                            nsl = (2, 0, 1)[grp + 1]
                            load_w(wsl[nsl], ("wsl", nsl), w_in, 0, 8, 1024 + (grp + 1) * 512, 512)
                        else:
                            load_w(wsl[2], ("wsl", 2), w_in, 0, 8, 2560, 512)
                    if bi == 0:
                        for j in range(4):
                            P.dve(lambda e, j=j: e.tensor_scalar(
                                out=scd[:, j, :], in0=idb[:],
                                scalar1=pc[:, PC_SCW + ch * 4 + j:PC_SCW + ch * 4 + j + 1],
                                scalar2=None, op0=ALU.mult), r=["idb", "pc"], w=[sck])
                        P.pe(lambda e: e.transpose(out=ps[6][:, 0:48], in_=ss_in[:, ch * 128:(ch + 1) * 128],
                                                   identity=ident_f[0:48, 0:48]), r=["ssin", "cst"], w=[PK(6)])
                        P.act(lambda e: e.activation(out=pres[:, :, 0:3],
                                                     in_=ps[6][:, 0:48].rearrange("p (s j) -> p s j", j=3),
                                                     func=AF.Copy), r=[PK(6)], w=[psk])
                    mm8(b, lambda kc: wsl[sl][:, kc, hh * 128:(hh + 1) * 128], lambda kc: hT[:, kc, t0:t0 + n], n,
                        [("hT", t0 // 512), ("wsl", sl)])
                    if bi < 4:
                        if bi == 0:
                            P.dve(lambda e: e.memset(pr_[:, 0:3], 0.0), w=[prk])
                        else:
                            po = pre[(i - 1) % 2]
                            P.dve(lambda e: e.tensor_copy(out=pr_[:, 0:3], in_=po[:, 512:515]),
                                  r=[("pre", (i - 1) % 2)], w=[prk])
                        P.dve(lambda e: e.tensor_copy(out=pr_[:, 3:515], in_=ps[b][:, 0:512]),
                              r=[PK(b)], w=[prk])
                        if bi == 3:
                            P.dve(lambda e: e.tensor_copy(out=ptail[:, ch, 0:3], in_=ps[b][:, 509:512]),
                                  r=[PK(b)], w=["ptail"])
                    else:
                        P.act(lambda e: e.activation(out=pres[:, :, 3:4], in_=ps[b][:, 0:NS].unsqueeze(2),
                                                     func=AF.Copy), r=[PK(b)], w=[psk])
                        P.dve(lambda e: e.tensor_copy(out=pnew_s[:, ch, :], in_=ps[b][:, 0:NS]),
                              r=[PK(b)], w=["pnews"])

                def S1(grp=grp, hh=hh, scd=scd, sck=sck, pres=pres, psk=psk, bi=bi, n=n, i=i):
                    b2 = 2 + i % 2
                    pr_ = pre[i % 2]
                    prk = ("pre", i % 2)
                    for j in range(4):
                        rhs = pr_[:, j:j + 512] if bi < 4 else pres[:, :, j]
                        P.pe(lambda e, j=j, rhs=rhs: e.matmul(ps[b2][:, 0:n], lhsT=scd[:, j, :], rhs=rhs,
                                                              start=(j == 0), stop=(j == 3)),
                             r=[sck, prk if bi < 4 else psk], w=[PK(b2)])
                    if grp == 2:
                        if bi < 4:
                            vt = vtmp[i % 2]
                            P.act(lambda e: e.activation(out=vt[:, :], in_=ps[b2][:, 0:512], func=AF.Silu),
                                  r=[PK(b2)], w=[("vtmp", i % 2)])
                        else:
                            P.act(lambda e: e.activation(out=vTs[:, hh, :], in_=ps[b2][:, 0:NS], func=AF.Silu),
                                  r=[PK(b2)], w=["vTs"])
                        return
                    sf = sfl[i % 2]
                    sfk = ("sfl", i % 2)
                    P.act(lambda e: e.activation(out=sf[:, 0:n], in_=ps[b2][:, 0:n], func=AF.Exp, scale=-1.0),
                          r=[PK(b2)], w=[sfk])
                    P.act(lambda e: e.activation(out=sf[:, 0:n], in_=sf[:, 0:n], func=AF.Ln, bias=1.0),
                          r=[sfk], w=[sfk])
                    P.act(lambda e: e.activation(out=sf[:, 0:n], in_=sf[:, 0:n], func=AF.Exp, scale=-1.0),
                          r=[sfk], w=[sfk])
                    P.dve(lambda e: e.tensor_tensor(out=sf[:, 0:n], in0=ps[b2][:, 0:n], in1=sf[:, 0:n], op=ALU.mult),
                          r=[PK(b2), sfk], w=[sfk])
                    sq = sqb[i % 2]
                    P.dve(lambda e: e.tensor_tensor(out=sq[:, 0:n], in0=sf[:, 0:n], in1=sf[:, 0:n], op=ALU.mult),
                          r=[sfk], w=[("sqb", i % 2)])

                def S2(grp=grp, hh=hh, bi=bi, t0=t0, n=n, i=i):
                    pb = 4 + i % 2
                    if grp == 2:
                        if bi == 4:
                            return
                        vt = vtmp[i % 2]
                        vk = ("vtmp", i % 2)
                        pv = psb(pb)
                        for tl in range(4):
                            P.pe(lambda e, tl=tl: e.transpose(out=pv[:, tl * 128:(tl + 1) * 128],
                                                              in_=vt[:, tl * 128:(tl + 1) * 128], identity=idb[:]),
                                 r=[vk, "idb"], w=[PK(pb)])
                        for tl in range(4):
                            tg = bi * 4 + tl
                            P.dve(lambda e, tl=tl, tg=tg: e.tensor_scalar(
                                out=vb[:, tg, hh * 128:(hh + 1) * 128], in0=pv[:, tl * 128:(tl + 1) * 128],
                                scalar1=btok[:, tg, hh:hh + 1], scalar2=None, op0=ALU.mult),
                                r=[PK(pb), "btok"], w=["vb"])
                        return
                    dst = qT if grp == 0 else kT
                    dk = "qT" if grp == 0 else "kT"
                    sf = sfl[i % 2]
                    sfk = ("sfl", i % 2)
                    sq = sqb[i % 2]
                    P.pe(lambda e: e.matmul(ps[pb][:, 0:n], lhsT=oneb[:], rhs=sq[:, 0:n], start=True, stop=True),
                         r=[("sqb", i % 2), "oneb"], w=[PK(pb)])
                    rn = rnb[i % 2]
                    rk_ = ("rnb", i % 2)
                    sc_ = 128.0 if grp == 0 else 1.0
                    P.act(lambda e: e.activation(out=rn[:, 0:n], in_=ps[pb][:, 0:n], func=AF.Ln, scale=sc_,
                                                 bias=1e-6 * sc_), r=[PK(pb)], w=[rk_])
                    P.act(lambda e: e.activation(out=rn[:, 0:n], in_=rn[:, 0:n], func=AF.Exp, scale=-0.5), r=[rk_],
                          w=[rk_])
                    P.dve(lambda e: e.tensor_tensor(out=dst[:, hh, t0:t0 + n], in0=sf[:, 0:n], in1=rn[:, 0:n],
                                                    op=ALU.mult), r=[sfk, rk_], w=[dk])

                def S3(grp=grp, hh=hh, bi=bi, t0=t0, i=i):
                    if not (grp == 1 and bi < 4):
                        return
                    pb2 = 6 + i % 2
                    pv = psb(pb2)
                    for tl in range(4):
                        P.pe(lambda e, tl=tl: e.transpose(out=pv[:, tl * 128:(tl + 1) * 128],
                                                          in_=kT[:, hh, t0 + tl * 128:t0 + (tl + 1) * 128],
                                                          identity=idb[:]), r=["kT", "idb"], w=[PK(pb2)])
                    P.dve(lambda e: e.tensor_copy(out=ktok[:, bi * 4:(bi + 1) * 4, hh * 128:(hh + 1) * 128],
                                                  in_=pv[:, 0:512].rearrange("p (t d) -> p t d", t=4)),
                          r=[PK(pb2)], w=["ktok"])

                iters.append([S0, S1, S2, S3])
    run_skewed(iters)
    otp = A.t([16, 1536], F32, "otp")
    for ch in range(12):
        b = ch // 4
        P.pe(lambda e, ch=ch, b=b: e.transpose(out=ps[b][0:3, (ch % 4) * 128:(ch % 4 + 1) * 128],
                                               in_=ptail[:, ch, 0:3], identity=ident_f), r=["ptail", "cst"],
             w=[PK(b)])
    for b in range(3):
        P.act(lambda e, b=b: e.activation(out=otp[0:3, b * 512:(b + 1) * 512], in_=ps[b][0:3, :], func=AF.Copy),
              r=[PK(b)], w=["otp"])
    P.dma(nsc_p[:, :], otp[0:3, :], r=["otp"], tag="ootp")
    otp2 = otp
    for ch in range(12):
        b = 3 + ch // 4
        P.pe(lambda e, ch=ch, b=b: e.transpose(out=ps[b][0:NS, (ch % 4) * 128:(ch % 4 + 1) * 128],
                                               in_=pnew_s[:, ch, :], identity=ident_f), r=["pnews", "cst"],
             w=[PK(b)])
    for b in range(3):
        P.act(lambda e, b=b: e.activation(out=otp2[:, b * 512:(b + 1) * 512], in_=ps[3 + b][0:NS, :], func=AF.Copy),
              r=[PK(3 + b)], w=["otp"])
    P.dma(nsc_s[:, 2, :], otp2[:, :], r=["otp"], tag="ootp")

    for t in range(NT):
        b = t % 2
        mm8(b, lambda kc: hT[:, kc, t * 128:(t + 1) * 128], lambda kc: wsl[2][:, kc, 0:512], 512,
            [("hT", t // 4), ("wsl", 2)])
        P.act(lambda e, t=t, b=b: e.activation(out=zs[:, t, :], in_=ps[b][:, 0:512], func=AF.Silu), r=[PK(b)],
              w=["zs"])
    for hh in range(4):
        b = 2 + hh % 2
        mm8(b, lambda kc: wsl[2][:, kc, hh * 128:(hh + 1) * 128], lambda kc: hT[:, kc, T:TT], NS,
            [("hT", 4), ("wsl", 2)])
        P.act(lambda e, hh=hh, b=b: e.activation(out=zsT[:, hh, :], in_=ps[b][:, 0:NS], func=AF.Silu), r=[PK(b)],
              w=["zsT"])

    stop_at(4)
    build_rest(nc, P, A, locals())
    return nc, P


def build_rest(nc, P, A, L):
    g = dict(L)
    from types import SimpleNamespace
    V = SimpleNamespace(**g)
    ps, PK, psb, cst, pc, idb, oneb = V.ps, V.PK, V.psb, V.cst, V.pc, V.idb, V.oneb
    ident_f, Uf, SLf, onef = V.ident_f, V.Uf, V.SLf, V.onef
    hT, cT, qT, kT, ktok, vb, zs, zsT, vTs, gtok, btok, gbS = (V.hT, V.cT, V.qT, V.kT, V.ktok, V.vb, V.zs, V.zsT,
                                                                 V.vTs, V.gtok, V.btok, V.gbS)
    load_w, load_gam, mm8, gam = V.load_w, V.load_gam, V.mm8, V.gam
    pr_d = V.pr_d

    P.barrier()
    A.reset(V.m_w)
    dT = hT

    f32t = lambda name, shape=(128, 512): A.t(list(shape), F32, name)
    bft = lambda name, shape=(128, 512): A.t(list(shape), BF16, name)
    dn4 = f32t("dn4")
    P.dma(dn4[:], pr_d[:, PR_DN4:PR_DN4 + 512], w=["dn4"], tag="c3")
    S = f32t("S")
    Sb = bft("Sb")
    P.dve(lambda e: e.memset(S[:], 0.0), w=["S"])
    P.dve(lambda e: e.memset(Sb[:], 0.0), w=["Sb"])
    e3_2 = [f32t("e3_%d" % i, (128, 16)) for i in range(2)]
    gSL = f32t("gSL")
    E = f32t("E")
    ET = f32t("ET")
    EBbd = f32t("EBbd")
    EBoff = bft("EBoff")
    ETm = bft("ETm")
    Y = [bft("Y0"), bft("Y1")]
    YT = [bft("YT0"), bft("YT1")]
    PT = [bft("PT0"), bft("PT1")]
    Loff = bft("Loff")
    Tbd = bft("Tbd")
    Xb = bft("Xb")
    TTm = bft("TTm")
    kbg = bft("kbg")
    kdec_2 = [bft("kdec%d" % i) for i in range(2)]
    qkT_2 = [bft("qkT%d" % i) for i in range(2)]
    wT_2 = [bft("wT%d" % i) for i in range(2)]
    u_2 = [f32t("u_sb%d" % i) for i in range(2)]
    vnew = bft("vnew")
    o_sb = f32t("o_sb")
    qS_sb = f32t("qS_sb")
    bg_m = f32t("bg_m", (128, 8))
    bg_c = f32t("bg_c", (128, 8))
    dtok = bft("dtok")
    osq = V.nrm_junk
    B4 = lambda ap: ap.unsqueeze(1).to_broadcast([128, 4, 128])
    H4 = lambda ap: ap.rearrange("p (h n) -> p h n", h=4)
    mBD = B4(cst[:, C_BD:C_BD + 128])
    mOFF = B4(cst[:, C_OFF:C_OFF + 128])
    mTIN = B4(cst[:, C_TIN:C_TIN + 128])
    mBD4 = f32t("mBD4")
    P.pool(lambda e: e.tensor_copy(out=H4(mBD4[:]), in_=mBD), r=["cst"], w=["mBD4"])
    HS = [slice(h * 128, (h + 1) * 128) for h in range(4)]

    def make_tile(t):
        tk = slice(t * 128, (t + 1) * 128)
        p = t % 2
        e3, e3k = e3_2[p], ("e3", p)
        egc, erem, etot = e3[:, 0:4], e3[:, 4:8], e3[:, 8:12]
        qkT, qkk = qkT_2[p], ("qkT", p)
        wT, wTk = wT_2[p], ("wT", p)
        u_sb, uk = u_2[p], ("u_sb", p)
        kdec, kdk = kdec_2[p], ("kdec", p)
        gt = gtok[:, t, :]
        st = {"cur": 0}

        def A_():
            P.pe(lambda e: e.matmul(ps[0][:, 0:4], lhsT=Uf, rhs=gt, start=True, stop=True), r=["gtok", "cst"],
                 w=[PK(0)])
            P.pe(lambda e: e.matmul(ps[0][:, 4:8], lhsT=SLf, rhs=gt, start=True, stop=True), r=["gtok", "cst"],
                 w=[PK(0)])
            P.pe(lambda e: e.matmul(ps[0][:, 8:12], lhsT=onef, rhs=gt, start=True, stop=True), r=["gtok", "cst"],
                 w=[PK(0)])
            P.act(lambda e: e.activation(out=e3[:, 0:12], in_=ps[0][:, 0:12], func=AF.Exp), r=[PK(0)], w=[e3k])
            for h in range(4):
                P.dve(lambda e, h=h: e.tensor_scalar(out=gSL[:, HS[h]], in0=SLf, scalar1=gt[:, h:h + 1],
                                                     scalar2=None, op0=ALU.mult), r=["gtok", "cst"], w=["gSL"])
            P.pe(lambda e: e.matmul(ps[1][:, :], lhsT=Uf, rhs=gSL[:, :], start=True, stop=True), r=["gSL", "cst"],
                 w=[PK(1)])
            for h in range(4):
                P.pe(lambda e, h=h: e.matmul(ps[2][:, HS[h]], lhsT=gSL[:, HS[h]], rhs=Uf, start=True, stop=True),
                     r=["gSL", "cst"], w=[PK(2)])
            P.act(lambda e: e.activation(out=E[:], in_=ps[1][:, :], func=AF.Exp), r=[PK(1)], w=["E"])
            P.act(lambda e: e.activation(out=ET[:], in_=ps[2][:, :], func=AF.Exp), r=[PK(2)], w=["ET"])
            for h in range(4):
                P.dve(lambda e, h=h: e.tensor_scalar(out=E[:, HS[h]], in0=E[:, HS[h]], scalar1=btok[:, t, h:h + 1],
                                                     scalar2=None, op0=ALU.mult), r=["E", "btok"], w=["E"])
            P.dve(lambda e: e.tensor_tensor(out=EBbd[:], in0=E[:], in1=mBD4[:], op=ALU.mult), r=["E", "mBD4"],
                  w=["EBbd"])
            P.pool(lambda e: e.tensor_tensor(out=H4(EBoff[:]), in0=H4(E[:]), in1=mOFF, op=ALU.mult), r=["E", "cst"],
                   w=["EBoff"])
            P.pool(lambda e: e.tensor_tensor(out=H4(ETm[:]), in0=H4(ET[:]), in1=mTIN, op=ALU.mult), r=["ET", "cst"],
                   w=["ETm"])
            for h in range(4):
                P.pe(lambda e, h=h: e.matmul(ps[3][:, HS[h]], lhsT=kT[:, h, tk], rhs=kT[:, h, tk], start=True,
                                             stop=True), r=["kT"], w=[PK(3)])
            for h in range(4):
                P.pe(lambda e, h=h: e.matmul(ps[4][:, HS[h]], lhsT=kT[:, h, tk], rhs=qT[:, h, tk], start=True,
                                             stop=True), r=["kT", "qT"], w=[PK(4)])
            P.dve(lambda e: e.tensor_tensor(out=Y[0][:], in0=ps[3][:, :], in1=EBbd[:], op=ALU.mult),
                  r=[PK(3), "EBbd"], w=[("Y", 0)])
            pv = psb(5)
            for h in range(4):
                P.pe(lambda e, h=h: e.transpose(out=pv[:, HS[h]], in_=Y[0][:, HS[h]], identity=idb[:]),
                     r=[("Y", 0), "idb"], w=[PK(5)])
            P.act(lambda e: e.activation(out=YT[0][:], in_=pv[:, 0:512], func=AF.Copy), r=[PK(5)], w=[("YT", 0)])
            P.pool(lambda e: e.tensor_tensor(out=H4(PT[0][:]), in0=H4(YT[0][:]), in1=B4(ident_f), op=ALU.add),
                   r=[("YT", 0), "cst"], w=[("PT", 0)])
            P.dve(lambda e: e.tensor_tensor(out=Loff[:], in0=ps[3][:, :], in1=EBoff[:], op=ALU.mult),
                  r=[PK(3), "EBoff"], w=["Loff"])
            P.dve(lambda e: e.tensor_tensor(out=qkT[:], in0=ps[4][:, :], in1=ETm[:], op=ALU.mult),
                  r=[PK(4), "ETm"], w=[qkk])
            st["cur"] = 0

        def N_(m):
            cur = st["cur"]
            nx = 1 - cur
            for h in range(4):
                P.pe(lambda e, h=h: e.matmul(ps[6][:, HS[h]], lhsT=YT[cur][:, HS[h]], rhs=Y[cur][:, HS[h]],
                                             start=True, stop=True), r=[("Y", cur), ("YT", cur)], w=[PK(6)])
            if m < 5:
                for h in range(4):
                    P.pe(lambda e, h=h: e.matmul(ps[7][:, HS[h]], lhsT=Y[cur][:, HS[h]], rhs=YT[cur][:, HS[h]],
                                                 start=True, stop=True), r=[("Y", cur), ("YT", cur)], w=[PK(7)])
            P.act(lambda e: e.activation(out=Y[nx][:], in_=ps[6][:, :], func=AF.Copy), r=[PK(6)], w=[("Y", nx)])
            if m < 5:
                P.dve(lambda e: e.tensor_copy(out=YT[nx][:], in_=ps[7][:, :]), r=[PK(7)], w=[("YT", nx)])
            for h in range(4):
                P.pe(lambda e, h=h: e.matmul(ps[5][:, HS[h]], lhsT=Y[nx][:, HS[h]], rhs=PT[cur][:, HS[h]],
                                             start=True, stop=True), r=[("Y", nx), ("PT", cur)], w=[PK(5)])
            P.dve(lambda e: e.tensor_tensor(out=PT[nx][:], in0=ps[5][:, :], in1=PT[cur][:], op=ALU.add),
                  r=[PK(5), ("PT", cur)], w=[("PT", nx)])
            st["cur"] = nx

        def M_():
            cur = st["cur"]
            PTf, ptk = PT[cur], ("PT", cur)
            pv = psb(6)
            for h in range(4):
                P.pe(lambda e, h=h: e.transpose(out=pv[:, HS[h]], in_=PTf[:, HS[h]], identity=idb[:]),
                     r=[ptk, "idb"], w=[PK(6)])
            P.act(lambda e: e.activation(out=Tbd[:], in_=pv[:, 0:512], func=AF.Copy), r=[PK(6)], w=["Tbd"])
            for h in range(4):
                P.pe(lambda e, h=h: e.matmul(ps[7][:, HS[h]], lhsT=Loff[:, HS[h]], rhs=PTf[:, HS[h]], start=True,
                                             stop=True), r=["Loff", ptk], w=[PK(7)])
            P.act(lambda e: e.activation(out=Xb[:], in_=ps[7][:, :], func=AF.Copy), r=[PK(7)], w=["Xb"])
            for h in range(4):
                P.pe(lambda e, h=h: e.matmul(ps[5][:, HS[h]], lhsT=Tbd[:, HS[h]], rhs=Xb[:, HS[h]], start=True,
                                             stop=True), r=["Tbd", "Xb"], w=[PK(5)])
            P.dve(lambda e: e.tensor_tensor(out=TTm[:], in0=PTf[:], in1=ps[5][:, :], op=ALU.subtract),
                  r=[PK(5), ptk], w=["TTm"])
            P.dve(lambda e: e.tensor_tensor(out=bg_m[:, 0:4], in0=btok[:, t, :], in1=egc, op=ALU.mult),
                  r=["btok", e3k], w=["bg_m"])
            for h in range(4):
                P.dve(lambda e, h=h: e.tensor_scalar(out=kbg[:, HS[h]], in0=ktok[:, t, HS[h]],
                                                     scalar1=bg_m[:, h:h + 1], scalar2=None, op0=ALU.mult),
                      r=["ktok", "bg_m"], w=["kbg"])
                P.dve(lambda e, h=h: e.tensor_scalar(out=kdec[:, HS[h]], in0=ktok[:, t, HS[h]],
                                                     scalar1=erem[:, h:h + 1], scalar2=None, op0=ALU.mult),
                      r=["ktok", e3k], w=[kdk])
            for h in range(4):
                P.pe(lambda e, h=h: e.matmul(ps[0][:, HS[h]], lhsT=TTm[:, HS[h]], rhs=vb[:, t, HS[h]], start=True,
                                             stop=True), r=["TTm", "vb"], w=[PK(0)])
            for h in range(4):
                P.pe(lambda e, h=h: e.matmul(ps[1][:, HS[h]], lhsT=kbg[:, HS[h]], rhs=TTm[:, HS[h]], start=True,
                                             stop=True), r=["TTm", "kbg"], w=[PK(1)])
            P.act(lambda e: e.activation(out=u_sb[:], in_=ps[0][:, :], func=AF.Copy), r=[PK(0)], w=[uk])
            P.act(lambda e: e.activation(out=wT[:], in_=ps[1][:, :], func=AF.Copy), r=[PK(1)], w=[wTk])

        def C1():
            for h in range(4):
                P.pe(lambda e, h=h: e.matmul(ps[2][:, HS[h]], lhsT=wT[:, HS[h]], rhs=Sb[:, HS[h]], start=True,
                                             stop=True), r=[wTk, "Sb"], w=[PK(2)])
            for h in range(4):
                P.pe(lambda e, h=h: e.matmul(ps[3][:, HS[h]], lhsT=qT[:, h, tk], rhs=Sb[:, HS[h]], start=True,
                                             stop=True), r=["qT", "Sb"], w=[PK(3)])
            P.dve(lambda e: e.tensor_tensor(out=vnew[:], in0=u_sb[:], in1=ps[2][:, :], op=ALU.subtract),
                  r=[uk, PK(2)], w=["vnew"])

        def C2():
            for h in range(4):
                P.pe(lambda e, h=h: e.matmul(ps[4][:, HS[h]], lhsT=qkT[:, HS[h]], rhs=vnew[:, HS[h]], start=True,
                                             stop=True), r=[qkk, "vnew"], w=[PK(4)])
            for h in range(4):
                P.pe(lambda e, h=h: e.matmul(ps[2][:, HS[h]], lhsT=kdec[:, HS[h]], rhs=vnew[:, HS[h]], start=True,
                                             stop=True), r=[kdk, "vnew"], w=[PK(2)])
            for h in range(4):
                P.act(lambda e, h=h: e.activation(out=qS_sb[:, HS[h]], in_=ps[3][:, HS[h]], func=AF.Copy,
                                                  scale=egc[:, h:h + 1]), r=[PK(3), e3k], w=["qS_sb"])

        def C3():
            for h in range(4):
                P.dve(lambda e, h=h: e.scalar_tensor_tensor(out=S[:, HS[h]], in0=S[:, HS[h]],
                                                            scalar=etot[:, h:h + 1], in1=ps[2][:, HS[h]],
                                                            op0=ALU.mult, op1=ALU.add),
                      r=["S", e3k, PK(2)], w=["S"])
            P.act(lambda e: e.activation(out=Sb[:], in_=S[:], func=AF.Copy), r=["S"], w=["Sb"])
            P.dve(lambda e: e.tensor_tensor(out=o_sb[:], in0=qS_sb[:], in1=ps[4][:, :], op=ALU.add),
                  r=["qS_sb", PK(4)], w=["o_sb"])

        def C4():
            for h in range(4):
                P.act(lambda e, h=h: e.activation(out=osq[:, HS[h]], in_=o_sb[:, HS[h]], func=AF.Square,
                                                  accum_out=bg_c[:, 4 + h:5 + h]), r=["o_sb"], w=["njunk", "bg_c"])
            P.dve(lambda e: e.tensor_scalar(out=bg_c[:, 4:8], in0=bg_c[:, 4:8], scalar1=1.0 / 128, scalar2=1e-6,
                                            op0=ALU.mult, op1=ALU.add), r=["bg_c"], w=["bg_c"])
            P.act(lambda e: e.activation(out=bg_c[:, 4:8], in_=bg_c[:, 4:8], func=AF.Ln), r=["bg_c"], w=["bg_c"])
            P.act(lambda e: e.activation(out=bg_c[:, 4:8], in_=bg_c[:, 4:8], func=AF.Exp, scale=-0.5), r=["bg_c"],
                  w=["bg_c"])

        def C5():
            for h in range(4):
                P.dve(lambda e, h=h: e.scalar_tensor_tensor(out=o_sb[:, HS[h]], in0=o_sb[:, HS[h]],
                                                            scalar=bg_c[:, 4 + h:5 + h], in1=dn4[:, HS[h]],
                                                            op0=ALU.mult, op1=ALU.mult),
                      r=["o_sb", "bg_c", "dn4"], w=["o_sb"])
            P.dve(lambda e: e.tensor_tensor(out=dtok[:], in0=o_sb[:], in1=zs[:, t, :], op=ALU.mult),
                  r=["o_sb", "zs"], w=["dtok"])
            pv = psb(3)
            for h in range(4):
                P.pe(lambda e, h=h: e.transpose(out=pv[:, HS[h]], in_=dtok[:, HS[h]], identity=idb[:]),
                     r=["dtok", "idb"], w=[PK(3)])
            P.act(lambda e: e.activation(out=dT[:, 0:4, t * 128:(t + 1) * 128],
                                         in_=pv[:, 0:512].rearrange("p (h n) -> p h n", h=4), func=AF.Copy),
                  r=[PK(3)], w=["dT"])

        return A_, N_, M_, [C1, C2, C3, C4, C5]

    tiles = [make_tile(t) for t in range(NT)]
    A0, N0, M0, _ = tiles[0]
    A0()
    for m in range(1, 6):
        N0(m)
    M0()
    for t in range(NT):
        Cs = tiles[t][3]
        if t + 1 < NT:
            An, Nn, Mn, _ = tiles[t + 1]
            An()
            for m in range(1, 6):
                Nn(m)
                Cs[m - 1]()
            Mn()
        else:
            for c in Cs:
                c()
    P.dma(V.ndel_p.rearrange("h d e -> d h e"), H4(S[:]), r=["S"], tag="out")

    stop_at(5)
    P.barrier()
    A.reset(V.m_w)
    sel4f = A.t([4, 512], F32, "sel4f")
    P.pool(lambda e: e.tensor_copy(out=sel4f[:].rearrange("p (s m) -> p s m", s=4),
                                   in_=cst[0:4, C_OH16:C_OH16 + 4].unsqueeze(2).to_broadcast([4, 4, 128])),
           r=["cst"], w=["sel4f"])
    sel4 = sel4f[0:4, :]
    egS = f32t("egS", (4, NS))
    P.act(lambda e: e.activation(out=egS[:], in_=gbS[:, 1, :], func=AF.Exp), r=["gbS"], w=["egS"])
    Bbc = f32t("Bbc", (128, 4, NS))
    EGbc = f32t("EGbc", (128, 4, NS))
    for h in range(4):
        P.pe(lambda e, h=h: e.matmul(ps[0][:, h * NS:(h + 1) * NS], lhsT=sel4[:, h * 128:(h + 1) * 128],
                                     rhs=gbS[:, 0, :], start=True, stop=True), r=["gbS", "sel4f"], w=[PK(0)])
        P.pe(lambda e, h=h: e.matmul(ps[1][:, h * NS:(h + 1) * NS], lhsT=sel4[:, h * 128:(h + 1) * 128],
                                     rhs=egS[:, :], start=True, stop=True), r=["egS", "sel4f"], w=[PK(1)])
    P.act(lambda e: e.activation(out=Bbc[:].rearrange("p h s -> p (h s)"), in_=ps[0][:, 0:64], func=AF.Copy),
          r=[PK(0)], w=["Bbc"])
    P.act(lambda e: e.activation(out=EGbc[:].rearrange("p h s -> p (h s)"), in_=ps[1][:, 0:64], func=AF.Copy),
          r=[PK(1)], w=["EGbc"])
    qkf = f32t("qkf", (128, 4, 2, NS))
    P.dve(lambda e: e.tensor_copy(out=qkf[:, :, 0, :], in_=kT[:, :, T:TT]), r=["kT"], w=["qkf"])
    P.dve(lambda e: e.tensor_copy(out=qkf[:, :, 1, :], in_=qT[:, :, T:TT]), r=["qT"], w=["qkf"])
    S0all = f32t("S0all", (128, NS, 4, 128))
    for s in range(NS):
        P.dma(S0all[:, s, :, :], V.sdel[s * 4:(s + 1) * 4, :, :].rearrange("h d e -> d h e"), w=[("S0", s)],
              tag="S0g%d" % (s // 4), group=True)
    for s in range(NS):
        for h in range(4):
            P.pe(lambda e, s=s, h=h: e.matmul(ps[2][:, (s * 4 + h) * 2:(s * 4 + h) * 2 + 2],
                                              lhsT=S0all[:, s, h, :], rhs=qkf[:, h, :, s], start=True,
                                              stop=True), r=[("S0", s), "qkf"], w=[PK(2)])
    kqS = f32t("kqS", (128, NS, 4, 2))
    P.act(lambda e: e.activation(out=kqS[:].rearrange("p s h t -> p (s h t)"), in_=ps[2][:, 0:128], func=AF.Copy),
          r=[PK(2)], w=["kqS"])
    kSv = kqS[:, :, :, 0].rearrange("p s h -> p h s")
    qSv = kqS[:, :, :, 1].rearrange("p s h -> p h s")
    vn = f32t("vn", (128, 4, NS))
    tmpS = f32t("tmpS", (128, 4, NS))
    P.dve(lambda e: e.tensor_tensor(out=tmpS[:], in0=EGbc[:], in1=kSv, op=ALU.mult), r=["EGbc", "kqS"], w=["tmpS"])
    P.dve(lambda e: e.tensor_tensor(out=vn[:], in0=vTs[:], in1=tmpS[:], op=ALU.subtract), r=["vTs", "tmpS"],
          w=["vn"])
    P.dve(lambda e: e.tensor_tensor(out=vn[:], in0=vn[:], in1=Bbc[:], op=ALU.mult), r=["vn", "Bbc"], w=["vn"])
    prodS = f32t("prodS", (128, 4, NS))
    P.dve(lambda e: e.tensor_tensor(out=prodS[:], in0=qkf[:, :, 0, :], in1=qkf[:, :, 1, :], op=ALU.mult),
          r=["qkf"], w=["prodS"])
    P.pe(lambda e: e.matmul(ps[3][:, 0:64], lhsT=onef, rhs=prodS[:].rearrange("p h s -> p (h s)"), start=True,
                            stop=True), r=["prodS", "cst"], w=[PK(3)])
    oS = f32t("oS", (128, 4, NS))
    P.dve(lambda e: e.tensor_tensor(out=oS[:].rearrange("p h s -> p (h s)"), in0=ps[3][:, 0:64],
                                    in1=vn[:].rearrange("p h s -> p (h s)"), op=ALU.mult), r=[PK(3), "vn"],
          w=["oS"])
    P.dve(lambda e: e.tensor_tensor(out=tmpS[:], in0=EGbc[:], in1=qSv, op=ALU.mult), r=["EGbc", "kqS"], w=["tmpS"])
    P.dve(lambda e: e.tensor_tensor(out=oS[:], in0=oS[:], in1=tmpS[:], op=ALU.add), r=["oS", "tmpS"], w=["oS"])
    P.dve(lambda e: e.tensor_tensor(out=tmpS[:], in0=oS[:], in1=oS[:], op=ALU.mult), r=["oS"], w=["tmpS"])
    P.pe(lambda e: e.matmul(ps[4][:, 0:64], lhsT=onef, rhs=tmpS[:].rearrange("p h s -> p (h s)"), start=True,
                            stop=True), r=["tmpS", "cst"], w=[PK(4)])
    rS = f32t("rS", (128, 64))
    P.dve(lambda e: e.tensor_scalar(out=rS[:], in0=ps[4][:, 0:64], scalar1=1.0 / 128, scalar2=1e-6, op0=ALU.mult,
                                    op1=ALU.add), r=[PK(4)], w=["rS"])
    P.act(lambda e: e.activation(out=rS[:], in_=rS[:], func=AF.Sqrt), r=["rS"], w=["rS"])
    P.dve(lambda e: e.reciprocal(out=rS[:], in_=rS[:]), r=["rS"], w=["rS"])
    P.dve(lambda e: e.tensor_tensor(out=oS[:].rearrange("p h s -> p (h s)"), in0=oS[:].rearrange("p h s -> p (h s)"),
                                    in1=rS[:], op=ALU.mult), r=["oS", "rS"], w=["oS"])
    P.dve(lambda e: e.scalar_tensor_tensor(out=dT[:, 0:4, T:TT], in0=oS[:], scalar=pc[:, PC_DN:PC_DN + 1],
                                           in1=zsT[:], op0=ALU.mult, op1=ALU.mult), r=["oS", "pc", "zsT"],
          w=["dT"])
    ktS = f32t("ktS", (16, 512))
    vtS = f32t("vtS", (16, 512))
    for h in range(4):
        P.pe(lambda e, h=h: e.transpose(out=ps[5][0:NS, h * 128:(h + 1) * 128], in_=qkf[:, h, 0, :],
                                        identity=ident_f), r=["qkf", "cst"], w=[PK(5)])
        P.pe(lambda e, h=h: e.transpose(out=ps[6][0:NS, h * 128:(h + 1) * 128], in_=vn[:, h, :], identity=ident_f),
             r=["vn", "cst"], w=[PK(6)])
    P.act(lambda e: e.activation(out=ktS[:], in_=ps[5][0:NS, :], func=AF.Copy), r=[PK(5)], w=["ktS"])
    P.act(lambda e: e.activation(out=vtS[:], in_=ps[6][0:NS, :], func=AF.Copy), r=[PK(6)], w=["vtS"])
    vmask = [f32t("vmask%d" % i, (16, 512)) for i in range(2)]
    oh16 = cst[0:16, C_OH16:C_OH16 + 16]
    for s in range(NS):
        sl = s % 2
        P.dve(lambda e, s=s, sl=sl: e.tensor_scalar(out=vmask[sl][:], in0=vtS[:], scalar1=oh16[:, s:s + 1],
                                                    scalar2=None, op0=ALU.mult), r=["vtS", "cst"],
              w=[("vmask", sl)])
        b = 7 if sl else 0
        for h in range(4):
            hs = slice(h * 128, (h + 1) * 128)
            P.pe(lambda e, hs=hs, sl=sl, b=b: e.matmul(ps[b][:, hs], lhsT=ktS[:, hs], rhs=vmask[sl][:, hs],
                                                       start=True, stop=True), r=["ktS", ("vmask", sl)], w=[PK(b)])
        for h in range(4):
            hs = slice(h * 128, (h + 1) * 128)
            P.dve(lambda e, hs=hs, h=h, s=s, b=b: e.scalar_tensor_tensor(
                out=S0all[:, s, h, :], in0=S0all[:, s, h, :], scalar=EGbc[:, h, s:s + 1], in1=ps[b][:, hs],
                op0=ALU.mult, op1=ALU.add), r=["EGbc", PK(b)], w=[("S0", s)])
        P.dma(V.ndel_s[s * 4:(s + 1) * 4, :, :].rearrange("h d e -> d h e"), S0all[:, s, :, :], r=[("S0", s)],
              tag="out")

    stop_at(6)
    P.barrier()
    mX, mXe = V.mX, V.mXe
    x1 = A.child(mX, mXe).t([128, NT + 1, 1024], F32, "x1")
    A.reset(mXe)
    wo = A.t([128, 8, 512], BF16, "wo_a")
    wo2 = A.t([128, 8, 512], BF16, "wo_b")
    xin = [A.t([128, 1024], F32, "xin2_%d" % i) for i in range(2)]
    load_w(wo, "wo", V.w_out, 0, 8, 0, 512)
    load_w(wo2, "wo2", V.w_out, 0, 8, 512, 512)
    for t in range(NT + 1):
        n = 128 if t < NT else NS
        c0 = t * 128
        s = t % 2
        src_d = V.x_p[t * 128:(t + 1) * 128, :] if t < NT else V.x_s[:, :]
        P.dma(xin[s][:n, :], src_d, w=[("xin", s)], tag="xin%d" % s)
        for half, wv in ((0, wo), (1, wo2)):
            b = (t % 2) * 2 + half
            for kc in range(8):
                src = cT[:, kc, c0:c0 + n] if kc < 4 else dT[:, kc - 4, c0:c0 + n]
                P.pe(lambda e, b=b, kc=kc, src=src, wv=wv, n=n: e.matmul(ps[b][:n, :], lhsT=src, rhs=wv[:, kc, :],
                                                                        start=(kc == 0), stop=(kc == 7)),
                     r=["cT", "dT", "wo", "wo2"], w=[PK(b)])
            P.dve(lambda e, b=b, t=t, n=n, half=half, s=s: e.tensor_tensor(
                out=x1[:n, t, half * 512:(half + 1) * 512], in0=ps[b][:n, :],
                in1=xin[s][:n, half * 512:(half + 1) * 512], op=ALU.add), r=[PK(b), ("xin", s)], w=[("x1", t)])

    stop_at(7)
    P.barrier()
    A.reset(mXe)
    AH = A.child(V.off_cT, mX)
    kmT = AH.t([128, 8, 256], BF16, "kmT")
    vmem = AH.t([128, 2, 1024], BF16, "vmem")
    mT = AH.t([128, 8, 256], BF16, "mT")
    hT3 = hT
    wk2 = [A.t([128, 8, 512], BF16, "wk%d" % i) for i in range(2)]
    mo_f = A.t([128, 1024], F32, "mo_f")
    min_ = [A.t([128, 1024], F32, "min%d" % i) for i in range(2)]
    load_gam(PR_GKV)
    items = []
    for mt in range(2):
        pre = (lambda mt=mt: P.dma(min_[mt][:], V.mem[mt * 128:(mt + 1) * 128, :], w=[("min", mt)],
                                   tag="min%d" % mt))
        items.append((min_[mt][:, :], [("min", mt)], 128, mt * 128, pre))
    V.norm_pass(items, mT, "mT")
    kvl = [(0, V.w_mk, V.mk_p, 0), (0, V.w_mk, V.mk_p, 1), (1, V.w_mv, V.mv_p, 0), (1, V.w_mv, V.mv_p, 1)]
    load_w(wk2[0], ("wk", 0), V.w_mk, 0, 8, 0, 512)
    for li, (which, wd, od, half) in enumerate(kvl):
        if True:
            wk = wk2[li % 2]
            wkk = ("wk", li % 2)
            if li + 1 < 4:
                load_w(wk2[(li + 1) % 2], ("wk", (li + 1) % 2), kvl[li + 1][1], 0, 8, kvl[li + 1][3] * 512, 512)
            for mt in range(2):
                b = 2 + mt
                for kc in range(8):
                    P.pe(lambda e, b=b, kc=kc, mt=mt: e.matmul(ps[b][:, :], lhsT=mT[:, kc, mt * 128:(mt + 1) * 128],
                                                               rhs=wk[:, kc, :], start=(kc == 0), stop=(kc == 7)),
                         r=[("mT", 0), wkk], w=[PK(b)])
                P.act(lambda e, b=b, half=half: e.activation(out=mo_f[:, half * 512:(half + 1) * 512],
                                                             in_=ps[b][:, :], func=AF.Copy), r=[PK(b)], w=["mo_f"])
                P.dma(od[mt * 128:(mt + 1) * 128, half * 512:(half + 1) * 512],
                      mo_f[:, half * 512:(half + 1) * 512], r=["mo_f"], tag="omo")
                if which == 1:
                    P.dve(lambda e, b=b, half=half, mt=mt: e.tensor_copy(
                        out=vmem[:, mt, half * 512:(half + 1) * 512], in_=ps[b][:, :]), r=[PK(b)], w=["vmem"])
            if which == 0:
                for c4 in range(4):
                    b = 4 + c4 % 2
                    for kc in range(8):
                        P.pe(lambda e, b=b, kc=kc, c4=c4: e.matmul(ps[b][:, 0:256],
                                                                   lhsT=wk[:, kc, c4 * 128:(c4 + 1) * 128],
                                                                   rhs=mT[:, kc, :], start=(kc == 0), stop=(kc == 7)),
                             r=[("mT", 0), wkk], w=[PK(b)])
                    P.act(lambda e, b=b, c4=c4, half=half: e.activation(out=kmT[:, half * 4 + c4, :],
                                                                        in_=ps[b][:, 0:256], func=AF.Copy),
                          r=[PK(b)], w=["kmT"])
    stop_at(8)
    P.barrier()
    A.reset(mXe)
    qa = A.t([128, 8, TT], BF16, "qa")
    m_qa = A.mark()
    wq = A.t([128, 8, 512], BF16, "wq")
    wq2 = A.t([128, 8, 512], BF16, "wq2")
    load_gam(PR_GMQ)
    V.norm_pass([(x1[:(128 if t < NT else NS), t, :], [("x1", t)], (128 if t < NT else NS), t * 128, None)
                 for t in range(NT + 1)], hT3, "hT")
    load_w(wq, "wq", V.w_mq, 0, 8, 0, 512)
    load_w(wq2, "wq2", V.w_mq, 0, 8, 512, 512)
    BLK = [(i * 512, 512) for i in range(4)] + [(T, NS)]
    qi = 0
    for c in range(8):
        wv = wq if c < 4 else wq2
        cc = c % 4
        for (t0, n) in BLK:
            b = qi % 2
            qi += 1
            mm8(b, lambda kc: wv[:, kc, cc * 128:(cc + 1) * 128], lambda kc: hT3[:, kc, t0:t0 + n], n,
                [("hT", t0 // 512), "wq", "wq2"])
            P.act(lambda e, b=b, c=c, n=n, t0=t0: e.activation(out=qa[:, c, t0:t0 + n], in_=ps[b][:, 0:n],
                                                               func=AF.Copy, scale=1.0 / 16), r=[PK(b)], w=["qa"])
    stop_at(9)
    P.barrier()
    A.reset(m_qa)
    aoT = hT
    qs_s = AH.t([128, 8, NS], BF16, "qs_s")
    P.dve(lambda e: e.tensor_copy(out=qs_s[:], in_=qa[:, :, T:TT]), r=["qa"], w=["qs_s"])
    m_3c = A.mark()
    msm = [A.t([128, 16], F32, "msm%d" % i) for i in range(2)]
    ex = [A.t([128, 1024], F32, "ex%d" % i) for i in range(2)]
    pbf = [A.t([128, 1024], BF16, "pbf%d" % i) for i in range(2)]
    ptb = [A.t([128, 8, 128], BF16, "ptb%d" % i) for i in range(2)]
    iters = []
    for tg in range(NT):
        def S0(tg=tg):
            p = tg % 2
            lt = slice(tg * 128, (tg + 1) * 128)
            sb = (2 * p, 2 * p + 1)
            m_, mk_ = msm[p], ("msm", p)
            for h in range(4):
                b = sb[h // 2]
                cs = slice((h % 2) * 256, (h % 2) * 256 + 256)
                for hf in range(2):
                    P.pe(lambda e, b=b, cs=cs, h=h, hf=hf: e.matmul(ps[b][:, cs], lhsT=qa[:, h * 2 + hf, lt],
                                                                    rhs=kmT[:, h * 2 + hf, :], start=(hf == 0),
                                                                    stop=(hf == 1)), r=["qa", "kmT"], w=[PK(b)])
            for j in range(2):
                P.dve(lambda e, j=j: e.tensor_reduce(out=m_[:, 2 * j:2 * j + 2],
                                                     in_=ps[sb[j]][:, :].rearrange("p (h m) -> p h m", h=2),
                                                     axis=AX.X, op=ALU.max), r=[PK(sb[j])], w=[mk_])
            P.dve(lambda e: e.tensor_scalar(out=m_[:, 12:16], in0=m_[:, 0:4], scalar1=-1.0, scalar2=None,
                                            op0=ALU.mult), r=[mk_], w=[mk_])

        def S1(tg=tg):
            p = tg % 2
            sb = (2 * p, 2 * p + 1)
            m_, mk_ = msm[p], ("msm", p)
            ex_, exk = ex[p], ("ex", p)
            pb_, pbk = pbf[p], ("pbf", p)
            for h in range(4):
                P.act(lambda e, h=h: e.activation(out=ex_[:, h * 256:(h + 1) * 256],
                                                  in_=ps[sb[h // 2]][:, (h % 2) * 256:(h % 2) * 256 + 256],
                                                  func=AF.Exp, bias=m_[:, 12 + h:13 + h]),
                      r=[PK(sb[h // 2]), mk_], w=[exk])
            P.dve(lambda e: e.tensor_reduce(out=m_[:, 4:8], in_=ex_[:].rearrange("p (h m) -> p h m", h=4), axis=AX.X,
                                            op=ALU.add), r=[exk], w=[mk_])
            P.dve(lambda e: e.reciprocal(out=m_[:, 8:12], in_=m_[:, 4:8]), r=[mk_], w=[mk_])
            for h in range(4):
                P.dve(lambda e, h=h: e.tensor_scalar(out=pb_[:, h * 256:(h + 1) * 256],
                                                     in0=ex_[:, h * 256:(h + 1) * 256], scalar1=m_[:, 8 + h:9 + h],
                                                     scalar2=None, op0=ALU.mult), r=[exk, mk_], w=[pbk])

        def S2(tg=tg):
            p = tg % 2
            pb_, pbk = pbf[p], ("pbf", p)
            pt_, ptk = ptb[p], ("ptb", p)
            pv = psb(4)
            for c8 in range(8):
                P.pe(lambda e, c8=c8: e.transpose(out=pv[:, c8 * 128:(c8 + 1) * 128],
                                                  in_=pb_[:, c8 * 128:(c8 + 1) * 128], identity=idb[:]),
                     r=[pbk, "idb"], w=[PK(4)])
            P.act(lambda e: e.activation(out=pt_[:].rearrange("p a b -> p (a b)"), in_=pv[:, 0:1024], func=AF.Copy),
                  r=[PK(4)], w=[ptk])

        def S3(tg=tg):
            p = tg % 2
            lt = slice(tg * 128, (tg + 1) * 128)
            pt_, ptk = ptb[p], ("ptb", p)
            for h in range(4):
                ob = 5 + h // 2
                for hf in range(2):
                    cs = slice(((h % 2) * 2 + hf) * 128, ((h % 2) * 2 + hf + 1) * 128)
                    for mt in range(2):
                        P.pe(lambda e, ob=ob, cs=cs, h=h, hf=hf, mt=mt: e.matmul(
                            ps[ob][:, cs], lhsT=vmem[:, mt, h * 256 + hf * 128:h * 256 + (hf + 1) * 128],
                            rhs=pt_[:, h * 2 + mt, :], start=(mt == 0), stop=(mt == 1)), r=["vmem", ptk],
                            w=[PK(ob)])
            P.act(lambda e: e.activation(out=aoT[:, 0:4, lt], in_=ps[5][:, :].rearrange("p (a n) -> p a n", a=4),
                                         func=AF.Copy), r=[PK(5)], w=["aoT"])
            P.dve(lambda e: e.tensor_copy(out=aoT[:, 4:8, lt], in_=ps[6][:, :].rearrange("p (a n) -> p a n", a=4)),
                  r=[PK(6)], w=["aoT"])

        iters.append([S0, S1, S2, S3])
    V.run_skewed(iters)
    P.barrier()
    A.reset(mXe)
    prod = A.t([128, 1024], F32, "prod")
    qtok = A.t([16, 1024], BF16, "qtok")
    sel16b = A.t([16, 2048], BF16, "sel16b")
    P.pool(lambda e: e.tensor_copy(out=sel16b[:].rearrange("p (s m) -> p s m", s=16),
                                   in_=cst[0:16, C_OH16:C_OH16 + 16].unsqueeze(2).to_broadcast([16, 16, 128])),
           r=["cst"], w=["sel16b"])
    NKS, NVS = 8, 6
    Kt2 = [A.t([128, 1024], BF16, "Kt%d" % i) for i in range(NKS)]
    Vt2 = [A.t([128, 2, 1024], BF16, "Vt%d" % i) for i in range(NVS)]
    scS = A.t([128, 2, 4], F32, "scS")
    sm4 = A.t([4, 256], F32, "sm4")
    sm4s = A.t([4, 8], F32, "sm4s")
    pS = A.t([128, 2, 4], BF16, "pS")
    aoS = A.t([128, 8, NS], F32, "aoS")
    pv = psb(2)
    for c in range(8):
        P.pe(lambda e, c=c: e.transpose(out=pv[0:NS, c * 128:(c + 1) * 128], in_=qs_s[:, c, :],
                                        identity=idb[:]), r=["qs_s", "idb"], w=[PK(2)])
    P.act(lambda e: e.activation(out=qtok[:], in_=pv[0:NS, 0:1024], func=AF.Copy), r=[PK(2)], w=["qtok"])
    Vt3 = Vt2
    scS2 = [scS, A.t([128, 2, 4], F32, "scSb")]
    sm42 = [sm4, A.t([4, 256], F32, "sm4b"), A.t([4, 256], F32, "sm4c")]
    sm4s2 = [sm4s, A.t([4, 8], F32, "sm4sb"), A.t([4, 8], F32, "sm4sc")]
    iters = []
    for s in range(NS):
        def S0(s=s):
            Vt = Vt3[s % NVS]
            vk = ("Vt", s % NVS)
            sc_, sck = scS2[s % 2], ("scS", s % 2)
            for mt in range(2):
                P.dma(Vt[:, mt, :], V.cmv[s, mt * 128:(mt + 1) * 128, :], w=[vk], tag="Vt%d" % (s % NVS), q="pool")
            qb = (3, 4) if s % 2 == 0 else (0, 1)
            for half in range(2):
                P.pe(lambda e, half=half: e.matmul(ps[qb[half]][:, :], lhsT=sel16b[:, s * 128:(s + 1) * 128],
                                                   rhs=qtok[:, half * 512:(half + 1) * 512], start=True, stop=True),
                     r=["qtok", "sel16b"], w=[PK(qb[half])])
            for mt in range(2):
                kti = s * 2 + mt
                Kt = Kt2[kti % NKS]
                kk = ("Kt", kti % NKS)
                P.dma(Kt[:, :], V.cmk[s, mt * 128:(mt + 1) * 128, :], w=[kk], tag="Kt%d" % (kti % NKS), q="pool")
                for half in range(2):
                    P.dve(lambda e, half=half, Kt=Kt: e.tensor_tensor(out=prod[:, half * 512:(half + 1) * 512],
                                                                      in0=ps[qb[half]][:, :],
                                                                      in1=Kt[:, half * 512:(half + 1) * 512],
                                                                      op=ALU.mult),
                          r=[PK(qb[half]), kk], w=["prod"])
                P.dve(lambda e, mt=mt: e.tensor_reduce(out=sc_[:, mt, :],
                                                       in_=prod[:].rearrange("p (h d) -> p h d", h=4),
                                                       axis=AX.X, op=ALU.add), r=["prod"], w=[sck])

        def S1(s=s):
            sc_, sck = scS2[s % 2], ("scS", s % 2)
            m4, m4k = sm42[s % 3], ("sm4", s % 3)
            m4s, m4sk = sm4s2[s % 3], ("sm4s", s % 3)
            for mt in range(2):
                P.pe(lambda e, mt=mt: e.transpose(out=ps[5][0:4, mt * 128:(mt + 1) * 128], in_=sc_[:, mt, :],
                                                  identity=ident_f), r=[sck, "cst"], w=[PK(5)])
            P.dve(lambda e: e.tensor_reduce(out=m4s[:, 0:1], in_=ps[5][0:4, 0:256], axis=AX.X, op=ALU.max),
                  r=[PK(5)], w=[m4sk])
            P.dve(lambda e: e.tensor_scalar(out=m4s[:, 1:2], in0=m4s[:, 0:1], scalar1=-1.0, scalar2=None,
                                            op0=ALU.mult), r=[m4sk], w=[m4sk])
            P.act(lambda e: e.activation(out=m4[:], in_=ps[5][0:4, 0:256], func=AF.Exp, bias=m4s[:, 1:2],
                                         accum_out=m4s[:, 2:3]), r=[PK(5), m4sk], w=[m4k, m4sk])

        def S1b(s=s):
            m4, m4k = sm42[s % 3], ("sm4", s % 3)
            m4s, m4sk = sm4s2[s % 3], ("sm4s", s % 3)
            P.dve(lambda e: e.reciprocal(out=m4s[:, 3:4], in_=m4s[:, 2:3]), r=[m4sk], w=[m4sk])
            P.dve(lambda e: e.tensor_scalar(out=m4[:], in0=m4[:], scalar1=m4s[:, 3:4], scalar2=None, op0=ALU.mult),
                  r=[m4k, m4sk], w=[m4k])

        def S2(s=s):
            Vt = Vt3[s % NVS]
            vk = ("Vt", s % NVS)
            m4, m4k = sm42[s % 3], ("sm4", s % 3)
            for mt in range(2):
                P.pe(lambda e, mt=mt: e.transpose(out=ps[6][:, mt * 4:(mt + 1) * 4],
                                                  in_=m4[:, mt * 128:(mt + 1) * 128], identity=ident_f[0:4, 0:4]),
                     r=[m4k, "cst"], w=[PK(6)])
            P.act(lambda e: e.activation(out=pS[:].rearrange("p a h -> p (a h)"), in_=ps[6][:, 0:8], func=AF.Copy),
                  r=[PK(6)], w=["pS"])
            for h in range(4):
                for hf in range(2):
                    c = h * 2 + hf
                    for mt in range(2):
                        P.pe(lambda e, c=c, mt=mt, h=h: e.matmul(ps[7][:, c:c + 1],
                                                                 lhsT=Vt[:, mt, c * 128:(c + 1) * 128],
                                                                 rhs=pS[:, mt, h:h + 1], start=(mt == 0),
                                                                 stop=(mt == 1)),
                             r=[vk, "pS"], w=[PK(7)])
            P.act(lambda e: e.activation(out=aoS[:, :, s], in_=ps[7][:, 0:8], func=AF.Copy), r=[PK(7)], w=["aoS"])

        iters.append([S0, S1, S1b, S2])
    V.run_skewed(iters)
    P.dve(lambda e: e.tensor_copy(out=aoT[:, :, T:TT], in_=aoS[:]), r=["aoS"], w=["aoT"])
    P.barrier()
    A.reset(mXe)
    wmo = A.t([128, 8, 512], BF16, "wmo")
    wmo2 = A.t([128, 8, 512], BF16, "wmo2")
    load_w(wmo, "wmo", V.w_mo, 0, 8, 0, 512)
    load_w(wmo2, "wmo2", V.w_mo, 0, 8, 512, 512)
    for t in range(NT + 1):
        n = 128 if t < NT else NS
        c0 = t * 128
        for half, wv in ((0, wmo), (1, wmo2)):
            b = (t % 2) * 2 + half
            for kc in range(8):
                P.pe(lambda e, b=b, kc=kc, wv=wv, n=n, c0=c0: e.matmul(ps[b][:n, :], lhsT=aoT[:, kc, c0:c0 + n],
                                                                      rhs=wv[:, kc, :], start=(kc == 0),
                                                                      stop=(kc == 7)), r=["aoT", "wmo", "wmo2"],
                     w=[PK(b)])
            P.dve(lambda e, b=b, t=t, n=n, half=half: e.tensor_tensor(
                out=x1[:n, t, half * 512:(half + 1) * 512], in0=ps[b][:n, :],
                in1=x1[:n, t, half * 512:(half + 1) * 512], op=ALU.add), r=[PK(b), ("x1", t)], w=[("x1", t)])

    stop_at(11)
    P.barrier()
    A.reset(mXe)
    TH = 1024
    AHh = A.child(V.off_hT, V.off_cT)
    hT4 = AHh.t([128, 8, TH + NS], BF16, "hT4")
    wg = [AHh.t([128, 8, 256], BF16, "wg%d" % i) for i in range(2)]
    wu = [AHh.t([128, 8, 256], BF16, "wu%d" % i) for i in range(2)]
    AHc = A.child(V.off_cT, mX)
    gF = AHc.t([128, 1024], F32, "gF")
    sgf = [AHc.t([128, 512], F32, "sgf%d" % i) for i in range(2)]
    fs = AHc.t([128, 8], F32, "fs")
    aT = A.t([128, 11, TH + NS], BF16, "aT")
    wdn = [A.t([128, 11, 512], BF16, "wdn%d" % i) for i in range(2)]
    fj = V.nrm_junk
    P.dma(gF[:], pr_d[:, PR_GF:PR_GF + 1024], w=["gF"], tag="c4")
    load_gam(PR_GFFN)
    gu_list = [(fh, gi) for _th in range(2) for fh in range(2) for gi in range(6)]

    def gu_load(idx):
        fh, gi = gu_list[idx]
        ncg = 256 if gi < 5 else 128
        c0 = fh * 1408 + gi * 256
        sl = idx % 2
        load_w(wg[sl], ("wg", sl), V.w_gate, 0, 8, c0, ncg)
        load_w(wu[sl], ("wu", sl), V.w_up, 0, 8, c0, ncg)

    fsl = [AHc.t([128, 8], F32, "fs%d" % i) for i in range(2)]
    fcnt = [0]

    def final_norm(t):
        n = 128 if t < NT else NS
        i_ = fcnt[0] % 2
        fcnt[0] += 1
        fs_ = fsl[i_]
        fk = ("fs", i_)
        P.act(lambda e: e.activation(out=fj[:n, :], in_=x1[:n, t, :], func=AF.Square, accum_out=fs_[:n, 0:1]),
              r=[("x1", t)], w=["njunk", fk])
        P.dve(lambda e: e.tensor_scalar(out=fs_[:n, 1:2], in0=fs_[:n, 0:1], scalar1=1.0 / 1024, scalar2=1e-6,
                                        op0=ALU.mult, op1=ALU.add), r=[fk], w=[fk])
        P.act(lambda e: e.activation(out=fs_[:n, 2:3], in_=fs_[:n, 1:2], func=AF.Sqrt), r=[fk], w=[fk])
        P.dve(lambda e: e.reciprocal(out=fs_[:n, 3:4], in_=fs_[:n, 2:3]), r=[fk], w=[fk])
        P.dve(lambda e: e.scalar_tensor_tensor(out=x1[:n, t, :], in0=x1[:n, t, :], scalar=fs_[:n, 3:4],
                                               in1=gF[:n, :], op0=ALU.mult, op1=ALU.mult),
              r=[("x1", t), fk, "gF"], w=[("x1", t)])
        dst_ = V.y_p[t * 128:(t + 1) * 128, :] if t < NT else V.y_s[:, :]
        P.dma(dst_, x1[:n, t, :], r=[("x1", t)], tag="out")

    gu_load(0)
    gidx = 0
    for th in range(2):
        tiles = list(range(th * 8, th * 8 + 8)) + ([NT] if th == 1 else [])
        V.norm_pass([(x1[:(128 if t < NT else NS), t, :], [("x1", t)], (128 if t < NT else NS), ti * 128, None)
                     for ti, t in enumerate(tiles)], hT4, "hT4")
        blks = [(0, 512), (512, 512)] + ([(1024, NS)] if th == 1 else [])
        for fh in range(2):
            for gi in range(6):
                cur = gidx
                gidx += 1
                if cur + 1 < len(gu_list):
                    gu_load(cur + 1)
                if gi == 2 or gi == 4:
                    hf_ = 0 if gi == 2 else 1
                    load_w(wdn[hf_], ("wdn", hf_), V.w_down, fh * 1408, 11, hf_ * 512, 512)
                ncg = 256 if gi < 5 else 128
                sl = cur % 2
                for cc in range(ncg // 128):
                    fc = gi * 2 + cc
                    for bi, (t0, n) in enumerate(blks):
                        bg, bu = (bi % 2) * 2, (bi % 2) * 2 + 1
                        mm8(bg, lambda kc: wg[sl][:, kc, cc * 128:(cc + 1) * 128],
                            lambda kc: hT4[:, kc, t0:t0 + n], n, [("hT4", t0 // 512), ("wg", sl)])
                        mm8(bu, lambda kc: wu[sl][:, kc, cc * 128:(cc + 1) * 128],
                            lambda kc: hT4[:, kc, t0:t0 + n], n, [("hT4", t0 // 512), ("wu", sl)])
                        sg = sgf[bi % 2]
                        P.act(lambda e, sg=sg, bg=bg, n=n: e.activation(out=sg[:, 0:n], in_=ps[bg][:, 0:n],
                                                                        func=AF.Silu), r=[PK(bg)],
                              w=[("sgf", bi % 2)])
                        P.dve(lambda e, sg=sg, bu=bu, n=n, fc=fc, t0=t0: e.tensor_tensor(
                            out=aT[:, fc, t0:t0 + n], in0=ps[bu][:, 0:n], in1=sg[:, 0:n], op=ALU.mult),
                            r=[PK(bu), ("sgf", bi % 2)], w=["aT"])
            for half in range(2):
                wslot = half
                wdk = ("wdn", wslot)
                for ti, t in enumerate(tiles):
                    n = 128 if t < NT else NS
                    b = 4 + (ti % 4)
                    for k in range(11):
                        P.pe(lambda e, b=b, k=k, ti=ti, n=n, wslot=wslot: e.matmul(
                            ps[b][:n, :], lhsT=aT[:, k, ti * 128:ti * 128 + n], rhs=wdn[wslot][:, k, :],
                            start=(k == 0), stop=(k == 10)), r=["aT", wdk], w=[PK(b)])
                    P.dve(lambda e, b=b, t=t, n=n, half=half: e.tensor_tensor(
                        out=x1[:n, t, half * 512:(half + 1) * 512], in0=ps[b][:n, :],
                        in1=x1[:n, t, half * 512:(half + 1) * 512], op=ALU.add), r=[PK(b), ("x1", t)],
                        w=[("x1", t)])
                    if fh == 1 and half == 1:
                        final_norm(t)
        continue
        for t in tiles:
            n = 128 if t < NT else NS
            P.act(lambda e, t=t, n=n: e.activation(out=fj[:n, :], in_=x1[:n, t, :], func=AF.Square,
                                                   accum_out=fs[:n, 0:1]), r=[("x1", t)], w=["njunk", "fs"])
            P.dve(lambda e, n=n: e.tensor_scalar(out=fs[:n, 1:2], in0=fs[:n, 0:1], scalar1=1.0 / 1024, scalar2=1e-6,
                                                 op0=ALU.mult, op1=ALU.add), r=["fs"], w=["fs"])
            P.act(lambda e, n=n: e.activation(out=fs[:n, 2:3], in_=fs[:n, 1:2], func=AF.Sqrt), r=["fs"], w=["fs"])
            P.dve(lambda e, n=n: e.reciprocal(out=fs[:n, 3:4], in_=fs[:n, 2:3]), r=["fs"], w=["fs"])
            P.dve(lambda e, t=t, n=n: e.scalar_tensor_tensor(out=x1[:n, t, :], in0=x1[:n, t, :], scalar=fs[:n, 3:4],
                                                             in1=gF[:n, :], op0=ALU.mult, op1=ALU.add if False else ALU.mult),
                  r=[("x1", t), "fs", "gF"], w=[("x1", t)])
            dst = V.y_p[t * 128:(t + 1) * 128, :] if t < NT else V.y_s[:, :]
            P.dma(dst, x1[:n, t, :], r=[("x1", t)], tag="out")

    P.emit(["out"])


_CACHE = {}


def kernel(**inputs):
    inp = {k: np.asarray(v) for k, v in inputs.items()}
    if "nc" not in _CACHE:
        _CACHE["nc"] = build_nc()[0]
    nc = _CACHE["nc"]
    cst = make_consts()
    pc, pr = make_params(inp)
    f = lambda a: np.ascontiguousarray(a, dtype=np.float32)
    shared = {
        "w_in": f(inp["w_in"][0]), "w_out": f(inp["w_out"][0]), "w_mq": f(inp["w_mq"][0]), "w_mk": f(inp["w_mk"][0]),
        "w_mv": f(inp["w_mv"][0]), "w_mo": f(inp["w_mo"][0]), "w_gate": f(inp["w_gate"][0]),
        "w_up": f(inp["w_up"][0]), "w_down": f(inp["w_down"][0]), "cst": cst, "pc": pc, "pr": pr,
    }
    in_maps = []
    for c in range(8):
        sl = slice(c * NS, (c + 1) * NS)
        m = dict(shared)
        m["x_p"] = f(inp["x_prompt"][c])
        m["x_s"] = f(inp["x_sample"][sl, 0, :])
        m["mem"] = f(inp["mem_prompt"][c])
        m["cconv"] = f(inp["cache_conv"][0, sl].reshape(NS * 30, 512))
        m["ssc"] = f(inp["state_short_conv"][0, sl].reshape(NS * 3, 1536))
        m["sdel"] = f(inp["state_delta"][0, sl].reshape(NS * 4, 128, 128))
        m["cmk"] = f(inp["cache_mem_k"][0, sl].reshape(NS, 256, 1024))
        m["cmv"] = f(inp["cache_mem_v"][0, sl].reshape(NS, 256, 1024))
        in_maps.append(m)
    res = run_bass_kernel_spmd(nc, in_maps, core_ids=list(range(8)))
    R = res.results
    cat = lambda k: np.stack([np.asarray(R[c][k]) for c in range(8)])
    y_p = cat("y_p")
    y_s = cat("y_s").reshape(128, 1, D)
    nconv_p = cat("nconv_p")[None]
    nsc_p = cat("nsc_p")[None]
    ndel_p = cat("ndel_p")[None]
    mk = cat("mk_p").reshape(1, 8, 256, 4, 256)
    mv = cat("mv_p").reshape(1, 8, 256, 4, 256)
    nconv_s = cat("nconv_s").reshape(1, 128, 30, 512)
    nsc_s = cat("nsc_s").reshape(1, 128, 3, 1536)
    ndel_s = cat("ndel_s").reshape(1, 128, 4, 128, 128)
    return tuple(np.ascontiguousarray(a, dtype=np.float32) for a in
                 (y_p, y_s, nconv_p, nsc_p, ndel_p, mk, mv, nconv_s, nsc_s, ndel_s))
```

```python
import numpy as np
import concourse.bass as bass
import concourse.mybir as mybir
from concourse.bass_utils import run_bass_kernel_spmd

F32 = mybir.dt.float32
BF16 = mybir.dt.bfloat16
AF = mybir.ActivationFunctionType
ALU = mybir.AluOpType
AX = mybir.AxisListType

T = 2048
NS = 16
TT = T + NS
NT = 16
D = 1024
DFF = 2816
NFC = 22


class Op:
    __slots__ = ("eng", "fn", "deps", "sig", "need", "dma", "tag", "cnt", "idx")


import os
STOP = float(os.environ.get("KSTOP", "99"))


class _Stop(Exception):
    pass


def stop_at(n):
    if STOP == n:
        raise _Stop()


class _Rec:
    def __init__(self):
        self.call = None

    def __getattr__(self, name):
        def f(*a, **k):
            self.call = (name, a, k)
            return self
        return f


class Prog:
    ENGS = ("sp", "act", "pool", "dve", "pe")

    def __init__(self, nc):
        self.nc = nc
        self.ops = {e: [] for e in self.ENGS}
        self.all = []
        self.lastw = {}
        self.readers = {}
        self.tagcnt = {}
        self.taggroup = {}
        self.base = []

    def _add(self, eng, fn, r, w, dma=False, tag=None, group=False):
        op = Op()
        rec = _Rec()
        fn(rec)
        name_, a_, k_ = rec.call
        fn = lambda e, name_=name_, a_=a_, k_=k_: getattr(e, name_)(*a_, **k_)
        op.eng, op.fn, op.dma, op.tag = eng, fn, dma, tag
        op.need = False
        op.sig = None
        psr = [k for k in r if isinstance(k, tuple) and k[0] == "ps"]
        if psr:
            r = [k for k in r if k not in psr]
            w = list(w) + psr
        deps = list(self.base)
        for k in r:
            if k in self.lastw:
                deps.append(self.lastw[k])
        for k in w:
            if k in self.lastw:
                deps.append(self.lastw[k])
            deps.extend(self.readers.get(k, ()))
        if dma:
            deps = [d for d in deps if not (d.dma and d.tag == tag)]
        if eng == "pe":
            deps = [d for d in deps if d.dma or d.eng != "pe"]
        op.deps = deps
        for k in r:
            self.readers.setdefault(k, []).append(op)
        for k in w:
            self.lastw[k] = op
            self.readers[k] = []
        if dma:
            self.tagcnt[tag] = self.tagcnt.get(tag, 0) + 1
            self.taggroup[tag] = group
            op.cnt = self.tagcnt[tag]
        op.idx = len(self.all)
        self.all.append(op)
        self.ops[eng].append(op)
        return op

    def pe(self, fn, r=(), w=()):
        return self._add("pe", fn, r, w)

    def dve(self, fn, r=(), w=()):
        return self._add("dve", fn, r, w)

    def act(self, fn, r=(), w=()):
        return self._add("act", fn, r, w)

    def pool(self, fn, r=(), w=()):
        return self._add("pool", fn, r, w)

    def dma(self, out, in_, r=(), w=(), tag="ld", group=False, q="sp"):
        return self._add(q, lambda e: e.dma_start(out=out, in_=in_), r, w, dma=True, tag=tag, group=group)

    def barrier(self):
        base = []
        for e in self.ENGS:
            last = None
            for op in reversed(self.ops[e]):
                if not op.dma:
                    last = op
                    break
            if last is not None:
                base.append(last)
        lastdma = {}
        for op in self.all:
            if op.dma:
                lastdma[op.tag] = op
        base.extend(lastdma.values())
        self.base = base
        self.lastw = {}
        self.readers = {}

    def emit(self, final_tags):
        nc = self.nc
        for op in self.all:
            for d in op.deps:
                if not d.dma:
                    d.need = True
        for e in self.ENGS:
            n = 0
            for op in self.ops[e]:
                if not op.dma and op.need:
                    n += 1
                    op.sig = n
        from contextlib import ExitStack
        with ExitStack() as st:
            esem = {e: st.enter_context(nc.semaphore("s_" + e)) for e in self.ENGS}
            tsem = {t: st.enter_context(nc.semaphore("t_%d" % i)) for i, t in enumerate(self.tagcnt)}
            block = st.enter_context(nc.Block())

            def run(e, eng):
                waited = {}
                for op in self.ops[e]:
                    need = {}
                    for d in op.deps:
                        if d.dma:
                            s = tsem[d.tag]
                            v = 16 * (self.tagcnt[d.tag] if self.taggroup[d.tag] else d.cnt)
                        else:
                            s = esem[d.eng]
                            v = d.sig
                        if v > need.get(s, (0, 0))[1] if s in need else True:
                            need[s] = (s, v)
                    for s, v in need.values():
                        if waited.get(s, 0) < v:
                            eng.wait_ge(s, v)
                            waited[s] = v
                    ins = op.fn(eng)
                    if op.dma:
                        ins.then_inc(tsem[op.tag], 16)
                    elif op.need:
                        ins.then_inc(esem[e], 1)
                if e == "sp":
                    for t in tsem:
                        eng.wait_ge(tsem[t], 16 * self.tagcnt[t])

            block.sync(lambda eng: run("sp", eng))
            block.scalar(lambda eng: run("act", eng))
            block.gpsimd(lambda eng: run("pool", eng))
            block.vector(lambda eng: run("dve", eng))
            block.tensor(lambda eng: run("pe", eng))


class Alloc:
    def __init__(self, nc):
        self.nc = nc
        self.off = (int(nc.sbuf_base) + 63) // 64 * 64
        self.top = int(nc.sbuf_top)
        self.n = 0

    def t(self, shape, dt, name=None):
        sz = 2 if dt == BF16 else 4
        nb = sz
        for s in shape[1:]:
            nb *= s
        nb = (nb + 63) // 64 * 64
        assert self.off + nb <= self.top, ("SBUF overflow", name, self.off, nb, self.top)
        self.n += 1
        h = self.nc.alloc_sbuf_tensor_at("%s_%d" % (name or "t", self.n), list(shape), dt, offset=self.off)
        self.off += nb
        return h

    def child(self, off, top):
        c = Alloc.__new__(Alloc)
        c.nc, c.off, c.top, c.n = self.nc, (off + 63) // 64 * 64, top, self.n + 1000 * (1 + off % 97)
        return c

    def mark(self):
        return self.off

    def reset(self, m):
        self.off = m


def make_consts():
    i = np.arange(128)
    ident = np.eye(128, dtype=np.float32)
    U = (i[:, None] <= i[None, :]).astype(np.float32)
    SL = (i[:, None] > i[None, :]).astype(np.float32)
    ones = np.ones((128, 128), np.float32)
    blk = (i[:, None] // 64) == (i[None, :] // 64)
    mBDneg = -(SL * blk).astype(np.float32)
    mOFF = (SL * (~blk)).astype(np.float32)
    mTin = U.copy()
    oh16 = np.zeros((128, 16), np.float32)
    oh16[:16, :16] = np.eye(16)
    parts = [ident, U, SL, ones, mBDneg, mOFF, mTin, oh16]
    return np.ascontiguousarray(np.concatenate(parts, axis=1))


C_ID, C_U, C_SL, C_ONE = 0, 128, 256, 384
C_BD, C_OFF, C_TIN = 512, 640, 768
C_OH16 = 896
NCST = C_OH16 + 16

PC_CONVW, PC_CONVB, PC_LNG, PC_LNB, PC_SCW, PC_DN, PC_ALOG, PC_DTB = 0, 124, 128, 132, 136, 184, 185, 186
NPC = 187
PR_GMIX, PR_GMQ, PR_GFFN, PR_GKV, PR_GF, PR_DN4, PR_ALOG, PR_DTB = 0, 1024, 2048, 3072, 4096, 5120, 5632, 5636
NPR = 5640


def make_params(inp):
    pc = np.zeros((128, NPC), np.float32)
    pc[:, PC_CONVW:PC_CONVW + 124] = inp["conv_w"][0].reshape(31, 4, 128).transpose(2, 1, 0).reshape(128, 124)
    pc[:, PC_CONVB:PC_CONVB + 4] = inp["conv_b"][0].reshape(4, 128).T
    pc[:, PC_LNG:PC_LNG + 4] = inp["conv_ln_g"][0].reshape(4, 128).T
    pc[:, PC_LNB:PC_LNB + 4] = inp["conv_ln_b"][0].reshape(4, 128).T
    pc[:, PC_SCW:PC_SCW + 48] = inp["sc_w"][0].reshape(4, 12, 128).transpose(2, 1, 0).reshape(128, 48)
    pc[:, PC_DN] = inp["dn_norm"][0]
    pc[:4, PC_ALOG] = inp["a_log"][0]
    pc[:4, PC_DTB] = inp["dt_bias"][0]
    pr = np.zeros((128, NPR), np.float32)
    bc = lambda v: np.broadcast_to(np.asarray(v, np.float32).reshape(1, -1), (128, np.asarray(v).size))
    pr[:, PR_GMIX:PR_GMIX + 1024] = bc(inp["norm_mix"][0])
    pr[:, PR_GMQ:PR_GMQ + 1024] = bc(inp["norm_mem_q"][0])
    pr[:, PR_GFFN:PR_GFFN + 1024] = bc(inp["norm_ffn"][0])
    pr[:, PR_GKV:PR_GKV + 1024] = bc(inp["norm_mem_kv"][0])
    pr[:, PR_GF:PR_GF + 1024] = bc(inp["norm_f"])
    pr[:, PR_DN4:PR_DN4 + 512] = bc(np.tile(inp["dn_norm"][0], 4))
    pr[:, PR_ALOG:PR_ALOG + 4] = bc(inp["a_log"][0])
    pr[:, PR_DTB:PR_DTB + 4] = bc(inp["dt_bias"][0])
    return pc, pr


def build_nc():
    nc = bass.Bass("TRN2", target_bir_lowering=False)
    P = Prog(nc)
    try:
        return _build_nc(nc, P)
    except _Stop:
        P.emit(["out"])
        return nc, P


def _build_nc(nc, P):
    A = Alloc(nc)

    def dr(name, shape, out=False):
        return nc.dram_tensor(name, list(shape), F32, kind="ExternalOutput" if out else "ExternalInput").ap()

    x_p = dr("x_p", [T, D]); x_s = dr("x_s", [NS, D]); mem = dr("mem", [256, D])
    cconv = dr("cconv", [NS * 30, 512]); ssc = dr("ssc", [NS * 3, 1536]); sdel = dr("sdel", [NS * 4, 128, 128])
    cmk = dr("cmk", [NS, 256, 1024]); cmv = dr("cmv", [NS, 256, 1024])
    w_in = dr("w_in", [D, 3080]); w_out = dr("w_out", [D, D]); w_mq = dr("w_mq", [D, D]); w_mk = dr("w_mk", [D, D])
    w_mv = dr("w_mv", [D, D]); w_mo = dr("w_mo", [D, D]); w_gate = dr("w_gate", [D, DFF]); w_up = dr("w_up", [D, DFF])
    w_down = dr("w_down", [DFF, D])
    cst_d = dr("cst", [128, NCST]); pc_d = dr("pc", [128, NPC]); pr_d = dr("pr", [128, NPR])
    y_p = dr("y_p", [T, D], True); y_s = dr("y_s", [NS, D], True)
    nconv_p = dr("nconv_p", [30, 512], True); nsc_p = dr("nsc_p", [3, 1536], True)
    ndel_p = dr("ndel_p", [4, 128, 128], True)
    mk_p = dr("mk_p", [256, D], True); mv_p = dr("mv_p", [256, D], True)
    nconv_s = dr("nconv_s", [NS, 30, 512], True); nsc_s = dr("nsc_s", [NS, 3, 1536], True)
    ndel_s = dr("ndel_s", [NS * 4, 128, 128], True)

    ps = [nc.alloc_psum_tensor("ps%d" % i, [128, 512], F32) for i in range(8)]
    PK = lambda i: ("ps", i)

    def psb(i):
        return ps[i][:].bitcast(BF16)

    cst = A.t([128, NCST], F32, "cst")
    pc = A.t([128, NPC], F32, "pc")
    idb = A.t([128, 128], BF16, "idb")
    oneb = A.t([128, 128], BF16, "oneb")
    onesc = A.t([128, 128], BF16, "onesc")
    gam = A.t([128, 1024], F32, "gam")
    P.dma(cst[:], cst_d[:, :], w=["cst"], tag="c0")
    P.dma(pc[:], pc_d[:, :], w=["pc"], tag="c1")
    P.dve(lambda e: e.tensor_copy(out=idb[:], in_=cst[:, C_ID:C_ID + 128]), r=["cst"], w=["idb"])
    P.dve(lambda e: e.tensor_copy(out=oneb[:], in_=cst[:, C_ONE:C_ONE + 128]), r=["cst"], w=["oneb"])
    P.dve(lambda e: e.tensor_scalar(out=onesc[:], in0=cst[:, C_ONE:C_ONE + 128], scalar1=1.0 / 512, scalar2=None,
                                    op0=ALU.mult), r=["cst"], w=["onesc"])
    ident_f = cst[:, C_ID:C_ID + 128]
    Uf = cst[:, C_U:C_U + 128]
    SLf = cst[:, C_SL:C_SL + 128]
    onef = cst[:, C_ONE:C_ONE + 128]

    def load_gam(off):
        P.dma(gam[:], pr_d[:, off:off + 1024], w=["gam"], tag="gam")

    nrm_junk = A.t([128, 1024], BF16, "njunk")
    nrm_xn3 = [A.t([128, 1024], BF16, "nxn%d" % i) for i in range(3)]
    nrm_s3 = [A.t([128, 8], F32, "nrs%d" % i) for i in range(3)]

    def norm_pass(items, dstT, dkey):
        L = len(items)

        def S1(i):
            src, rkeys, n, col0, pre = items[i]
            if pre is not None:
                pre()
            ns_, nsk = nrm_s3[i % 3], ("nrs", i % 3)
            P.act(lambda e: e.activation(out=nrm_junk[:n, :], in_=src, func=AF.Square, accum_out=ns_[:n, 0:1]),
                  r=rkeys, w=[nsk, "njunk"])
            P.dve(lambda e: e.tensor_scalar(out=ns_[:n, 1:2], in0=ns_[:n, 0:1], scalar1=1.0 / 1024, scalar2=1e-6,
                                            op0=ALU.mult, op1=ALU.add), r=[nsk], w=[nsk])

        def S2(i):
            src, rkeys, n, col0, pre = items[i]
            ns_, nsk = nrm_s3[i % 3], ("nrs", i % 3)
            xn_, nxk = nrm_xn3[i % 3], ("nxn", i % 3)
            P.act(lambda e: e.activation(out=ns_[:n, 2:3], in_=ns_[:n, 1:2], func=AF.Sqrt), r=[nsk], w=[nsk])
            P.dve(lambda e: e.reciprocal(out=ns_[:n, 3:4], in_=ns_[:n, 2:3]), r=[nsk], w=[nsk])
            P.dve(lambda e: e.scalar_tensor_tensor(out=xn_[:n, :], in0=src, scalar=ns_[:n, 3:4], in1=gam[:n, :],
                                                   op0=ALU.mult, op1=ALU.mult), r=rkeys + [nsk, "gam"], w=[nxk])

        def S3(i):
            src, rkeys, n, col0, pre = items[i]
            xn_, nxk = nrm_xn3[i % 3], ("nxn", i % 3)
            bank = i % 2
            pv = psb(bank)
            for kc in range(8):
                P.pe(lambda e, kc=kc: e.transpose(out=pv[:, kc * 128:kc * 128 + n],
                                                  in_=xn_[:n, kc * 128:(kc + 1) * 128], identity=idb[:n, :n]),
                     r=[nxk, "idb"], w=[PK(bank)])
            pv3 = pv.rearrange("p (c n) -> p c n", c=8)
            dk_ = (dkey, col0 // 512)
            if i % 2 == 0:
                P.act(lambda e: e.activation(out=dstT[:, :, col0:col0 + n], in_=pv3[:, :, 0:n], func=AF.Copy),
                      r=[PK(bank)], w=[dk_])
            else:
                P.dve(lambda e: e.tensor_copy(out=dstT[:, :, col0:col0 + n], in_=pv3[:, :, 0:n]),
                      r=[PK(bank)], w=[dk_])

        for step in range(L + 2):
            if step < L:
                S1(step)
            if 0 <= step - 1 < L:
                S2(step - 1)
            if 0 <= step - 2 < L:
                S3(step - 2)

    def load_w(dst, dkey, wd, r0, nk, c0, ncols):
        for k in range(nk):
            P.dma(dst[:, k, 0:ncols], wd[r0 + k * 128:r0 + (k + 1) * 128, c0:c0 + ncols], w=[dkey],
                  tag="w_" + str(dkey), q="pool")

    off_hT = A.mark()
    hT = A.t([128, 8, TT], BF16, "hT")
    off_cT = A.mark()
    cT = A.t([128, 4, TT], BF16, "cT")
    mX = A.mark()
    XSZ = 17 * 1024 * 4
    A.off += XSZ
    mXe = A.mark()
    AX_ = A.child(mX, mXe)
    zs = A.t([128, NT, 512], BF16, "zs")
    zsT = A.t([128, 4, NS], F32, "zsT")
    vTs = A.t([128, 4, NS], F32, "vTs")
    gtok = A.t([128, NT, 4], F32, "gtok")
    btok = A.t([128, NT, 4], F32, "btok")
    gbS = A.t([4, 2, NS], F32, "gbS")
    utail = A.t([128, 4, 32], F32, "utail")
    ptail = A.t([128, 12, 4], F32, "ptail")
    unew_s = A.t([128, 4, NS], F32, "unews")
    pnew_s = A.t([128, 12, NS], F32, "pnews")
    m_w = A.mark()
    wsl = [A.t([128, 8, 512], BF16, "wsl%d" % i) for i in range(3)]
    m_p1 = A.mark()
    A1 = A
    A = AX_

    xin = [A.t([128, 1024], F32, "xin%d" % i) for i in range(3)]
    load_gam(PR_GMIX)
    items = []
    for t in range(NT + 1):
        s = t % 3
        n = 128 if t < NT else NS
        src_d = x_p[t * 128:(t + 1) * 128, :] if t < NT else x_s[:, :]
        pre = (lambda s=s, n=n, src_d=src_d: P.dma(xin[s][:n, :], src_d, w=[("xin", s)], tag="xin%d" % s))
        items.append((xin[s][:n, :], [("xin", s)], n, t * 128, pre))
    norm_pass(items, hT, "hT")

    stop_at(1)
    BLK = [(i * 512, 512) for i in range(4)] + [(T, NS)]

    def mm8(bank, w_ap_fn, rhs_fn, ncols, rk, mrows=128):
        for kc in range(8):
            P.pe(lambda e, kc=kc: e.matmul(ps[bank][0:mrows, 0:ncols], lhsT=w_ap_fn(kc), rhs=rhs_fn(kc),
                                           start=(kc == 0), stop=(kc == 7)), r=rk, w=[PK(bank)])

    load_w(wsl[2], ("wsl", 2), w_in, 0, 8, 3072, 8)
    load_w(wsl[0], ("wsl", 0), w_in, 0, 8, 0, 512)
    load_w(wsl[1], ("wsl", 1), w_in, 0, 8, 512, 512)
    prb = A.t([128, 16], F32, "prb")
    P.dma(prb[:, 0:8], pr_d[:, PR_ALOG:PR_ALOG + 8], w=["prb"], tag="c2")
    P.act(lambda e: e.activation(out=prb[:, 8:12], in_=prb[:, 0:4], func=AF.Exp), r=["prb"], w=["prb"])
    P.dve(lambda e: e.tensor_scalar(out=prb[:, 8:12], in0=prb[:, 8:12], scalar1=-1.0, scalar2=None, op0=ALU.mult),
          r=["prb"], w=["prb"])
    negA_c = A.t([4, 1], F32, "negAc")
    P.act(lambda e: e.activation(out=negA_c[:], in_=pc[0:4, PC_ALOG:PC_ALOG + 1], func=AF.Exp), r=["pc"], w=["negAc"])
    P.dve(lambda e: e.tensor_scalar(out=negA_c[:], in0=negA_c[:], scalar1=-1.0, scalar2=None, op0=ALU.mult),
          r=["negAc"], w=["negAc"])
    gx = A.t([128, NT, 4], F32, "gx")
    dtb64 = A.t([128, NT, 4], F32, "dtb64")
    nga64 = A.t([128, NT, 4], F32, "nga64")
    P.dve(lambda e: e.tensor_copy(out=dtb64[:], in_=prb[:, 4:8].unsqueeze(1).to_broadcast([128, NT, 4])),
          r=["prb"], w=["dtb64"])
    P.dve(lambda e: e.tensor_copy(out=nga64[:], in_=prb[:, 8:12].unsqueeze(1).to_broadcast([128, NT, 4])),
          r=["prb"], w=["nga64"])
    for t in range(NT):
        for kc in range(8):
            P.pe(lambda e, t=t, kc=kc: e.matmul(ps[2][:, t * 8:(t + 1) * 8], lhsT=hT[:, kc, t * 128:(t + 1) * 128],
                                                rhs=wsl[2][:, kc, 0:8], start=(kc == 0), stop=(kc == 7)),
                 r=[("hT", t // 4), ("wsl", 2)], w=[PK(2)])
    ps3 = ps[2][:, 0:NT * 8].rearrange("p (t c) -> p t c", c=8)
    P.act(lambda e: e.activation(out=btok[:, :, :], in_=ps3[:, :, 0:4], func=AF.Exp, scale=-1.0), r=[PK(2)],
          w=["btok"])
    P.dve(lambda e: e.tensor_scalar(out=btok[:, :, :], in0=btok[:, :, :], scalar1=1.0, scalar2=None, op0=ALU.add),
          r=["btok"], w=["btok"])
    P.dve(lambda e: e.reciprocal(out=btok[:, :, :], in_=btok[:, :, :]), r=["btok"], w=["btok"])
    P.dve(lambda e: e.tensor_tensor(out=gx[:], in0=ps3[:, :, 4:8], in1=dtb64[:], op=ALU.add), r=[PK(2), "dtb64"],
          w=["gx"])
    P.act(lambda e: e.activation(out=gx[:], in_=gx[:], func=AF.Exp), r=["gx"], w=["gx"])
    P.act(lambda e: e.activation(out=gx[:], in_=gx[:], func=AF.Ln, bias=1.0), r=["gx"], w=["gx"])
    P.dve(lambda e: e.tensor_tensor(out=gtok[:, :, :], in0=gx[:], in1=nga64[:], op=ALU.mult), r=["gx", "nga64"],
          w=["gtok"])
    for half in range(2):
        b = 2 + half
        mm8(b, lambda kc: wsl[2][:, kc, half * 4:half * 4 + 4], lambda kc: hT[:, kc, T:TT], NS, [("hT", 4), ("wsl", 2)],
            mrows=4)
    P.act(lambda e: e.activation(out=gbS[:, 0, :], in_=ps[2][0:4, 0:NS], func=AF.Exp, scale=-1.0), r=[PK(2)],
          w=["gbS"])
    P.dve(lambda e: e.tensor_scalar(out=gbS[:, 0, :], in0=gbS[:, 0, :], scalar1=1.0, scalar2=None, op0=ALU.add),
          r=["gbS"], w=["gbS"])
    P.dve(lambda e: e.reciprocal(out=gbS[:, 0, :], in_=gbS[:, 0, :]), r=["gbS"], w=["gbS"])
    gts = A.t([4, 2, NS], F32, "gts")
    P.act(lambda e: e.activation(out=gts[:, 0, :], in_=ps[3][0:4, 0:NS], func=AF.Exp,
                                 bias=pc[0:4, PC_DTB:PC_DTB + 1]), r=[PK(3), "pc"], w=["gts"])
    P.act(lambda e: e.activation(out=gts[:, 1, :], in_=gts[:, 0, :], func=AF.Ln, bias=1.0), r=["gts"], w=["gts"])
    P.dve(lambda e: e.tensor_scalar(out=gbS[:, 1, :], in0=gts[:, 1, :], scalar1=negA_c[:, 0:1], scalar2=None,
                                    op0=ALU.mult), r=["gts", "negAc"], w=["gbS"])

    stop_at(2)
    upad = A.t([128, 4, 30 + T], BF16, "upad")
    us = A.t([128, 4, NS, 32], BF16, "us")
    sig = [A.t([128, 512], F32, "sig%d" % i) for i in range(2)]
    diag = A.t([128, 31, 128], BF16, "diag")
    P.dve(lambda e: e.memset(upad[:, :, 0:30], 0.0), w=["upad"])
    load_w(wsl[2], ("wsl", 2), w_in, 0, 8, 1024, 512)
    cc_in = A.t([120, 4, 512], F32, "ccin")
    for g4 in range(4):
        P.dma(cc_in[:, g4, :], cconv[g4 * 120:(g4 + 1) * 120, :], w=[("ccin", g4)], tag="cc", group=True)
    P.dma(nconv_s[:, 0:29, :], cconv.rearrange("(s j) c -> s j c", j=30)[:, 1:30, :], tag="out")
    for c in range(4):
        for g4 in range(4):
            b = 4 + (g4 % 2)
            P.pe(lambda e, c=c, g4=g4, b=b: e.transpose(out=ps[b][:, 0:120], in_=cc_in[:, g4, c * 128:(c + 1) * 128],
                                                        identity=ident_f[0:120, 0:120]), r=[("ccin", g4), "cst"], w=[PK(b)])
            P.act(lambda e, c=c, g4=g4, b=b: e.activation(
                out=us[:, c, g4 * 4:(g4 + 1) * 4, 0:30],
                in_=ps[b][:, 0:120].rearrange("p (s j) -> p s j", j=30), func=AF.Copy), r=[PK(b)], w=["us"])
    for c in range(4):
        for bi, (t0, n) in enumerate(BLK):
            ba, bb = (bi % 2) * 2, (bi % 2) * 2 + 1
            mm8(ba, lambda kc: wsl[0][:, kc, c * 128:(c + 1) * 128], lambda kc: hT[:, kc, t0:t0 + n], n,
                [("hT", t0 // 512), ("wsl", 0)])
            mm8(bb, lambda kc: wsl[1][:, kc, c * 128:(c + 1) * 128], lambda kc: hT[:, kc, t0:t0 + n], n,
                [("hT", t0 // 512), ("wsl", 1)])
            sg = sig[bi % 2]
            sk = ("sig", bi % 2)
            P.act(lambda e, sg=sg, bb=bb, n=n: e.activation(out=sg[:, 0:n], in_=ps[bb][:, 0:n], func=AF.Sigmoid),
                  r=[PK(bb)], w=[sk])
            if bi < 4:
                P.dve(lambda e, sg=sg, ba=ba, c=c, t0=t0: e.tensor_tensor(
                    out=upad[:, c, 30 + t0:30 + t0 + 512], in0=ps[ba][:, 0:512], in1=sg[:, 0:512], op=ALU.mult),
                    r=[PK(ba), sk], w=["upad"])
                if bi == 3:
                    P.dve(lambda e, sg=sg, ba=ba, c=c: e.tensor_tensor(
                        out=utail[:, c, 0:30], in0=ps[ba][:, 482:512], in1=sg[:, 482:512], op=ALU.mult),
                        r=[PK(ba), sk], w=["utail"])
            else:
                P.dve(lambda e, sg=sg, ba=ba, c=c: e.tensor_tensor(
                    out=unew_s[:, c, :], in0=ps[ba][:, 0:NS], in1=sg[:, 0:NS], op=ALU.mult),
                    r=[PK(ba), sk], w=["unews"])
                P.act(lambda e, c=c: e.activation(out=us[:, c, :, 30:31], in_=unew_s[:, c, :].unsqueeze(2),
                                                  func=AF.Copy), r=["unews"], w=["us"])
        for j in range(31):
            P.dve(lambda e, c=c, j=j: e.tensor_scalar(
                out=diag[:, j, :], in0=idb[:], scalar1=pc[:, PC_CONVW + c * 31 + j:PC_CONVW + c * 31 + j + 1],
                scalar2=None, op0=ALU.mult), r=["idb", "pc"], w=["diag"])
        for bi, (t0, n) in enumerate(BLK):
            b = 4 + bi % 2
            for j in range(31):
                if bi < 4:
                    rhs = upad[:, c, t0 + j:t0 + j + 512]
                else:
                    rhs = us[:, c, :, j]
                P.pe(lambda e, b=b, j=j, rhs=rhs, n=n: e.matmul(ps[b][:, 0:n], lhsT=diag[:, j, :], rhs=rhs,
                                                               start=(j == 0), stop=(j == 30)),
                     r=["diag", "upad", "us"], w=[PK(b)])
            P.act(lambda e, b=b, c=c, t0=t0, n=n: e.activation(
                out=cT[:, c, t0:t0 + n], in_=ps[b][:, 0:n], func=AF.Identity,
                bias=pc[:, PC_CONVB + c:PC_CONVB + c + 1]), r=[PK(b), "pc"], w=["cT"])
    csq = [A.t([128, 512], BF16, "csq%d" % i) for i in range(2)]
    lnm = A.t([128, 512], F32, "lnm")
    lnv = A.t([128, 512], F32, "lnv")
    lnt = [A.t([128, 512], F32, "lnt%d" % i) for i in range(2)]
    for bi, (t0, n) in enumerate(BLK):
        for c in range(4):
            P.pe(lambda e, c=c, t0=t0, n=n: e.matmul(ps[0][:, 0:n], lhsT=onesc[:], rhs=cT[:, c, t0:t0 + n],
                                                     start=(c == 0), stop=(c == 3)), r=["cT", "onesc"], w=[PK(0)])
        for c in range(4):
            q = csq[c % 2]
            P.dve(lambda e, q=q, c=c, t0=t0, n=n: e.tensor_tensor(out=q[:, 0:n], in0=cT[:, c, t0:t0 + n],
                                                                  in1=cT[:, c, t0:t0 + n], op=ALU.mult),
                  r=["cT"], w=[("csq", c % 2)])
            P.pe(lambda e, q=q, c=c, n=n: e.matmul(ps[1][:, 0:n], lhsT=onesc[:], rhs=q[:, 0:n],
                                                   start=(c == 0), stop=(c == 3)), r=[("csq", c % 2), "onesc"],
                 w=[PK(1)])
        P.act(lambda e, n=n: e.activation(out=lnm[:, 0:n], in_=ps[0][:, 0:n], func=AF.Copy), r=[PK(0)], w=["lnm"])
        P.dve(lambda e, n=n: e.tensor_tensor(out=lnv[:, 0:n], in0=lnm[:, 0:n], in1=lnm[:, 0:n], op=ALU.mult),
              r=["lnm"], w=["lnv"])
        P.dve(lambda e, n=n: e.tensor_tensor(out=lnv[:, 0:n], in0=ps[1][:, 0:n], in1=lnv[:, 0:n], op=ALU.subtract),
              r=[PK(1), "lnv"], w=["lnv"])
        P.dve(lambda e, n=n: e.tensor_scalar(out=lnv[:, 0:n], in0=lnv[:, 0:n], scalar1=0.0, scalar2=1e-5,
                                             op0=ALU.max, op1=ALU.add), r=["lnv"], w=["lnv"])
        P.act(lambda e, n=n: e.activation(out=lnv[:, 0:n], in_=lnv[:, 0:n], func=AF.Ln), r=["lnv"], w=["lnv"])
        P.act(lambda e, n=n: e.activation(out=lnv[:, 0:n], in_=lnv[:, 0:n], func=AF.Exp, scale=-0.5), r=["lnv"],
              w=["lnv"])
        for c in range(4):
            tt_ = lnt[c % 2]
            tk = ("lnt", c % 2)
            P.dve(lambda e, tt_=tt_, c=c, t0=t0, n=n: e.tensor_tensor(out=tt_[:, 0:n], in0=cT[:, c, t0:t0 + n],
                                                                      in1=lnm[:, 0:n], op=ALU.subtract),
                  r=["cT", "lnm"], w=[tk])
            P.dve(lambda e, tt_=tt_, n=n: e.tensor_tensor(out=tt_[:, 0:n], in0=tt_[:, 0:n], in1=lnv[:, 0:n],
                                                          op=ALU.mult), r=[tk, "lnv"], w=[tk])
            P.act(lambda e, tt_=tt_, c=c, t0=t0, n=n: e.activation(
                out=cT[:, c, t0:t0 + n], in_=tt_[:, 0:n], func=AF.Silu,
                scale=pc[:, PC_LNG + c:PC_LNG + c + 1], bias=pc[:, PC_LNB + c:PC_LNB + c + 1]),
                r=[tk, "pc"], w=["cT"])
    otl = A.t([32, 512], F32, "otl")
    for c in range(4):
        P.pe(lambda e, c=c: e.transpose(out=ps[2][0:30, c * 128:(c + 1) * 128], in_=utail[:, c, 0:30],
                                        identity=ident_f), r=["utail", "cst"], w=[PK(2)])
    P.act(lambda e: e.activation(out=otl[0:30, :], in_=ps[2][0:30, :], func=AF.Copy), r=[PK(2)], w=["otl"])
    P.dma(nconv_p[:, :], otl[0:30, :], r=["otl"], tag="out")
    otl2 = A.t([16, 512], F32, "otl2")
    for c in range(4):
        P.pe(lambda e, c=c: e.transpose(out=ps[3][0:NS, c * 128:(c + 1) * 128], in_=unew_s[:, c, :],
                                        identity=ident_f), r=["unews", "cst"], w=[PK(3)])
    P.act(lambda e: e.activation(out=otl2[:, :], in_=ps[3][0:NS, :], func=AF.Copy), r=[PK(3)], w=["otl2"])
    P.dma(nconv_s[:, 29, :], otl2[:, :], r=["otl2"], tag="out")

    stop_at(3)
    P.barrier()
    AX_ = A1.child(mX, mXe)
    qT = AX_.t([128, 4, TT], BF16, "qT")
    kT = AX_.t([128, 4, TT], BF16, "kT")
    ktok = AX_.t([128, NT, 512], BF16, "ktok")
    vb = AX_.t([128, NT, 512], BF16, "vb")
    A = A1
    scd2 = [A.t([128, 4, 128], BF16, "scd%d" % i) for i in range(2)]
    pre = [A.t([128, 3 + 512], BF16, "pre%d" % i) for i in range(2)]
    pres2 = [A.t([128, NS, 4], BF16, "pres%d" % i) for i in range(2)]
    sfl = [A.t([128, 512], F32, "sfl%d" % i) for i in range(2)]
    sqb = [A.t([128, 512], BF16, "sqb%d" % i) for i in range(2)]
    rnb = [A.t([128, 512], F32, "rnb%d" % i) for i in range(2)]
    vtmp = [A.t([128, 512], BF16, "vtmp%d" % i) for i in range(2)]
    ss_in = A.t([48, 1536], F32, "ssin")
    P.dma(ss_in[:, :], ssc[:, :], w=["ssin"], tag="ssin")
    P.dma(nsc_s[:, 0:2, :], ssc.rearrange("(s j) c -> s j c", j=3)[:, 1:3, :], tag="out")

    def run_skewed(iters):
        L = len(iters)
        S_ = max(len(x) for x in iters)
        for step in range(L + S_ - 1):
            for k in range(S_):
                i = step - k
                if 0 <= i < L and k < len(iters[i]):
                    iters[i][k]()

    iters = []
    it = [0]
    for grp in range(3):
        sl = (2, 0, 1)[grp]
        for hh in range(4):
            ch = grp * 4 + hh
            scd = scd2[ch % 2]
            sck = ("scd", ch % 2)
            pres = pres2[ch % 2]
            psk = ("pres", ch % 2)
            for bi, (t0, n) in enumerate(BLK):
                i = it[0]
                it[0] += 1

                def S0(grp=grp, sl=sl, hh=hh, ch=ch, scd=scd, sck=sck, pres=pres, psk=psk, bi=bi, t0=t0, n=n, i=i):
                    b = i % 2
                    pr_ = pre[i % 2]
                    prk = ("pre", i % 2)
                    if hh == 0 and bi == 0:
                        if grp < 2:
                            nsl = (2, 0, 1)[grp + 1]
                            load_w(wsl[nsl], ("wsl", nsl), w_in, 0, 8, 1024 + (grp + 1) * 512, 512)
                        else:
                            load_w(wsl[2], ("wsl", 2), w_in, 0, 8, 2560, 512)
                    if bi == 0:
                        for j in range(4):
                            P.dve(lambda e, j=j: e.tensor_scalar(
                                out=scd[:, j, :], in0=idb[:],
                                scalar1=pc[:, PC_SCW + ch * 4 + j:PC_SCW + ch * 4 + j + 1],
                                scalar2=None, op0=ALU.mult), r=["idb", "pc"], w=[sck])
                        P.pe(lambda e: e.transpose(out=ps[6][:, 0:48], in_=ss_in[:, ch * 128:(ch + 1) * 128],
                                                   identity=ident_f[0:48, 0:48]), r=["ssin", "cst"], w=[PK(6)])
                        P.act(lambda e: e.activation(out=pres[:, :, 0:3],
                                                     in_=ps[6][:, 0:48].rearrange("p (s j) -> p s j", j=3),
                                                     func=AF.Copy), r=[PK(6)], w=[psk])
                    mm8(b, lambda kc: wsl[sl][:, kc, hh * 128:(hh + 1) * 128], lambda kc: hT[:, kc, t0:t0 + n], n,
                        [("hT", t0 // 512), ("wsl", sl)])
                    if bi < 4:
                        if bi == 0:
                            P.dve(lambda e: e.memset(pr_[:, 0:3], 0.0), w=[prk])
                        else:
                            po = pre[(i - 1) % 2]
                            P.dve(lambda e: e.tensor_copy(out=pr_[:, 0:3], in_=po[:, 512:515]),
                                  r=[("pre", (i - 1) % 2)], w=[prk])
                        P.dve(lambda e: e.tensor_copy(out=pr_[:, 3:515], in_=ps[b][:, 0:512]),
                              r=[PK(b)], w=[prk])
                        if bi == 3:
                            P.dve(lambda e: e.tensor_copy(out=ptail[:, ch, 0:3], in_=ps[b][:, 509:512]),
                                  r=[PK(b)], w=["ptail"])
                    else:
                        P.act(lambda e: e.activation(out=pres[:, :, 3:4], in_=ps[b][:, 0:NS].unsqueeze(2),
                                                     func=AF.Copy), r=[PK(b)], w=[psk])
                        P.dve(lambda e: e.tensor_copy(out=pnew_s[:, ch, :], in_=ps[b][:, 0:NS]),
                              r=[PK(b)], w=["pnews"])

                def S1(grp=grp, hh=hh, scd=scd, sck=sck, pres=pres, psk=psk, bi=bi, n=n, i=i):
                    b2 = 2 + i % 2
                    pr_ = pre[i % 2]
                    prk = ("pre", i % 2)
                    for j in range(4):
                        rhs = pr_[:, j:j + 512] if bi < 4 else pres[:, :, j]
                        P.pe(lambda e, j=j, rhs=rhs: e.matmul(ps[b2][:, 0:n], lhsT=scd[:, j, :], rhs=rhs,
                                                              start=(j == 0), stop=(j == 3)),
                             r=[sck, prk if bi < 4 else psk], w=[PK(b2)])
                    if grp == 2:
                        if bi < 4:
                            vt = vtmp[i % 2]
                            P.act(lambda e: e.activation(out=vt[:, :], in_=ps[b2][:, 0:512], func=AF.Silu),
                                  r=[PK(b2)], w=[("vtmp", i % 2)])
                        else:
                            P.act(lambda e: e.activation(out=vTs[:, hh, :], in_=ps[b2][:, 0:NS], func=AF.Silu),
                                  r=[PK(b2)], w=["vTs"])
                        return
                    sf = sfl[i % 2]
                    sfk = ("sfl", i % 2)
                    P.act(lambda e: e.activation(out=sf[:, 0:n], in_=ps[b2][:, 0:n], func=AF.Exp, scale=-1.0),
                          r=[PK(b2)], w=[sfk])
                    P.act(lambda e: e.activation(out=sf[:, 0:n], in_=sf[:, 0:n], func=AF.Ln, bias=1.0),
                          r=[sfk], w=[sfk])
                    P.act(lambda e: e.activation(out=sf[:, 0:n], in_=sf[:, 0:n], func=AF.Exp, scale=-1.0),
                          r=[sfk], w=[sfk])
                    P.dve(lambda e: e.tensor_tensor(out=sf[:, 0:n], in0=ps[b2][:, 0:n], in1=sf[:, 0:n], op=ALU.mult),
                          r=[PK(b2), sfk], w=[sfk])
                    sq = sqb[i % 2]
                    P.dve(lambda e: e.tensor_tensor(out=sq[:, 0:n], in0=sf[:, 0:n], in1=sf[:, 0:n], op=ALU.mult),
                          r=[sfk], w=[("sqb", i % 2)])

                def S2(grp=grp, hh=hh, bi=bi, t0=t0, n=n, i=i):
                    pb = 4 + i % 2
                    if grp == 2:
                        if bi == 4:
                            return
                        vt = vtmp[i % 2]
                        vk = ("vtmp", i % 2)
                        pv = psb(pb)
                        for tl in range(4):
                            P.pe(lambda e, tl=tl: e.transpose(out=pv[:, tl * 128:(tl + 1) * 128],
                                                              in_=vt[:, tl * 128:(tl + 1) * 128], identity=idb[:]),
                                 r=[vk, "idb"], w=[PK(pb)])
                        for tl in range(4):
                            tg = bi * 4 + tl
                            P.dve(lambda e, tl=tl, tg=tg: e.tensor_scalar(
                                out=vb[:, tg, hh * 128:(hh + 1) * 128], in0=pv[:, tl * 128:(tl + 1) * 128],
                                scalar1=btok[:, tg, hh:hh + 1], scalar2=None, op0=ALU.mult),
                                r=[PK(pb), "btok"], w=["vb"])
                        return
                    dst = qT if grp == 0 else kT
                    dk = "qT" if grp == 0 else "kT"
                    sf = sfl[i % 2]
                    sfk = ("sfl", i % 2)
                    sq = sqb[i % 2]
                    P.pe(lambda e: e.matmul(ps[pb][:, 0:n], lhsT=oneb[:], rhs=sq[:, 0:n], start=True, stop=True),
                         r=[("sqb", i % 2), "oneb"], w=[PK(pb)])
                    rn = rnb[i % 2]
                    rk_ = ("rnb", i % 2)
                    sc_ = 128.0 if grp == 0 else 1.0
                    P.act(lambda e: e.activation(out=rn[:, 0:n], in_=ps[pb][:, 0:n], func=AF.Ln, scale=sc_,
                                                 bias=1e-6 * sc_), r=[PK(pb)], w=[rk_])
                    P.act(lambda e: e.activation(out=rn[:, 0:n], in_=rn[:, 0:n], func=AF.Exp, scale=-0.5), r=[rk_],
                          w=[rk_])
                    P.dve(lambda e: e.tensor_tensor(out=dst[:, hh, t0:t0 + n], in0=sf[:, 0:n], in1=rn[:, 0:n],
                                                    op=ALU.mult), r=[sfk, rk_], w=[dk])

                def S3(grp=grp, hh=hh, bi=bi, t0=t0, i=i):
                    if not (grp == 1 and bi < 4):
                        return
                    pb2 = 6 + i % 2
                    pv = psb(pb2)
                    for tl in range(4):
                        P.pe(lambda e, tl=tl: e.transpose(out=pv[:, tl * 128:(tl + 1) * 128],
                                                          in_=kT[:, hh, t0 + tl * 128:t0 + (tl + 1) * 128],
                                                          identity=idb[:]), r=["kT", "idb"], w=[PK(pb2)])
                    P.dve(lambda e: e.tensor_copy(out=ktok[:, bi * 4:(bi + 1) * 4, hh * 128:(hh + 1) * 128],
                                                  in_=pv[:, 0:512].rearrange("p (t d) -> p t d", t=4)),
                          r=[PK(pb2)], w=["ktok"])

                iters.append([S0, S1, S2, S3])
    run_skewed(iters)
    otp = A.t([16, 1536], F32, "otp")
    for ch in range(12):
        b = ch // 4
        P.pe(lambda e, ch=ch, b=b: e.transpose(out=ps[b][0:3, (ch % 4) * 128:(ch % 4 + 1) * 128],
                                               in_=ptail[:, ch, 0:3], identity=ident_f), r=["ptail", "cst"],
             w=[PK(b)])
    for b in range(3):
        P.act(lambda e, b=b: e.activation(out=otp[0:3, b * 512:(b + 1) * 512], in_=ps[b][0:3, :], func=AF.Copy),
              r=[PK(b)], w=["otp"])
    P.dma(nsc_p[:, :], otp[0:3, :], r=["otp"], tag="ootp")
    otp2 = otp
    for ch in range(12):
        b = 3 + ch // 4
        P.pe(lambda e, ch=ch, b=b: e.transpose(out=ps[b][0:NS, (ch % 4) * 128:(ch % 4 + 1) * 128],
                                               in_=pnew_s[:, ch, :], identity=ident_f), r=["pnews", "cst"],
             w=[PK(b)])
    for b in range(3):
        P.act(lambda e, b=b: e.activation(out=otp2[:, b * 512:(b + 1) * 512], in_=ps[3 + b][0:NS, :], func=AF.Copy),
              r=[PK(3 + b)], w=["otp"])
    P.dma(nsc_s[:, 2, :], otp2[:, :], r=["otp"], tag="ootp")

    for t in range(NT):
        b = t % 2
        mm8(b, lambda kc: hT[:, kc, t * 128:(t + 1) * 128], lambda kc: wsl[2][:, kc, 0:512], 512,
            [("hT", t // 4), ("wsl", 2)])
        P.act(lambda e, t=t, b=b: e.activation(out=zs[:, t, :], in_=ps[b][:, 0:512], func=AF.Silu), r=[PK(b)],
              w=["zs"])
    for hh in range(4):
        b = 2 + hh % 2
        mm8(b, lambda kc: wsl[2][:, kc, hh * 128:(hh + 1) * 128], lambda kc: hT[:, kc, T:TT], NS,
            [("hT", 4), ("wsl", 2)])
        P.act(lambda e, hh=hh, b=b: e.activation(out=zsT[:, hh, :], in_=ps[b][:, 0:NS], func=AF.Silu), r=[PK(b)],
              w=["zsT"])

    stop_at(4)
    build_rest(nc, P, A, locals())
    return nc, P


def build_rest(nc, P, A, L):
    g = dict(L)
    from types import SimpleNamespace
    V = SimpleNamespace(**g)
    ps, PK, psb, cst, pc, idb, oneb = V.ps, V.PK, V.psb, V.cst, V.pc, V.idb, V.oneb
    ident_f, Uf, SLf, onef = V.ident_f, V.Uf, V.SLf, V.onef
    hT, cT, qT, kT, ktok, vb, zs, zsT, vTs, gtok, btok, gbS = (V.hT, V.cT, V.qT, V.kT, V.ktok, V.vb, V.zs, V.zsT,
                                                                 V.vTs, V.gtok, V.btok, V.gbS)
    load_w, load_gam, mm8, gam = V.load_w, V.load_gam, V.mm8, V.gam
    pr_d = V.pr_d

    P.barrier()
    A.reset(V.m_w)
    dT = hT

    f32t = lambda name, shape=(128, 512): A.t(list(shape), F32, name)
    bft = lambda name, shape=(128, 512): A.t(list(shape), BF16, name)
    dn4 = f32t("dn4")
    P.dma(dn4[:], pr_d[:, PR_DN4:PR_DN4 + 512], w=["dn4"], tag="c3")
    S = f32t("S")
    Sb = bft("Sb")
    P.dve(lambda e: e.memset(S[:], 0.0), w=["S"])
    P.dve(lambda e: e.memset(Sb[:], 0.0), w=["Sb"])
    e3_2 = [f32t("e3_%d" % i, (128, 16)) for i in range(2)]
    gSL = f32t("gSL")
    E = f32t("E")
    ET = f32t("ET")
    EBbd = f32t("EBbd")
    EBoff = bft("EBoff")
    ETm = bft("ETm")
    Y = [bft("Y0"), bft("Y1")]
    YT = [bft("YT0"), bft("YT1")]
    PT = [bft("PT0"), bft("PT1")]
    Loff = bft("Loff")
    Tbd = bft("Tbd")
    Xb = bft("Xb")
    TTm = bft("TTm")
    kbg = bft("kbg")
    kdec_2 = [bft("kdec%d" % i) for i in range(2)]
    qkT_2 = [bft("qkT%d" % i) for i in range(2)]
    wT_2 = [bft("wT%d" % i) for i in range(2)]
    u_2 = [f32t("u_sb%d" % i) for i in range(2)]
    vnew = bft("vnew")
    o_sb = f32t("o_sb")
    qS_sb = f32t("qS_sb")
    bg_m = f32t("bg_m", (128, 8))
    bg_c = f32t("bg_c", (128, 8))
    dtok = bft("dtok")
    osq = V.nrm_junk
    B4 = lambda ap: ap.unsqueeze(1).to_broadcast([128, 4, 128])
    H4 = lambda ap: ap.rearrange("p (h n) -> p h n", h=4)
    mBD = B4(cst[:, C_BD:C_BD + 128])
    mOFF = B4(cst[:, C_OFF:C_OFF + 128])
    mTIN = B4(cst[:, C_TIN:C_TIN + 128])
    mBD4 = f32t("mBD4")
    P.pool(lambda e: e.tensor_copy(out=H4(mBD4[:]), in_=mBD), r=["cst"], w=["mBD4"])
    HS = [slice(h * 128, (h + 1) * 128) for h in range(4)]

    def make_tile(t):
        tk = slice(t * 128, (t + 1) * 128)
        p = t % 2
        e3, e3k = e3_2[p], ("e3", p)
        egc, erem, etot = e3[:, 0:4], e3[:, 4:8], e3[:, 8:12]
        qkT, qkk = qkT_2[p], ("qkT", p)
        wT, wTk = wT_2[p], ("wT", p)
        u_sb, uk = u_2[p], ("u_sb", p)
        kdec, kdk = kdec_2[p], ("kdec", p)
        gt = gtok[:, t, :]
        st = {"cur": 0}

        def A_():
            P.pe(lambda e: e.matmul(ps[0][:, 0:4], lhsT=Uf, rhs=gt, start=True, stop=True), r=["gtok", "cst"],
                 w=[PK(0)])
            P.pe(lambda e: e.matmul(ps[0][:, 4:8], lhsT=SLf, rhs=gt, start=True, stop=True), r=["gtok", "cst"],
                 w=[PK(0)])
            P.pe(lambda e: e.matmul(ps[0][:, 8:12], lhsT=onef, rhs=gt, start=True, stop=True), r=["gtok", "cst"],
                 w=[PK(0)])
            P.act(lambda e: e.activation(out=e3[:, 0:12], in_=ps[0][:, 0:12], func=AF.Exp), r=[PK(0)], w=[e3k])
            for h in range(4):
                P.dve(lambda e, h=h: e.tensor_scalar(out=gSL[:, HS[h]], in0=SLf, scalar1=gt[:, h:h + 1],
                                                     scalar2=None, op0=ALU.mult), r=["gtok", "cst"], w=["gSL"])
            P.pe(lambda e: e.matmul(ps[1][:, :], lhsT=Uf, rhs=gSL[:, :], start=True, stop=True), r=["gSL", "cst"],
                 w=[PK(1)])
            for h in range(4):
                P.pe(lambda e, h=h: e.matmul(ps[2][:, HS[h]], lhsT=gSL[:, HS[h]], rhs=Uf, start=True, stop=True),
                     r=["gSL", "cst"], w=[PK(2)])
            P.act(lambda e: e.activation(out=E[:], in_=ps[1][:, :], func=AF.Exp), r=[PK(1)], w=["E"])
            P.act(lambda e: e.activation(out=ET[:], in_=ps[2][:, :], func=AF.Exp), r=[PK(2)], w=["ET"])
            for h in range(4):
                P.dve(lambda e, h=h: e.tensor_scalar(out=E[:, HS[h]], in0=E[:, HS[h]], scalar1=btok[:, t, h:h + 1],
                                                     scalar2=None, op0=ALU.mult), r=["E", "btok"], w=["E"])
            P.dve(lambda e: e.tensor_tensor(out=EBbd[:], in0=E[:], in1=mBD4[:], op=ALU.mult), r=["E", "mBD4"],
                  w=["EBbd"])
            P.pool(lambda e: e.tensor_tensor(out=H4(EBoff[:]), in0=H4(E[:]), in1=mOFF, op=ALU.mult), r=["E", "cst"],
                   w=["EBoff"])
            P.pool(lambda e: e.tensor_tensor(out=H4(ETm[:]), in0=H4(ET[:]), in1=mTIN, op=ALU.mult), r=["ET", "cst"],
                   w=["ETm"])
            for h in range(4):
                P.pe(lambda e, h=h: e.matmul(ps[3][:, HS[h]], lhsT=kT[:, h, tk], rhs=kT[:, h, tk], start=True,
                                             stop=True), r=["kT"], w=[PK(3)])
            for h in range(4):
                P.pe(lambda e, h=h: e.matmul(ps[4][:, HS[h]], lhsT=kT[:, h, tk], rhs=qT[:, h, tk], start=True,
                                             stop=True), r=["kT", "qT"], w=[PK(4)])
            P.dve(lambda e: e.tensor_tensor(out=Y[0][:], in0=ps[3][:, :], in1=EBbd[:], op=ALU.mult),
                  r=[PK(3), "EBbd"], w=[("Y", 0)])
            pv = psb(5)
            for h in range(4):
                P.pe(lambda e, h=h: e.transpose(out=pv[:, HS[h]], in_=Y[0][:, HS[h]], identity=idb[:]),
                     r=[("Y", 0), "idb"], w=[PK(5)])
            P.act(lambda e: e.activation(out=YT[0][:], in_=pv[:, 0:512], func=AF.Copy), r=[PK(5)], w=[("YT", 0)])
            P.pool(lambda e: e.tensor_tensor(out=H4(PT[0][:]), in0=H4(YT[0][:]), in1=B4(ident_f), op=ALU.add),
                   r=[("YT", 0), "cst"], w=[("PT", 0)])
            P.dve(lambda e: e.tensor_tensor(out=Loff[:], in0=ps[3][:, :], in1=EBoff[:], op=ALU.mult),
                  r=[PK(3), "EBoff"], w=["Loff"])
            P.dve(lambda e: e.tensor_tensor(out=qkT[:], in0=ps[4][:, :], in1=ETm[:], op=ALU.mult),
                  r=[PK(4), "ETm"], w=[qkk])
            st["cur"] = 0

        def N_(m):
            cur = st["cur"]
            nx = 1 - cur
            for h in range(4):
                P.pe(lambda e, h=h: e.matmul(ps[6][:, HS[h]], lhsT=YT[cur][:, HS[h]], rhs=Y[cur][:, HS[h]],
                                             start=True, stop=True), r=[("Y", cur), ("YT", cur)], w=[PK(6)])
            if m < 5:
                for h in range(4):
                    P.pe(lambda e, h=h: e.matmul(ps[7][:, HS[h]], lhsT=Y[cur][:, HS[h]], rhs=YT[cur][:, HS[h]],
                                                 start=True, stop=True), r=[("Y", cur), ("YT", cur)], w=[PK(7)])
            P.act(lambda e: e.activation(out=Y[nx][:], in_=ps[6][:, :], func=AF.Copy), r=[PK(6)], w=[("Y", nx)])
            if m < 5:
                P.dve(lambda e: e.tensor_copy(out=YT[nx][:], in_=ps[7][:, :]), r=[PK(7)], w=[("YT", nx)])
            for h in range(4):
                P.pe(lambda e, h=h: e.matmul(ps[5][:, HS[h]], lhsT=Y[nx][:, HS[h]], rhs=PT[cur][:, HS[h]],
                                             start=True, stop=True), r=[("Y", nx), ("PT", cur)], w=[PK(5)])
            P.dve(lambda e: e.tensor_tensor(out=PT[nx][:], in0=ps[5][:, :], in1=PT[cur][:], op=ALU.add),
                  r=[PK(5), ("PT", cur)], w=[("PT", nx)])
            st["cur"] = nx

        def M_():
            cur = st["cur"]
            PTf, ptk = PT[cur], ("PT", cur)
            pv = psb(6)
            for h in range(4):
                P.pe(lambda e, h=h: e.transpose(out=pv[:, HS[h]], in_=PTf[:, HS[h]], identity=idb[:]),
                     r=[ptk, "idb"], w=[PK(6)])
            P.act(lambda e: e.activation(out=Tbd[:], in_=pv[:, 0:512], func=AF.Copy), r=[PK(6)], w=["Tbd"])
            for h in range(4):
                P.pe(lambda e, h=h: e.matmul(ps[7][:, HS[h]], lhsT=Loff[:, HS[h]], rhs=PTf[:, HS[h]], start=True,
                                             stop=True), r=["Loff", ptk], w=[PK(7)])
            P.act(lambda e: e.activation(out=Xb[:], in_=ps[7][:, :], func=AF.Copy), r=[PK(7)], w=["Xb"])
            for h in range(4):
                P.pe(lambda e, h=h: e.matmul(ps[5][:, HS[h]], lhsT=Tbd[:, HS[h]], rhs=Xb[:, HS[h]], start=True,
                                             stop=True), r=["Tbd", "Xb"], w=[PK(5)])
            P.dve(lambda e: e.tensor_tensor(out=TTm[:], in0=PTf[:], in1=ps[5][:, :], op=ALU.subtract),
                  r=[PK(5), ptk], w=["TTm"])
            P.dve(lambda e: e.tensor_tensor(out=bg_m[:, 0:4], in0=btok[:, t, :], in1=egc, op=ALU.mult),
                  r=["btok", e3k], w=["bg_m"])
            for h in range(4):
                P.dve(lambda e, h=h: e.tensor_scalar(out=kbg[:, HS[h]], in0=ktok[:, t, HS[h]],
                                                     scalar1=bg_m[:, h:h + 1], scalar2=None, op0=ALU.mult),
                      r=["ktok", "bg_m"], w=["kbg"])
                P.dve(lambda e, h=h: e.tensor_scalar(out=kdec[:, HS[h]], in0=ktok[:, t, HS[h]],
                                                     scalar1=erem[:, h:h + 1], scalar2=None, op0=ALU.mult),
                      r=["ktok", e3k], w=[kdk])
            for h in range(4):
                P.pe(lambda e, h=h: e.matmul(ps[0][:, HS[h]], lhsT=TTm[:, HS[h]], rhs=vb[:, t, HS[h]], start=True,
                                             stop=True), r=["TTm", "vb"], w=[PK(0)])
            for h in range(4):
                P.pe(lambda e, h=h: e.matmul(ps[1][:, HS[h]], lhsT=kbg[:, HS[h]], rhs=TTm[:, HS[h]], start=True,
                                             stop=True), r=["TTm", "kbg"], w=[PK(1)])
            P.act(lambda e: e.activation(out=u_sb[:], in_=ps[0][:, :], func=AF.Copy), r=[PK(0)], w=[uk])
            P.act(lambda e: e.activation(out=wT[:], in_=ps[1][:, :], func=AF.Copy), r=[PK(1)], w=[wTk])

        def C1():
            for h in range(4):
                P.pe(lambda e, h=h: e.matmul(ps[2][:, HS[h]], lhsT=wT[:, HS[h]], rhs=Sb[:, HS[h]], start=True,
                                             stop=True), r=[wTk, "Sb"], w=[PK(2)])
            for h in range(4):
                P.pe(lambda e, h=h: e.matmul(ps[3][:, HS[h]], lhsT=qT[:, h, tk], rhs=Sb[:, HS[h]], start=True,
                                             stop=True), r=["qT", "Sb"], w=[PK(3)])
            P.dve(lambda e: e.tensor_tensor(out=vnew[:], in0=u_sb[:], in1=ps[2][:, :], op=ALU.subtract),
                  r=[uk, PK(2)], w=["vnew"])

        def C2():
            for h in range(4):
                P.pe(lambda e, h=h: e.matmul(ps[4][:, HS[h]], lhsT=qkT[:, HS[h]], rhs=vnew[:, HS[h]], start=True,
                                             stop=True), r=[qkk, "vnew"], w=[PK(4)])
            for h in range(4):
                P.pe(lambda e, h=h: e.matmul(ps[2][:, HS[h]], lhsT=kdec[:, HS[h]], rhs=vnew[:, HS[h]], start=True,
                                             stop=True), r=[kdk, "vnew"], w=[PK(2)])
            for h in range(4):
                P.act(lambda e, h=h: e.activation(out=qS_sb[:, HS[h]], in_=ps[3][:, HS[h]], func=AF.Copy,
                                                  scale=egc[:, h:h + 1]), r=[PK(3), e3k], w=["qS_sb"])

        def C3():
            for h in range(4):
                P.dve(lambda e, h=h: e.scalar_tensor_tensor(out=S[:, HS[h]], in0=S[:, HS[h]],
                                                            scalar=etot[:, h:h + 1], in1=ps[2][:, HS[h]],
                                                            op0=ALU.mult, op1=ALU.add),
                      r=["S", e3k, PK(2)], w=["S"])
            P.act(lambda e: e.activation(out=Sb[:], in_=S[:], func=AF.Copy), r=["S"], w=["Sb"])
            P.dve(lambda e: e.tensor_tensor(out=o_sb[:], in0=qS_sb[:], in1=ps[4][:, :], op=ALU.add),
                  r=["qS_sb", PK(4)], w=["o_sb"])

        def C4():
            for h in range(4):
                P.act(lambda e, h=h: e.activation(out=osq[:, HS[h]], in_=o_sb[:, HS[h]], func=AF.Square,
                                                  accum_out=bg_c[:, 4 + h:5 + h]), r=["o_sb"], w=["njunk", "bg_c"])
            P.dve(lambda e: e.tensor_scalar(out=bg_c[:, 4:8], in0=bg_c[:, 4:8], scalar1=1.0 / 128, scalar2=1e-6,
                                            op0=ALU.mult, op1=ALU.add), r=["bg_c"], w=["bg_c"])
            P.act(lambda e: e.activation(out=bg_c[:, 4:8], in_=bg_c[:, 4:8], func=AF.Ln), r=["bg_c"], w=["bg_c"])
            P.act(lambda e: e.activation(out=bg_c[:, 4:8], in_=bg_c[:, 4:8], func=AF.Exp, scale=-0.5), r=["bg_c"],
                  w=["bg_c"])

        def C5():
            for h in range(4):
                P.dve(lambda e, h=h: e.scalar_tensor_tensor(out=o_sb[:, HS[h]], in0=o_sb[:, HS[h]],
                                                            scalar=bg_c[:, 4 + h:5 + h], in1=dn4[:, HS[h]],
                                                            op0=ALU.mult, op1=ALU.mult),
                      r=["o_sb", "bg_c", "dn4"], w=["o_sb"])
            P.dve(lambda e: e.tensor_tensor(out=dtok[:], in0=o_sb[:], in1=zs[:, t, :], op=ALU.mult),
                  r=["o_sb", "zs"], w=["dtok"])
            pv = psb(3)
            for h in range(4):
                P.pe(lambda e, h=h: e.transpose(out=pv[:, HS[h]], in_=dtok[:, HS[h]], identity=idb[:]),
                     r=["dtok", "idb"], w=[PK(3)])
            P.act(lambda e: e.activation(out=dT[:, 0:4, t * 128:(t + 1) * 128],
                                         in_=pv[:, 0:512].rearrange("p (h n) -> p h n", h=4), func=AF.Copy),
                  r=[PK(3)], w=["dT"])

        return A_, N_, M_, [C1, C2, C3, C4, C5]

    tiles = [make_tile(t) for t in range(NT)]
    A0, N0, M0, _ = tiles[0]
    A0()
    for m in range(1, 6):
        N0(m)
    M0()
    for t in range(NT):
        Cs = tiles[t][3]
        if t + 1 < NT:
            An, Nn, Mn, _ = tiles[t + 1]
            An()
            for m in range(1, 6):
                Nn(m)
                Cs[m - 1]()
            Mn()
        else:
            for c in Cs:
                c()
    P.dma(V.ndel_p.rearrange("h d e -> d h e"), H4(S[:]), r=["S"], tag="out")

    stop_at(5)
    P.barrier()
    A.reset(V.m_w)
    sel4f = A.t([4, 512], F32, "sel4f")
    P.pool(lambda e: e.tensor_copy(out=sel4f[:].rearrange("p (s m) -> p s m", s=4),
                                   in_=cst[0:4, C_OH16:C_OH16 + 4].unsqueeze(2).to_broadcast([4, 4, 128])),
           r=["cst"], w=["sel4f"])
    sel4 = sel4f[0:4, :]
    egS = f32t("egS", (4, NS))
    P.act(lambda e: e.activation(out=egS[:], in_=gbS[:, 1, :], func=AF.Exp), r=["gbS"], w=["egS"])
    Bbc = f32t("Bbc", (128, 4, NS))
    EGbc = f32t("EGbc", (128, 4, NS))
    for h in range(4):
        P.pe(lambda e, h=h: e.matmul(ps[0][:, h * NS:(h + 1) * NS], lhsT=sel4[:, h * 128:(h + 1) * 128],
                                     rhs=gbS[:, 0, :], start=True, stop=True), r=["gbS", "sel4f"], w=[PK(0)])
        P.pe(lambda e, h=h: e.matmul(ps[1][:, h * NS:(h + 1) * NS], lhsT=sel4[:, h * 128:(h + 1) * 128],
                                     rhs=egS[:, :], start=True, stop=True), r=["egS", "sel4f"], w=[PK(1)])
    P.act(lambda e: e.activation(out=Bbc[:].rearrange("p h s -> p (h s)"), in_=ps[0][:, 0:64], func=AF.Copy),
          r=[PK(0)], w=["Bbc"])
    P.act(lambda e: e.activation(out=EGbc[:].rearrange("p h s -> p (h s)"), in_=ps[1][:, 0:64], func=AF.Copy),
          r=[PK(1)], w=["EGbc"])
    qkf = f32t("qkf", (128, 4, 2, NS))
    P.dve(lambda e: e.tensor_copy(out=qkf[:, :, 0, :], in_=kT[:, :, T:TT]), r=["kT"], w=["qkf"])
    P.dve(lambda e: e.tensor_copy(out=qkf[:, :, 1, :], in_=qT[:, :, T:TT]), r=["qT"], w=["qkf"])
    S0all = f32t("S0all", (128, NS, 4, 128))
    for s in range(NS):
        P.dma(S0all[:, s, :, :], V.sdel[s * 4:(s + 1) * 4, :, :].rearrange("h d e -> d h e"), w=[("S0", s)],
              tag="S0g%d" % (s // 4), group=True)
    for s in range(NS):
        for h in range(4):
            P.pe(lambda e, s=s, h=h: e.matmul(ps[2][:, (s * 4 + h) * 2:(s * 4 + h) * 2 + 2],
                                              lhsT=S0all[:, s, h, :], rhs=qkf[:, h, :, s], start=True,
                                              stop=True), r=[("S0", s), "qkf"], w=[PK(2)])
    kqS = f32t("kqS", (128, NS, 4, 2))
    P.act(lambda e: e.activation(out=kqS[:].rearrange("p s h t -> p (s h t)"), in_=ps[2][:, 0:128], func=AF.Copy),
          r=[PK(2)], w=["kqS"])
    kSv = kqS[:, :, :, 0].rearrange("p s h -> p h s")
    qSv = kqS[:, :, :, 1].rearrange("p s h -> p h s")
    vn = f32t("vn", (128, 4, NS))
    tmpS = f32t("tmpS", (128, 4, NS))
    P.dve(lambda e: e.tensor_tensor(out=tmpS[:], in0=EGbc[:], in1=kSv, op=ALU.mult), r=["EGbc", "kqS"], w=["tmpS"])
    P.dve(lambda e: e.tensor_tensor(out=vn[:], in0=vTs[:], in1=tmpS[:], op=ALU.subtract), r=["vTs", "tmpS"],
          w=["vn"])
    P.dve(lambda e: e.tensor_tensor(out=vn[:], in0=vn[:], in1=Bbc[:], op=ALU.mult), r=["vn", "Bbc"], w=["vn"])
    prodS = f32t("prodS", (128, 4, NS))
    P.dve(lambda e: e.tensor_tensor(out=prodS[:], in0=qkf[:, :, 0, :], in1=qkf[:, :, 1, :], op=ALU.mult),
          r=["qkf"], w=["prodS"])
    P.pe(lambda e: e.matmul(ps[3][:, 0:64], lhsT=onef, rhs=prodS[:].rearrange("p h s -> p (h s)"), start=True,
                            stop=True), r=["prodS", "cst"], w=[PK(3)])
    oS = f32t("oS", (128, 4, NS))
    P.dve(lambda e: e.tensor_tensor(out=oS[:].rearrange("p h s -> p (h s)"), in0=ps[3][:, 0:64],
                                    in1=vn[:].rearrange("p h s -> p (h s)"), op=ALU.mult), r=[PK(3), "vn"],
          w=["oS"])
    P.dve(lambda e: e.tensor_tensor(out=tmpS[:], in0=EGbc[:], in1=qSv, op=ALU.mult), r=["EGbc", "kqS"], w=["tmpS"])
    P.dve(lambda e: e.tensor_tensor(out=oS[:], in0=oS[:], in1=tmpS[:], op=ALU.add), r=["oS", "tmpS"], w=["oS"])
    P.dve(lambda e: e.tensor_tensor(out=tmpS[:], in0=oS[:], in1=oS[:], op=ALU.mult), r=["oS"], w=["tmpS"])
    P.pe(lambda e: e.matmul(ps[4][:, 0:64], lhsT=onef, rhs=tmpS[:].rearrange("p h s -> p (h s)"), start=True,
                            stop=True), r=["tmpS", "cst"], w=[PK(4)])
    rS = f32t("rS", (128, 64))
    P.dve(lambda e: e.tensor_scalar(out=rS[:], in0=ps[4][:, 0:64], scalar1=1.0 / 128, scalar2=1e-6, op0=ALU.mult,
                                    op1=ALU.add), r=[PK(4)], w=["rS"])
    P.act(lambda e: e.activation(out=rS[:], in_=rS[:], func=AF.Sqrt), r=["rS"], w=["rS"])
    P.dve(lambda e: e.reciprocal(out=rS[:], in_=rS[:]), r=["rS"], w=["rS"])
    P.dve(lambda e: e.tensor_tensor(out=oS[:].rearrange("p h s -> p (h s)"), in0=oS[:].rearrange("p h s -> p (h s)"),
                                    in1=rS[:], op=ALU.mult), r=["oS", "rS"], w=["oS"])
    P.dve(lambda e: e.scalar_tensor_tensor(out=dT[:, 0:4, T:TT], in0=oS[:], scalar=pc[:, PC_DN:PC_DN + 1],
                                           in1=zsT[:], op0=ALU.mult, op1=ALU.mult), r=["oS", "pc", "zsT"],
          w=["dT"])
    ktS = f32t("ktS", (16, 512))
    vtS = f32t("vtS", (16, 512))
    for h in range(4):
        P.pe(lambda e, h=h: e.transpose(out=ps[5][0:NS, h * 128:(h + 1) * 128], in_=qkf[:, h, 0, :],
                                        identity=ident_f), r=["qkf", "cst"], w=[PK(5)])
        P.pe(lambda e, h=h: e.transpose(out=ps[6][0:NS, h * 128:(h + 1) * 128], in_=vn[:, h, :], identity=ident_f),
             r=["vn", "cst"], w=[PK(6)])
    P.act(lambda e: e.activation(out=ktS[:], in_=ps[5][0:NS, :], func=AF.Copy), r=[PK(5)], w=["ktS"])
    P.act(lambda e: e.activation(out=vtS[:], in_=ps[6][0:NS, :], func=AF.Copy), r=[PK(6)], w=["vtS"])
    vmask = [f32t("vmask%d" % i, (16, 512)) for i in range(2)]
    oh16 = cst[0:16, C_OH16:C_OH16 + 16]
    for s in range(NS):
        sl = s % 2
        P.dve(lambda e, s=s, sl=sl: e.tensor_scalar(out=vmask[sl][:], in0=vtS[:], scalar1=oh16[:, s:s + 1],
                                                    scalar2=None, op0=ALU.mult), r=["vtS", "cst"],
              w=[("vmask", sl)])
        b = 7 if sl else 0
        for h in range(4):
            hs = slice(h * 128, (h + 1) * 128)
            P.pe(lambda e, hs=hs, sl=sl, b=b: e.matmul(ps[b][:, hs], lhsT=ktS[:, hs], rhs=vmask[sl][:, hs],
                                                       start=True, stop=True), r=["ktS", ("vmask", sl)], w=[PK(b)])
        for h in range(4):
            hs = slice(h * 128, (h + 1) * 128)
            P.dve(lambda e, hs=hs, h=h, s=s, b=b: e.scalar_tensor_tensor(
                out=S0all[:, s, h, :], in0=S0all[:, s, h, :], scalar=EGbc[:, h, s:s + 1], in1=ps[b][:, hs],
                op0=ALU.mult, op1=ALU.add), r=["EGbc", PK(b)], w=[("S0", s)])
        P.dma(V.ndel_s[s * 4:(s + 1) * 4, :, :].rearrange("h d e -> d h e"), S0all[:, s, :, :], r=[("S0", s)],
              tag="out")

    stop_at(6)
    P.barrier()
    mX, mXe = V.mX, V.mXe
    x1 = A.child(mX, mXe).t([128, NT + 1, 1024], F32, "x1")
    A.reset(mXe)
    wo = A.t([128, 8, 512], BF16, "wo_a")
    wo2 = A.t([128, 8, 512], BF16, "wo_b")
    xin = [A.t([128, 1024], F32, "xin2_%d" % i) for i in range(2)]
    load_w(wo, "wo", V.w_out, 0, 8, 0, 512)
    load_w(wo2, "wo2", V.w_out, 0, 8, 512, 512)
    for t in range(NT + 1):
        n = 128 if t < NT else NS
        c0 = t * 128
        s = t % 2
        src_d = V.x_p[t * 128:(t + 1) * 128, :] if t < NT else V.x_s[:, :]
        P.dma(xin[s][:n, :], src_d, w=[("xin", s)], tag="xin%d" % s)
        for half, wv in ((0, wo), (1, wo2)):
            b = (t % 2) * 2 + half
            for kc in range(8):
                src = cT[:, kc, c0:c0 + n] if kc < 4 else dT[:, kc - 4, c0:c0 + n]
                P.pe(lambda e, b=b, kc=kc, src=src, wv=wv, n=n: e.matmul(ps[b][:n, :], lhsT=src, rhs=wv[:, kc, :],
                                                                        start=(kc == 0), stop=(kc == 7)),
                     r=["cT", "dT", "wo", "wo2"], w=[PK(b)])
            P.dve(lambda e, b=b, t=t, n=n, half=half, s=s: e.tensor_tensor(
                out=x1[:n, t, half * 512:(half + 1) * 512], in0=ps[b][:n, :],
                in1=xin[s][:n, half * 512:(half + 1) * 512], op=ALU.add), r=[PK(b), ("xin", s)], w=[("x1", t)])

    stop_at(7)
    P.barrier()
    A.reset(mXe)
    AH = A.child(V.off_cT, mX)
    kmT = AH.t([128, 8, 256], BF16, "kmT")
    vmem = AH.t([128, 2, 1024], BF16, "vmem")
    mT = AH.t([128, 8, 256], BF16, "mT")
    hT3 = hT
    wk2 = [A.t([128, 8, 512], BF16, "wk%d" % i) for i in range(2)]
    mo_f = A.t([128, 1024], F32, "mo_f")
    min_ = [A.t([128, 1024], F32, "min%d" % i) for i in range(2)]
    load_gam(PR_GKV)
    items = []
    for mt in range(2):
        pre = (lambda mt=mt: P.dma(min_[mt][:], V.mem[mt * 128:(mt + 1) * 128, :], w=[("min", mt)],
                                   tag="min%d" % mt))
        items.append((min_[mt][:, :], [("min", mt)], 128, mt * 128, pre))
    V.norm_pass(items, mT, "mT")
    kvl = [(0, V.w_mk, V.mk_p, 0), (0, V.w_mk, V.mk_p, 1), (1, V.w_mv, V.mv_p, 0), (1, V.w_mv, V.mv_p, 1)]
    load_w(wk2[0], ("wk", 0), V.w_mk, 0, 8, 0, 512)
    for li, (which, wd, od, half) in enumerate(kvl):
        if True:
            wk = wk2[li % 2]
            wkk = ("wk", li % 2)
            if li + 1 < 4:
                load_w(wk2[(li + 1) % 2], ("wk", (li + 1) % 2), kvl[li + 1][1], 0, 8, kvl[li + 1][3] * 512, 512)
            for mt in range(2):
                b = 2 + mt
                for kc in range(8):
                    P.pe(lambda e, b=b, kc=kc, mt=mt: e.matmul(ps[b][:, :], lhsT=mT[:, kc, mt * 128:(mt + 1) * 128],
                                                               rhs=wk[:, kc, :], start=(kc == 0), stop=(kc == 7)),
                         r=[("mT", 0), wkk], w=[PK(b)])
                P.act(lambda e, b=b, half=half: e.activation(out=mo_f[:, half * 512:(half + 1) * 512],
                                                             in_=ps[b][:, :], func=AF.Copy), r=[PK(b)], w=["mo_f"])
                P.dma(od[mt * 128:(mt + 1) * 128, half * 512:(half + 1) * 512],
                      mo_f[:, half * 512:(half + 1) * 512], r=["mo_f"], tag="omo")
                if which == 1:
                    P.dve(lambda e, b=b, half=half, mt=mt: e.tensor_copy(
                        out=vmem[:, mt, half * 512:(half + 1) * 512], in_=ps[b][:, :]), r=[PK(b)], w=["vmem"])
            if which == 0:
                for c4 in range(4):
                    b = 4 + c4 % 2
                    for kc in range(8):
                        P.pe(lambda e, b=b, kc=kc, c4=c4: e.matmul(ps[b][:, 0:256],
                                                                   lhsT=wk[:, kc, c4 * 128:(c4 + 1) * 128],
                                                                   rhs=mT[:, kc, :], start=(kc == 0), stop=(kc == 7)),
                             r=[("mT", 0), wkk], w=[PK(b)])
                    P.act(lambda e, b=b, c4=c4, half=half: e.activation(out=kmT[:, half * 4 + c4, :],
                                                                        in_=ps[b][:, 0:256], func=AF.Copy),
                          r=[PK(b)], w=["kmT"])
    stop_at(8)
    P.barrier()
    A.reset(mXe)
    qa = A.t([128, 8, TT], BF16, "qa")
    m_qa = A.mark()
    wq = A.t([128, 8, 512], BF16, "wq")
    wq2 = A.t([128, 8, 512], BF16, "wq2")
    load_gam(PR_GMQ)
    V.norm_pass([(x1[:(128 if t < NT else NS), t, :], [("x1", t)], (128 if t < NT else NS), t * 128, None)
                 for t in range(NT + 1)], hT3, "hT")
    load_w(wq, "wq", V.w_mq, 0, 8, 0, 512)
    load_w(wq2, "wq2", V.w_mq, 0, 8, 512, 512)
    BLK = [(i * 512, 512) for i in range(4)] + [(T, NS)]
    qi = 0
    for c in range(8):
        wv = wq if c < 4 else wq2
        cc = c % 4
        for (t0, n) in BLK:
            b = qi % 2
            qi += 1
            mm8(b, lambda kc: wv[:, kc, cc * 128:(cc + 1) * 128], lambda kc: hT3[:, kc, t0:t0 + n], n,
                [("hT", t0 // 512), "wq", "wq2"])
            P.act(lambda e, b=b, c=c, n=n, t0=t0: e.activation(out=qa[:, c, t0:t0 + n], in_=ps[b][:, 0:n],
                                                               func=AF.Copy, scale=1.0 / 16), r=[PK(b)], w=["qa"])
    stop_at(9)
    P.barrier()
    A.reset(m_qa)
    aoT = hT
    qs_s = AH.t([128, 8, NS], BF16, "qs_s")
    P.dve(lambda e: e.tensor_copy(out=qs_s[:], in_=qa[:, :, T:TT]), r=["qa"], w=["qs_s"])
    m_3c = A.mark()
    msm = [A.t([128, 16], F32, "msm%d" % i) for i in range(2)]
    ex = [A.t([128, 1024], F32, "ex%d" % i) for i in range(2)]
    pbf = [A.t([128, 1024], BF16, "pbf%d" % i) for i in range(2)]
    ptb = [A.t([128, 8, 128], BF16, "ptb%d" % i) for i in range(2)]
    iters = []
    for tg in range(NT):
        def S0(tg=tg):
            p = tg % 2
            lt = slice(tg * 128, (tg + 1) * 128)
            sb = (2 * p, 2 * p + 1)
            m_, mk_ = msm[p], ("msm", p)
            for h in range(4):
                b = sb[h // 2]
                cs = slice((h % 2) * 256, (h % 2) * 256 + 256)
                for hf in range(2):
                    P.pe(lambda e, b=b, cs=cs, h=h, hf=hf: e.matmul(ps[b][:, cs], lhsT=qa[:, h * 2 + hf, lt],
                                                                    rhs=kmT[:, h * 2 + hf, :], start=(hf == 0),
                                                                    stop=(hf == 1)), r=["qa", "kmT"], w=[PK(b)])
            for j in range(2):
                P.dve(lambda e, j=j: e.tensor_reduce(out=m_[:, 2 * j:2 * j + 2],
                                                     in_=ps[sb[j]][:, :].rearrange("p (h m) -> p h m", h=2),
                                                     axis=AX.X, op=ALU.max), r=[PK(sb[j])], w=[mk_])
            P.dve(lambda e: e.tensor_scalar(out=m_[:, 12:16], in0=m_[:, 0:4], scalar1=-1.0, scalar2=None,
                                            op0=ALU.mult), r=[mk_], w=[mk_])

        def S1(tg=tg):
            p = tg % 2
            sb = (2 * p, 2 * p + 1)
            m_, mk_ = msm[p], ("msm", p)
            ex_, exk = ex[p], ("ex", p)
            pb_, pbk = pbf[p], ("pbf", p)
            for h in range(4):
                P.act(lambda e, h=h: e.activation(out=ex_[:, h * 256:(h + 1) * 256],
                                                  in_=ps[sb[h // 2]][:, (h % 2) * 256:(h % 2) * 256 + 256],
                                                  func=AF.Exp, bias=m_[:, 12 + h:13 + h],
                                                  accum_out=m_[:, 4 + h:5 + h]),
                      r=[PK(sb[h // 2]), mk_], w=[exk, mk_])
            P.dve(lambda e: e.reciprocal(out=m_[:, 8:12], in_=m_[:, 4:8]), r=[mk_], w=[mk_])
            for h in range(4):
                P.dve(lambda e, h=h: e.tensor_scalar(out=pb_[:, h * 256:(h + 1) * 256],
                                                     in0=ex_[:, h * 256:(h + 1) * 256], scalar1=m_[:, 8 + h:9 + h],
                                                     scalar2=None, op0=ALU.mult), r=[exk, mk_], w=[pbk])

        def S2(tg=tg):
            p = tg % 2
            pb_, pbk = pbf[p], ("pbf", p)
            pt_, ptk = ptb[p], ("ptb", p)
            pv = psb(4)
            for c8 in range(8):
                P.pe(lambda e, c8=c8: e.transpose(out=pv[:, c8 * 128:(c8 + 1) * 128],
                                                  in_=pb_[:, c8 * 128:(c8 + 1) * 128], identity=idb[:]),
                     r=[pbk, "idb"], w=[PK(4)])
            P.act(lambda e: e.activation(out=pt_[:].rearrange("p a b -> p (a b)"), in_=pv[:, 0:1024], func=AF.Copy),
                  r=[PK(4)], w=[ptk])

        def S3(tg=tg):
            p = tg % 2
            lt = slice(tg * 128, (tg + 1) * 128)
            pt_, ptk = ptb[p], ("ptb", p)
            for h in range(4):
                ob = 5 + h // 2
                for hf in range(2):
                    cs = slice(((h % 2) * 2 + hf) * 128, ((h % 2) * 2 + hf + 1) * 128)
                    for mt in range(2):
                        P.pe(lambda e, ob=ob, cs=cs, h=h, hf=hf, mt=mt: e.matmul(
                            ps[ob][:, cs], lhsT=vmem[:, mt, h * 256 + hf * 128:h * 256 + (hf + 1) * 128],
                            rhs=pt_[:, h * 2 + mt, :], start=(mt == 0), stop=(mt == 1)), r=["vmem", ptk],
                            w=[PK(ob)])
            P.act(lambda e: e.activation(out=aoT[:, 0:4, lt], in_=ps[5][:, :].rearrange("p (a n) -> p a n", a=4),
                                         func=AF.Copy), r=[PK(5)], w=["aoT"])
            P.dve(lambda e: e.tensor_copy(out=aoT[:, 4:8, lt], in_=ps[6][:, :].rearrange("p (a n) -> p a n", a=4)),
                  r=[PK(6)], w=["aoT"])

        iters.append([S0, S1, S2, S3])
    V.run_skewed(iters)
    P.barrier()
    A.reset(mXe)
    prod = A.t([128, 1024], F32, "prod")
    qtok = A.t([16, 1024], BF16, "qtok")
    sel16b = A.t([16, 2048], BF16, "sel16b")
    P.pool(lambda e: e.tensor_copy(out=sel16b[:].rearrange("p (s m) -> p s m", s=16),
                                   in_=cst[0:16, C_OH16:C_OH16 + 16].unsqueeze(2).to_broadcast([16, 16, 128])),
           r=["cst"], w=["sel16b"])
    NKS, NVS = 8, 6
    Kt2 = [A.t([128, 1024], BF16, "Kt%d" % i) for i in range(NKS)]
    Vt2 = [A.t([128, 2, 1024], BF16, "Vt%d" % i) for i in range(NVS)]
    scS = A.t([128, 2, 4], F32, "scS")
    sm4 = A.t([4, 256], F32, "sm4")
    sm4s = A.t([4, 8], F32, "sm4s")
    pS = A.t([128, 2, 4], BF16, "pS")
    aoS = A.t([128, 8, NS], F32, "aoS")
    pv = psb(2)
    for c in range(8):
        P.pe(lambda e, c=c: e.transpose(out=pv[0:NS, c * 128:(c + 1) * 128], in_=qs_s[:, c, :],
                                        identity=idb[:]), r=["qs_s", "idb"], w=[PK(2)])
    P.act(lambda e: e.activation(out=qtok[:], in_=pv[0:NS, 0:1024], func=AF.Copy), r=[PK(2)], w=["qtok"])
    Vt3 = Vt2
    scS2 = [scS, A.t([128, 2, 4], F32, "scSb")]
    sm42 = [sm4, A.t([4, 256], F32, "sm4b")]
    sm4s2 = [sm4s, A.t([4, 8], F32, "sm4sb")]
    iters = []
    for s in range(NS):
        def S0(s=s):
            Vt = Vt3[s % NVS]
            vk = ("Vt", s % NVS)
            sc_, sck = scS2[s % 2], ("scS", s % 2)
            for mt in range(2):
                P.dma(Vt[:, mt, :], V.cmv[s, mt * 128:(mt + 1) * 128, :], w=[vk], tag="Vt%d" % (s % NVS), q="pool")
            qb = (3, 4) if s % 2 == 0 else (0, 1)
            for half in range(2):
                P.pe(lambda e, half=half: e.matmul(ps[qb[half]][:, :], lhsT=sel16b[:, s * 128:(s + 1) * 128],
                                                   rhs=qtok[:, half * 512:(half + 1) * 512], start=True, stop=True),
                     r=["qtok", "sel16b"], w=[PK(qb[half])])
            for mt in range(2):
                kti = s * 2 + mt
                Kt = Kt2[kti % NKS]
                kk = ("Kt", kti % NKS)
                P.dma(Kt[:, :], V.cmk[s, mt * 128:(mt + 1) * 128, :], w=[kk], tag="Kt%d" % (kti % NKS), q="pool")
                for half in range(2):
                    P.dve(lambda e, half=half, Kt=Kt: e.tensor_tensor(out=prod[:, half * 512:(half + 1) * 512],
                                                                      in0=ps[qb[half]][:, :],
                                                                      in1=Kt[:, half * 512:(half + 1) * 512],
                                                                      op=ALU.mult),
                          r=[PK(qb[half]), kk], w=["prod"])
                P.dve(lambda e, mt=mt: e.tensor_reduce(out=sc_[:, mt, :],
                                                       in_=prod[:].rearrange("p (h d) -> p h d", h=4),
                                                       axis=AX.X, op=ALU.add), r=["prod"], w=[sck])

        def S1(s=s):
            sc_, sck = scS2[s % 2], ("scS", s % 2)
            m4, m4k = sm42[s % 2], ("sm4", s % 2)
            m4s, m4sk = sm4s2[s % 2], ("sm4s", s % 2)
            for mt in range(2):
                P.pe(lambda e, mt=mt: e.transpose(out=ps[5][0:4, mt * 128:(mt + 1) * 128], in_=sc_[:, mt, :],
                                                  identity=ident_f), r=[sck, "cst"], w=[PK(5)])
            P.dve(lambda e: e.tensor_reduce(out=m4s[:, 0:1], in_=ps[5][0:4, 0:256], axis=AX.X, op=ALU.max),
                  r=[PK(5)], w=[m4sk])
            P.dve(lambda e: e.tensor_scalar(out=m4s[:, 1:2], in0=m4s[:, 0:1], scalar1=-1.0, scalar2=None,
                                            op0=ALU.mult), r=[m4sk], w=[m4sk])
            P.act(lambda e: e.activation(out=m4[:], in_=ps[5][0:4, 0:256], func=AF.Exp, bias=m4s[:, 1:2],
                                         accum_out=m4s[:, 2:3]), r=[PK(5), m4sk], w=[m4k, m4sk])
            P.dve(lambda e: e.reciprocal(out=m4s[:, 3:4], in_=m4s[:, 2:3]), r=[m4sk], w=[m4sk])
            P.dve(lambda e: e.tensor_scalar(out=m4[:], in0=m4[:], scalar1=m4s[:, 3:4], scalar2=None, op0=ALU.mult),
                  r=[m4k, m4sk], w=[m4k])

        def S2(s=s):
            Vt = Vt3[s % NVS]
            vk = ("Vt", s % NVS)
            m4, m4k = sm42[s % 2], ("sm4", s % 2)
            for mt in range(2):
                P.pe(lambda e, mt=mt: e.transpose(out=ps[6][:, mt * 4:(mt + 1) * 4],
                                                  in_=m4[:, mt * 128:(mt + 1) * 128], identity=ident_f[0:4, 0:4]),
                     r=[m4k, "cst"], w=[PK(6)])
            P.act(lambda e: e.activation(out=pS[:].rearrange("p a h -> p (a h)"), in_=ps[6][:, 0:8], func=AF.Copy),
                  r=[PK(6)], w=["pS"])
            for h in range(4):
                for hf in range(2):
                    c = h * 2 + hf
                    for mt in range(2):
                        P.pe(lambda e, c=c, mt=mt, h=h: e.matmul(ps[7][:, c:c + 1],
                                                                 lhsT=Vt[:, mt, c * 128:(c + 1) * 128],
                                                                 rhs=pS[:, mt, h:h + 1], start=(mt == 0),
                                                                 stop=(mt == 1)),
                             r=[vk, "pS"], w=[PK(7)])
            P.act(lambda e: e.activation(out=aoS[:, :, s], in_=ps[7][:, 0:8], func=AF.Copy), r=[PK(7)], w=["aoS"])

        iters.append([S0, S1, S2])
    V.run_skewed(iters)
    P.dve(lambda e: e.tensor_copy(out=aoT[:, :, T:TT], in_=aoS[:]), r=["aoS"], w=["aoT"])
    P.barrier()
    A.reset(mXe)
    wmo = A.t([128, 8, 512], BF16, "wmo")
    wmo2 = A.t([128, 8, 512], BF16, "wmo2")
    load_w(wmo, "wmo", V.w_mo, 0, 8, 0, 512)
    load_w(wmo2, "wmo2", V.w_mo, 0, 8, 512, 512)
    for t in range(NT + 1):
        n = 128 if t < NT else NS
        c0 = t * 128
        for half, wv in ((0, wmo), (1, wmo2)):
            b = (t % 2) * 2 + half
            for kc in range(8):
                P.pe(lambda e, b=b, kc=kc, wv=wv, n=n, c0=c0: e.matmul(ps[b][:n, :], lhsT=aoT[:, kc, c0:c0 + n],
                                                                      rhs=wv[:, kc, :], start=(kc == 0),
                                                                      stop=(kc == 7)), r=["aoT", "wmo", "wmo2"],
                     w=[PK(b)])
            P.dve(lambda e, b=b, t=t, n=n, half=half: e.tensor_tensor(
                out=x1[:n, t, half * 512:(half + 1) * 512], in0=ps[b][:n, :],
                in1=x1[:n, t, half * 512:(half + 1) * 512], op=ALU.add), r=[PK(b), ("x1", t)], w=[("x1", t)])

    stop_at(11)
    P.barrier()
    A.reset(mXe)
    TH = 1024
    AHh = A.child(V.off_hT, V.off_cT)
    hT4 = AHh.t([128, 8, TH + NS], BF16, "hT4")
    wg = [AHh.t([128, 8, 256], BF16, "wg%d" % i) for i in range(2)]
    wu = [AHh.t([128, 8, 256], BF16, "wu%d" % i) for i in range(2)]
    AHc = A.child(V.off_cT, mX)
    gF = AHc.t([128, 1024], F32, "gF")
    sgf = [AHc.t([128, 512], F32, "sgf%d" % i) for i in range(2)]
    fs = AHc.t([128, 8], F32, "fs")
    aT = A.t([128, 11, TH + NS], BF16, "aT")
    wdn = [A.t([128, 11, 512], BF16, "wdn%d" % i) for i in range(2)]
    fj = V.nrm_junk
    P.dma(gF[:], pr_d[:, PR_GF:PR_GF + 1024], w=["gF"], tag="c4")
    load_gam(PR_GFFN)
    gu_list = [(fh, gi) for _th in range(2) for fh in range(2) for gi in range(6)]

    def gu_load(idx):
        fh, gi = gu_list[idx]
        ncg = 256 if gi < 5 else 128
        c0 = fh * 1408 + gi * 256
        sl = idx % 2
        load_w(wg[sl], ("wg", sl), V.w_gate, 0, 8, c0, ncg)
        load_w(wu[sl], ("wu", sl), V.w_up, 0, 8, c0, ncg)

    fsl = [AHc.t([128, 8], F32, "fs%d" % i) for i in range(2)]
    fcnt = [0]

    def final_norm(t):
        n = 128 if t < NT else NS
        i_ = fcnt[0] % 2
        fcnt[0] += 1
        fs_ = fsl[i_]
        fk = ("fs", i_)
        P.act(lambda e: e.activation(out=fj[:n, :], in_=x1[:n, t, :], func=AF.Square, accum_out=fs_[:n, 0:1]),
              r=[("x1", t)], w=["njunk", fk])
        P.dve(lambda e: e.tensor_scalar(out=fs_[:n, 1:2], in0=fs_[:n, 0:1], scalar1=1.0 / 1024, scalar2=1e-6,
                                        op0=ALU.mult, op1=ALU.add), r=[fk], w=[fk])
        P.act(lambda e: e.activation(out=fs_[:n, 2:3], in_=fs_[:n, 1:2], func=AF.Sqrt), r=[fk], w=[fk])
        P.dve(lambda e: e.reciprocal(out=fs_[:n, 3:4], in_=fs_[:n, 2:3]), r=[fk], w=[fk])
        P.dve(lambda e: e.scalar_tensor_tensor(out=x1[:n, t, :], in0=x1[:n, t, :], scalar=fs_[:n, 3:4],
                                               in1=gF[:n, :], op0=ALU.mult, op1=ALU.mult),
              r=[("x1", t), fk, "gF"], w=[("x1", t)])
        dst_ = V.y_p[t * 128:(t + 1) * 128, :] if t < NT else V.y_s[:, :]
        P.dma(dst_, x1[:n, t, :], r=[("x1", t)], tag="out")

    gu_load(0)
    gidx = 0
    for th in range(2):
        tiles = list(range(th * 8, th * 8 + 8)) + ([NT] if th == 1 else [])
        V.norm_pass([(x1[:(128 if t < NT else NS), t, :], [("x1", t)], (128 if t < NT else NS), ti * 128, None)
                     for ti, t in enumerate(tiles)], hT4, "hT4")
        blks = [(0, 512), (512, 512)] + ([(1024, NS)] if th == 1 else [])
        for fh in range(2):
            for gi in range(6):
                cur = gidx
                gidx += 1
                if cur + 1 < len(gu_list):
                    gu_load(cur + 1)
                if gi == 2 or gi == 4:
                    hf_ = 0 if gi == 2 else 1
                    load_w(wdn[hf_], ("wdn", hf_), V.w_down, fh * 1408, 11, hf_ * 512, 512)
                ncg = 256 if gi < 5 else 128
                sl = cur % 2
                for cc in range(ncg // 128):
                    fc = gi * 2 + cc
                    for bi, (t0, n) in enumerate(blks):
                        bg, bu = (bi % 2) * 2, (bi % 2) * 2 + 1
                        mm8(bg, lambda kc: wg[sl][:, kc, cc * 128:(cc + 1) * 128],
                            lambda kc: hT4[:, kc, t0:t0 + n], n, [("hT4", t0 // 512), ("wg", sl)])
                        mm8(bu, lambda kc: wu[sl][:, kc, cc * 128:(cc + 1) * 128],
                            lambda kc: hT4[:, kc, t0:t0 + n], n, [("hT4", t0 // 512), ("wu", sl)])
                        sg = sgf[bi % 2]
                        P.act(lambda e, sg=sg, bg=bg, n=n: e.activation(out=sg[:, 0:n], in_=ps[bg][:, 0:n],
                                                                        func=AF.Silu), r=[PK(bg)],
                              w=[("sgf", bi % 2)])
                        P.dve(lambda e, sg=sg, bu=bu, n=n, fc=fc, t0=t0: e.tensor_tensor(
                            out=aT[:, fc, t0:t0 + n], in0=ps[bu][:, 0:n], in1=sg[:, 0:n], op=ALU.mult),
                            r=[PK(bu), ("sgf", bi % 2)], w=["aT"])
            for half in range(2):
                wslot = half
                wdk = ("wdn", wslot)
                for ti, t in enumerate(tiles):
                    n = 128 if t < NT else NS
                    b = 4 + (ti % 4)
                    for k in range(11):
                        P.pe(lambda e, b=b, k=k, ti=ti, n=n, wslot=wslot: e.matmul(
                            ps[b][:n, :], lhsT=aT[:, k, ti * 128:ti * 128 + n], rhs=wdn[wslot][:, k, :],
                            start=(k == 0), stop=(k == 10)), r=["aT", wdk], w=[PK(b)])
                    P.dve(lambda e, b=b, t=t, n=n, half=half: e.tensor_tensor(
                        out=x1[:n, t, half * 512:(half + 1) * 512], in0=ps[b][:n, :],
                        in1=x1[:n, t, half * 512:(half + 1) * 512], op=ALU.add), r=[PK(b), ("x1", t)],
                        w=[("x1", t)])
                    if fh == 1 and half == 1:
                        final_norm(t)
        continue
        for t in tiles:
            n = 128 if t < NT else NS
            P.act(lambda e, t=t, n=n: e.activation(out=fj[:n, :], in_=x1[:n, t, :], func=AF.Square,
                                                   accum_out=fs[:n, 0:1]), r=[("x1", t)], w=["njunk", "fs"])
            P.dve(lambda e, n=n: e.tensor_scalar(out=fs[:n, 1:2], in0=fs[:n, 0:1], scalar1=1.0 / 1024, scalar2=1e-6,
                                                 op0=ALU.mult, op1=ALU.add), r=["fs"], w=["fs"])
            P.act(lambda e, n=n: e.activation(out=fs[:n, 2:3], in_=fs[:n, 1:2], func=AF.Sqrt), r=["fs"], w=["fs"])
            P.dve(lambda e, n=n: e.reciprocal(out=fs[:n, 3:4], in_=fs[:n, 2:3]), r=["fs"], w=["fs"])
            P.dve(lambda e, t=t, n=n: e.scalar_tensor_tensor(out=x1[:n, t, :], in0=x1[:n, t, :], scalar=fs[:n, 3:4],
                                                             in1=gF[:n, :], op0=ALU.mult, op1=ALU.add if False else ALU.mult),
                  r=[("x1", t), "fs", "gF"], w=[("x1", t)])
            dst = V.y_p[t * 128:(t + 1) * 128, :] if t < NT else V.y_s[:, :]
            P.dma(dst, x1[:n, t, :], r=[("x1", t)], tag="out")

    P.emit(["out"])


_CACHE = {}


def kernel(**inputs):
    inp = {k: np.asarray(v) for k, v in inputs.items()}
    if "nc" not in _CACHE:
        _CACHE["nc"] = build_nc()[0]
    nc = _CACHE["nc"]
    cst = make_consts()
    pc, pr = make_params(inp)
    f = lambda a: np.ascontiguousarray(a, dtype=np.float32)
    shared = {
        "w_in": f(inp["w_in"][0]), "w_out": f(inp["w_out"][0]), "w_mq": f(inp["w_mq"][0]), "w_mk": f(inp["w_mk"][0]),
        "w_mv": f(inp["w_mv"][0]), "w_mo": f(inp["w_mo"][0]), "w_gate": f(inp["w_gate"][0]),
        "w_up": f(inp["w_up"][0]), "w_down": f(inp["w_down"][0]), "cst": cst, "pc": pc, "pr": pr,
    }
    in_maps = []
    for c in range(8):
        sl = slice(c * NS, (c + 1) * NS)
        m = dict(shared)
        m["x_p"] = f(inp["x_prompt"][c])
        m["x_s"] = f(inp["x_sample"][sl, 0, :])
        m["mem"] = f(inp["mem_prompt"][c])
        m["cconv"] = f(inp["cache_conv"][0, sl].reshape(NS * 30, 512))
        m["ssc"] = f(inp["state_short_conv"][0, sl].reshape(NS * 3, 1536))
        m["sdel"] = f(inp["state_delta"][0, sl].reshape(NS * 4, 128, 128))
        m["cmk"] = f(inp["cache_mem_k"][0, sl].reshape(NS, 256, 1024))
        m["cmv"] = f(inp["cache_mem_v"][0, sl].reshape(NS, 256, 1024))
        in_maps.append(m)
    res = run_bass_kernel_spmd(nc, in_maps, core_ids=list(range(8)))
    R = res.results
    cat = lambda k: np.stack([np.asarray(R[c][k]) for c in range(8)])
    y_p = cat("y_p")
    y_s = cat("y_s").reshape(128, 1, D)
    nconv_p = cat("nconv_p")[None]
    nsc_p = cat("nsc_p")[None]
    ndel_p = cat("ndel_p")[None]
    mk = cat("mk_p").reshape(1, 8, 256, 4, 256)
    mv = cat("mv_p").reshape(1, 8, 256, 4, 256)
    nconv_s = cat("nconv_s").reshape(1, 128, 30, 512)
    nsc_s = cat("nsc_s").reshape(1, 128, 3, 1536)
    ndel_s = cat("ndel_s").reshape(1, 128, 4, 128, 128)
    return tuple(np.ascontiguousarray(a, dtype=np.float32) for a in
                 (y_p, y_s, nconv_p, nsc_p, ndel_p, mk, mv, nconv_s, nsc_s, ndel_s))
```

```python
import numpy as np
import concourse.bass as bass
import concourse.mybir as mybir
from concourse.bass_utils import run_bass_kernel_spmd

F32 = mybir.dt.float32
BF16 = mybir.dt.bfloat16
AF = mybir.ActivationFunctionType
ALU = mybir.AluOpType
AX = mybir.AxisListType

T = 2048
NS = 16
TT = T + NS
NT = 16
D = 1024
DFF = 2816
NFC = 22


class Op:
    __slots__ = ("eng", "fn", "deps", "sig", "need", "dma", "tag", "cnt", "idx")


import os
STOP = float(os.environ.get("KSTOP", "99"))


class _Stop(Exception):
    pass


def stop_at(n):
    if STOP == n:
        raise _Stop()


class _Rec:
    def __init__(self):
        self.call = None

    def __getattr__(self, name):
        def f(*a, **k):
            self.call = (name, a, k)
            return self
        return f


class Prog:
    ENGS = ("sp", "act", "pool", "dve", "pe")

    def __init__(self, nc):
        self.nc = nc
        self.ops = {e: [] for e in self.ENGS}
        self.all = []
        self.lastw = {}
        self.readers = {}
        self.tagcnt = {}
        self.taggroup = {}
        self.base = []

    def _add(self, eng, fn, r, w, dma=False, tag=None, group=False):
        op = Op()
        rec = _Rec()
        fn(rec)
        name_, a_, k_ = rec.call
        fn = lambda e, name_=name_, a_=a_, k_=k_: getattr(e, name_)(*a_, **k_)
        op.eng, op.fn, op.dma, op.tag = eng, fn, dma, tag
        op.need = False
        op.sig = None
        psr = [k for k in r if isinstance(k, tuple) and k[0] == "ps"]
        if psr:
            r = [k for k in r if k not in psr]
            w = list(w) + psr
        deps = list(self.base)
        for k in r:
            if k in self.lastw:
                deps.append(self.lastw[k])
        for k in w:
            if k in self.lastw:
                deps.append(self.lastw[k])
            deps.extend(self.readers.get(k, ()))
        if dma:
            deps = [d for d in deps if not (d.dma and d.tag == tag)]
        if eng == "pe":
            deps = [d for d in deps if d.dma or d.eng != "pe"]
        op.deps = deps
        for k in r:
            self.readers.setdefault(k, []).append(op)
        for k in w:
            self.lastw[k] = op
            self.readers[k] = []
        if dma:
            self.tagcnt[tag] = self.tagcnt.get(tag, 0) + 1
            self.taggroup[tag] = group
            op.cnt = self.tagcnt[tag]
        op.idx = len(self.all)
        self.all.append(op)
        self.ops[eng].append(op)
        return op

    def pe(self, fn, r=(), w=()):
        return self._add("pe", fn, r, w)

    def dve(self, fn, r=(), w=()):
        return self._add("dve", fn, r, w)

    def act(self, fn, r=(), w=()):
        return self._add("act", fn, r, w)

    def pool(self, fn, r=(), w=()):
        return self._add("pool", fn, r, w)

    def dma(self, out, in_, r=(), w=(), tag="ld", group=False, q="sp"):
        return self._add(q, lambda e: e.dma_start(out=out, in_=in_), r, w, dma=True, tag=tag, group=group)

    def barrier(self):
        base = []
        for e in self.ENGS:
            last = None
            for op in reversed(self.ops[e]):
                if not op.dma:
                    last = op
                    break
            if last is not None:
                base.append(last)
        lastdma = {}
        for op in self.all:
            if op.dma:
                lastdma[op.tag] = op
        base.extend(lastdma.values())
        self.base = base
        self.lastw = {}
        self.readers = {}

    def emit(self, final_tags):
        nc = self.nc
        for op in self.all:
            for d in op.deps:
                if not d.dma:
                    d.need = True
        for e in self.ENGS:
            n = 0
            for op in self.ops[e]:
                if not op.dma and op.need:
                    n += 1
                    op.sig = n
        from contextlib import ExitStack
        with ExitStack() as st:
            esem = {e: st.enter_context(nc.semaphore("s_" + e)) for e in self.ENGS}
            tsem = {t: st.enter_context(nc.semaphore("t_%d" % i)) for i, t in enumerate(self.tagcnt)}
            block = st.enter_context(nc.Block())

            def run(e, eng):
                waited = {}
                for op in self.ops[e]:
                    need = {}
                    for d in op.deps:
                        if d.dma:
                            s = tsem[d.tag]
                            v = 16 * (self.tagcnt[d.tag] if self.taggroup[d.tag] else d.cnt)
                        else:
                            s = esem[d.eng]
                            v = d.sig
                        if v > need.get(s, (0, 0))[1] if s in need else True:
                            need[s] = (s, v)
                    for s, v in need.values():
                        if waited.get(s, 0) < v:
                            eng.wait_ge(s, v)
                            waited[s] = v
                    ins = op.fn(eng)
                    if op.dma:
                        ins.then_inc(tsem[op.tag], 16)
                    elif op.need:
                        ins.then_inc(esem[e], 1)
                if e == "sp":
                    for t in tsem:
                        eng.wait_ge(tsem[t], 16 * self.tagcnt[t])

            block.sync(lambda eng: run("sp", eng))
            block.scalar(lambda eng: run("act", eng))
            block.gpsimd(lambda eng: run("pool", eng))
            block.vector(lambda eng: run("dve", eng))
            block.tensor(lambda eng: run("pe", eng))


class Alloc:
    def __init__(self, nc):
        self.nc = nc
        self.off = (int(nc.sbuf_base) + 63) // 64 * 64
        self.top = int(nc.sbuf_top)
        self.n = 0

    def t(self, shape, dt, name=None):
        sz = 2 if dt == BF16 else 4
        nb = sz
        for s in shape[1:]:
            nb *= s
        nb = (nb + 63) // 64 * 64
        assert self.off + nb <= self.top, ("SBUF overflow", name, self.off, nb, self.top)
        self.n += 1
        h = self.nc.alloc_sbuf_tensor_at("%s_%d" % (name or "t", self.n), list(shape), dt, offset=self.off)
        self.off += nb
        return h

    def child(self, off, top):
        c = Alloc.__new__(Alloc)
        c.nc, c.off, c.top, c.n = self.nc, (off + 63) // 64 * 64, top, self.n + 1000 * (1 + off % 97)
        return c

    def mark(self):
        return self.off

    def reset(self, m):
        self.off = m


def make_consts():
    i = np.arange(128)
    ident = np.eye(128, dtype=np.float32)
    U = (i[:, None] <= i[None, :]).astype(np.float32)
    SL = (i[:, None] > i[None, :]).astype(np.float32)
    ones = np.ones((128, 128), np.float32)
    blk = (i[:, None] // 64) == (i[None, :] // 64)
    mBDneg = -(SL * blk).astype(np.float32)
    mOFF = (SL * (~blk)).astype(np.float32)
    mTin = U.copy()
    oh16 = np.zeros((128, 16), np.float32)
    oh16[:16, :16] = np.eye(16)
    parts = [ident, U, SL, ones, mBDneg, mOFF, mTin, oh16]
    return np.ascontiguousarray(np.concatenate(parts, axis=1))


C_ID, C_U, C_SL, C_ONE = 0, 128, 256, 384
C_BD, C_OFF, C_TIN = 512, 640, 768
C_OH16 = 896
NCST = C_OH16 + 16

PC_CONVW, PC_CONVB, PC_LNG, PC_LNB, PC_SCW, PC_DN, PC_ALOG, PC_DTB = 0, 124, 128, 132, 136, 184, 185, 186
NPC = 187
PR_GMIX, PR_GMQ, PR_GFFN, PR_GKV, PR_GF, PR_DN4, PR_ALOG, PR_DTB = 0, 1024, 2048, 3072, 4096, 5120, 5632, 5636
NPR = 5640


def make_params(inp):
    pc = np.zeros((128, NPC), np.float32)
    pc[:, PC_CONVW:PC_CONVW + 124] = inp["conv_w"][0].reshape(31, 4, 128).transpose(2, 1, 0).reshape(128, 124)
    pc[:, PC_CONVB:PC_CONVB + 4] = inp["conv_b"][0].reshape(4, 128).T
    pc[:, PC_LNG:PC_LNG + 4] = inp["conv_ln_g"][0].reshape(4, 128).T
    pc[:, PC_LNB:PC_LNB + 4] = inp["conv_ln_b"][0].reshape(4, 128).T
    pc[:, PC_SCW:PC_SCW + 48] = inp["sc_w"][0].reshape(4, 12, 128).transpose(2, 1, 0).reshape(128, 48)
    pc[:, PC_DN] = inp["dn_norm"][0]
    pc[:4, PC_ALOG] = inp["a_log"][0]
    pc[:4, PC_DTB] = inp["dt_bias"][0]
    pr = np.zeros((128, NPR), np.float32)
    bc = lambda v: np.broadcast_to(np.asarray(v, np.float32).reshape(1, -1), (128, np.asarray(v).size))
    pr[:, PR_GMIX:PR_GMIX + 1024] = bc(inp["norm_mix"][0])
    pr[:, PR_GMQ:PR_GMQ + 1024] = bc(inp["norm_mem_q"][0])
    pr[:, PR_GFFN:PR_GFFN + 1024] = bc(inp["norm_ffn"][0])
    pr[:, PR_GKV:PR_GKV + 1024] = bc(inp["norm_mem_kv"][0])
    pr[:, PR_GF:PR_GF + 1024] = bc(inp["norm_f"])
    pr[:, PR_DN4:PR_DN4 + 512] = bc(np.tile(inp["dn_norm"][0], 4))
    pr[:, PR_ALOG:PR_ALOG + 4] = bc(inp["a_log"][0])
    pr[:, PR_DTB:PR_DTB + 4] = bc(inp["dt_bias"][0])
    return pc, pr


def build_nc():
    nc = bass.Bass("TRN2", target_bir_lowering=False)
    P = Prog(nc)
    try:
        return _build_nc(nc, P)
    except _Stop:
        P.emit(["out"])
        return nc, P


def _build_nc(nc, P):
    A = Alloc(nc)

    def dr(name, shape, out=False):
        return nc.dram_tensor(name, list(shape), F32, kind="ExternalOutput" if out else "ExternalInput").ap()

    x_p = dr("x_p", [T, D]); x_s = dr("x_s", [NS, D]); mem = dr("mem", [256, D])
    cconv = dr("cconv", [NS * 30, 512]); ssc = dr("ssc", [NS * 3, 1536]); sdel = dr("sdel", [NS * 4, 128, 128])
    cmk = dr("cmk", [NS, 256, 1024]); cmv = dr("cmv", [NS, 256, 1024])
    w_in = dr("w_in", [D, 3080]); w_out = dr("w_out", [D, D]); w_mq = dr("w_mq", [D, D]); w_mk = dr("w_mk", [D, D])
    w_mv = dr("w_mv", [D, D]); w_mo = dr("w_mo", [D, D]); w_gate = dr("w_gate", [D, DFF]); w_up = dr("w_up", [D, DFF])
    w_down = dr("w_down", [DFF, D])
    cst_d = dr("cst", [128, NCST]); pc_d = dr("pc", [128, NPC]); pr_d = dr("pr", [128, NPR])
    y_p = dr("y_p", [T, D], True); y_s = dr("y_s", [NS, D], True)
    nconv_p = dr("nconv_p", [30, 512], True); nsc_p = dr("nsc_p", [3, 1536], True)
    ndel_p = dr("ndel_p", [4, 128, 128], True)
    mk_p = dr("mk_p", [256, D], True); mv_p = dr("mv_p", [256, D], True)
    nconv_s = dr("nconv_s", [NS, 30, 512], True); nsc_s = dr("nsc_s", [NS, 3, 1536], True)
    ndel_s = dr("ndel_s", [NS * 4, 128, 128], True)

    ps = [nc.alloc_psum_tensor("ps%d" % i, [128, 512], F32) for i in range(8)]
    PK = lambda i: ("ps", i)

    def psb(i):
        return ps[i][:].bitcast(BF16)

    cst = A.t([128, NCST], F32, "cst")
    pc = A.t([128, NPC], F32, "pc")
    idb = A.t([128, 128], BF16, "idb")
    oneb = A.t([128, 128], BF16, "oneb")
    onesc = A.t([128, 128], BF16, "onesc")
    gam = A.t([128, 1024], F32, "gam")
    P.dma(cst[:], cst_d[:, :], w=["cst"], tag="c0")
    P.dma(pc[:], pc_d[:, :], w=["pc"], tag="c1")
    P.dve(lambda e: e.tensor_copy(out=idb[:], in_=cst[:, C_ID:C_ID + 128]), r=["cst"], w=["idb"])
    P.dve(lambda e: e.tensor_copy(out=oneb[:], in_=cst[:, C_ONE:C_ONE + 128]), r=["cst"], w=["oneb"])
    P.dve(lambda e: e.tensor_scalar(out=onesc[:], in0=cst[:, C_ONE:C_ONE + 128], scalar1=1.0 / 512, scalar2=None,
                                    op0=ALU.mult), r=["cst"], w=["onesc"])
    ident_f = cst[:, C_ID:C_ID + 128]
    Uf = cst[:, C_U:C_U + 128]
    SLf = cst[:, C_SL:C_SL + 128]
    onef = cst[:, C_ONE:C_ONE + 128]

    def load_gam(off):
        P.dma(gam[:], pr_d[:, off:off + 1024], w=["gam"], tag="gam")

    nrm_junk = A.t([128, 1024], BF16, "njunk")
    nrm_xn3 = [A.t([128, 1024], BF16, "nxn%d" % i) for i in range(3)]
    nrm_s3 = [A.t([128, 8], F32, "nrs%d" % i) for i in range(3)]

    def norm_pass(items, dstT, dkey):
        L = len(items)

        def S1(i):
            src, rkeys, n, col0, pre = items[i]
            if pre is not None:
                pre()
            ns_, nsk = nrm_s3[i % 3], ("nrs", i % 3)
            P.act(lambda e: e.activation(out=nrm_junk[:n, :], in_=src, func=AF.Square, accum_out=ns_[:n, 0:1]),
                  r=rkeys, w=[nsk, "njunk"])
            P.dve(lambda e: e.tensor_scalar(out=ns_[:n, 1:2], in0=ns_[:n, 0:1], scalar1=1.0 / 1024, scalar2=1e-6,
                                            op0=ALU.mult, op1=ALU.add), r=[nsk], w=[nsk])

        def S2(i):
            src, rkeys, n, col0, pre = items[i]
            ns_, nsk = nrm_s3[i % 3], ("nrs", i % 3)
            xn_, nxk = nrm_xn3[i % 3], ("nxn", i % 3)
            P.act(lambda e: e.activation(out=ns_[:n, 2:3], in_=ns_[:n, 1:2], func=AF.Sqrt), r=[nsk], w=[nsk])
            P.dve(lambda e: e.reciprocal(out=ns_[:n, 3:4], in_=ns_[:n, 2:3]), r=[nsk], w=[nsk])
            P.dve(lambda e: e.scalar_tensor_tensor(out=xn_[:n, :], in0=src, scalar=ns_[:n, 3:4], in1=gam[:n, :],
                                                   op0=ALU.mult, op1=ALU.mult), r=rkeys + [nsk, "gam"], w=[nxk])

        def S3(i):
            src, rkeys, n, col0, pre = items[i]
            xn_, nxk = nrm_xn3[i % 3], ("nxn", i % 3)
            bank = i % 2
            pv = psb(bank)
            for kc in range(8):
                P.pe(lambda e, kc=kc: e.transpose(out=pv[:, kc * 128:kc * 128 + n],
                                                  in_=xn_[:n, kc * 128:(kc + 1) * 128], identity=idb[:n, :n]),
                     r=[nxk, "idb"], w=[PK(bank)])
            pv3 = pv.rearrange("p (c n) -> p c n", c=8)
            dk_ = (dkey, col0 // 512)
            if i % 2 == 0:
                P.act(lambda e: e.activation(out=dstT[:, :, col0:col0 + n], in_=pv3[:, :, 0:n], func=AF.Copy),
                      r=[PK(bank)], w=[dk_])
            else:
                P.dve(lambda e: e.tensor_copy(out=dstT[:, :, col0:col0 + n], in_=pv3[:, :, 0:n]),
                      r=[PK(bank)], w=[dk_])

        for step in range(L + 2):
            if step < L:
                S1(step)
            if 0 <= step - 1 < L:
                S2(step - 1)
            if 0 <= step - 2 < L:
                S3(step - 2)

    def load_w(dst, dkey, wd, r0, nk, c0, ncols):
        for k in range(nk):
            P.dma(dst[:, k, 0:ncols], wd[r0 + k * 128:r0 + (k + 1) * 128, c0:c0 + ncols], w=[dkey],
                  tag="w_" + str(dkey), q="pool")

    off_hT = A.mark()
    hT = A.t([128, 8, TT], BF16, "hT")
    off_cT = A.mark()
    cT = A.t([128, 4, TT], BF16, "cT")
    mX = A.mark()
    XSZ = 17 * 1024 * 4
    A.off += XSZ
    mXe = A.mark()
    AX_ = A.child(mX, mXe)
    zs = A.t([128, NT, 512], BF16, "zs")
    zsT = A.t([128, 4, NS], F32, "zsT")
    vTs = A.t([128, 4, NS], F32, "vTs")
    gtok = A.t([128, NT, 4], F32, "gtok")
    btok = A.t([128, NT, 4], F32, "btok")
    gbS = A.t([4, 2, NS], F32, "gbS")
    utail = A.t([128, 4, 32], F32, "utail")
    ptail = A.t([128, 12, 4], F32, "ptail")
    unew_s = A.t([128, 4, NS], F32, "unews")
    pnew_s = A.t([128, 12, NS], F32, "pnews")
    m_w = A.mark()
    wsl = [A.t([128, 8, 512], BF16, "wsl%d" % i) for i in range(3)]
    m_p1 = A.mark()
    A1 = A
    A = AX_

    xin = [A.t([128, 1024], F32, "xin%d" % i) for i in range(3)]
    load_gam(PR_GMIX)
    items = []
    for t in range(NT + 1):
        s = t % 3
        n = 128 if t < NT else NS
        src_d = x_p[t * 128:(t + 1) * 128, :] if t < NT else x_s[:, :]
        pre = (lambda s=s, n=n, src_d=src_d: P.dma(xin[s][:n, :], src_d, w=[("xin", s)], tag="xin%d" % s))
        items.append((xin[s][:n, :], [("xin", s)], n, t * 128, pre))
    norm_pass(items, hT, "hT")

    stop_at(1)
    BLK = [(i * 512, 512) for i in range(4)] + [(T, NS)]

    def mm8(bank, w_ap_fn, rhs_fn, ncols, rk, mrows=128):
        for kc in range(8):
            P.pe(lambda e, kc=kc: e.matmul(ps[bank][0:mrows, 0:ncols], lhsT=w_ap_fn(kc), rhs=rhs_fn(kc),
                                           start=(kc == 0), stop=(kc == 7)), r=rk, w=[PK(bank)])

    load_w(wsl[2], ("wsl", 2), w_in, 0, 8, 3072, 8)
    load_w(wsl[0], ("wsl", 0), w_in, 0, 8, 0, 512)
    load_w(wsl[1], ("wsl", 1), w_in, 0, 8, 512, 512)
    prb = A.t([128, 16], F32, "prb")
    P.dma(prb[:, 0:8], pr_d[:, PR_ALOG:PR_ALOG + 8], w=["prb"], tag="c2")
    P.act(lambda e: e.activation(out=prb[:, 8:12], in_=prb[:, 0:4], func=AF.Exp), r=["prb"], w=["prb"])
    P.dve(lambda e: e.tensor_scalar(out=prb[:, 8:12], in0=prb[:, 8:12], scalar1=-1.0, scalar2=None, op0=ALU.mult),
          r=["prb"], w=["prb"])
    negA_c = A.t([4, 1], F32, "negAc")
    P.act(lambda e: e.activation(out=negA_c[:], in_=pc[0:4, PC_ALOG:PC_ALOG + 1], func=AF.Exp), r=["pc"], w=["negAc"])
    P.dve(lambda e: e.tensor_scalar(out=negA_c[:], in0=negA_c[:], scalar1=-1.0, scalar2=None, op0=ALU.mult),
          r=["negAc"], w=["negAc"])
    gx = A.t([128, NT, 4], F32, "gx")
    dtb64 = A.t([128, NT, 4], F32, "dtb64")
    nga64 = A.t([128, NT, 4], F32, "nga64")
    P.dve(lambda e: e.tensor_copy(out=dtb64[:], in_=prb[:, 4:8].unsqueeze(1).to_broadcast([128, NT, 4])),
          r=["prb"], w=["dtb64"])
    P.dve(lambda e: e.tensor_copy(out=nga64[:], in_=prb[:, 8:12].unsqueeze(1).to_broadcast([128, NT, 4])),
          r=["prb"], w=["nga64"])
    for t in range(NT):
        for kc in range(8):
            P.pe(lambda e, t=t, kc=kc: e.matmul(ps[2][:, t * 8:(t + 1) * 8], lhsT=hT[:, kc, t * 128:(t + 1) * 128],
                                                rhs=wsl[2][:, kc, 0:8], start=(kc == 0), stop=(kc == 7)),
                 r=[("hT", t // 4), ("wsl", 2)], w=[PK(2)])
    ps3 = ps[2][:, 0:NT * 8].rearrange("p (t c) -> p t c", c=8)
    P.act(lambda e: e.activation(out=btok[:, :, :], in_=ps3[:, :, 0:4], func=AF.Exp, scale=-1.0), r=[PK(2)],
          w=["btok"])
    P.dve(lambda e: e.tensor_scalar(out=btok[:, :, :], in0=btok[:, :, :], scalar1=1.0, scalar2=None, op0=ALU.add),
          r=["btok"], w=["btok"])
    P.dve(lambda e: e.reciprocal(out=btok[:, :, :], in_=btok[:, :, :]), r=["btok"], w=["btok"])
    P.dve(lambda e: e.tensor_tensor(out=gx[:], in0=ps3[:, :, 4:8], in1=dtb64[:], op=ALU.add), r=[PK(2), "dtb64"],
          w=["gx"])
    P.act(lambda e: e.activation(out=gx[:], in_=gx[:], func=AF.Exp), r=["gx"], w=["gx"])
    P.act(lambda e: e.activation(out=gx[:], in_=gx[:], func=AF.Ln, bias=1.0), r=["gx"], w=["gx"])
    P.dve(lambda e: e.tensor_tensor(out=gtok[:, :, :], in0=gx[:], in1=nga64[:], op=ALU.mult), r=["gx", "nga64"],
          w=["gtok"])
    for half in range(2):
        b = 2 + half
        mm8(b, lambda kc: wsl[2][:, kc, half * 4:half * 4 + 4], lambda kc: hT[:, kc, T:TT], NS, [("hT", 4), ("wsl", 2)],
            mrows=4)
    P.act(lambda e: e.activation(out=gbS[:, 0, :], in_=ps[2][0:4, 0:NS], func=AF.Exp, scale=-1.0), r=[PK(2)],
          w=["gbS"])
    P.dve(lambda e: e.tensor_scalar(out=gbS[:, 0, :], in0=gbS[:, 0, :], scalar1=1.0, scalar2=None, op0=ALU.add),
          r=["gbS"], w=["gbS"])
    P.dve(lambda e: e.reciprocal(out=gbS[:, 0, :], in_=gbS[:, 0, :]), r=["gbS"], w=["gbS"])
    gts = A.t([4, 2, NS], F32, "gts")
    P.act(lambda e: e.activation(out=gts[:, 0, :], in_=ps[3][0:4, 0:NS], func=AF.Exp,
                                 bias=pc[0:4, PC_DTB:PC_DTB + 1]), r=[PK(3), "pc"], w=["gts"])
    P.act(lambda e: e.activation(out=gts[:, 1, :], in_=gts[:, 0, :], func=AF.Ln, bias=1.0), r=["gts"], w=["gts"])
    P.dve(lambda e: e.tensor_scalar(out=gbS[:, 1, :], in0=gts[:, 1, :], scalar1=negA_c[:, 0:1], scalar2=None,
                                    op0=ALU.mult), r=["gts", "negAc"], w=["gbS"])

    stop_at(2)
    upad = A.t([128, 4, 30 + T], BF16, "upad")
    us = A.t([128, 4, NS, 32], BF16, "us")
    sig = [A.t([128, 512], F32, "sig%d" % i) for i in range(2)]
    diag = A.t([128, 31, 128], BF16, "diag")
    P.dve(lambda e: e.memset(upad[:, :, 0:30], 0.0), w=["upad"])
    load_w(wsl[2], ("wsl", 2), w_in, 0, 8, 1024, 512)
    cc_in = A.t([120, 4, 512], F32, "ccin")
    for g4 in range(4):
        P.dma(cc_in[:, g4, :], cconv[g4 * 120:(g4 + 1) * 120, :], w=[("ccin", g4)], tag="cc", group=True)
    P.dma(nconv_s[:, 0:29, :], cconv.rearrange("(s j) c -> s j c", j=30)[:, 1:30, :], tag="out")
    for c in range(4):
        for g4 in range(4):
            b = 4 + (g4 % 2)
            P.pe(lambda e, c=c, g4=g4, b=b: e.transpose(out=ps[b][:, 0:120], in_=cc_in[:, g4, c * 128:(c + 1) * 128],
                                                        identity=ident_f[0:120, 0:120]), r=[("ccin", g4), "cst"], w=[PK(b)])
            P.act(lambda e, c=c, g4=g4, b=b: e.activation(
                out=us[:, c, g4 * 4:(g4 + 1) * 4, 0:30],
                in_=ps[b][:, 0:120].rearrange("p (s j) -> p s j", j=30), func=AF.Copy), r=[PK(b)], w=["us"])
    for c in range(4):
        for bi, (t0, n) in enumerate(BLK):
            ba, bb = (bi % 2) * 2, (bi % 2) * 2 + 1
            mm8(ba, lambda kc: wsl[0][:, kc, c * 128:(c + 1) * 128], lambda kc: hT[:, kc, t0:t0 + n], n,
                [("hT", t0 // 512), ("wsl", 0)])
            mm8(bb, lambda kc: wsl[1][:, kc, c * 128:(c + 1) * 128], lambda kc: hT[:, kc, t0:t0 + n], n,
                [("hT", t0 // 512), ("wsl", 1)])
            sg = sig[bi % 2]
            sk = ("sig", bi % 2)
            P.act(lambda e, sg=sg, bb=bb, n=n: e.activation(out=sg[:, 0:n], in_=ps[bb][:, 0:n], func=AF.Sigmoid),
                  r=[PK(bb)], w=[sk])
            if bi < 4:
                P.dve(lambda e, sg=sg, ba=ba, c=c, t0=t0: e.tensor_tensor(
                    out=upad[:, c, 30 + t0:30 + t0 + 512], in0=ps[ba][:, 0:512], in1=sg[:, 0:512], op=ALU.mult),
                    r=[PK(ba), sk], w=["upad"])
                if bi == 3:
                    P.dve(lambda e, sg=sg, ba=ba, c=c: e.tensor_tensor(
                        out=utail[:, c, 0:30], in0=ps[ba][:, 482:512], in1=sg[:, 482:512], op=ALU.mult),
                        r=[PK(ba), sk], w=["utail"])
            else:
                P.dve(lambda e, sg=sg, ba=ba, c=c: e.tensor_tensor(
                    out=unew_s[:, c, :], in0=ps[ba][:, 0:NS], in1=sg[:, 0:NS], op=ALU.mult),
                    r=[PK(ba), sk], w=["unews"])
                P.act(lambda e, c=c: e.activation(out=us[:, c, :, 30:31], in_=unew_s[:, c, :].unsqueeze(2),
                                                  func=AF.Copy), r=["unews"], w=["us"])
        for j in range(31):
            P.dve(lambda e, c=c, j=j: e.tensor_scalar(
                out=diag[:, j, :], in0=idb[:], scalar1=pc[:, PC_CONVW + c * 31 + j:PC_CONVW + c * 31 + j + 1],
                scalar2=None, op0=ALU.mult), r=["idb", "pc"], w=["diag"])
        for bi, (t0, n) in enumerate(BLK):
            b = 4 + bi % 2
            for j in range(31):
                if bi < 4:
                    rhs = upad[:, c, t0 + j:t0 + j + 512]
                else:
                    rhs = us[:, c, :, j]
                P.pe(lambda e, b=b, j=j, rhs=rhs, n=n: e.matmul(ps[b][:, 0:n], lhsT=diag[:, j, :], rhs=rhs,
                                                               start=(j == 0), stop=(j == 30)),
                     r=["diag", "upad", "us"], w=[PK(b)])
            P.act(lambda e, b=b, c=c, t0=t0, n=n: e.activation(
                out=cT[:, c, t0:t0 + n], in_=ps[b][:, 0:n], func=AF.Identity,
                bias=pc[:, PC_CONVB + c:PC_CONVB + c + 1]), r=[PK(b), "pc"], w=["cT"])
    csq = [A.t([128, 512], BF16, "csq%d" % i) for i in range(2)]
    lnm = A.t([128, 512], F32, "lnm")
    lnv = A.t([128, 512], F32, "lnv")
    lnt = [A.t([128, 512], F32, "lnt%d" % i) for i in range(2)]
    for bi, (t0, n) in enumerate(BLK):
        for c in range(4):
            P.pe(lambda e, c=c, t0=t0, n=n: e.matmul(ps[0][:, 0:n], lhsT=onesc[:], rhs=cT[:, c, t0:t0 + n],
                                                     start=(c == 0), stop=(c == 3)), r=["cT", "onesc"], w=[PK(0)])
        for c in range(4):
            q = csq[c % 2]
            P.dve(lambda e, q=q, c=c, t0=t0, n=n: e.tensor_tensor(out=q[:, 0:n], in0=cT[:, c, t0:t0 + n],
                                                                  in1=cT[:, c, t0:t0 + n], op=ALU.mult),
                  r=["cT"], w=[("csq", c % 2)])
            P.pe(lambda e, q=q, c=c, n=n: e.matmul(ps[1][:, 0:n], lhsT=onesc[:], rhs=q[:, 0:n],
                                                   start=(c == 0), stop=(c == 3)), r=[("csq", c % 2), "onesc"],
                 w=[PK(1)])
        P.act(lambda e, n=n: e.activation(out=lnm[:, 0:n], in_=ps[0][:, 0:n], func=AF.Copy), r=[PK(0)], w=["lnm"])
        P.dve(lambda e, n=n: e.tensor_tensor(out=lnv[:, 0:n], in0=lnm[:, 0:n], in1=lnm[:, 0:n], op=ALU.mult),
              r=["lnm"], w=["lnv"])
        P.dve(lambda e, n=n: e.tensor_tensor(out=lnv[:, 0:n], in0=ps[1][:, 0:n], in1=lnv[:, 0:n], op=ALU.subtract),
              r=[PK(1), "lnv"], w=["lnv"])
        P.dve(lambda e, n=n: e.tensor_scalar(out=lnv[:, 0:n], in0=lnv[:, 0:n], scalar1=0.0, scalar2=1e-5,
                                             op0=ALU.max, op1=ALU.add), r=["lnv"], w=["lnv"])
        P.act(lambda e, n=n: e.activation(out=lnv[:, 0:n], in_=lnv[:, 0:n], func=AF.Ln), r=["lnv"], w=["lnv"])
        P.act(lambda e, n=n: e.activation(out=lnv[:, 0:n], in_=lnv[:, 0:n], func=AF.Exp, scale=-0.5), r=["lnv"],
              w=["lnv"])
        for c in range(4):
            tt_ = lnt[c % 2]
            tk = ("lnt", c % 2)
            P.dve(lambda e, tt_=tt_, c=c, t0=t0, n=n: e.tensor_tensor(out=tt_[:, 0:n], in0=cT[:, c, t0:t0 + n],
                                                                      in1=lnm[:, 0:n], op=ALU.subtract),
                  r=["cT", "lnm"], w=[tk])
            P.dve(lambda e, tt_=tt_, n=n: e.tensor_tensor(out=tt_[:, 0:n], in0=tt_[:, 0:n], in1=lnv[:, 0:n],
                                                          op=ALU.mult), r=[tk, "lnv"], w=[tk])
            P.act(lambda e, tt_=tt_, c=c, t0=t0, n=n: e.activation(
                out=cT[:, c, t0:t0 + n], in_=tt_[:, 0:n], func=AF.Silu,
                scale=pc[:, PC_LNG + c:PC_LNG + c + 1], bias=pc[:, PC_LNB + c:PC_LNB + c + 1]),
                r=[tk, "pc"], w=["cT"])
    otl = A.t([32, 512], F32, "otl")
    for c in range(4):
        P.pe(lambda e, c=c: e.transpose(out=ps[2][0:30, c * 128:(c + 1) * 128], in_=utail[:, c, 0:30],
                                        identity=ident_f), r=["utail", "cst"], w=[PK(2)])
    P.act(lambda e: e.activation(out=otl[0:30, :], in_=ps[2][0:30, :], func=AF.Copy), r=[PK(2)], w=["otl"])
    P.dma(nconv_p[:, :], otl[0:30, :], r=["otl"], tag="out")
    otl2 = A.t([16, 512], F32, "otl2")
    for c in range(4):
        P.pe(lambda e, c=c: e.transpose(out=ps[3][0:NS, c * 128:(c + 1) * 128], in_=unew_s[:, c, :],
                                        identity=ident_f), r=["unews", "cst"], w=[PK(3)])
    P.act(lambda e: e.activation(out=otl2[:, :], in_=ps[3][0:NS, :], func=AF.Copy), r=[PK(3)], w=["otl2"])
    P.dma(nconv_s[:, 29, :], otl2[:, :], r=["otl2"], tag="out")

    stop_at(3)
    P.barrier()
    AX_ = A1.child(mX, mXe)
    qT = AX_.t([128, 4, TT], BF16, "qT")
    kT = AX_.t([128, 4, TT], BF16, "kT")
    ktok = AX_.t([128, NT, 512], BF16, "ktok")
    vb = AX_.t([128, NT, 512], BF16, "vb")
    A = A1
    scd2 = [A.t([128, 4, 128], BF16, "scd%d" % i) for i in range(2)]
    pre = [A.t([128, 3 + 512], BF16, "pre%d" % i) for i in range(2)]
    pres2 = [A.t([128, NS, 4], BF16, "pres%d" % i) for i in range(2)]
    sfl = [A.t([128, 512], F32, "sfl%d" % i) for i in range(2)]
    sqb = [A.t([128, 512], BF16, "sqb%d" % i) for i in range(2)]
    rnb = [A.t([128, 512], F32, "rnb%d" % i) for i in range(2)]
    vtmp = [A.t([128, 512], BF16, "vtmp%d" % i) for i in range(2)]
    ss_in = A.t([48, 1536], F32, "ssin")
    P.dma(ss_in[:, :], ssc[:, :], w=["ssin"], tag="ssin")
    P.dma(nsc_s[:, 0:2, :], ssc.rearrange("(s j) c -> s j c", j=3)[:, 1:3, :], tag="out")

    def run_skewed(iters):
        L = len(iters)
        S_ = max(len(x) for x in iters)
        for step in range(L + S_ - 1):
            for k in range(S_):
                i = step - k
                if 0 <= i < L and k < len(iters[i]):
                    iters[i][k]()

    iters = []
    it = [0]
    for grp in range(3):
        sl = (2, 0, 1)[grp]
        for hh in range(4):
            ch = grp * 4 + hh
            scd = scd2[ch % 2]
            sck = ("scd", ch % 2)
            pres = pres2[ch % 2]
            psk = ("pres", ch % 2)
            for bi, (t0, n) in enumerate(BLK):
                i = it[0]
                it[0] += 1

                def S0(grp=grp, sl=sl, hh=hh, ch=ch, scd=scd, sck=sck, pres=pres, psk=psk, bi=bi, t0=t0, n=n, i=i):
                    b = i % 2
                    pr_ = pre[i % 2]
                    prk = ("pre", i % 2)
                    if hh == 0 and bi == 0:
                        if grp < 2:
                            nsl = (2, 0, 1)[grp + 1]
                            load_w(wsl[nsl], ("wsl", nsl), w_in, 0, 8, 1024 + (grp + 1) * 512, 512)
                        else:
                            load_w(wsl[2], ("wsl", 2), w_in, 0, 8, 2560, 512)
                    if bi == 0:
                        for j in range(4):
                            P.dve(lambda e, j=j: e.tensor_scalar(
                                out=scd[:, j, :], in0=idb[:],
                                scalar1=pc[:, PC_SCW + ch * 4 + j:PC_SCW + ch * 4 + j + 1],
                                scalar2=None, op0=ALU.mult), r=["idb", "pc"], w=[sck])
                        P.pe(lambda e: e.transpose(out=ps[6][:, 0:48], in_=ss_in[:, ch * 128:(ch + 1) * 128],
                                                   identity=ident_f[0:48, 0:48]), r=["ssin", "cst"], w=[PK(6)])
                        P.act(lambda e: e.activation(out=pres[:, :, 0:3],
                                                     in_=ps[6][:, 0:48].rearrange("p (s j) -> p s j", j=3),
                                                     func=AF.Copy), r=[PK(6)], w=[psk])
                    mm8(b, lambda kc: wsl[sl][:, kc, hh * 128:(hh + 1) * 128], lambda kc: hT[:, kc, t0:t0 + n], n,
                        [("hT", t0 // 512), ("wsl", sl)])
                    if bi < 4:
                        if bi == 0:
                            P.dve(lambda e: e.memset(pr_[:, 0:3], 0.0), w=[prk])
                        else:
                            po = pre[(i - 1) % 2]
                            P.dve(lambda e: e.tensor_copy(out=pr_[:, 0:3], in_=po[:, 512:515]),
                                  r=[("pre", (i - 1) % 2)], w=[prk])
                        P.dve(lambda e: e.tensor_copy(out=pr_[:, 3:515], in_=ps[b][:, 0:512]),
                              r=[PK(b)], w=[prk])
                        if bi == 3:
                            P.dve(lambda e: e.tensor_copy(out=ptail[:, ch, 0:3], in_=ps[b][:, 509:512]),
                                  r=[PK(b)], w=["ptail"])
                    else:
                        P.act(lambda e: e.activation(out=pres[:, :, 3:4], in_=ps[b][:, 0:NS].unsqueeze(2),
                                                     func=AF.Copy), r=[PK(b)], w=[psk])
                        P.dve(lambda e: e.tensor_copy(out=pnew_s[:, ch, :], in_=ps[b][:, 0:NS]),
                              r=[PK(b)], w=["pnews"])

                def S1(grp=grp, hh=hh, scd=scd, sck=sck, pres=pres, psk=psk, bi=bi, n=n, i=i):
                    b2 = 2 + i % 2
                    pr_ = pre[i % 2]
                    prk = ("pre", i % 2)
                    for j in range(4):
                        rhs = pr_[:, j:j + 512] if bi < 4 else pres[:, :, j]
                        P.pe(lambda e, j=j, rhs=rhs: e.matmul(ps[b2][:, 0:n], lhsT=scd[:, j, :], rhs=rhs,
                                                              start=(j == 0), stop=(j == 3)),
                             r=[sck, prk if bi < 4 else psk], w=[PK(b2)])
                    if grp == 2:
                        if bi < 4:
                            vt = vtmp[i % 2]
                            P.act(lambda e: e.activation(out=vt[:, :], in_=ps[b2][:, 0:512], func=AF.Silu),
                                  r=[PK(b2)], w=[("vtmp", i % 2)])
                        else:
                            P.act(lambda e: e.activation(out=vTs[:, hh, :], in_=ps[b2][:, 0:NS], func=AF.Silu),
                                  r=[PK(b2)], w=["vTs"])
                        return
                    sf = sfl[i % 2]
                    sfk = ("sfl", i % 2)
                    P.act(lambda e: e.activation(out=sf[:, 0:n], in_=ps[b2][:, 0:n], func=AF.Exp, scale=-1.0),
                          r=[PK(b2)], w=[sfk])
                    P.act(lambda e: e.activation(out=sf[:, 0:n], in_=sf[:, 0:n], func=AF.Ln, bias=1.0),
                          r=[sfk], w=[sfk])
                    P.act(lambda e: e.activation(out=sf[:, 0:n], in_=sf[:, 0:n], func=AF.Exp, scale=-1.0),
                          r=[sfk], w=[sfk])
                    P.dve(lambda e: e.tensor_tensor(out=sf[:, 0:n], in0=ps[b2][:, 0:n], in1=sf[:, 0:n], op=ALU.mult),
                          r=[PK(b2), sfk], w=[sfk])
                    sq = sqb[i % 2]
                    P.dve(lambda e: e.tensor_tensor(out=sq[:, 0:n], in0=sf[:, 0:n], in1=sf[:, 0:n], op=ALU.mult),
                          r=[sfk], w=[("sqb", i % 2)])

                def S2(grp=grp, hh=hh, bi=bi, t0=t0, n=n, i=i):
                    pb = 4 + i % 2
                    if grp == 2:
                        if bi == 4:
                            return
                        vt = vtmp[i % 2]
                        vk = ("vtmp", i % 2)
                        pv = psb(pb)
                        for tl in range(4):
                            P.pe(lambda e, tl=tl: e.transpose(out=pv[:, tl * 128:(tl + 1) * 128],
                                                              in_=vt[:, tl * 128:(tl + 1) * 128], identity=idb[:]),
                                 r=[vk, "idb"], w=[PK(pb)])
                        for tl in range(4):
                            tg = bi * 4 + tl
                            P.dve(lambda e, tl=tl, tg=tg: e.tensor_scalar(
                                out=vb[:, tg, hh * 128:(hh + 1) * 128], in0=pv[:, tl * 128:(tl + 1) * 128],
                                scalar1=btok[:, tg, hh:hh + 1], scalar2=None, op0=ALU.mult),
                                r=[PK(pb), "btok"], w=["vb"])
                        return
                    dst = qT if grp == 0 else kT
                    dk = "qT" if grp == 0 else "kT"
                    sf = sfl[i % 2]
                    sfk = ("sfl", i % 2)
                    sq = sqb[i % 2]
                    P.pe(lambda e: e.matmul(ps[pb][:, 0:n], lhsT=oneb[:], rhs=sq[:, 0:n], start=True, stop=True),
                         r=[("sqb", i % 2), "oneb"], w=[PK(pb)])
                    rn = rnb[i % 2]
                    rk_ = ("rnb", i % 2)
                    sc_ = 128.0 if grp == 0 else 1.0
                    P.act(lambda e: e.activation(out=rn[:, 0:n], in_=ps[pb][:, 0:n], func=AF.Ln, scale=sc_,
                                                 bias=1e-6 * sc_), r=[PK(pb)], w=[rk_])
                    P.act(lambda e: e.activation(out=rn[:, 0:n], in_=rn[:, 0:n], func=AF.Exp, scale=-0.5), r=[rk_],
                          w=[rk_])
                    P.dve(lambda e: e.tensor_tensor(out=dst[:, hh, t0:t0 + n], in0=sf[:, 0:n], in1=rn[:, 0:n],
                                                    op=ALU.mult), r=[sfk, rk_], w=[dk])

                def S3(grp=grp, hh=hh, bi=bi, t0=t0, i=i):
                    if not (grp == 1 and bi < 4):
                        return
                    pb2 = 6 + i % 2
                    pv = psb(pb2)
                    for tl in range(4):
                        P.pe(lambda e, tl=tl: e.transpose(out=pv[:, tl * 128:(tl + 1) * 128],
                                                          in_=kT[:, hh, t0 + tl * 128:t0 + (tl + 1) * 128],
                                                          identity=idb[:]), r=["kT", "idb"], w=[PK(pb2)])
                    P.dve(lambda e: e.tensor_copy(out=ktok[:, bi * 4:(bi + 1) * 4, hh * 128:(hh + 1) * 128],
                                                  in_=pv[:, 0:512].rearrange("p (t d) -> p t d", t=4)),
                          r=[PK(pb2)], w=["ktok"])

                iters.append([S0, S1, S2, S3])
    run_skewed(iters)
    otp = A.t([16, 1536], F32, "otp")
    for ch in range(12):
        b = ch // 4
        P.pe(lambda e, ch=ch, b=b: e.transpose(out=ps[b][0:3, (ch % 4) * 128:(ch % 4 + 1) * 128],
                                               in_=ptail[:, ch, 0:3], identity=ident_f), r=["ptail", "cst"],
             w=[PK(b)])
    for b in range(3):
        P.act(lambda e, b=b: e.activation(out=otp[0:3, b * 512:(b + 1) * 512], in_=ps[b][0:3, :], func=AF.Copy),
              r=[PK(b)], w=["otp"])
    P.dma(nsc_p[:, :], otp[0:3, :], r=["otp"], tag="ootp")
    otp2 = otp
    for ch in range(12):
        b = 3 + ch // 4
        P.pe(lambda e, ch=ch, b=b: e.transpose(out=ps[b][0:NS, (ch % 4) * 128:(ch % 4 + 1) * 128],
                                               in_=pnew_s[:, ch, :], identity=ident_f), r=["pnews", "cst"],
             w=[PK(b)])
    for b in range(3):
        P.act(lambda e, b=b: e.activation(out=otp2[:, b * 512:(b + 1) * 512], in_=ps[3 + b][0:NS, :], func=AF.Copy),
              r=[PK(3 + b)], w=["otp"])
    P.dma(nsc_s[:, 2, :], otp2[:, :], r=["otp"], tag="ootp")

    for t in range(NT):
        b = t % 2
        mm8(b, lambda kc: hT[:, kc, t * 128:(t + 1) * 128], lambda kc: wsl[2][:, kc, 0:512], 512,
            [("hT", t // 4), ("wsl", 2)])
        P.act(lambda e, t=t, b=b: e.activation(out=zs[:, t, :], in_=ps[b][:, 0:512], func=AF.Silu), r=[PK(b)],
              w=["zs"])
    for hh in range(4):
        b = 2 + hh % 2
        mm8(b, lambda kc: wsl[2][:, kc, hh * 128:(hh + 1) * 128], lambda kc: hT[:, kc, T:TT], NS,
            [("hT", 4), ("wsl", 2)])
        P.act(lambda e, hh=hh, b=b: e.activation(out=zsT[:, hh, :], in_=ps[b][:, 0:NS], func=AF.Silu), r=[PK(b)],
              w=["zsT"])

    stop_at(4)
    build_rest(nc, P, A, locals())
    return nc, P


def build_rest(nc, P, A, L):
    g = dict(L)
    from types import SimpleNamespace
    V = SimpleNamespace(**g)
    ps, PK, psb, cst, pc, idb, oneb = V.ps, V.PK, V.psb, V.cst, V.pc, V.idb, V.oneb
    ident_f, Uf, SLf, onef = V.ident_f, V.Uf, V.SLf, V.onef
    hT, cT, qT, kT, ktok, vb, zs, zsT, vTs, gtok, btok, gbS = (V.hT, V.cT, V.qT, V.kT, V.ktok, V.vb, V.zs, V.zsT,
                                                                 V.vTs, V.gtok, V.btok, V.gbS)
    load_w, load_gam, mm8, gam = V.load_w, V.load_gam, V.mm8, V.gam
    pr_d = V.pr_d

    P.barrier()
    A.reset(V.m_w)
    dT = hT

    f32t = lambda name, shape=(128, 512): A.t(list(shape), F32, name)
    bft = lambda name, shape=(128, 512): A.t(list(shape), BF16, name)
    dn4 = f32t("dn4")
    P.dma(dn4[:], pr_d[:, PR_DN4:PR_DN4 + 512], w=["dn4"], tag="c3")
    S = f32t("S")
    Sb = bft("Sb")
    P.dve(lambda e: e.memset(S[:], 0.0), w=["S"])
    P.dve(lambda e: e.memset(Sb[:], 0.0), w=["Sb"])
    e3_2 = [f32t("e3_%d" % i, (128, 16)) for i in range(2)]
    gSL = f32t("gSL")
    E = f32t("E")
    ET = f32t("ET")
    EBbd = f32t("EBbd")
    EBoff = bft("EBoff")
    ETm = bft("ETm")
    Y = [bft("Y0"), bft("Y1")]
    YT = [bft("YT0"), bft("YT1")]
    PT = [bft("PT0"), bft("PT1")]
    Loff = bft("Loff")
    Tbd = bft("Tbd")
    Xb = bft("Xb")
    TTm = bft("TTm")
    kbg = bft("kbg")
    kdec_2 = [bft("kdec%d" % i) for i in range(2)]
    qkT_2 = [bft("qkT%d" % i) for i in range(2)]
    wT_2 = [bft("wT%d" % i) for i in range(2)]
    u_2 = [f32t("u_sb%d" % i) for i in range(2)]
    vnew = bft("vnew")
    o_sb = f32t("o_sb")
    qS_sb = f32t("qS_sb")
    bg_m = f32t("bg_m", (128, 8))
    bg_c = f32t("bg_c", (128, 8))
    dtok = bft("dtok")
    osq = V.nrm_junk
    B4 = lambda ap: ap.unsqueeze(1).to_broadcast([128, 4, 128])
    H4 = lambda ap: ap.rearrange("p (h n) -> p h n", h=4)
    mBD = B4(cst[:, C_BD:C_BD + 128])
    mOFF = B4(cst[:, C_OFF:C_OFF + 128])
    mTIN = B4(cst[:, C_TIN:C_TIN + 128])
    mBD4 = f32t("mBD4")
    P.pool(lambda e: e.tensor_copy(out=H4(mBD4[:]), in_=mBD), r=["cst"], w=["mBD4"])
    HS = [slice(h * 128, (h + 1) * 128) for h in range(4)]

    def make_tile(t):
        tk = slice(t * 128, (t + 1) * 128)
        p = t % 2
        e3, e3k = e3_2[p], ("e3", p)
        egc, erem, etot = e3[:, 0:4], e3[:, 4:8], e3[:, 8:12]
        qkT, qkk = qkT_2[p], ("qkT", p)
        wT, wTk = wT_2[p], ("wT", p)
        u_sb, uk = u_2[p], ("u_sb", p)
        kdec, kdk = kdec_2[p], ("kdec", p)
        gt = gtok[:, t, :]
        st = {"cur": 0}

        def A_():
            P.pe(lambda e: e.matmul(ps[0][:, 0:4], lhsT=Uf, rhs=gt, start=True, stop=True), r=["gtok", "cst"],
                 w=[PK(0)])
            P.pe(lambda e: e.matmul(ps[0][:, 4:8], lhsT=SLf, rhs=gt, start=True, stop=True), r=["gtok", "cst"],
                 w=[PK(0)])
            P.pe(lambda e: e.matmul(ps[0][:, 8:12], lhsT=onef, rhs=gt, start=True, stop=True), r=["gtok", "cst"],
                 w=[PK(0)])
            P.act(lambda e: e.activation(out=e3[:, 0:12], in_=ps[0][:, 0:12], func=AF.Exp), r=[PK(0)], w=[e3k])
            for h in range(4):
                P.dve(lambda e, h=h: e.tensor_scalar(out=gSL[:, HS[h]], in0=SLf, scalar1=gt[:, h:h + 1],
                                                     scalar2=None, op0=ALU.mult), r=["gtok", "cst"], w=["gSL"])
            P.pe(lambda e: e.matmul(ps[1][:, :], lhsT=Uf, rhs=gSL[:, :], start=True, stop=True), r=["gSL", "cst"],
                 w=[PK(1)])
            for h in range(4):
                P.pe(lambda e, h=h: e.matmul(ps[2][:, HS[h]], lhsT=gSL[:, HS[h]], rhs=Uf, start=True, stop=True),
                     r=["gSL", "cst"], w=[PK(2)])
            P.act(lambda e: e.activation(out=E[:], in_=ps[1][:, :], func=AF.Exp), r=[PK(1)], w=["E"])
            P.act(lambda e: e.activation(out=ET[:], in_=ps[2][:, :], func=AF.Exp), r=[PK(2)], w=["ET"])
            for h in range(4):
                P.dve(lambda e, h=h: e.tensor_scalar(out=E[:, HS[h]], in0=E[:, HS[h]], scalar1=btok[:, t, h:h + 1],
                                                     scalar2=None, op0=ALU.mult), r=["E", "btok"], w=["E"])
            P.dve(lambda e: e.tensor_tensor(out=EBbd[:], in0=E[:], in1=mBD4[:], op=ALU.mult), r=["E", "mBD4"],
                  w=["EBbd"])
            P.pool(lambda e: e.tensor_tensor(out=H4(EBoff[:]), in0=H4(E[:]), in1=mOFF, op=ALU.mult), r=["E", "cst"],
                   w=["EBoff"])
            P.pool(lambda e: e.tensor_tensor(out=H4(ETm[:]), in0=H4(ET[:]), in1=mTIN, op=ALU.mult), r=["ET", "cst"],
                   w=["ETm"])
            for h in range(4):
                P.pe(lambda e, h=h: e.matmul(ps[3][:, HS[h]], lhsT=kT[:, h, tk], rhs=kT[:, h, tk], start=True,
                                             stop=True), r=["kT"], w=[PK(3)])
            for h in range(4):
                P.pe(lambda e, h=h: e.matmul(ps[4][:, HS[h]], lhsT=kT[:, h, tk], rhs=qT[:, h, tk], start=True,
                                             stop=True), r=["kT", "qT"], w=[PK(4)])
            P.dve(lambda e: e.tensor_tensor(out=Y[0][:], in0=ps[3][:, :], in1=EBbd[:], op=ALU.mult),
                  r=[PK(3), "EBbd"], w=[("Y", 0)])
            pv = psb(5)
            for h in range(4):
                P.pe(lambda e, h=h: e.transpose(out=pv[:, HS[h]], in_=Y[0][:, HS[h]], identity=idb[:]),
                     r=[("Y", 0), "idb"], w=[PK(5)])
            P.act(lambda e: e.activation(out=YT[0][:], in_=pv[:, 0:512], func=AF.Copy), r=[PK(5)], w=[("YT", 0)])
            P.pool(lambda e: e.tensor_tensor(out=H4(PT[0][:]), in0=H4(YT[0][:]), in1=B4(ident_f), op=ALU.add),
                   r=[("YT", 0), "cst"], w=[("PT", 0)])
            P.dve(lambda e: e.tensor_tensor(out=Loff[:], in0=ps[3][:, :], in1=EBoff[:], op=ALU.mult),
                  r=[PK(3), "EBoff"], w=["Loff"])
            P.dve(lambda e: e.tensor_tensor(out=qkT[:], in0=ps[4][:, :], in1=ETm[:], op=ALU.mult),
                  r=[PK(4), "ETm"], w=[qkk])
            st["cur"] = 0

        def N_(m):
            cur = st["cur"]
            nx = 1 - cur
            for h in range(4):
                P.pe(lambda e, h=h: e.matmul(ps[6][:, HS[h]], lhsT=YT[cur][:, HS[h]], rhs=Y[cur][:, HS[h]],
                                             start=True, stop=True), r=[("Y", cur), ("YT", cur)], w=[PK(6)])
            if m < 5:
                for h in range(4):
                    P.pe(lambda e, h=h: e.matmul(ps[7][:, HS[h]], lhsT=Y[cur][:, HS[h]], rhs=YT[cur][:, HS[h]],
                                                 start=True, stop=True), r=[("Y", cur), ("YT", cur)], w=[PK(7)])
            P.act(lambda e: e.activation(out=Y[nx][:], in_=ps[6][:, :], func=AF.Copy), r=[PK(6)], w=[("Y", nx)])
            if m < 5:
                P.dve(lambda e: e.tensor_copy(out=YT[nx][:], in_=ps[7][:, :]), r=[PK(7)], w=[("YT", nx)])
            for h in range(4):
                P.pe(lambda e, h=h: e.matmul(ps[5][:, HS[h]], lhsT=Y[nx][:, HS[h]], rhs=PT[cur][:, HS[h]],
                                             start=True, stop=True), r=[("Y", nx), ("PT", cur)], w=[PK(5)])
            P.dve(lambda e: e.tensor_tensor(out=PT[nx][:], in0=ps[5][:, :], in1=PT[cur][:], op=ALU.add),
                  r=[PK(5), ("PT", cur)], w=[("PT", nx)])
            st["cur"] = nx

        def M_():
            cur = st["cur"]
            PTf, ptk = PT[cur], ("PT", cur)
            pv = psb(6)
            for h in range(4):
                P.pe(lambda e, h=h: e.transpose(out=pv[:, HS[h]], in_=PTf[:, HS[h]], identity=idb[:]),
                     r=[ptk, "idb"], w=[PK(6)])
            P.act(lambda e: e.activation(out=Tbd[:], in_=pv[:, 0:512], func=AF.Copy), r=[PK(6)], w=["Tbd"])
            for h in range(4):
                P.pe(lambda e, h=h: e.matmul(ps[7][:, HS[h]], lhsT=Loff[:, HS[h]], rhs=PTf[:, HS[h]], start=True,
                                             stop=True), r=["Loff", ptk], w=[PK(7)])
            P.act(lambda e: e.activation(out=Xb[:], in_=ps[7][:, :], func=AF.Copy), r=[PK(7)], w=["Xb"])
            for h in range(4):
                P.pe(lambda e, h=h: e.matmul(ps[5][:, HS[h]], lhsT=Tbd[:, HS[h]], rhs=Xb[:, HS[h]], start=True,
                                             stop=True), r=["Tbd", "Xb"], w=[PK(5)])
            P.dve(lambda e: e.tensor_tensor(out=TTm[:], in0=PTf[:], in1=ps[5][:, :], op=ALU.subtract),
                  r=[PK(5), ptk], w=["TTm"])
            P.dve(lambda e: e.tensor_tensor(out=bg_m[:, 0:4], in0=btok[:, t, :], in1=egc, op=ALU.mult),
                  r=["btok", e3k], w=["bg_m"])
            for h in range(4):
                P.dve(lambda e, h=h: e.tensor_scalar(out=kbg[:, HS[h]], in0=ktok[:, t, HS[h]],
                                                     scalar1=bg_m[:, h:h + 1], scalar2=None, op0=ALU.mult),
                      r=["ktok", "bg_m"], w=["kbg"])
                P.dve(lambda e, h=h: e.tensor_scalar(out=kdec[:, HS[h]], in0=ktok[:, t, HS[h]],
                                                     scalar1=erem[:, h:h + 1], scalar2=None, op0=ALU.mult),
                      r=["ktok", e3k], w=[kdk])
            for h in range(4):
                P.pe(lambda e, h=h: e.matmul(ps[0][:, HS[h]], lhsT=TTm[:, HS[h]], rhs=vb[:, t, HS[h]], start=True,
                                             stop=True), r=["TTm", "vb"], w=[PK(0)])
            for h in range(4):
                P.pe(lambda e, h=h: e.matmul(ps[1][:, HS[h]], lhsT=kbg[:, HS[h]], rhs=TTm[:, HS[h]], start=True,
                                             stop=True), r=["TTm", "kbg"], w=[PK(1)])
            P.act(lambda e: e.activation(out=u_sb[:], in_=ps[0][:, :], func=AF.Copy), r=[PK(0)], w=[uk])
            P.act(lambda e: e.activation(out=wT[:], in_=ps[1][:, :], func=AF.Copy), r=[PK(1)], w=[wTk])

        def C1():
            for h in range(4):
                P.pe(lambda e, h=h: e.matmul(ps[2][:, HS[h]], lhsT=wT[:, HS[h]], rhs=Sb[:, HS[h]], start=True,
                                             stop=True), r=[wTk, "Sb"], w=[PK(2)])
            for h in range(4):
                P.pe(lambda e, h=h: e.matmul(ps[3][:, HS[h]], lhsT=qT[:, h, tk], rhs=Sb[:, HS[h]], start=True,
                                             stop=True), r=["qT", "Sb"], w=[PK(3)])
            P.dve(lambda e: e.tensor_tensor(out=vnew[:], in0=u_sb[:], in1=ps[2][:, :], op=ALU.subtract),
                  r=[uk, PK(2)], w=["vnew"])

        def C2():
            for h in range(4):
                P.pe(lambda e, h=h: e.matmul(ps[4][:, HS[h]], lhsT=qkT[:, HS[h]], rhs=vnew[:, HS[h]], start=True,
                                             stop=True), r=[qkk, "vnew"], w=[PK(4)])
            for h in range(4):
                P.pe(lambda e, h=h: e.matmul(ps[2][:, HS[h]], lhsT=kdec[:, HS[h]], rhs=vnew[:, HS[h]], start=True,
                                             stop=True), r=[kdk, "vnew"], w=[PK(2)])
            for h in range(4):
                P.act(lambda e, h=h: e.activation(out=qS_sb[:, HS[h]], in_=ps[3][:, HS[h]], func=AF.Copy,
                                                  scale=egc[:, h:h + 1]), r=[PK(3), e3k], w=["qS_sb"])

        def C3():
            for h in range(4):
                P.dve(lambda e, h=h: e.scalar_tensor_tensor(out=S[:, HS[h]], in0=S[:, HS[h]],
                                                            scalar=etot[:, h:h + 1], in1=ps[2][:, HS[h]],
                                                            op0=ALU.mult, op1=ALU.add),
                      r=["S", e3k, PK(2)], w=["S"])
            P.act(lambda e: e.activation(out=Sb[:], in_=S[:], func=AF.Copy), r=["S"], w=["Sb"])
            P.dve(lambda e: e.tensor_tensor(out=o_sb[:], in0=qS_sb[:], in1=ps[4][:, :], op=ALU.add),
                  r=["qS_sb", PK(4)], w=["o_sb"])

        def C4():
            for h in range(4):
                P.act(lambda e, h=h: e.activation(out=osq[:, HS[h]], in_=o_sb[:, HS[h]], func=AF.Square,
                                                  accum_out=bg_c[:, 4 + h:5 + h]), r=["o_sb"], w=["njunk", "bg_c"])
            P.dve(lambda e: e.tensor_scalar(out=bg_c[:, 4:8], in0=bg_c[:, 4:8], scalar1=1.0 / 128, scalar2=1e-6,
                                            op0=ALU.mult, op1=ALU.add), r=["bg_c"], w=["bg_c"])
            P.act(lambda e: e.activation(out=bg_c[:, 4:8], in_=bg_c[:, 4:8], func=AF.Ln), r=["bg_c"], w=["bg_c"])
            P.act(lambda e: e.activation(out=bg_c[:, 4:8], in_=bg_c[:, 4:8], func=AF.Exp, scale=-0.5), r=["bg_c"],
                  w=["bg_c"])

        def C5():
            for h in range(4):
                P.dve(lambda e, h=h: e.scalar_tensor_tensor(out=o_sb[:, HS[h]], in0=o_sb[:, HS[h]],
                                                            scalar=bg_c[:, 4 + h:5 + h], in1=dn4[:, HS[h]],
                                                            op0=ALU.mult, op1=ALU.mult),
                      r=["o_sb", "bg_c", "dn4"], w=["o_sb"])
            P.dve(lambda e: e.tensor_tensor(out=dtok[:], in0=o_sb[:], in1=zs[:, t, :], op=ALU.mult),
                  r=["o_sb", "zs"], w=["dtok"])
            pv = psb(3)
            for h in range(4):
                P.pe(lambda e, h=h: e.transpose(out=pv[:, HS[h]], in_=dtok[:, HS[h]], identity=idb[:]),
                     r=["dtok", "idb"], w=[PK(3)])
            P.act(lambda e: e.activation(out=dT[:, 0:4, t * 128:(t + 1) * 128],
                                         in_=pv[:, 0:512].rearrange("p (h n) -> p h n", h=4), func=AF.Copy),
                  r=[PK(3)], w=["dT"])

        return A_, N_, M_, [C1, C2, C3, C4, C5]

    tiles = [make_tile(t) for t in range(NT)]
    A0, N0, M0, _ = tiles[0]
    A0()
    for m in range(1, 6):
        N0(m)
    M0()
    for t in range(NT):
        Cs = tiles[t][3]
        if t + 1 < NT:
            An, Nn, Mn, _ = tiles[t + 1]
            An()
            for m in range(1, 6):
                Nn(m)
                Cs[m - 1]()
            Mn()
        else:
            for c in Cs:
                c()
    P.dma(V.ndel_p.rearrange("h d e -> d h e"), H4(S[:]), r=["S"], tag="out")

    stop_at(5)
    P.barrier()
    A.reset(V.m_w)
    sel4f = A.t([4, 512], F32, "sel4f")
    P.pool(lambda e: e.tensor_copy(out=sel4f[:].rearrange("p (s m) -> p s m", s=4),
                                   in_=cst[0:4, C_OH16:C_OH16 + 4].unsqueeze(2).to_broadcast([4, 4, 128])),
           r=["cst"], w=["sel4f"])
    sel4 = sel4f[0:4, :]
    egS = f32t("egS", (4, NS))
    P.act(lambda e: e.activation(out=egS[:], in_=gbS[:, 1, :], func=AF.Exp), r=["gbS"], w=["egS"])
    Bbc = f32t("Bbc", (128, 4, NS))
    EGbc = f32t("EGbc", (128, 4, NS))
    for h in range(4):
        P.pe(lambda e, h=h: e.matmul(ps[0][:, h * NS:(h + 1) * NS], lhsT=sel4[:, h * 128:(h + 1) * 128],
                                     rhs=gbS[:, 0, :], start=True, stop=True), r=["gbS", "sel4f"], w=[PK(0)])
        P.pe(lambda e, h=h: e.matmul(ps[1][:, h * NS:(h + 1) * NS], lhsT=sel4[:, h * 128:(h + 1) * 128],
                                     rhs=egS[:, :], start=True, stop=True), r=["egS", "sel4f"], w=[PK(1)])
    P.act(lambda e: e.activation(out=Bbc[:].rearrange("p h s -> p (h s)"), in_=ps[0][:, 0:64], func=AF.Copy),
          r=[PK(0)], w=["Bbc"])
    P.act(lambda e: e.activation(out=EGbc[:].rearrange("p h s -> p (h s)"), in_=ps[1][:, 0:64], func=AF.Copy),
          r=[PK(1)], w=["EGbc"])
    qkf = f32t("qkf", (128, 4, 2, NS))
    P.dve(lambda e: e.tensor_copy(out=qkf[:, :, 0, :], in_=kT[:, :, T:TT]), r=["kT"], w=["qkf"])
    P.dve(lambda e: e.tensor_copy(out=qkf[:, :, 1, :], in_=qT[:, :, T:TT]), r=["qT"], w=["qkf"])
    S0all = f32t("S0all", (128, NS, 4, 128))
    for s in range(NS):
        P.dma(S0all[:, s, :, :], V.sdel[s * 4:(s + 1) * 4, :, :].rearrange("h d e -> d h e"), w=[("S0", s)],
              tag="S0g%d" % (s // 4), group=True)
    for s in range(NS):
        for h in range(4):
            P.pe(lambda e, s=s, h=h: e.matmul(ps[2][:, (s * 4 + h) * 2:(s * 4 + h) * 2 + 2],
                                              lhsT=S0all[:, s, h, :], rhs=qkf[:, h, :, s], start=True,
                                              stop=True), r=[("S0", s), "qkf"], w=[PK(2)])
    kqS = f32t("kqS", (128, NS, 4, 2))
    P.act(lambda e: e.activation(out=kqS[:].rearrange("p s h t -> p (s h t)"), in_=ps[2][:, 0:128], func=AF.Copy),
          r=[PK(2)], w=["kqS"])
    kSv = kqS[:, :, :, 0].rearrange("p s h -> p h s")
    qSv = kqS[:, :, :, 1].rearrange("p s h -> p h s")
    vn = f32t("vn", (128, 4, NS))
    tmpS = f32t("tmpS", (128, 4, NS))
    P.dve(lambda e: e.tensor_tensor(out=tmpS[:], in0=EGbc[:], in1=kSv, op=ALU.mult), r=["EGbc", "kqS"], w=["tmpS"])
    P.dve(lambda e: e.tensor_tensor(out=vn[:], in0=vTs[:], in1=tmpS[:], op=ALU.subtract), r=["vTs", "tmpS"],
          w=["vn"])
    P.dve(lambda e: e.tensor_tensor(out=vn[:], in0=vn[:], in1=Bbc[:], op=ALU.mult), r=["vn", "Bbc"], w=["vn"])
    prodS = f32t("prodS", (128, 4, NS))
    P.dve(lambda e: e.tensor_tensor(out=prodS[:], in0=qkf[:, :, 0, :], in1=qkf[:, :, 1, :], op=ALU.mult),
          r=["qkf"], w=["prodS"])
    P.pe(lambda e: e.matmul(ps[3][:, 0:64], lhsT=onef, rhs=prodS[:].rearrange("p h s -> p (h s)"), start=True,
                            stop=True), r=["prodS", "cst"], w=[PK(3)])
    oS = f32t("oS", (128, 4, NS))
    P.dve(lambda e: e.tensor_tensor(out=oS[:].rearrange("p h s -> p (h s)"), in0=ps[3][:, 0:64],
                                    in1=vn[:].rearrange("p h s -> p (h s)"), op=ALU.mult), r=[PK(3), "vn"],
          w=["oS"])
    P.dve(lambda e: e.tensor_tensor(out=tmpS[:], in0=EGbc[:], in1=qSv, op=ALU.mult), r=["EGbc", "kqS"], w=["tmpS"])
    P.dve(lambda e: e.tensor_tensor(out=oS[:], in0=oS[:], in1=tmpS[:], op=ALU.add), r=["oS", "tmpS"], w=["oS"])
    P.dve(lambda e: e.tensor_tensor(out=tmpS[:], in0=oS[:], in1=oS[:], op=ALU.mult), r=["oS"], w=["tmpS"])
    P.pe(lambda e: e.matmul(ps[4][:, 0:64], lhsT=onef, rhs=tmpS[:].rearrange("p h s -> p (h s)"), start=True,
                            stop=True), r=["tmpS", "cst"], w=[PK(4)])
    rS = f32t("rS", (128, 64))
    P.dve(lambda e: e.tensor_scalar(out=rS[:], in0=ps[4][:, 0:64], scalar1=1.0 / 128, scalar2=1e-6, op0=ALU.mult,
                                    op1=ALU.add), r=[PK(4)], w=["rS"])
    P.act(lambda e: e.activation(out=rS[:], in_=rS[:], func=AF.Sqrt), r=["rS"], w=["rS"])
    P.dve(lambda e: e.reciprocal(out=rS[:], in_=rS[:]), r=["rS"], w=["rS"])
    P.dve(lambda e: e.tensor_tensor(out=oS[:].rearrange("p h s -> p (h s)"), in0=oS[:].rearrange("p h s -> p (h s)"),
                                    in1=rS[:], op=ALU.mult), r=["oS", "rS"], w=["oS"])
    P.dve(lambda e: e.scalar_tensor_tensor(out=dT[:, 0:4, T:TT], in0=oS[:], scalar=pc[:, PC_DN:PC_DN + 1],
                                           in1=zsT[:], op0=ALU.mult, op1=ALU.mult), r=["oS", "pc", "zsT"],
          w=["dT"])
    ktS = f32t("ktS", (16, 512))
    vtS = f32t("vtS", (16, 512))
    for h in range(4):
        P.pe(lambda e, h=h: e.transpose(out=ps[5][0:NS, h * 128:(h + 1) * 128], in_=qkf[:, h, 0, :],
                                        identity=ident_f), r=["qkf", "cst"], w=[PK(5)])
        P.pe(lambda e, h=h: e.transpose(out=ps[6][0:NS, h * 128:(h + 1) * 128], in_=vn[:, h, :], identity=ident_f),
             r=["vn", "cst"], w=[PK(6)])
    P.act(lambda e: e.activation(out=ktS[:], in_=ps[5][0:NS, :], func=AF.Copy), r=[PK(5)], w=["ktS"])
    P.act(lambda e: e.activation(out=vtS[:], in_=ps[6][0:NS, :], func=AF.Copy), r=[PK(6)], w=["vtS"])
    vmask = [f32t("vmask%d" % i, (16, 512)) for i in range(2)]
    oh16 = cst[0:16, C_OH16:C_OH16 + 16]
    for s in range(NS):
        sl = s % 2
        P.dve(lambda e, s=s, sl=sl: e.tensor_scalar(out=vmask[sl][:], in0=vtS[:], scalar1=oh16[:, s:s + 1],
                                                    scalar2=None, op0=ALU.mult), r=["vtS", "cst"],
              w=[("vmask", sl)])
        b = 7 if sl else 0
        for h in range(4):
            hs = slice(h * 128, (h + 1) * 128)
            P.pe(lambda e, hs=hs, sl=sl, b=b: e.matmul(ps[b][:, hs], lhsT=ktS[:, hs], rhs=vmask[sl][:, hs],
                                                       start=True, stop=True), r=["ktS", ("vmask", sl)], w=[PK(b)])
        for h in range(4):
            hs = slice(h * 128, (h + 1) * 128)
            P.dve(lambda e, hs=hs, h=h, s=s, b=b: e.scalar_tensor_tensor(
                out=S0all[:, s, h, :], in0=S0all[:, s, h, :], scalar=EGbc[:, h, s:s + 1], in1=ps[b][:, hs],
                op0=ALU.mult, op1=ALU.add), r=["EGbc", PK(b)], w=[("S0", s)])
        P.dma(V.ndel_s[s * 4:(s + 1) * 4, :, :].rearrange("h d e -> d h e"), S0all[:, s, :, :], r=[("S0", s)],
              tag="out")

    stop_at(6)
    P.barrier()
    mX, mXe = V.mX, V.mXe
    x1 = A.child(mX, mXe).t([128, NT + 1, 1024], F32, "x1")
    A.reset(mXe)
    wo = A.t([128, 8, 512], BF16, "wo_a")
    wo2 = A.t([128, 8, 512], BF16, "wo_b")
    xin = [A.t([128, 1024], F32, "xin2_%d" % i) for i in range(2)]
    load_w(wo, "wo", V.w_out, 0, 8, 0, 512)
    load_w(wo2, "wo2", V.w_out, 0, 8, 512, 512)
    for t in range(NT + 1):
        n = 128 if t < NT else NS
        c0 = t * 128
        s = t % 2
        src_d = V.x_p[t * 128:(t + 1) * 128, :] if t < NT else V.x_s[:, :]
        P.dma(xin[s][:n, :], src_d, w=[("xin", s)], tag="xin%d" % s)
        for half, wv in ((0, wo), (1, wo2)):
            b = (t % 2) * 2 + half
            for kc in range(8):
                src = cT[:, kc, c0:c0 + n] if kc < 4 else dT[:, kc - 4, c0:c0 + n]
                P.pe(lambda e, b=b, kc=kc, src=src, wv=wv, n=n: e.matmul(ps[b][:n, :], lhsT=src, rhs=wv[:, kc, :],
                                                                        start=(kc == 0), stop=(kc == 7)),
                     r=["cT", "dT", "wo", "wo2"], w=[PK(b)])
            P.dve(lambda e, b=b, t=t, n=n, half=half, s=s: e.tensor_tensor(
                out=x1[:n, t, half * 512:(half + 1) * 512], in0=ps[b][:n, :],
                in1=xin[s][:n, half * 512:(half + 1) * 512], op=ALU.add), r=[PK(b), ("xin", s)], w=[("x1", t)])

    stop_at(7)
    P.barrier()
    A.reset(mXe)
    AH = A.child(V.off_cT, mX)
    kmT = AH.t([128, 8, 256], BF16, "kmT")
    vmem = AH.t([128, 2, 1024], BF16, "vmem")
    mT = AH.t([128, 8, 256], BF16, "mT")
    hT3 = hT
    wk2 = [A.t([128, 8, 512], BF16, "wk%d" % i) for i in range(2)]
    mo_f = A.t([128, 1024], F32, "mo_f")
    min_ = [A.t([128, 1024], F32, "min%d" % i) for i in range(2)]
    load_gam(PR_GKV)
    items = []
    for mt in range(2):
        pre = (lambda mt=mt: P.dma(min_[mt][:], V.mem[mt * 128:(mt + 1) * 128, :], w=[("min", mt)],
                                   tag="min%d" % mt))
        items.append((min_[mt][:, :], [("min", mt)], 128, mt * 128, pre))
    V.norm_pass(items, mT, "mT")
    kvl = [(0, V.w_mk, V.mk_p, 0), (0, V.w_mk, V.mk_p, 1), (1, V.w_mv, V.mv_p, 0), (1, V.w_mv, V.mv_p, 1)]
    load_w(wk2[0], ("wk", 0), V.w_mk, 0, 8, 0, 512)
    for li, (which, wd, od, half) in enumerate(kvl):
        if True:
            wk = wk2[li % 2]
            wkk = ("wk", li % 2)
            if li + 1 < 4:
                load_w(wk2[(li + 1) % 2], ("wk", (li + 1) % 2), kvl[li + 1][1], 0, 8, kvl[li + 1][3] * 512, 512)
            for mt in range(2):
                b = 2 + mt
                for kc in range(8):
                    P.pe(lambda e, b=b, kc=kc, mt=mt: e.matmul(ps[b][:, :], lhsT=mT[:, kc, mt * 128:(mt + 1) * 128],
                                                               rhs=wk[:, kc, :], start=(kc == 0), stop=(kc == 7)),
                         r=[("mT", 0), wkk], w=[PK(b)])
                P.act(lambda e, b=b, half=half: e.activation(out=mo_f[:, half * 512:(half + 1) * 512],
                                                             in_=ps[b][:, :], func=AF.Copy), r=[PK(b)], w=["mo_f"])
                P.dma(od[mt * 128:(mt + 1) * 128, half * 512:(half + 1) * 512],
                      mo_f[:, half * 512:(half + 1) * 512], r=["mo_f"], tag="omo")
                if which == 1:
                    P.dve(lambda e, b=b, half=half, mt=mt: e.tensor_copy(
                        out=vmem[:, mt, half * 512:(half + 1) * 512], in_=ps[b][:, :]), r=[PK(b)], w=["vmem"])
            if which == 0:
                for c4 in range(4):
                    b = 4 + c4 % 2
                    for kc in range(8):
                        P.pe(lambda e, b=b, kc=kc, c4=c4: e.matmul(ps[b][:, 0:256],
                                                                   lhsT=wk[:, kc, c4 * 128:(c4 + 1) * 128],
                                                                   rhs=mT[:, kc, :], start=(kc == 0), stop=(kc == 7)),
                             r=[("mT", 0), wkk], w=[PK(b)])
                    P.act(lambda e, b=b, c4=c4, half=half: e.activation(out=kmT[:, half * 4 + c4, :],
                                                                        in_=ps[b][:, 0:256], func=AF.Copy),
                          r=[PK(b)], w=["kmT"])
    stop_at(8)
    P.barrier()
    A.reset(mXe)
    qa = A.t([128, 8, TT], BF16, "qa")
    m_qa = A.mark()
    wq = A.t([128, 8, 512], BF16, "wq")
    wq2 = A.t([128, 8, 512], BF16, "wq2")
    load_gam(PR_GMQ)
    V.norm_pass([(x1[:(128 if t < NT else NS), t, :], [("x1", t)], (128 if t < NT else NS), t * 128, None)
                 for t in range(NT + 1)], hT3, "hT")
    load_w(wq, "wq", V.w_mq, 0, 8, 0, 512)
    load_w(wq2, "wq2", V.w_mq, 0, 8, 512, 512)
    BLK = [(i * 512, 512) for i in range(4)] + [(T, NS)]
    qi = 0
    for c in range(8):
        wv = wq if c < 4 else wq2
        cc = c % 4
        for (t0, n) in BLK:
            b = qi % 2
            qi += 1
            mm8(b, lambda kc: wv[:, kc, cc * 128:(cc + 1) * 128], lambda kc: hT3[:, kc, t0:t0 + n], n,
                [("hT", t0 // 512), "wq", "wq2"])
            P.act(lambda e, b=b, c=c, n=n, t0=t0: e.activation(out=qa[:, c, t0:t0 + n], in_=ps[b][:, 0:n],
                                                               func=AF.Copy, scale=1.0 / 16), r=[PK(b)], w=["qa"])
    stop_at(9)
    P.barrier()
    A.reset(m_qa)
    aoT = hT
    qs_s = AH.t([128, 8, NS], BF16, "qs_s")
    P.dve(lambda e: e.tensor_copy(out=qs_s[:], in_=qa[:, :, T:TT]), r=["qa"], w=["qs_s"])
    m_3c = A.mark()
    msm = [A.t([128, 16], F32, "msm%d" % i) for i in range(2)]
    ex = [A.t([128, 1024], F32, "ex%d" % i) for i in range(2)]
    pbf = [A.t([128, 1024], BF16, "pbf%d" % i) for i in range(2)]
    ptb = [A.t([128, 8, 128], BF16, "ptb%d" % i) for i in range(2)]
    iters = []
    for tg in range(NT):
        def S0(tg=tg):
            p = tg % 2
            lt = slice(tg * 128, (tg + 1) * 128)
            sb = (2 * p, 2 * p + 1)
            m_, mk_ = msm[p], ("msm", p)
            for h in range(4):
                b = sb[h // 2]
                cs = slice((h % 2) * 256, (h % 2) * 256 + 256)
                for hf in range(2):
                    P.pe(lambda e, b=b, cs=cs, h=h, hf=hf: e.matmul(ps[b][:, cs], lhsT=qa[:, h * 2 + hf, lt],
                                                                    rhs=kmT[:, h * 2 + hf, :], start=(hf == 0),
                                                                    stop=(hf == 1)), r=["qa", "kmT"], w=[PK(b)])
            for j in range(2):
                P.dve(lambda e, j=j: e.tensor_reduce(out=m_[:, 12 + 2 * j:12 + 2 * j + 2],
                                                     in_=ps[sb[j]][:, :].rearrange("p (h m) -> p h m", h=2),
                                                     axis=AX.X, op=ALU.max, negate=True), r=[PK(sb[j])], w=[mk_])

        def S1(tg=tg):
            p = tg % 2
            sb = (2 * p, 2 * p + 1)
            m_, mk_ = msm[p], ("msm", p)
            ex_, exk = ex[p], ("ex", p)
            pb_, pbk = pbf[p], ("pbf", p)
            for h in range(4):
                P.act(lambda e, h=h: e.activation(out=ex_[:, h * 256:(h + 1) * 256],
                                                  in_=ps[sb[h // 2]][:, (h % 2) * 256:(h % 2) * 256 + 256],
                                                  func=AF.Exp, bias=m_[:, 12 + h:13 + h],
                                                  accum_out=m_[:, 4 + h:5 + h]),
                      r=[PK(sb[h // 2]), mk_], w=[exk, mk_])
            P.dve(lambda e: e.reciprocal(out=m_[:, 8:12], in_=m_[:, 4:8]), r=[mk_], w=[mk_])
            for h in range(4):
                P.dve(lambda e, h=h: e.tensor_scalar(out=pb_[:, h * 256:(h + 1) * 256],
                                                     in0=ex_[:, h * 256:(h + 1) * 256], scalar1=m_[:, 8 + h:9 + h],
                                                     scalar2=None, op0=ALU.mult), r=[exk, mk_], w=[pbk])

        def S2(tg=tg):
            p = tg % 2
            pb_, pbk = pbf[p], ("pbf", p)
            pt_, ptk = ptb[p], ("ptb", p)
            pv = psb(4)
            for c8 in range(8):
                P.pe(lambda e, c8=c8: e.transpose(out=pv[:, c8 * 128:(c8 + 1) * 128],
                                                  in_=pb_[:, c8 * 128:(c8 + 1) * 128], identity=idb[:]),
                     r=[pbk, "idb"], w=[PK(4)])
            P.act(lambda e: e.activation(out=pt_[:].rearrange("p a b -> p (a b)"), in_=pv[:, 0:1024], func=AF.Copy),
                  r=[PK(4)], w=[ptk])

        def S3(tg=tg):
            p = tg % 2
            lt = slice(tg * 128, (tg + 1) * 128)
            pt_, ptk = ptb[p], ("ptb", p)
            for h in range(4):
                ob = 5 + h // 2
                for hf in range(2):
                    cs = slice(((h % 2) * 2 + hf) * 128, ((h % 2) * 2 + hf + 1) * 128)
                    for mt in range(2):
                        P.pe(lambda e, ob=ob, cs=cs, h=h, hf=hf, mt=mt: e.matmul(
                            ps[ob][:, cs], lhsT=vmem[:, mt, h * 256 + hf * 128:h * 256 + (hf + 1) * 128],
                            rhs=pt_[:, h * 2 + mt, :], start=(mt == 0), stop=(mt == 1)), r=["vmem", ptk],
                            w=[PK(ob)])
            P.act(lambda e: e.activation(out=aoT[:, 0:4, lt], in_=ps[5][:, :].rearrange("p (a n) -> p a n", a=4),
                                         func=AF.Copy), r=[PK(5)], w=["aoT"])
            P.dve(lambda e: e.tensor_copy(out=aoT[:, 4:8, lt], in_=ps[6][:, :].rearrange("p (a n) -> p a n", a=4)),
                  r=[PK(6)], w=["aoT"])

        iters.append([S0, S1, S2, S3])
    V.run_skewed(iters)
    P.barrier()
    A.reset(mXe)
    prod = A.t([128, 1024], F32, "prod")
    qtok = A.t([16, 1024], BF16, "qtok")
    sel16b = A.t([16, 2048], BF16, "sel16b")
    P.pool(lambda e: e.tensor_copy(out=sel16b[:].rearrange("p (s m) -> p s m", s=16),
                                   in_=cst[0:16, C_OH16:C_OH16 + 16].unsqueeze(2).to_broadcast([16, 16, 128])),
           r=["cst"], w=["sel16b"])
    NKS, NVS = 8, 6
    Kt2 = [A.t([128, 1024], BF16, "Kt%d" % i) for i in range(NKS)]
    Vt2 = [A.t([128, 2, 1024], BF16, "Vt%d" % i) for i in range(NVS)]
    scS = A.t([128, 2, 4], F32, "scS")
    sm4 = A.t([4, 256], F32, "sm4")
    sm4s = A.t([4, 8], F32, "sm4s")
    pS = A.t([128, 2, 4], BF16, "pS")
    aoS = A.t([128, 8, NS], F32, "aoS")
    pv = psb(2)
    for c in range(8):
        P.pe(lambda e, c=c: e.transpose(out=pv[0:NS, c * 128:(c + 1) * 128], in_=qs_s[:, c, :],
                                        identity=idb[:]), r=["qs_s", "idb"], w=[PK(2)])
    P.act(lambda e: e.activation(out=qtok[:], in_=pv[0:NS, 0:1024], func=AF.Copy), r=[PK(2)], w=["qtok"])
    Vt3 = Vt2
    scS2 = [scS, A.t([128, 2, 4], F32, "scSb")]
    sm42 = [sm4, A.t([4, 256], F32, "sm4b")]
    sm4s2 = [sm4s, A.t([4, 8], F32, "sm4sb")]
    iters = []
    for s in range(NS):
        def S0(s=s):
            Vt = Vt3[s % NVS]
            vk = ("Vt", s % NVS)
            sc_, sck = scS2[s % 2], ("scS", s % 2)
            for mt in range(2):
                P.dma(Vt[:, mt, :], V.cmv[s, mt * 128:(mt + 1) * 128, :], w=[vk], tag="Vt%d" % (s % NVS), q="pool")
            qb = (3, 4) if s % 2 == 0 else (0, 1)
            for half in range(2):
                P.pe(lambda e, half=half: e.matmul(ps[qb[half]][:, :], lhsT=sel16b[:, s * 128:(s + 1) * 128],
                                                   rhs=qtok[:, half * 512:(half + 1) * 512], start=True, stop=True),
                     r=["qtok", "sel16b"], w=[PK(qb[half])])
            for mt in range(2):
                kti = s * 2 + mt
                Kt = Kt2[kti % NKS]
                kk = ("Kt", kti % NKS)
                P.dma(Kt[:, :], V.cmk[s, mt * 128:(mt + 1) * 128, :], w=[kk], tag="Kt%d" % (kti % NKS), q="pool")
                for half in range(2):
                    P.dve(lambda e, half=half, Kt=Kt: e.tensor_tensor(out=prod[:, half * 512:(half + 1) * 512],
                                                                      in0=ps[qb[half]][:, :],
                                                                      in1=Kt[:, half * 512:(half + 1) * 512],
                                                                      op=ALU.mult),
                          r=[PK(qb[half]), kk], w=["prod"])
                P.dve(lambda e, mt=mt: e.tensor_reduce(out=sc_[:, mt, :],
                                                       in_=prod[:].rearrange("p (h d) -> p h d", h=4),
                                                       axis=AX.X, op=ALU.add), r=["prod"], w=[sck])

        def S1(s=s):
            sc_, sck = scS2[s % 2], ("scS", s % 2)
            m4, m4k = sm42[s % 2], ("sm4", s % 2)
            m4s, m4sk = sm4s2[s % 2], ("sm4s", s % 2)
            for mt in range(2):
                P.pe(lambda e, mt=mt: e.transpose(out=ps[5][0:4, mt * 128:(mt + 1) * 128], in_=sc_[:, mt, :],
                                                  identity=ident_f), r=[sck, "cst"], w=[PK(5)])
            P.dve(lambda e: e.tensor_reduce(out=m4s[:, 0:1], in_=ps[5][0:4, 0:256], axis=AX.X, op=ALU.max),
                  r=[PK(5)], w=[m4sk])
            P.dve(lambda e: e.tensor_scalar(out=m4s[:, 1:2], in0=m4s[:, 0:1], scalar1=-1.0, scalar2=None,
                                            op0=ALU.mult), r=[m4sk], w=[m4sk])
            P.act(lambda e: e.activation(out=m4[:], in_=ps[5][0:4, 0:256], func=AF.Exp, bias=m4s[:, 1:2],
                                         accum_out=m4s[:, 2:3]), r=[PK(5), m4sk], w=[m4k, m4sk])
            P.dve(lambda e: e.reciprocal(out=m4s[:, 3:4], in_=m4s[:, 2:3]), r=[m4sk], w=[m4sk])
            P.dve(lambda e: e.tensor_scalar(out=m4[:], in0=m4[:], scalar1=m4s[:, 3:4], scalar2=None, op0=ALU.mult),
                  r=[m4k, m4sk], w=[m4k])

        def S2(s=s):
            Vt = Vt3[s % NVS]
            vk = ("Vt", s % NVS)
            m4, m4k = sm42[s % 2], ("sm4", s % 2)
            for mt in range(2):
                P.pe(lambda e, mt=mt: e.transpose(out=ps[6][:, mt * 4:(mt + 1) * 4],
                                                  in_=m4[:, mt * 128:(mt + 1) * 128], identity=ident_f[0:4, 0:4]),
                     r=[m4k, "cst"], w=[PK(6)])
            P.act(lambda e: e.activation(out=pS[:].rearrange("p a h -> p (a h)"), in_=ps[6][:, 0:8], func=AF.Copy),
                  r=[PK(6)], w=["pS"])
            for h in range(4):
                for hf in range(2):
                    c = h * 2 + hf
                    for mt in range(2):
                        P.pe(lambda e, c=c, mt=mt, h=h: e.matmul(ps[7][:, c:c + 1],
                                                                 lhsT=Vt[:, mt, c * 128:(c + 1) * 128],
                                                                 rhs=pS[:, mt, h:h + 1], start=(mt == 0),
                                                                 stop=(mt == 1)),
                             r=[vk, "pS"], w=[PK(7)])
            P.act(lambda e: e.activation(out=aoS[:, :, s], in_=ps[7][:, 0:8], func=AF.Copy), r=[PK(7)], w=["aoS"])

        iters.append([S0, S1, S2])
    V.run_skewed(iters)
    P.dve(lambda e: e.tensor_copy(out=aoT[:, :, T:TT], in_=aoS[:]), r=["aoS"], w=["aoT"])
    P.barrier()
    A.reset(mXe)
    wmo = A.t([128, 8, 512], BF16, "wmo")
    wmo2 = A.t([128, 8, 512], BF16, "wmo2")
    load_w(wmo, "wmo", V.w_mo, 0, 8, 0, 512)
    load_w(wmo2, "wmo2", V.w_mo, 0, 8, 512, 512)
    for t in range(NT + 1):
        n = 128 if t < NT else NS
        c0 = t * 128
        for half, wv in ((0, wmo), (1, wmo2)):
            b = (t % 2) * 2 + half
            for kc in range(8):
                P.pe(lambda e, b=b, kc=kc, wv=wv, n=n, c0=c0: e.matmul(ps[b][:n, :], lhsT=aoT[:, kc, c0:c0 + n],
                                                                      rhs=wv[:, kc, :], start=(kc == 0),
                                                                      stop=(kc == 7)), r=["aoT", "wmo", "wmo2"],
                     w=[PK(b)])
            P.dve(lambda e, b=b, t=t, n=n, half=half: e.tensor_tensor(
                out=x1[:n, t, half * 512:(half + 1) * 512], in0=ps[b][:n, :],
                in1=x1[:n, t, half * 512:(half + 1) * 512], op=ALU.add), r=[PK(b), ("x1", t)], w=[("x1", t)])

    stop_at(11)
    P.barrier()
    A.reset(mXe)
    TH = 1024
    AHh = A.child(V.off_hT, V.off_cT)
    hT4 = AHh.t([128, 8, TH + NS], BF16, "hT4")
    wg = [AHh.t([128, 8, 256], BF16, "wg%d" % i) for i in range(2)]
    wu = [AHh.t([128, 8, 256], BF16, "wu%d" % i) for i in range(2)]
    AHc = A.child(V.off_cT, mX)
    gF = AHc.t([128, 1024], F32, "gF")
    sgf = [AHc.t([128, 512], F32, "sgf%d" % i) for i in range(2)]
    fs = AHc.t([128, 8], F32, "fs")
    aT = A.t([128, 11, TH + NS], BF16, "aT")
    wdn = [A.t([128, 11, 512], BF16, "wdn%d" % i) for i in range(2)]
    fj = V.nrm_junk
    P.dma(gF[:], pr_d[:, PR_GF:PR_GF + 1024], w=["gF"], tag="c4")
    load_gam(PR_GFFN)
    gu_list = [(fh, gi) for _th in range(2) for fh in range(2) for gi in range(6)]

    def gu_load(idx):
        fh, gi = gu_list[idx]
        ncg = 256 if gi < 5 else 128
        c0 = fh * 1408 + gi * 256
        sl = idx % 2
        load_w(wg[sl], ("wg", sl), V.w_gate, 0, 8, c0, ncg)
        load_w(wu[sl], ("wu", sl), V.w_up, 0, 8, c0, ncg)

    fsl = [AHc.t([128, 8], F32, "fs%d" % i) for i in range(2)]
    fcnt = [0]

    def final_norm(t):
        n = 128 if t < NT else NS
        i_ = fcnt[0] % 2
        fcnt[0] += 1
        fs_ = fsl[i_]
        fk = ("fs", i_)
        P.act(lambda e: e.activation(out=fj[:n, :], in_=x1[:n, t, :], func=AF.Square, accum_out=fs_[:n, 0:1]),
              r=[("x1", t)], w=["njunk", fk])
        P.dve(lambda e: e.tensor_scalar(out=fs_[:n, 1:2], in0=fs_[:n, 0:1], scalar1=1.0 / 1024, scalar2=1e-6,
                                        op0=ALU.mult, op1=ALU.add), r=[fk], w=[fk])
        P.act(lambda e: e.activation(out=fs_[:n, 2:3], in_=fs_[:n, 1:2], func=AF.Sqrt), r=[fk], w=[fk])
        P.dve(lambda e: e.reciprocal(out=fs_[:n, 3:4], in_=fs_[:n, 2:3]), r=[fk], w=[fk])
        P.dve(lambda e: e.scalar_tensor_tensor(out=x1[:n, t, :], in0=x1[:n, t, :], scalar=fs_[:n, 3:4],
                                               in1=gF[:n, :], op0=ALU.mult, op1=ALU.mult),
              r=[("x1", t), fk, "gF"], w=[("x1", t)])
        dst_ = V.y_p[t * 128:(t + 1) * 128, :] if t < NT else V.y_s[:, :]
        P.dma(dst_, x1[:n, t, :], r=[("x1", t)], tag="out")

    gu_load(0)
    gidx = 0
    for th in range(2):
        tiles = list(range(th * 8, th * 8 + 8)) + ([NT] if th == 1 else [])
        V.norm_pass([(x1[:(128 if t < NT else NS), t, :], [("x1", t)], (128 if t < NT else NS), ti * 128, None)
                     for ti, t in enumerate(tiles)], hT4, "hT4")
        blks = [(0, 512), (512, 512)] + ([(1024, NS)] if th == 1 else [])
        for fh in range(2):
            for gi in range(6):
                cur = gidx
                gidx += 1
                if cur + 1 < len(gu_list):
                    gu_load(cur + 1)
                if gi == 2 or gi == 4:
                    hf_ = 0 if gi == 2 else 1
                    load_w(wdn[hf_], ("wdn", hf_), V.w_down, fh * 1408, 11, hf_ * 512, 512)
                ncg = 256 if gi < 5 else 128
                sl = cur % 2
                for cc in range(ncg // 128):
                    fc = gi * 2 + cc
                    for bi, (t0, n) in enumerate(blks):
                        bg, bu = (bi % 2) * 2, (bi % 2) * 2 + 1
                        mm8(bg, lambda kc: wg[sl][:, kc, cc * 128:(cc + 1) * 128],
                            lambda kc: hT4[:, kc, t0:t0 + n], n, [("hT4", t0 // 512), ("wg", sl)])
                        mm8(bu, lambda kc: wu[sl][:, kc, cc * 128:(cc + 1) * 128],
                            lambda kc: hT4[:, kc, t0:t0 + n], n, [("hT4", t0 // 512), ("wu", sl)])
                        sg = sgf[bi % 2]
                        P.act(lambda e, sg=sg, bg=bg, n=n: e.activation(out=sg[:, 0:n], in_=ps[bg][:, 0:n],
                                                                        func=AF.Silu), r=[PK(bg)],
                              w=[("sgf", bi % 2)])
                        P.dve(lambda e, sg=sg, bu=bu, n=n, fc=fc, t0=t0: e.tensor_tensor(
                            out=aT[:, fc, t0:t0 + n], in0=ps[bu][:, 0:n], in1=sg[:, 0:n], op=ALU.mult),
                            r=[PK(bu), ("sgf", bi % 2)], w=["aT"])
            for half in range(2):
                wslot = half
                wdk = ("wdn", wslot)
                for ti, t in enumerate(tiles):
                    n = 128 if t < NT else NS
                    b = 4 + (ti % 4)
                    for k in range(11):
                        P.pe(lambda e, b=b, k=k, ti=ti, n=n, wslot=wslot: e.matmul(
                            ps[b][:n, :], lhsT=aT[:, k, ti * 128:ti * 128 + n], rhs=wdn[wslot][:, k, :],
                            start=(k == 0), stop=(k == 10)), r=["aT", wdk], w=[PK(b)])
                    P.dve(lambda e, b=b, t=t, n=n, half=half: e.tensor_tensor(
                        out=x1[:n, t, half * 512:(half + 1) * 512], in0=ps[b][:n, :],
                        in1=x1[:n, t, half * 512:(half + 1) * 512], op=ALU.add), r=[PK(b), ("x1", t)],
                        w=[("x1", t)])
                    if fh == 1 and half == 1:
                        final_norm(t)
        continue
        for t in tiles:
            n = 128 if t < NT else NS
            P.act(lambda e, t=t, n=n: e.activation(out=fj[:n, :], in_=x1[:n, t, :], func=AF.Square,
                                                   accum_out=fs[:n, 0:1]), r=[("x1", t)], w=["njunk", "fs"])
            P.dve(lambda e, n=n: e.tensor_scalar(out=fs[:n, 1:2], in0=fs[:n, 0:1], scalar1=1.0 / 1024, scalar2=1e-6,
                                                 op0=ALU.mult, op1=ALU.add), r=["fs"], w=["fs"])
            P.act(lambda e, n=n: e.activation(out=fs[:n, 2:3], in_=fs[:n, 1:2], func=AF.Sqrt), r=["fs"], w=["fs"])
            P.dve(lambda e, n=n: e.reciprocal(out=fs[:n, 3:4], in_=fs[:n, 2:3]), r=["fs"], w=["fs"])
            P.dve(lambda e, t=t, n=n: e.scalar_tensor_tensor(out=x1[:n, t, :], in0=x1[:n, t, :], scalar=fs[:n, 3:4],
                                                             in1=gF[:n, :], op0=ALU.mult, op1=ALU.add if False else ALU.mult),
                  r=[("x1", t), "fs", "gF"], w=[("x1", t)])
            dst = V.y_p[t * 128:(t + 1) * 128, :] if t < NT else V.y_s[:, :]
            P.dma(dst, x1[:n, t, :], r=[("x1", t)], tag="out")

    P.emit(["out"])


_CACHE = {}


def kernel(**inputs):
    inp = {k: np.asarray(v) for k, v in inputs.items()}
    if "nc" not in _CACHE:
        _CACHE["nc"] = build_nc()[0]
    nc = _CACHE["nc"]
    cst = make_consts()
    pc, pr = make_params(inp)
    f = lambda a: np.ascontiguousarray(a, dtype=np.float32)
    shared = {
        "w_in": f(inp["w_in"][0]), "w_out": f(inp["w_out"][0]), "w_mq": f(inp["w_mq"][0]), "w_mk": f(inp["w_mk"][0]),
        "w_mv": f(inp["w_mv"][0]), "w_mo": f(inp["w_mo"][0]), "w_gate": f(inp["w_gate"][0]),
        "w_up": f(inp["w_up"][0]), "w_down": f(inp["w_down"][0]), "cst": cst, "pc": pc, "pr": pr,
    }
    in_maps = []
    for c in range(8):
        sl = slice(c * NS, (c + 1) * NS)
        m = dict(shared)
        m["x_p"] = f(inp["x_prompt"][c])
        m["x_s"] = f(inp["x_sample"][sl, 0, :])
        m["mem"] = f(inp["mem_prompt"][c])
        m["cconv"] = f(inp["cache_conv"][0, sl].reshape(NS * 30, 512))
        m["ssc"] = f(inp["state_short_conv"][0, sl].reshape(NS * 3, 1536))
        m["sdel"] = f(inp["state_delta"][0, sl].reshape(NS * 4, 128, 128))
        m["cmk"] = f(inp["cache_mem_k"][0, sl].reshape(NS, 256, 1024))
        m["cmv"] = f(inp["cache_mem_v"][0, sl].reshape(NS, 256, 1024))
        in_maps.append(m)
    res = run_bass_kernel_spmd(nc, in_maps, core_ids=list(range(8)))
    R = res.results
    cat = lambda k: np.stack([np.asarray(R[c][k]) for c in range(8)])
    y_p = cat("y_p")
    y_s = cat("y_s").reshape(128, 1, D)
    nconv_p = cat("nconv_p")[None]
    nsc_p = cat("nsc_p")[None]
    ndel_p = cat("ndel_p")[None]
    mk = cat("mk_p").reshape(1, 8, 256, 4, 256)
    mv = cat("mv_p").reshape(1, 8, 256, 4, 256)
    nconv_s = cat("nconv_s").reshape(1, 128, 30, 512)
    nsc_s = cat("nsc_s").reshape(1, 128, 3, 1536)
    ndel_s = cat("ndel_s").reshape(1, 128, 4, 128, 128)
    return tuple(np.ascontiguousarray(a, dtype=np.float32) for a in
                 (y_p, y_s, nconv_p, nsc_p, ndel_p, mk, mv, nconv_s, nsc_s, ndel_s))
```

```python
import numpy as np
import concourse.bass as bass
import concourse.mybir as mybir
from concourse.bass_utils import run_bass_kernel_spmd

F32 = mybir.dt.float32
BF16 = mybir.dt.bfloat16
AF = mybir.ActivationFunctionType
ALU = mybir.AluOpType
AX = mybir.AxisListType

T = 2048
NS = 16
TT = T + NS
NT = 16
D = 1024
DFF = 2816
NFC = 22


class Op:
    __slots__ = ("eng", "fn", "deps", "sig", "need", "dma", "tag", "cnt", "idx")


import os
STOP = float(os.environ.get("KSTOP", "99"))


class _Stop(Exception):
    pass


def stop_at(n):
    if STOP == n:
        raise _Stop()


class _Rec:
    def __init__(self):
        self.call = None

    def __getattr__(self, name):
        def f(*a, **k):
            self.call = (name, a, k)
            return self
        return f


class Prog:
    ENGS = ("sp", "act", "pool", "dve", "pe")

    def __init__(self, nc):
        self.nc = nc
        self.ops = {e: [] for e in self.ENGS}
        self.all = []
        self.lastw = {}
        self.readers = {}
        self.tagcnt = {}
        self.taggroup = {}
        self.base = []

    def _add(self, eng, fn, r, w, dma=False, tag=None, group=False):
        op = Op()
        rec = _Rec()
        fn(rec)
        name_, a_, k_ = rec.call
        fn = lambda e, name_=name_, a_=a_, k_=k_: getattr(e, name_)(*a_, **k_)
        op.eng, op.fn, op.dma, op.tag = eng, fn, dma, tag
        op.need = False
        op.sig = None
        psr = [k for k in r if isinstance(k, tuple) and k[0] == "ps"]
        if psr:
            r = [k for k in r if k not in psr]
            w = list(w) + psr
        deps = list(self.base)
        for k in r:
            if k in self.lastw:
                deps.append(self.lastw[k])
        for k in w:
            if k in self.lastw:
                deps.append(self.lastw[k])
            deps.extend(self.readers.get(k, ()))
        if dma:
            deps = [d for d in deps if not (d.dma and d.tag == tag)]
        if eng == "pe":
            deps = [d for d in deps if d.dma or d.eng != "pe"]
        op.deps = deps
        for k in r:
            self.readers.setdefault(k, []).append(op)
        for k in w:
            self.lastw[k] = op
            self.readers[k] = []
        if dma:
            self.tagcnt[tag] = self.tagcnt.get(tag, 0) + 1
            self.taggroup[tag] = group
            op.cnt = self.tagcnt[tag]
        op.idx = len(self.all)
        self.all.append(op)
        self.ops[eng].append(op)
        return op

    def pe(self, fn, r=(), w=()):
        return self._add("pe", fn, r, w)

    def dve(self, fn, r=(), w=()):
        return self._add("dve", fn, r, w)

    def act(self, fn, r=(), w=()):
        return self._add("act", fn, r, w)

    def pool(self, fn, r=(), w=()):
        return self._add("pool", fn, r, w)

    def dma(self, out, in_, r=(), w=(), tag="ld", group=False, q="sp"):
        return self._add(q, lambda e: e.dma_start(out=out, in_=in_), r, w, dma=True, tag=tag, group=group)

    def barrier(self):
        base = []
        for e in self.ENGS:
            last = None
            for op in reversed(self.ops[e]):
                if not op.dma:
                    last = op
                    break
            if last is not None:
                base.append(last)
        lastdma = {}
        for op in self.all:
            if op.dma:
                lastdma[op.tag] = op
        base.extend(lastdma.values())
        self.base = base
        self.lastw = {}
        self.readers = {}

    def emit(self, final_tags):
        nc = self.nc
        for op in self.all:
            for d in op.deps:
                if not d.dma:
                    d.need = True
        for e in self.ENGS:
            n = 0
            for op in self.ops[e]:
                if not op.dma and op.need:
                    n += 1
                    op.sig = n
        from contextlib import ExitStack
        with ExitStack() as st:
            esem = {e: st.enter_context(nc.semaphore("s_" + e)) for e in self.ENGS}
            tsem = {t: st.enter_context(nc.semaphore("t_%d" % i)) for i, t in enumerate(self.tagcnt)}
            block = st.enter_context(nc.Block())

            def run(e, eng):
                waited = {}
                for op in self.ops[e]:
                    need = {}
                    for d in op.deps:
                        if d.dma:
                            s = tsem[d.tag]
                            v = 16 * (self.tagcnt[d.tag] if self.taggroup[d.tag] else d.cnt)
                        else:
                            s = esem[d.eng]
                            v = d.sig
                        if v > need.get(s, (0, 0))[1] if s in need else True:
                            need[s] = (s, v)
                    for s, v in need.values():
                        if waited.get(s, 0) < v:
                            eng.wait_ge(s, v)
                            waited[s] = v
                    ins = op.fn(eng)
                    if op.dma:
                        ins.then_inc(tsem[op.tag], 16)
                    elif op.need:
                        ins.then_inc(esem[e], 1)
                if e == "sp":
                    for t in tsem:
                        eng.wait_ge(tsem[t], 16 * self.tagcnt[t])

            block.sync(lambda eng: run("sp", eng))
            block.scalar(lambda eng: run("act", eng))
            block.gpsimd(lambda eng: run("pool", eng))
            block.vector(lambda eng: run("dve", eng))
            block.tensor(lambda eng: run("pe", eng))


class Alloc:
    def __init__(self, nc):
        self.nc = nc
        self.off = (int(nc.sbuf_base) + 63) // 64 * 64
        self.top = int(nc.sbuf_top)
        self.n = 0

    def t(self, shape, dt, name=None):
        sz = 2 if dt == BF16 else 4
        nb = sz
        for s in shape[1:]:
            nb *= s
        nb = (nb + 63) // 64 * 64
        assert self.off + nb <= self.top, ("SBUF overflow", name, self.off, nb, self.top)
        self.n += 1
        h = self.nc.alloc_sbuf_tensor_at("%s_%d" % (name or "t", self.n), list(shape), dt, offset=self.off)
        self.off += nb
        return h

    def child(self, off, top):
        c = Alloc.__new__(Alloc)
        c.nc, c.off, c.top, c.n = self.nc, (off + 63) // 64 * 64, top, self.n + 1000 * (1 + off % 97)
        return c

    def mark(self):
        return self.off

    def reset(self, m):
        self.off = m


def make_consts():
    i = np.arange(128)
    ident = np.eye(128, dtype=np.float32)
    U = (i[:, None] <= i[None, :]).astype(np.float32)
    SL = (i[:, None] > i[None, :]).astype(np.float32)
    ones = np.ones((128, 128), np.float32)
    blk = (i[:, None] // 64) == (i[None, :] // 64)
    mBDneg = -(SL * blk).astype(np.float32)
    mOFF = (SL * (~blk)).astype(np.float32)
    mTin = U.copy()
    oh16 = np.zeros((128, 16), np.float32)
    oh16[:16, :16] = np.eye(16)
    parts = [ident, U, SL, ones, mBDneg, mOFF, mTin, oh16]
    return np.ascontiguousarray(np.concatenate(parts, axis=1))


C_ID, C_U, C_SL, C_ONE = 0, 128, 256, 384
C_BD, C_OFF, C_TIN = 512, 640, 768
C_OH16 = 896
NCST = C_OH16 + 16

PC_CONVW, PC_CONVB, PC_LNG, PC_LNB, PC_SCW, PC_DN, PC_ALOG, PC_DTB = 0, 124, 128, 132, 136, 184, 185, 186
NPC = 187
PR_GMIX, PR_GMQ, PR_GFFN, PR_GKV, PR_GF, PR_DN4, PR_ALOG, PR_DTB = 0, 1024, 2048, 3072, 4096, 5120, 5632, 5636
NPR = 5640


def make_params(inp):
    pc = np.zeros((128, NPC), np.float32)
    pc[:, PC_CONVW:PC_CONVW + 124] = inp["conv_w"][0].reshape(31, 4, 128).transpose(2, 1, 0).reshape(128, 124)
    pc[:, PC_CONVB:PC_CONVB + 4] = inp["conv_b"][0].reshape(4, 128).T
    pc[:, PC_LNG:PC_LNG + 4] = inp["conv_ln_g"][0].reshape(4, 128).T
    pc[:, PC_LNB:PC_LNB + 4] = inp["conv_ln_b"][0].reshape(4, 128).T
    pc[:, PC_SCW:PC_SCW + 48] = inp["sc_w"][0].reshape(4, 12, 128).transpose(2, 1, 0).reshape(128, 48)
    pc[:, PC_DN] = inp["dn_norm"][0]
    pc[:4, PC_ALOG] = inp["a_log"][0]
    pc[:4, PC_DTB] = inp["dt_bias"][0]
    pr = np.zeros((128, NPR), np.float32)
    bc = lambda v: np.broadcast_to(np.asarray(v, np.float32).reshape(1, -1), (128, np.asarray(v).size))
    pr[:, PR_GMIX:PR_GMIX + 1024] = bc(inp["norm_mix"][0])
    pr[:, PR_GMQ:PR_GMQ + 1024] = bc(inp["norm_mem_q"][0])
    pr[:, PR_GFFN:PR_GFFN + 1024] = bc(inp["norm_ffn"][0])
    pr[:, PR_GKV:PR_GKV + 1024] = bc(inp["norm_mem_kv"][0])
    pr[:, PR_GF:PR_GF + 1024] = bc(inp["norm_f"])
    pr[:, PR_DN4:PR_DN4 + 512] = bc(np.tile(inp["dn_norm"][0], 4))
    pr[:, PR_ALOG:PR_ALOG + 4] = bc(inp["a_log"][0])
    pr[:, PR_DTB:PR_DTB + 4] = bc(inp["dt_bias"][0])
    return pc, pr


def build_nc():
    nc = bass.Bass("TRN2", target_bir_lowering=False)
    P = Prog(nc)
    try:
        return _build_nc(nc, P)
    except _Stop:
        P.emit(["out"])
        return nc, P


def _build_nc(nc, P):
    A = Alloc(nc)

    def dr(name, shape, out=False):
        return nc.dram_tensor(name, list(shape), F32, kind="ExternalOutput" if out else "ExternalInput").ap()

    x_p = dr("x_p", [T, D]); x_s = dr("x_s", [NS, D]); mem = dr("mem", [256, D])
    cconv = dr("cconv", [NS * 30, 512]); ssc = dr("ssc", [NS * 3, 1536]); sdel = dr("sdel", [NS * 4, 128, 128])
    cmk = dr("cmk", [NS, 256, 1024]); cmv = dr("cmv", [NS, 256, 1024])
    w_in = dr("w_in", [D, 3080]); w_out = dr("w_out", [D, D]); w_mq = dr("w_mq", [D, D]); w_mk = dr("w_mk", [D, D])
    w_mv = dr("w_mv", [D, D]); w_mo = dr("w_mo", [D, D]); w_gate = dr("w_gate", [D, DFF]); w_up = dr("w_up", [D, DFF])
    w_down = dr("w_down", [DFF, D])
    cst_d = dr("cst", [128, NCST]); pc_d = dr("pc", [128, NPC]); pr_d = dr("pr", [128, NPR])
    y_p = dr("y_p", [T, D], True); y_s = dr("y_s", [NS, D], True)
    nconv_p = dr("nconv_p", [30, 512], True); nsc_p = dr("nsc_p", [3, 1536], True)
    ndel_p = dr("ndel_p", [4, 128, 128], True)
    mk_p = dr("mk_p", [256, D], True); mv_p = dr("mv_p", [256, D], True)
    nconv_s = dr("nconv_s", [NS, 30, 512], True); nsc_s = dr("nsc_s", [NS, 3, 1536], True)
    ndel_s = dr("ndel_s", [NS * 4, 128, 128], True)

    ps = [nc.alloc_psum_tensor("ps%d" % i, [128, 512], F32) for i in range(8)]
    PK = lambda i: ("ps", i)

    def psb(i):
        return ps[i][:].bitcast(BF16)

    cst = A.t([128, NCST], F32, "cst")
    pc = A.t([128, NPC], F32, "pc")
    idb = A.t([128, 128], BF16, "idb")
    oneb = A.t([128, 128], BF16, "oneb")
    onesc = A.t([128, 128], BF16, "onesc")
    gam = A.t([128, 1024], F32, "gam")
    P.dma(cst[:], cst_d[:, :], w=["cst"], tag="c0")
    P.dma(pc[:], pc_d[:, :], w=["pc"], tag="c1")
    P.dve(lambda e: e.tensor_copy(out=idb[:], in_=cst[:, C_ID:C_ID + 128]), r=["cst"], w=["idb"])
    P.dve(lambda e: e.tensor_copy(out=oneb[:], in_=cst[:, C_ONE:C_ONE + 128]), r=["cst"], w=["oneb"])
    P.dve(lambda e: e.tensor_scalar(out=onesc[:], in0=cst[:, C_ONE:C_ONE + 128], scalar1=1.0 / 512, scalar2=None,
                                    op0=ALU.mult), r=["cst"], w=["onesc"])
    ident_f = cst[:, C_ID:C_ID + 128]
    Uf = cst[:, C_U:C_U + 128]
    SLf = cst[:, C_SL:C_SL + 128]
    onef = cst[:, C_ONE:C_ONE + 128]

    def load_gam(off):
        P.dma(gam[:], pr_d[:, off:off + 1024], w=["gam"], tag="gam")

    nrm_junk = A.t([128, 1024], BF16, "njunk")
    nrm_xn3 = [A.t([128, 1024], BF16, "nxn%d" % i) for i in range(3)]
    nrm_s3 = [A.t([128, 8], F32, "nrs%d" % i) for i in range(3)]

    def norm_pass(items, dstT, dkey):
        L = len(items)

        def S1(i):
            src, rkeys, n, col0, pre = items[i]
            if pre is not None:
                pre()
            ns_, nsk = nrm_s3[i % 3], ("nrs", i % 3)
            P.act(lambda e: e.activation(out=nrm_junk[:n, :], in_=src, func=AF.Square, accum_out=ns_[:n, 0:1]),
                  r=rkeys, w=[nsk, "njunk"])
            P.dve(lambda e: e.tensor_scalar(out=ns_[:n, 1:2], in0=ns_[:n, 0:1], scalar1=1.0 / 1024, scalar2=1e-6,
                                            op0=ALU.mult, op1=ALU.add), r=[nsk], w=[nsk])

        def S2(i):
            src, rkeys, n, col0, pre = items[i]
            ns_, nsk = nrm_s3[i % 3], ("nrs", i % 3)
            xn_, nxk = nrm_xn3[i % 3], ("nxn", i % 3)
            P.act(lambda e: e.activation(out=ns_[:n, 2:3], in_=ns_[:n, 1:2], func=AF.Sqrt), r=[nsk], w=[nsk])
            P.dve(lambda e: e.reciprocal(out=ns_[:n, 3:4], in_=ns_[:n, 2:3]), r=[nsk], w=[nsk])
            P.dve(lambda e: e.scalar_tensor_tensor(out=xn_[:n, :], in0=src, scalar=ns_[:n, 3:4], in1=gam[:n, :],
                                                   op0=ALU.mult, op1=ALU.mult), r=rkeys + [nsk, "gam"], w=[nxk])

        def S3(i):
            src, rkeys, n, col0, pre = items[i]
            xn_, nxk = nrm_xn3[i % 3], ("nxn", i % 3)
            bank = i % 2
            pv = psb(bank)
            for kc in range(8):
                P.pe(lambda e, kc=kc: e.transpose(out=pv[:, kc * 128:kc * 128 + n],
                                                  in_=xn_[:n, kc * 128:(kc + 1) * 128], identity=idb[:n, :n]),
                     r=[nxk, "idb"], w=[PK(bank)])
            pv3 = pv.rearrange("p (c n) -> p c n", c=8)
            dk_ = (dkey, col0 // 512)
            if i % 2 == 0:
                P.act(lambda e: e.activation(out=dstT[:, :, col0:col0 + n], in_=pv3[:, :, 0:n], func=AF.Copy),
                      r=[PK(bank)], w=[dk_])
            else:
                P.dve(lambda e: e.tensor_copy(out=dstT[:, :, col0:col0 + n], in_=pv3[:, :, 0:n]),
                      r=[PK(bank)], w=[dk_])

        for step in range(L + 2):
            if step < L:
                S1(step)
            if 0 <= step - 1 < L:
                S2(step - 1)
            if 0 <= step - 2 < L:
                S3(step - 2)

    def load_w(dst, dkey, wd, r0, nk, c0, ncols):
        for k in range(nk):
            P.dma(dst[:, k, 0:ncols], wd[r0 + k * 128:r0 + (k + 1) * 128, c0:c0 + ncols], w=[dkey],
                  tag="w_" + str(dkey), q="pool")

    off_hT = A.mark()
    hT = A.t([128, 8, TT], BF16, "hT")
    off_cT = A.mark()
    cT = A.t([128, 4, TT], BF16, "cT")
    mX = A.mark()
    XSZ = 17 * 1024 * 4
    A.off += XSZ
    mXe = A.mark()
    AX_ = A.child(mX, mXe)
    zs = A.t([128, NT, 512], BF16, "zs")
    zsT = A.t([128, 4, NS], F32, "zsT")
    vTs = A.t([128, 4, NS], F32, "vTs")
    gtok = A.t([128, NT, 4], F32, "gtok")
    btok = A.t([128, NT, 4], F32, "btok")
    gbS = A.t([4, 2, NS], F32, "gbS")
    utail = A.t([128, 4, 32], F32, "utail")
    ptail = A.t([128, 12, 4], F32, "ptail")
    unew_s = A.t([128, 4, NS], F32, "unews")
    pnew_s = A.t([128, 12, NS], F32, "pnews")
    m_w = A.mark()
    wsl = [A.t([128, 8, 512], BF16, "wsl%d" % i) for i in range(3)]
    m_p1 = A.mark()
    A1 = A
    A = AX_

    xin = [A.t([128, 1024], F32, "xin%d" % i) for i in range(3)]
    load_gam(PR_GMIX)
    items = []
    for t in range(NT + 1):
        s = t % 3
        n = 128 if t < NT else NS
        src_d = x_p[t * 128:(t + 1) * 128, :] if t < NT else x_s[:, :]
        pre = (lambda s=s, n=n, src_d=src_d: P.dma(xin[s][:n, :], src_d, w=[("xin", s)], tag="xin%d" % s))
        items.append((xin[s][:n, :], [("xin", s)], n, t * 128, pre))
    norm_pass(items, hT, "hT")

    stop_at(1)
    BLK = [(i * 512, 512) for i in range(4)] + [(T, NS)]

    def mm8(bank, w_ap_fn, rhs_fn, ncols, rk, mrows=128):
        for kc in range(8):
            P.pe(lambda e, kc=kc: e.matmul(ps[bank][0:mrows, 0:ncols], lhsT=w_ap_fn(kc), rhs=rhs_fn(kc),
                                           start=(kc == 0), stop=(kc == 7)), r=rk, w=[PK(bank)])

    load_w(wsl[2], ("wsl", 2), w_in, 0, 8, 3072, 8)
    load_w(wsl[0], ("wsl", 0), w_in, 0, 8, 0, 512)
    load_w(wsl[1], ("wsl", 1), w_in, 0, 8, 512, 512)
    prb = A.t([128, 16], F32, "prb")
    P.dma(prb[:, 0:8], pr_d[:, PR_ALOG:PR_ALOG + 8], w=["prb"], tag="c2")
    P.act(lambda e: e.activation(out=prb[:, 8:12], in_=prb[:, 0:4], func=AF.Exp), r=["prb"], w=["prb"])
    P.dve(lambda e: e.tensor_scalar(out=prb[:, 8:12], in0=prb[:, 8:12], scalar1=-1.0, scalar2=None, op0=ALU.mult),
          r=["prb"], w=["prb"])
    negA_c = A.t([4, 1], F32, "negAc")
    P.act(lambda e: e.activation(out=negA_c[:], in_=pc[0:4, PC_ALOG:PC_ALOG + 1], func=AF.Exp), r=["pc"], w=["negAc"])
    P.dve(lambda e: e.tensor_scalar(out=negA_c[:], in0=negA_c[:], scalar1=-1.0, scalar2=None, op0=ALU.mult),
          r=["negAc"], w=["negAc"])
    gx = A.t([128, NT, 4], F32, "gx")
    dtb64 = A.t([128, NT, 4], F32, "dtb64")
    nga64 = A.t([128, NT, 4], F32, "nga64")
    P.dve(lambda e: e.tensor_copy(out=dtb64[:], in_=prb[:, 4:8].unsqueeze(1).to_broadcast([128, NT, 4])),
          r=["prb"], w=["dtb64"])
    P.dve(lambda e: e.tensor_copy(out=nga64[:], in_=prb[:, 8:12].unsqueeze(1).to_broadcast([128, NT, 4])),
          r=["prb"], w=["nga64"])
    for t in range(NT):
        for kc in range(8):
            P.pe(lambda e, t=t, kc=kc: e.matmul(ps[2][:, t * 8:(t + 1) * 8], lhsT=hT[:, kc, t * 128:(t + 1) * 128],
                                                rhs=wsl[2][:, kc, 0:8], start=(kc == 0), stop=(kc == 7)),
                 r=[("hT", t // 4), ("wsl", 2)], w=[PK(2)])
    ps3 = ps[2][:, 0:NT * 8].rearrange("p (t c) -> p t c", c=8)
    P.act(lambda e: e.activation(out=btok[:, :, :], in_=ps3[:, :, 0:4], func=AF.Exp, scale=-1.0), r=[PK(2)],
          w=["btok"])
    P.dve(lambda e: e.tensor_scalar(out=btok[:, :, :], in0=btok[:, :, :], scalar1=1.0, scalar2=None, op0=ALU.add),
          r=["btok"], w=["btok"])
    P.dve(lambda e: e.reciprocal(out=btok[:, :, :], in_=btok[:, :, :]), r=["btok"], w=["btok"])
    P.dve(lambda e: e.tensor_tensor(out=gx[:], in0=ps3[:, :, 4:8], in1=dtb64[:], op=ALU.add), r=[PK(2), "dtb64"],
          w=["gx"])
    P.act(lambda e: e.activation(out=gx[:], in_=gx[:], func=AF.Exp), r=["gx"], w=["gx"])
    P.act(lambda e: e.activation(out=gx[:], in_=gx[:], func=AF.Ln, bias=1.0), r=["gx"], w=["gx"])
    P.dve(lambda e: e.tensor_tensor(out=gtok[:, :, :], in0=gx[:], in1=nga64[:], op=ALU.mult), r=["gx", "nga64"],
          w=["gtok"])
    for half in range(2):
        b = 2 + half
        mm8(b, lambda kc: wsl[2][:, kc, half * 4:half * 4 + 4], lambda kc: hT[:, kc, T:TT], NS, [("hT", 4), ("wsl", 2)],
            mrows=4)
    P.act(lambda e: e.activation(out=gbS[:, 0, :], in_=ps[2][0:4, 0:NS], func=AF.Exp, scale=-1.0), r=[PK(2)],
          w=["gbS"])
    P.dve(lambda e: e.tensor_scalar(out=gbS[:, 0, :], in0=gbS[:, 0, :], scalar1=1.0, scalar2=None, op0=ALU.add),
          r=["gbS"], w=["gbS"])
    P.dve(lambda e: e.reciprocal(out=gbS[:, 0, :], in_=gbS[:, 0, :]), r=["gbS"], w=["gbS"])
    gts = A.t([4, 2, NS], F32, "gts")
    P.act(lambda e: e.activation(out=gts[:, 0, :], in_=ps[3][0:4, 0:NS], func=AF.Exp,
                                 bias=pc[0:4, PC_DTB:PC_DTB + 1]), r=[PK(3), "pc"], w=["gts"])
    P.act(lambda e: e.activation(out=gts[:, 1, :], in_=gts[:, 0, :], func=AF.Ln, bias=1.0), r=["gts"], w=["gts"])
    P.dve(lambda e: e.tensor_scalar(out=gbS[:, 1, :], in0=gts[:, 1, :], scalar1=negA_c[:, 0:1], scalar2=None,
                                    op0=ALU.mult), r=["gts", "negAc"], w=["gbS"])

    stop_at(2)
    upad = A.t([128, 4, 30 + T], BF16, "upad")
    us = A.t([128, 4, NS, 32], BF16, "us")
    sig = [A.t([128, 512], F32, "sig%d" % i) for i in range(2)]
    diag = A.t([128, 31, 128], BF16, "diag")
    P.dve(lambda e: e.memset(upad[:, :, 0:30], 0.0), w=["upad"])
    load_w(wsl[2], ("wsl", 2), w_in, 0, 8, 1024, 512)
    cc_in = A.t([120, 4, 512], F32, "ccin")
    for g4 in range(4):
        P.dma(cc_in[:, g4, :], cconv[g4 * 120:(g4 + 1) * 120, :], w=[("ccin", g4)], tag="cc", group=True)
    P.dma(nconv_s[:, 0:29, :], cconv.rearrange("(s j) c -> s j c", j=30)[:, 1:30, :], tag="out")
    for c in range(4):
        for g4 in range(4):
            b = 4 + (g4 % 2)
            P.pe(lambda e, c=c, g4=g4, b=b: e.transpose(out=ps[b][:, 0:120], in_=cc_in[:, g4, c * 128:(c + 1) * 128],
                                                        identity=ident_f[0:120, 0:120]), r=[("ccin", g4), "cst"], w=[PK(b)])
            P.act(lambda e, c=c, g4=g4, b=b: e.activation(
                out=us[:, c, g4 * 4:(g4 + 1) * 4, 0:30],
                in_=ps[b][:, 0:120].rearrange("p (s j) -> p s j", j=30), func=AF.Copy), r=[PK(b)], w=["us"])
    for c in range(4):
        for bi, (t0, n) in enumerate(BLK):
            ba, bb = (bi % 2) * 2, (bi % 2) * 2 + 1
            mm8(ba, lambda kc: wsl[0][:, kc, c * 128:(c + 1) * 128], lambda kc: hT[:, kc, t0:t0 + n], n,
                [("hT", t0 // 512), ("wsl", 0)])
            mm8(bb, lambda kc: wsl[1][:, kc, c * 128:(c + 1) * 128], lambda kc: hT[:, kc, t0:t0 + n], n,
                [("hT", t0 // 512), ("wsl", 1)])
            sg = sig[bi % 2]
            sk = ("sig", bi % 2)
            P.act(lambda e, sg=sg, bb=bb, n=n: e.activation(out=sg[:, 0:n], in_=ps[bb][:, 0:n], func=AF.Sigmoid),
                  r=[PK(bb)], w=[sk])
            if bi < 4:
                P.dve(lambda e, sg=sg, ba=ba, c=c, t0=t0: e.tensor_tensor(
                    out=upad[:, c, 30 + t0:30 + t0 + 512], in0=ps[ba][:, 0:512], in1=sg[:, 0:512], op=ALU.mult),
                    r=[PK(ba), sk], w=["upad"])
                if bi == 3:
                    P.dve(lambda e, sg=sg, ba=ba, c=c: e.tensor_tensor(
                        out=utail[:, c, 0:30], in0=ps[ba][:, 482:512], in1=sg[:, 482:512], op=ALU.mult),
                        r=[PK(ba), sk], w=["utail"])
            else:
                P.dve(lambda e, sg=sg, ba=ba, c=c: e.tensor_tensor(
                    out=unew_s[:, c, :], in0=ps[ba][:, 0:NS], in1=sg[:, 0:NS], op=ALU.mult),
                    r=[PK(ba), sk], w=["unews"])
                P.act(lambda e, c=c: e.activation(out=us[:, c, :, 30:31], in_=unew_s[:, c, :].unsqueeze(2),
                                                  func=AF.Copy), r=["unews"], w=["us"])
        for j in range(31):
            P.dve(lambda e, c=c, j=j: e.tensor_scalar(
                out=diag[:, j, :], in0=idb[:], scalar1=pc[:, PC_CONVW + c * 31 + j:PC_CONVW + c * 31 + j + 1],
                scalar2=None, op0=ALU.mult), r=["idb", "pc"], w=["diag"])
        for bi, (t0, n) in enumerate(BLK):
            b = 4 + bi % 2
            for j in range(31):
                if bi < 4:
                    rhs = upad[:, c, t0 + j:t0 + j + 512]
                else:
                    rhs = us[:, c, :, j]
                P.pe(lambda e, b=b, j=j, rhs=rhs, n=n: e.matmul(ps[b][:, 0:n], lhsT=diag[:, j, :], rhs=rhs,
                                                               start=(j == 0), stop=(j == 30)),
                     r=["diag", "upad", "us"], w=[PK(b)])
            P.act(lambda e, b=b, c=c, t0=t0, n=n: e.activation(
                out=cT[:, c, t0:t0 + n], in_=ps[b][:, 0:n], func=AF.Identity,
                bias=pc[:, PC_CONVB + c:PC_CONVB + c + 1]), r=[PK(b), "pc"], w=["cT"])
    csq = [A.t([128, 512], BF16, "csq%d" % i) for i in range(2)]
    lnm = A.t([128, 512], F32, "lnm")
    lnv = A.t([128, 512], F32, "lnv")
    lnt = [A.t([128, 512], F32, "lnt%d" % i) for i in range(2)]
    for bi, (t0, n) in enumerate(BLK):
        for c in range(4):
            P.pe(lambda e, c=c, t0=t0, n=n: e.matmul(ps[0][:, 0:n], lhsT=onesc[:], rhs=cT[:, c, t0:t0 + n],
                                                     start=(c == 0), stop=(c == 3)), r=["cT", "onesc"], w=[PK(0)])
        for c in range(4):
            q = csq[c % 2]
            P.dve(lambda e, q=q, c=c, t0=t0, n=n: e.tensor_tensor(out=q[:, 0:n], in0=cT[:, c, t0:t0 + n],
                                                                  in1=cT[:, c, t0:t0 + n], op=ALU.mult),
                  r=["cT"], w=[("csq", c % 2)])
            P.pe(lambda e, q=q, c=c, n=n: e.matmul(ps[1][:, 0:n], lhsT=onesc[:], rhs=q[:, 0:n],
                                                   start=(c == 0), stop=(c == 3)), r=[("csq", c % 2), "onesc"],
                 w=[PK(1)])
        P.act(lambda e, n=n: e.activation(out=lnm[:, 0:n], in_=ps[0][:, 0:n], func=AF.Copy), r=[PK(0)], w=["lnm"])
        P.dve(lambda e, n=n: e.tensor_tensor(out=lnv[:, 0:n], in0=lnm[:, 0:n], in1=lnm[:, 0:n], op=ALU.mult),
              r=["lnm"], w=["lnv"])
        P.dve(lambda e, n=n: e.tensor_tensor(out=lnv[:, 0:n], in0=ps[1][:, 0:n], in1=lnv[:, 0:n], op=ALU.subtract),
              r=[PK(1), "lnv"], w=["lnv"])
        P.dve(lambda e, n=n: e.tensor_scalar(out=lnv[:, 0:n], in0=lnv[:, 0:n], scalar1=0.0, scalar2=1e-5,
                                             op0=ALU.max, op1=ALU.add), r=["lnv"], w=["lnv"])
        P.act(lambda e, n=n: e.activation(out=lnv[:, 0:n], in_=lnv[:, 0:n], func=AF.Ln), r=["lnv"], w=["lnv"])
        P.act(lambda e, n=n: e.activation(out=lnv[:, 0:n], in_=lnv[:, 0:n], func=AF.Exp, scale=-0.5), r=["lnv"],
              w=["lnv"])
        for c in range(4):
            tt_ = lnt[c % 2]
            tk = ("lnt", c % 2)
            P.dve(lambda e, tt_=tt_, c=c, t0=t0, n=n: e.tensor_tensor(out=tt_[:, 0:n], in0=cT[:, c, t0:t0 + n],
                                                                      in1=lnm[:, 0:n], op=ALU.subtract),
                  r=["cT", "lnm"], w=[tk])
            P.dve(lambda e, tt_=tt_, n=n: e.tensor_tensor(out=tt_[:, 0:n], in0=tt_[:, 0:n], in1=lnv[:, 0:n],
                                                          op=ALU.mult), r=[tk, "lnv"], w=[tk])
            P.act(lambda e, tt_=tt_, c=c, t0=t0, n=n: e.activation(
                out=cT[:, c, t0:t0 + n], in_=tt_[:, 0:n], func=AF.Silu,
                scale=pc[:, PC_LNG + c:PC_LNG + c + 1], bias=pc[:, PC_LNB + c:PC_LNB + c + 1]),
                r=[tk, "pc"], w=["cT"])
    otl = A.t([32, 512], F32, "otl")
    for c in range(4):
        P.pe(lambda e, c=c: e.transpose(out=ps[2][0:30, c * 128:(c + 1) * 128], in_=utail[:, c, 0:30],
                                        identity=ident_f), r=["utail", "cst"], w=[PK(2)])
    P.act(lambda e: e.activation(out=otl[0:30, :], in_=ps[2][0:30, :], func=AF.Copy), r=[PK(2)], w=["otl"])
    P.dma(nconv_p[:, :], otl[0:30, :], r=["otl"], tag="out")
    otl2 = A.t([16, 512], F32, "otl2")
    for c in range(4):
        P.pe(lambda e, c=c: e.transpose(out=ps[3][0:NS, c * 128:(c + 1) * 128], in_=unew_s[:, c, :],
                                        identity=ident_f), r=["unews", "cst"], w=[PK(3)])
    P.act(lambda e: e.activation(out=otl2[:, :], in_=ps[3][0:NS, :], func=AF.Copy), r=[PK(3)], w=["otl2"])
    P.dma(nconv_s[:, 29, :], otl2[:, :], r=["otl2"], tag="out")

    stop_at(3)
    P.barrier()
    AX_ = A1.child(mX, mXe)
    qT = AX_.t([128, 4, TT], BF16, "qT")
    kT = AX_.t([128, 4, TT], BF16, "kT")
    ktok = AX_.t([128, NT, 512], BF16, "ktok")
    vb = AX_.t([128, NT, 512], BF16, "vb")
    A = A1
    scd2 = [A.t([128, 4, 128], BF16, "scd%d" % i) for i in range(2)]
    pre = [A.t([128, 3 + 512], BF16, "pre%d" % i) for i in range(2)]
    pres2 = [A.t([128, NS, 4], BF16, "pres%d" % i) for i in range(2)]
    sfl = [A.t([128, 512], F32, "sfl%d" % i) for i in range(2)]
    sqb = [A.t([128, 512], BF16, "sqb%d" % i) for i in range(2)]
    rnb = [A.t([128, 512], F32, "rnb%d" % i) for i in range(2)]
    vtmp = [A.t([128, 512], BF16, "vtmp%d" % i) for i in range(2)]
    ss_in = A.t([48, 1536], F32, "ssin")
    P.dma(ss_in[:, :], ssc[:, :], w=["ssin"], tag="ssin")
    P.dma(nsc_s[:, 0:2, :], ssc.rearrange("(s j) c -> s j c", j=3)[:, 1:3, :], tag="out")

    def run_skewed(iters):
        L = len(iters)
        S_ = max(len(x) for x in iters)
        for step in range(L + S_ - 1):
            for k in range(S_):
                i = step - k
                if 0 <= i < L and k < len(iters[i]):
                    iters[i][k]()

    iters = []
    it = [0]
    for grp in range(3):
        sl = (2, 0, 1)[grp]
        for hh in range(4):
            ch = grp * 4 + hh
            scd = scd2[ch % 2]
            sck = ("scd", ch % 2)
            pres = pres2[ch % 2]
            psk = ("pres", ch % 2)
            for bi, (t0, n) in enumerate(BLK):
                i = it[0]
                it[0] += 1

                def S0(grp=grp, sl=sl, hh=hh, ch=ch, scd=scd, sck=sck, pres=pres, psk=psk, bi=bi, t0=t0, n=n, i=i):
                    b = i % 2
                    pr_ = pre[i % 2]
                    prk = ("pre", i % 2)
                    if hh == 0 and bi == 0:
                        if grp < 2:
                            nsl = (2, 0, 1)[grp + 1]
                            load_w(wsl[nsl], ("wsl", nsl), w_in, 0, 8, 1024 + (grp + 1) * 512, 512)
                        else:
                            load_w(wsl[2], ("wsl", 2), w_in, 0, 8, 2560, 512)
                    if bi == 0:
                        for j in range(4):
                            P.dve(lambda e, j=j: e.tensor_scalar(
                                out=scd[:, j, :], in0=idb[:],
                                scalar1=pc[:, PC_SCW + ch * 4 + j:PC_SCW + ch * 4 + j + 1],
                                scalar2=None, op0=ALU.mult), r=["idb", "pc"], w=[sck])
                        P.pe(lambda e: e.transpose(out=ps[6][:, 0:48], in_=ss_in[:, ch * 128:(ch + 1) * 128],
                                                   identity=ident_f[0:48, 0:48]), r=["ssin", "cst"], w=[PK(6)])
                        P.act(lambda e: e.activation(out=pres[:, :, 0:3],
                                                     in_=ps[6][:, 0:48].rearrange("p (s j) -> p s j", j=3),
                                                     func=AF.Copy), r=[PK(6)], w=[psk])
                    mm8(b, lambda kc: wsl[sl][:, kc, hh * 128:(hh + 1) * 128], lambda kc: hT[:, kc, t0:t0 + n], n,
                        [("hT", t0 // 512), ("wsl", sl)])
                    if bi < 4:
                        if bi == 0:
                            P.dve(lambda e: e.memset(pr_[:, 0:3], 0.0), w=[prk])
                        else:
                            po = pre[(i - 1) % 2]
                            P.dve(lambda e: e.tensor_copy(out=pr_[:, 0:3], in_=po[:, 512:515]),
                                  r=[("pre", (i - 1) % 2)], w=[prk])
                        P.dve(lambda e: e.tensor_copy(out=pr_[:, 3:515], in_=ps[b][:, 0:512]),
                              r=[PK(b)], w=[prk])
                        if bi == 3:
                            P.dve(lambda e: e.tensor_copy(out=ptail[:, ch, 0:3], in_=ps[b][:, 509:512]),
                                  r=[PK(b)], w=["ptail"])
                    else:
                        P.act(lambda e: e.activation(out=pres[:, :, 3:4], in_=ps[b][:, 0:NS].unsqueeze(2),
                                                     func=AF.Copy), r=[PK(b)], w=[psk])
                        P.dve(lambda e: e.tensor_copy(out=pnew_s[:, ch, :], in_=ps[b][:, 0:NS]),
                              r=[PK(b)], w=["pnews"])

                def S1(grp=grp, hh=hh, scd=scd, sck=sck, pres=pres, psk=psk, bi=bi, n=n, i=i):
                    b2 = 2 + i % 2
                    pr_ = pre[i % 2]
                    prk = ("pre", i % 2)
                    for j in range(4):
                        rhs = pr_[:, j:j + 512] if bi < 4 else pres[:, :, j]
                        P.pe(lambda e, j=j, rhs=rhs: e.matmul(ps[b2][:, 0:n], lhsT=scd[:, j, :], rhs=rhs,
                                                              start=(j == 0), stop=(j == 3)),
                             r=[sck, prk if bi < 4 else psk], w=[PK(b2)])
                    if grp == 2:
                        if bi < 4:
                            vt = vtmp[i % 2]
                            P.act(lambda e: e.activation(out=vt[:, :], in_=ps[b2][:, 0:512], func=AF.Silu),
                                  r=[PK(b2)], w=[("vtmp", i % 2)])
                        else:
                            P.act(lambda e: e.activation(out=vTs[:, hh, :], in_=ps[b2][:, 0:NS], func=AF.Silu),
                                  r=[PK(b2)], w=["vTs"])
                        return
                    sf = sfl[i % 2]
                    sfk = ("sfl", i % 2)
                    P.act(lambda e: e.activation(out=sf[:, 0:n], in_=ps[b2][:, 0:n], func=AF.Exp, scale=-1.0),
                          r=[PK(b2)], w=[sfk])
                    P.act(lambda e: e.activation(out=sf[:, 0:n], in_=sf[:, 0:n], func=AF.Ln, bias=1.0),
                          r=[sfk], w=[sfk])
                    P.act(lambda e: e.activation(out=sf[:, 0:n], in_=sf[:, 0:n], func=AF.Exp, scale=-1.0),
                          r=[sfk], w=[sfk])
                    P.dve(lambda e: e.tensor_tensor(out=sf[:, 0:n], in0=ps[b2][:, 0:n], in1=sf[:, 0:n], op=ALU.mult),
                          r=[PK(b2), sfk], w=[sfk])
                    sq = sqb[i % 2]
                    P.dve(lambda e: e.tensor_tensor(out=sq[:, 0:n], in0=sf[:, 0:n], in1=sf[:, 0:n], op=ALU.mult),
                          r=[sfk], w=[("sqb", i % 2)])

                def S2(grp=grp, hh=hh, bi=bi, t0=t0, n=n, i=i):
                    pb = 4 + i % 2
                    if grp == 2:
                        if bi == 4:
                            return
                        vt = vtmp[i % 2]
                        vk = ("vtmp", i % 2)
                        pv = psb(pb)
                        for tl in range(4):
                            P.pe(lambda e, tl=tl: e.transpose(out=pv[:, tl * 128:(tl + 1) * 128],
                                                              in_=vt[:, tl * 128:(tl + 1) * 128], identity=idb[:]),
                                 r=[vk, "idb"], w=[PK(pb)])
                        for tl in range(4):
                            tg = bi * 4 + tl
                            P.dve(lambda e, tl=tl, tg=tg: e.tensor_scalar(
                                out=vb[:, tg, hh * 128:(hh + 1) * 128], in0=pv[:, tl * 128:(tl + 1) * 128],
                                scalar1=btok[:, tg, hh:hh + 1], scalar2=None, op0=ALU.mult),
                                r=[PK(pb), "btok"], w=["vb"])
                        return
                    dst = qT if grp == 0 else kT
                    dk = "qT" if grp == 0 else "kT"
                    sf = sfl[i % 2]
                    sfk = ("sfl", i % 2)
                    sq = sqb[i % 2]
                    P.pe(lambda e: e.matmul(ps[pb][:, 0:n], lhsT=oneb[:], rhs=sq[:, 0:n], start=True, stop=True),
                         r=[("sqb", i % 2), "oneb"], w=[PK(pb)])
                    rn = rnb[i % 2]
                    rk_ = ("rnb", i % 2)
                    sc_ = 128.0 if grp == 0 else 1.0
                    P.act(lambda e: e.activation(out=rn[:, 0:n], in_=ps[pb][:, 0:n], func=AF.Ln, scale=sc_,
                                                 bias=1e-6 * sc_), r=[PK(pb)], w=[rk_])
                    P.act(lambda e: e.activation(out=rn[:, 0:n], in_=rn[:, 0:n], func=AF.Exp, scale=-0.5), r=[rk_],
                          w=[rk_])
                    P.dve(lambda e: e.tensor_tensor(out=dst[:, hh, t0:t0 + n], in0=sf[:, 0:n], in1=rn[:, 0:n],
                                                    op=ALU.mult), r=[sfk, rk_], w=[dk])

                def S3(grp=grp, hh=hh, bi=bi, t0=t0, i=i):
                    if not (grp == 1 and bi < 4):
                        return
                    pb2 = 6 + i % 2
                    pv = psb(pb2)
                    for tl in range(4):
                        P.pe(lambda e, tl=tl: e.transpose(out=pv[:, tl * 128:(tl + 1) * 128],
                                                          in_=kT[:, hh, t0 + tl * 128:t0 + (tl + 1) * 128],
                                                          identity=idb[:]), r=["kT", "idb"], w=[PK(pb2)])
                    P.dve(lambda e: e.tensor_copy(out=ktok[:, bi * 4:(bi + 1) * 4, hh * 128:(hh + 1) * 128],
                                                  in_=pv[:, 0:512].rearrange("p (t d) -> p t d", t=4)),
                          r=[PK(pb2)], w=["ktok"])

                iters.append([S0, S1, S2, S3])
    run_skewed(iters)
    otp = A.t([16, 1536], F32, "otp")
    for ch in range(12):
        b = ch // 4
        P.pe(lambda e, ch=ch, b=b: e.transpose(out=ps[b][0:3, (ch % 4) * 128:(ch % 4 + 1) * 128],
                                               in_=ptail[:, ch, 0:3], identity=ident_f), r=["ptail", "cst"],
             w=[PK(b)])
    for b in range(3):
        P.act(lambda e, b=b: e.activation(out=otp[0:3, b * 512:(b + 1) * 512], in_=ps[b][0:3, :], func=AF.Copy),
              r=[PK(b)], w=["otp"])
    P.dma(nsc_p[:, :], otp[0:3, :], r=["otp"], tag="ootp")
    otp2 = otp
    for ch in range(12):
        b = 3 + ch // 4
        P.pe(lambda e, ch=ch, b=b: e.transpose(out=ps[b][0:NS, (ch % 4) * 128:(ch % 4 + 1) * 128],
                                               in_=pnew_s[:, ch, :], identity=ident_f), r=["pnews", "cst"],
             w=[PK(b)])
    for b in range(3):
        P.act(lambda e, b=b: e.activation(out=otp2[:, b * 512:(b + 1) * 512], in_=ps[3 + b][0:NS, :], func=AF.Copy),
              r=[PK(3 + b)], w=["otp"])
    P.dma(nsc_s[:, 2, :], otp2[:, :], r=["otp"], tag="ootp")

    for t in range(NT):
        b = t % 2
        mm8(b, lambda kc: hT[:, kc, t * 128:(t + 1) * 128], lambda kc: wsl[2][:, kc, 0:512], 512,
            [("hT", t // 4), ("wsl", 2)])
        P.act(lambda e, t=t, b=b: e.activation(out=zs[:, t, :], in_=ps[b][:, 0:512], func=AF.Silu), r=[PK(b)],
              w=["zs"])
    for hh in range(4):
        b = 2 + hh % 2
        mm8(b, lambda kc: wsl[2][:, kc, hh * 128:(hh + 1) * 128], lambda kc: hT[:, kc, T:TT], NS,
            [("hT", 4), ("wsl", 2)])
        P.act(lambda e, hh=hh, b=b: e.activation(out=zsT[:, hh, :], in_=ps[b][:, 0:NS], func=AF.Silu), r=[PK(b)],
              w=["zsT"])

    stop_at(4)
    build_rest(nc, P, A, locals())
    return nc, P


def build_rest(nc, P, A, L):
    g = dict(L)
    from types import SimpleNamespace
    V = SimpleNamespace(**g)
    ps, PK, psb, cst, pc, idb, oneb = V.ps, V.PK, V.psb, V.cst, V.pc, V.idb, V.oneb
    ident_f, Uf, SLf, onef = V.ident_f, V.Uf, V.SLf, V.onef
    hT, cT, qT, kT, ktok, vb, zs, zsT, vTs, gtok, btok, gbS = (V.hT, V.cT, V.qT, V.kT, V.ktok, V.vb, V.zs, V.zsT,
                                                                 V.vTs, V.gtok, V.btok, V.gbS)
    load_w, load_gam, mm8, gam = V.load_w, V.load_gam, V.mm8, V.gam
    pr_d = V.pr_d

    P.barrier()
    A.reset(V.m_w)
    dT = hT

    f32t = lambda name, shape=(128, 512): A.t(list(shape), F32, name)
    bft = lambda name, shape=(128, 512): A.t(list(shape), BF16, name)
    dn4 = f32t("dn4")
    P.dma(dn4[:], pr_d[:, PR_DN4:PR_DN4 + 512], w=["dn4"], tag="c3")
    S = f32t("S")
    Sb = bft("Sb")
    P.dve(lambda e: e.memset(S[:], 0.0), w=["S"])
    P.dve(lambda e: e.memset(Sb[:], 0.0), w=["Sb"])
    e3_2 = [f32t("e3_%d" % i, (128, 16)) for i in range(2)]
    gSL = f32t("gSL")
    E = f32t("E")
    ET = f32t("ET")
    EBbd = f32t("EBbd")
    EBoff = bft("EBoff")
    ETm = bft("ETm")
    Y = [bft("Y0"), bft("Y1")]
    YT = [bft("YT0"), bft("YT1")]
    PT = [bft("PT0"), bft("PT1")]
    Loff = bft("Loff")
    Tbd = bft("Tbd")
    Xb = bft("Xb")
    TTm = bft("TTm")
    kbg = bft("kbg")
    kdec_2 = [bft("kdec%d" % i) for i in range(2)]
    qkT_2 = [bft("qkT%d" % i) for i in range(2)]
    wT_2 = [bft("wT%d" % i) for i in range(2)]
    u_2 = [f32t("u_sb%d" % i) for i in range(2)]
    vnew = bft("vnew")
    o_sb = f32t("o_sb")
    qS_sb = f32t("qS_sb")
    bg_m = f32t("bg_m", (128, 8))
    bg_c = f32t("bg_c", (128, 8))
    dtok = bft("dtok")
    osq = V.nrm_junk
    B4 = lambda ap: ap.unsqueeze(1).to_broadcast([128, 4, 128])
    H4 = lambda ap: ap.rearrange("p (h n) -> p h n", h=4)
    mBD = B4(cst[:, C_BD:C_BD + 128])
    mOFF = B4(cst[:, C_OFF:C_OFF + 128])
    mTIN = B4(cst[:, C_TIN:C_TIN + 128])
    mBD4 = f32t("mBD4")
    P.pool(lambda e: e.tensor_copy(out=H4(mBD4[:]), in_=mBD), r=["cst"], w=["mBD4"])
    HS = [slice(h * 128, (h + 1) * 128) for h in range(4)]

    def make_tile(t):
        tk = slice(t * 128, (t + 1) * 128)
        p = t % 2
        e3, e3k = e3_2[p], ("e3", p)
        egc, erem, etot = e3[:, 0:4], e3[:, 4:8], e3[:, 8:12]
        qkT, qkk = qkT_2[p], ("qkT", p)
        wT, wTk = wT_2[p], ("wT", p)
        u_sb, uk = u_2[p], ("u_sb", p)
        kdec, kdk = kdec_2[p], ("kdec", p)
        gt = gtok[:, t, :]
        st = {"cur": 0}

        def A_():
            P.pe(lambda e: e.matmul(ps[0][:, 0:4], lhsT=Uf, rhs=gt, start=True, stop=True), r=["gtok", "cst"],
                 w=[PK(0)])
            P.pe(lambda e: e.matmul(ps[0][:, 4:8], lhsT=SLf, rhs=gt, start=True, stop=True), r=["gtok", "cst"],
                 w=[PK(0)])
            P.pe(lambda e: e.matmul(ps[0][:, 8:12], lhsT=onef, rhs=gt, start=True, stop=True), r=["gtok", "cst"],
                 w=[PK(0)])
            P.act(lambda e: e.activation(out=e3[:, 0:12], in_=ps[0][:, 0:12], func=AF.Exp), r=[PK(0)], w=[e3k])
            for h in range(4):
                P.dve(lambda e, h=h: e.tensor_scalar(out=gSL[:, HS[h]], in0=SLf, scalar1=gt[:, h:h + 1],
                                                     scalar2=None, op0=ALU.mult), r=["gtok", "cst"], w=["gSL"])
            P.pe(lambda e: e.matmul(ps[1][:, :], lhsT=Uf, rhs=gSL[:, :], start=True, stop=True), r=["gSL", "cst"],
                 w=[PK(1)])
            for h in range(4):
                P.pe(lambda e, h=h: e.matmul(ps[2][:, HS[h]], lhsT=gSL[:, HS[h]], rhs=Uf, start=True, stop=True),
                     r=["gSL", "cst"], w=[PK(2)])
            P.act(lambda e: e.activation(out=E[:], in_=ps[1][:, :], func=AF.Exp), r=[PK(1)], w=["E"])
            P.act(lambda e: e.activation(out=ET[:], in_=ps[2][:, :], func=AF.Exp), r=[PK(2)], w=["ET"])
            for h in range(4):
                P.dve(lambda e, h=h: e.tensor_scalar(out=E[:, HS[h]], in0=E[:, HS[h]], scalar1=btok[:, t, h:h + 1],
                                                     scalar2=None, op0=ALU.mult), r=["E", "btok"], w=["E"])
            P.dve(lambda e: e.tensor_tensor(out=EBbd[:], in0=E[:], in1=mBD4[:], op=ALU.mult), r=["E", "mBD4"],
                  w=["EBbd"])
            P.pool(lambda e: e.tensor_tensor(out=H4(EBoff[:]), in0=H4(E[:]), in1=mOFF, op=ALU.mult), r=["E", "cst"],
                   w=["EBoff"])
            P.pool(lambda e: e.tensor_tensor(out=H4(ETm[:]), in0=H4(ET[:]), in1=mTIN, op=ALU.mult), r=["ET", "cst"],
                   w=["ETm"])
            for h in range(4):
                P.pe(lambda e, h=h: e.matmul(ps[3][:, HS[h]], lhsT=kT[:, h, tk], rhs=kT[:, h, tk], start=True,
                                             stop=True), r=["kT"], w=[PK(3)])
            for h in range(4):
                P.pe(lambda e, h=h: e.matmul(ps[4][:, HS[h]], lhsT=kT[:, h, tk], rhs=qT[:, h, tk], start=True,
                                             stop=True), r=["kT", "qT"], w=[PK(4)])
            P.dve(lambda e: e.tensor_tensor(out=Y[0][:], in0=ps[3][:, :], in1=EBbd[:], op=ALU.mult),
                  r=[PK(3), "EBbd"], w=[("Y", 0)])
            pv = psb(5)
            for h in range(4):
                P.pe(lambda e, h=h: e.transpose(out=pv[:, HS[h]], in_=Y[0][:, HS[h]], identity=idb[:]),
                     r=[("Y", 0), "idb"], w=[PK(5)])
            P.act(lambda e: e.activation(out=YT[0][:], in_=pv[:, 0:512], func=AF.Copy), r=[PK(5)], w=[("YT", 0)])
            P.pool(lambda e: e.tensor_tensor(out=H4(PT[0][:]), in0=H4(YT[0][:]), in1=B4(ident_f), op=ALU.add),
                   r=[("YT", 0), "cst"], w=[("PT", 0)])
            P.dve(lambda e: e.tensor_tensor(out=Loff[:], in0=ps[3][:, :], in1=EBoff[:], op=ALU.mult),
                  r=[PK(3), "EBoff"], w=["Loff"])
            P.dve(lambda e: e.tensor_tensor(out=qkT[:], in0=ps[4][:, :], in1=ETm[:], op=ALU.mult),
                  r=[PK(4), "ETm"], w=[qkk])
            st["cur"] = 0

        def N_(m):
            cur = st["cur"]
            nx = 1 - cur
            for h in range(4):
                P.pe(lambda e, h=h: e.matmul(ps[6][:, HS[h]], lhsT=YT[cur][:, HS[h]], rhs=Y[cur][:, HS[h]],
                                             start=True, stop=True), r=[("Y", cur), ("YT", cur)], w=[PK(6)])
            if m < 5:
                for h in range(4):
                    P.pe(lambda e, h=h: e.matmul(ps[7][:, HS[h]], lhsT=Y[cur][:, HS[h]], rhs=YT[cur][:, HS[h]],
                                                 start=True, stop=True), r=[("Y", cur), ("YT", cur)], w=[PK(7)])
            P.act(lambda e: e.activation(out=Y[nx][:], in_=ps[6][:, :], func=AF.Copy), r=[PK(6)], w=[("Y", nx)])
            if m < 5:
                P.dve(lambda e: e.tensor_copy(out=YT[nx][:], in_=ps[7][:, :]), r=[PK(7)], w=[("YT", nx)])
            for h in range(4):
                P.pe(lambda e, h=h: e.matmul(ps[5][:, HS[h]], lhsT=Y[nx][:, HS[h]], rhs=PT[cur][:, HS[h]],
                                             start=True, stop=True), r=[("Y", nx), ("PT", cur)], w=[PK(5)])
            P.dve(lambda e: e.tensor_tensor(out=PT[nx][:], in0=ps[5][:, :], in1=PT[cur][:], op=ALU.add),
                  r=[PK(5), ("PT", cur)], w=[("PT", nx)])
            st["cur"] = nx

        def M_():
            cur = st["cur"]
            PTf, ptk = PT[cur], ("PT", cur)
            pv = psb(6)
            for h in range(4):
                P.pe(lambda e, h=h: e.transpose(out=pv[:, HS[h]], in_=PTf[:, HS[h]], identity=idb[:]),
                     r=[ptk, "idb"], w=[PK(6)])
            P.act(lambda e: e.activation(out=Tbd[:], in_=pv[:, 0:512], func=AF.Copy), r=[PK(6)], w=["Tbd"])
            for h in range(4):
                P.pe(lambda e, h=h: e.matmul(ps[7][:, HS[h]], lhsT=Loff[:, HS[h]], rhs=PTf[:, HS[h]], start=True,
                                             stop=True), r=["Loff", ptk], w=[PK(7)])
            P.act(lambda e: e.activation(out=Xb[:], in_=ps[7][:, :], func=AF.Copy), r=[PK(7)], w=["Xb"])
            for h in range(4):
                P.pe(lambda e, h=h: e.matmul(ps[5][:, HS[h]], lhsT=Tbd[:, HS[h]], rhs=Xb[:, HS[h]], start=True,
                                             stop=True), r=["Tbd", "Xb"], w=[PK(5)])
            P.dve(lambda e: e.tensor_tensor(out=TTm[:], in0=PTf[:], in1=ps[5][:, :], op=ALU.subtract),
                  r=[PK(5), ptk], w=["TTm"])
            P.dve(lambda e: e.tensor_tensor(out=bg_m[:, 0:4], in0=btok[:, t, :], in1=egc, op=ALU.mult),
                  r=["btok", e3k], w=["bg_m"])
            for h in range(4):
                P.dve(lambda e, h=h: e.tensor_scalar(out=kbg[:, HS[h]], in0=ktok[:, t, HS[h]],
                                                     scalar1=bg_m[:, h:h + 1], scalar2=None, op0=ALU.mult),
                      r=["ktok", "bg_m"], w=["kbg"])
                P.dve(lambda e, h=h: e.tensor_scalar(out=kdec[:, HS[h]], in0=ktok[:, t, HS[h]],
                                                     scalar1=erem[:, h:h + 1], scalar2=None, op0=ALU.mult),
                      r=["ktok", e3k], w=[kdk])
            for h in range(4):
                P.pe(lambda e, h=h: e.matmul(ps[0][:, HS[h]], lhsT=TTm[:, HS[h]], rhs=vb[:, t, HS[h]], start=True,
                                             stop=True), r=["TTm", "vb"], w=[PK(0)])
            for h in range(4):
                P.pe(lambda e, h=h: e.matmul(ps[1][:, HS[h]], lhsT=kbg[:, HS[h]], rhs=TTm[:, HS[h]], start=True,
                                             stop=True), r=["TTm", "kbg"], w=[PK(1)])
            P.act(lambda e: e.activation(out=u_sb[:], in_=ps[0][:, :], func=AF.Copy), r=[PK(0)], w=[uk])
            P.act(lambda e: e.activation(out=wT[:], in_=ps[1][:, :], func=AF.Copy), r=[PK(1)], w=[wTk])

        def C1():
            for h in range(4):
                P.pe(lambda e, h=h: e.matmul(ps[2][:, HS[h]], lhsT=wT[:, HS[h]], rhs=Sb[:, HS[h]], start=True,
                                             stop=True), r=[wTk, "Sb"], w=[PK(2)])
            for h in range(4):
                P.pe(lambda e, h=h: e.matmul(ps[3][:, HS[h]], lhsT=qT[:, h, tk], rhs=Sb[:, HS[h]], start=True,
                                             stop=True), r=["qT", "Sb"], w=[PK(3)])
            P.dve(lambda e: e.tensor_tensor(out=vnew[:], in0=u_sb[:], in1=ps[2][:, :], op=ALU.subtract),
                  r=[uk, PK(2)], w=["vnew"])

        def C2():
            for h in range(4):
                P.pe(lambda e, h=h: e.matmul(ps[4][:, HS[h]], lhsT=qkT[:, HS[h]], rhs=vnew[:, HS[h]], start=True,
                                             stop=True), r=[qkk, "vnew"], w=[PK(4)])
            for h in range(4):
                P.pe(lambda e, h=h: e.matmul(ps[2][:, HS[h]], lhsT=kdec[:, HS[h]], rhs=vnew[:, HS[h]], start=True,
                                             stop=True), r=[kdk, "vnew"], w=[PK(2)])
            for h in range(4):
                P.act(lambda e, h=h: e.activation(out=qS_sb[:, HS[h]], in_=ps[3][:, HS[h]], func=AF.Copy,
                                                  scale=egc[:, h:h + 1]), r=[PK(3), e3k], w=["qS_sb"])

        def C3():
            for h in range(4):
                P.dve(lambda e, h=h: e.scalar_tensor_tensor(out=S[:, HS[h]], in0=S[:, HS[h]],
                                                            scalar=etot[:, h:h + 1], in1=ps[2][:, HS[h]],
                                                            op0=ALU.mult, op1=ALU.add),
                      r=["S", e3k, PK(2)], w=["S"])
            P.act(lambda e: e.activation(out=Sb[:], in_=S[:], func=AF.Copy), r=["S"], w=["Sb"])
            P.dve(lambda e: e.tensor_tensor(out=o_sb[:], in0=qS_sb[:], in1=ps[4][:, :], op=ALU.add),
                  r=["qS_sb", PK(4)], w=["o_sb"])

        def C4():
            for h in range(4):
                P.act(lambda e, h=h: e.activation(out=osq[:, HS[h]], in_=o_sb[:, HS[h]], func=AF.Square,
                                                  accum_out=bg_c[:, 4 + h:5 + h]), r=["o_sb"], w=["njunk", "bg_c"])
            P.dve(lambda e: e.tensor_scalar(out=bg_c[:, 4:8], in0=bg_c[:, 4:8], scalar1=1.0 / 128, scalar2=1e-6,
                                            op0=ALU.mult, op1=ALU.add), r=["bg_c"], w=["bg_c"])
            P.act(lambda e: e.activation(out=bg_c[:, 4:8], in_=bg_c[:, 4:8], func=AF.Ln), r=["bg_c"], w=["bg_c"])
            P.act(lambda e: e.activation(out=bg_c[:, 4:8], in_=bg_c[:, 4:8], func=AF.Exp, scale=-0.5), r=["bg_c"],
                  w=["bg_c"])

        def C5():
            for h in range(4):
                P.dve(lambda e, h=h: e.scalar_tensor_tensor(out=o_sb[:, HS[h]], in0=o_sb[:, HS[h]],
                                                            scalar=bg_c[:, 4 + h:5 + h], in1=dn4[:, HS[h]],
                                                            op0=ALU.mult, op1=ALU.mult),
                      r=["o_sb", "bg_c", "dn4"], w=["o_sb"])
            P.dve(lambda e: e.tensor_tensor(out=dtok[:], in0=o_sb[:], in1=zs[:, t, :], op=ALU.mult),
                  r=["o_sb", "zs"], w=["dtok"])
            pv = psb(3)
            for h in range(4):
                P.pe(lambda e, h=h: e.transpose(out=pv[:, HS[h]], in_=dtok[:, HS[h]], identity=idb[:]),
                     r=["dtok", "idb"], w=[PK(3)])
            P.act(lambda e: e.activation(out=dT[:, 0:4, t * 128:(t + 1) * 128],
                                         in_=pv[:, 0:512].rearrange("p (h n) -> p h n", h=4), func=AF.Copy),
                  r=[PK(3)], w=["dT"])

        return A_, N_, M_, [C1, C2, C3, C4, C5]

    tiles = [make_tile(t) for t in range(NT)]
    A0, N0, M0, _ = tiles[0]
    A0()
    for m in range(1, 6):
        N0(m)
    M0()
    for t in range(NT):
        Cs = tiles[t][3]
        if t + 1 < NT:
            An, Nn, Mn, _ = tiles[t + 1]
            An()
            for m in range(1, 6):
                Nn(m)
                Cs[m - 1]()
            Mn()
        else:
            for c in Cs:
                c()
    P.dma(V.ndel_p.rearrange("h d e -> d h e"), H4(S[:]), r=["S"], tag="out")

    stop_at(5)
    P.barrier()
    A.reset(V.m_w)
    sel4f = A.t([4, 512], F32, "sel4f")
    P.pool(lambda e: e.tensor_copy(out=sel4f[:].rearrange("p (s m) -> p s m", s=4),
                                   in_=cst[0:4, C_OH16:C_OH16 + 4].unsqueeze(2).to_broadcast([4, 4, 128])),
           r=["cst"], w=["sel4f"])
    sel4 = sel4f[0:4, :]
    egS = f32t("egS", (4, NS))
    P.act(lambda e: e.activation(out=egS[:], in_=gbS[:, 1, :], func=AF.Exp), r=["gbS"], w=["egS"])
    Bbc = f32t("Bbc", (128, 4, NS))
    EGbc = f32t("EGbc", (128, 4, NS))
    for h in range(4):
        P.pe(lambda e, h=h: e.matmul(ps[0][:, h * NS:(h + 1) * NS], lhsT=sel4[:, h * 128:(h + 1) * 128],
                                     rhs=gbS[:, 0, :], start=True, stop=True), r=["gbS", "sel4f"], w=[PK(0)])
        P.pe(lambda e, h=h: e.matmul(ps[1][:, h * NS:(h + 1) * NS], lhsT=sel4[:, h * 128:(h + 1) * 128],
                                     rhs=egS[:, :], start=True, stop=True), r=["egS", "sel4f"], w=[PK(1)])
    P.act(lambda e: e.activation(out=Bbc[:].rearrange("p h s -> p (h s)"), in_=ps[0][:, 0:64], func=AF.Copy),
          r=[PK(0)], w=["Bbc"])
    P.act(lambda e: e.activation(out=EGbc[:].rearrange("p h s -> p (h s)"), in_=ps[1][:, 0:64], func=AF.Copy),
          r=[PK(1)], w=["EGbc"])
    qkf = f32t("qkf", (128, 4, 2, NS))
    P.dve(lambda e: e.tensor_copy(out=qkf[:, :, 0, :], in_=kT[:, :, T:TT]), r=["kT"], w=["qkf"])
    P.dve(lambda e: e.tensor_copy(out=qkf[:, :, 1, :], in_=qT[:, :, T:TT]), r=["qT"], w=["qkf"])
    S0all = f32t("S0all", (128, NS, 4, 128))
    for s in range(NS):
        P.dma(S0all[:, s, :, :], V.sdel[s * 4:(s + 1) * 4, :, :].rearrange("h d e -> d h e"), w=[("S0", s)],
              tag="S0g%d" % (s // 4), group=True)
    for s in range(NS):
        for h in range(4):
            P.pe(lambda e, s=s, h=h: e.matmul(ps[2][:, (s * 4 + h) * 2:(s * 4 + h) * 2 + 2],
                                              lhsT=S0all[:, s, h, :], rhs=qkf[:, h, :, s], start=True,
                                              stop=True), r=[("S0", s), "qkf"], w=[PK(2)])
    kqS = f32t("kqS", (128, NS, 4, 2))
    P.act(lambda e: e.activation(out=kqS[:].rearrange("p s h t -> p (s h t)"), in_=ps[2][:, 0:128], func=AF.Copy),
          r=[PK(2)], w=["kqS"])
    kSv = kqS[:, :, :, 0].rearrange("p s h -> p h s")
    qSv = kqS[:, :, :, 1].rearrange("p s h -> p h s")
    vn = f32t("vn", (128, 4, NS))
    tmpS = f32t("tmpS", (128, 4, NS))
    P.dve(lambda e: e.tensor_tensor(out=tmpS[:], in0=EGbc[:], in1=kSv, op=ALU.mult), r=["EGbc", "kqS"], w=["tmpS"])
    P.dve(lambda e: e.tensor_tensor(out=vn[:], in0=vTs[:], in1=tmpS[:], op=ALU.subtract), r=["vTs", "tmpS"],
          w=["vn"])
    P.dve(lambda e: e.tensor_tensor(out=vn[:], in0=vn[:], in1=Bbc[:], op=ALU.mult), r=["vn", "Bbc"], w=["vn"])
    prodS = f32t("prodS", (128, 4, NS))
    P.dve(lambda e: e.tensor_tensor(out=prodS[:], in0=qkf[:, :, 0, :], in1=qkf[:, :, 1, :], op=ALU.mult),
          r=["qkf"], w=["prodS"])
    P.pe(lambda e: e.matmul(ps[3][:, 0:64], lhsT=onef, rhs=prodS[:].rearrange("p h s -> p (h s)"), start=True,
                            stop=True), r=["prodS", "cst"], w=[PK(3)])
    oS = f32t("oS", (128, 4, NS))
    P.dve(lambda e: e.tensor_tensor(out=oS[:].rearrange("p h s -> p (h s)"), in0=ps[3][:, 0:64],
                                    in1=vn[:].rearrange("p h s -> p (h s)"), op=ALU.mult), r=[PK(3), "vn"],
          w=["oS"])
    P.dve(lambda e: e.tensor_tensor(out=tmpS[:], in0=EGbc[:], in1=qSv, op=ALU.mult), r=["EGbc", "kqS"], w=["tmpS"])
    P.dve(lambda e: e.tensor_tensor(out=oS[:], in0=oS[:], in1=tmpS[:], op=ALU.add), r=["oS", "tmpS"], w=["oS"])
    P.dve(lambda e: e.tensor_tensor(out=tmpS[:], in0=oS[:], in1=oS[:], op=ALU.mult), r=["oS"], w=["tmpS"])
    P.pe(lambda e: e.matmul(ps[4][:, 0:64], lhsT=onef, rhs=tmpS[:].rearrange("p h s -> p (h s)"), start=True,
                            stop=True), r=["tmpS", "cst"], w=[PK(4)])
    rS = f32t("rS", (128, 64))
    P.dve(lambda e: e.tensor_scalar(out=rS[:], in0=ps[4][:, 0:64], scalar1=1.0 / 128, scalar2=1e-6, op0=ALU.mult,
                                    op1=ALU.add), r=[PK(4)], w=["rS"])
    P.act(lambda e: e.activation(out=rS[:], in_=rS[:], func=AF.Sqrt), r=["rS"], w=["rS"])
    P.dve(lambda e: e.reciprocal(out=rS[:], in_=rS[:]), r=["rS"], w=["rS"])
    P.dve(lambda e: e.tensor_tensor(out=oS[:].rearrange("p h s -> p (h s)"), in0=oS[:].rearrange("p h s -> p (h s)"),
                                    in1=rS[:], op=ALU.mult), r=["oS", "rS"], w=["oS"])
    P.dve(lambda e: e.scalar_tensor_tensor(out=dT[:, 0:4, T:TT], in0=oS[:], scalar=pc[:, PC_DN:PC_DN + 1],
                                           in1=zsT[:], op0=ALU.mult, op1=ALU.mult), r=["oS", "pc", "zsT"],
          w=["dT"])
    ktS = f32t("ktS", (16, 512))
    vtS = f32t("vtS", (16, 512))
    for h in range(4):
        P.pe(lambda e, h=h: e.transpose(out=ps[5][0:NS, h * 128:(h + 1) * 128], in_=qkf[:, h, 0, :],
                                        identity=ident_f), r=["qkf", "cst"], w=[PK(5)])
        P.pe(lambda e, h=h: e.transpose(out=ps[6][0:NS, h * 128:(h + 1) * 128], in_=vn[:, h, :], identity=ident_f),
             r=["vn", "cst"], w=[PK(6)])
    P.act(lambda e: e.activation(out=ktS[:], in_=ps[5][0:NS, :], func=AF.Copy), r=[PK(5)], w=["ktS"])
    P.act(lambda e: e.activation(out=vtS[:], in_=ps[6][0:NS, :], func=AF.Copy), r=[PK(6)], w=["vtS"])
    vmask = [f32t("vmask%d" % i, (16, 512)) for i in range(2)]
    oh16 = cst[0:16, C_OH16:C_OH16 + 16]
    for s in range(NS):
        sl = s % 2
        P.dve(lambda e, s=s, sl=sl: e.tensor_scalar(out=vmask[sl][:], in0=vtS[:], scalar1=oh16[:, s:s + 1],
                                                    scalar2=None, op0=ALU.mult), r=["vtS", "cst"],
              w=[("vmask", sl)])
        b = 7 if sl else 0
        for h in range(4):
            hs = slice(h * 128, (h + 1) * 128)
            P.pe(lambda e, hs=hs, sl=sl, b=b: e.matmul(ps[b][:, hs], lhsT=ktS[:, hs], rhs=vmask[sl][:, hs],
                                                       start=True, stop=True), r=["ktS", ("vmask", sl)], w=[PK(b)])
        for h in range(4):
            hs = slice(h * 128, (h + 1) * 128)
            P.dve(lambda e, hs=hs, h=h, s=s, b=b: e.scalar_tensor_tensor(
                out=S0all[:, s, h, :], in0=S0all[:, s, h, :], scalar=EGbc[:, h, s:s + 1], in1=ps[b][:, hs],
                op0=ALU.mult, op1=ALU.add), r=["EGbc", PK(b)], w=[("S0", s)])
        P.dma(V.ndel_s[s * 4:(s + 1) * 4, :, :].rearrange("h d e -> d h e"), S0all[:, s, :, :], r=[("S0", s)],
              tag="out")

    stop_at(6)
    P.barrier()
    mX, mXe = V.mX, V.mXe
    x1 = A.child(mX, mXe).t([128, NT + 1, 1024], F32, "x1")
    A.reset(mXe)
    wo = A.t([128, 8, 512], BF16, "wo_a")
    wo2 = A.t([128, 8, 512], BF16, "wo_b")
    xin = [A.t([128, 1024], F32, "xin2_%d" % i) for i in range(2)]
    load_w(wo, "wo", V.w_out, 0, 8, 0, 512)
    load_w(wo2, "wo2", V.w_out, 0, 8, 512, 512)
    for t in range(NT + 1):
        n = 128 if t < NT else NS
        c0 = t * 128
        s = t % 2
        src_d = V.x_p[t * 128:(t + 1) * 128, :] if t < NT else V.x_s[:, :]
        P.dma(xin[s][:n, :], src_d, w=[("xin", s)], tag="xin%d" % s)
        for half, wv in ((0, wo), (1, wo2)):
            b = (t % 2) * 2 + half
            for kc in range(8):
                src = cT[:, kc, c0:c0 + n] if kc < 4 else dT[:, kc - 4, c0:c0 + n]
                P.pe(lambda e, b=b, kc=kc, src=src, wv=wv, n=n: e.matmul(ps[b][:n, :], lhsT=src, rhs=wv[:, kc, :],
                                                                        start=(kc == 0), stop=(kc == 7)),
                     r=["cT", "dT", "wo", "wo2"], w=[PK(b)])
            P.dve(lambda e, b=b, t=t, n=n, half=half, s=s: e.tensor_tensor(
                out=x1[:n, t, half * 512:(half + 1) * 512], in0=ps[b][:n, :],
                in1=xin[s][:n, half * 512:(half + 1) * 512], op=ALU.add), r=[PK(b), ("xin", s)], w=[("x1", t)])

    stop_at(7)
    P.barrier()
    A.reset(mXe)
    AH = A.child(V.off_cT, mX)
    kmT = AH.t([128, 8, 256], BF16, "kmT")
    vmem = AH.t([128, 2, 1024], BF16, "vmem")
    mT = AH.t([128, 8, 256], BF16, "mT")
    hT3 = hT
    wk2 = [A.t([128, 8, 512], BF16, "wk%d" % i) for i in range(2)]
    mo_f = A.t([128, 1024], F32, "mo_f")
    min_ = [A.t([128, 1024], F32, "min%d" % i) for i in range(2)]
    load_gam(PR_GKV)
    items = []
    for mt in range(2):
        pre = (lambda mt=mt: P.dma(min_[mt][:], V.mem[mt * 128:(mt + 1) * 128, :], w=[("min", mt)],
                                   tag="min%d" % mt))
        items.append((min_[mt][:, :], [("min", mt)], 128, mt * 128, pre))
    V.norm_pass(items, mT, "mT")
    kvl = [(0, V.w_mk, V.mk_p, 0), (0, V.w_mk, V.mk_p, 1), (1, V.w_mv, V.mv_p, 0), (1, V.w_mv, V.mv_p, 1)]
    load_w(wk2[0], ("wk", 0), V.w_mk, 0, 8, 0, 512)
    for li, (which, wd, od, half) in enumerate(kvl):
        if True:
            wk = wk2[li % 2]
            wkk = ("wk", li % 2)
            if li + 1 < 4:
                load_w(wk2[(li + 1) % 2], ("wk", (li + 1) % 2), kvl[li + 1][1], 0, 8, kvl[li + 1][3] * 512, 512)
            for mt in range(2):
                b = 2 + mt
                for kc in range(8):
                    P.pe(lambda e, b=b, kc=kc, mt=mt: e.matmul(ps[b][:, :], lhsT=mT[:, kc, mt * 128:(mt + 1) * 128],
                                                               rhs=wk[:, kc, :], start=(kc == 0), stop=(kc == 7)),
                         r=[("mT", 0), wkk], w=[PK(b)])
                P.act(lambda e, b=b, half=half: e.activation(out=mo_f[:, half * 512:(half + 1) * 512],
                                                             in_=ps[b][:, :], func=AF.Copy), r=[PK(b)], w=["mo_f"])
                P.dma(od[mt * 128:(mt + 1) * 128, half * 512:(half + 1) * 512],
                      mo_f[:, half * 512:(half + 1) * 512], r=["mo_f"], tag="omo")
                if which == 1:
                    P.dve(lambda e, b=b, half=half, mt=mt: e.tensor_copy(
                        out=vmem[:, mt, half * 512:(half + 1) * 512], in_=ps[b][:, :]), r=[PK(b)], w=["vmem"])
            if which == 0:
                for c4 in range(4):
                    b = 4 + c4 % 2
                    for kc in range(8):
                        P.pe(lambda e, b=b, kc=kc, c4=c4: e.matmul(ps[b][:, 0:256],
                                                                   lhsT=wk[:, kc, c4 * 128:(c4 + 1) * 128],
                                                                   rhs=mT[:, kc, :], start=(kc == 0), stop=(kc == 7)),
                             r=[("mT", 0), wkk], w=[PK(b)])
                    P.act(lambda e, b=b, c4=c4, half=half: e.activation(out=kmT[:, half * 4 + c4, :],
                                                                        in_=ps[b][:, 0:256], func=AF.Copy),
                          r=[PK(b)], w=["kmT"])
    stop_at(8)
    P.barrier()
    A.reset(mXe)
    qa = A.t([128, 8, TT], BF16, "qa")
    m_qa = A.mark()
    wq = A.t([128, 8, 512], BF16, "wq")
    wq2 = A.t([128, 8, 512], BF16, "wq2")
    load_gam(PR_GMQ)
    V.norm_pass([(x1[:(128 if t < NT else NS), t, :], [("x1", t)], (128 if t < NT else NS), t * 128, None)
                 for t in range(NT + 1)], hT3, "hT")
    load_w(wq, "wq", V.w_mq, 0, 8, 0, 512)
    load_w(wq2, "wq2", V.w_mq, 0, 8, 512, 512)
    BLK = [(i * 512, 512) for i in range(4)] + [(T, NS)]
    qi = 0
    for c in range(8):
        wv = wq if c < 4 else wq2
        cc = c % 4
        for (t0, n) in BLK:
            b = qi % 2
            qi += 1
            mm8(b, lambda kc: wv[:, kc, cc * 128:(cc + 1) * 128], lambda kc: hT3[:, kc, t0:t0 + n], n,
                [("hT", t0 // 512), "wq", "wq2"])
            P.act(lambda e, b=b, c=c, n=n, t0=t0: e.activation(out=qa[:, c, t0:t0 + n], in_=ps[b][:, 0:n],
                                                               func=AF.Copy, scale=1.0 / 16), r=[PK(b)], w=["qa"])
    stop_at(9)
    P.barrier()
    A.reset(m_qa)
    aoT = hT
    qs_s = AH.t([128, 8, NS], BF16, "qs_s")
    P.dve(lambda e: e.tensor_copy(out=qs_s[:], in_=qa[:, :, T:TT]), r=["qa"], w=["qs_s"])
    m_3c = A.mark()
    msm = [A.t([128, 16], F32, "msm%d" % i) for i in range(2)]
    ex = [A.t([128, 1024], F32, "ex%d" % i) for i in range(2)]
    pbf = [A.t([128, 1024], BF16, "pbf%d" % i) for i in range(2)]
    ptb = [A.t([128, 8, 128], BF16, "ptb%d" % i) for i in range(2)]
    iters = []
    for tg in range(NT):
        def S0(tg=tg):
            p = tg % 2
            lt = slice(tg * 128, (tg + 1) * 128)
            sb = (2 * p, 2 * p + 1)
            m_, mk_ = msm[p], ("msm", p)
            for h in range(4):
                b = sb[h // 2]
                cs = slice((h % 2) * 256, (h % 2) * 256 + 256)
                for hf in range(2):
                    P.pe(lambda e, b=b, cs=cs, h=h, hf=hf: e.matmul(ps[b][:, cs], lhsT=qa[:, h * 2 + hf, lt],
                                                                    rhs=kmT[:, h * 2 + hf, :], start=(hf == 0),
                                                                    stop=(hf == 1)), r=["qa", "kmT"], w=[PK(b)])
            for j in range(2):
                P.dve(lambda e, j=j: e.tensor_reduce(out=m_[:, 12 + 2 * j:12 + 2 * j + 2],
                                                     in_=ps[sb[j]][:, :].rearrange("p (h m) -> p h m", h=2),
                                                     axis=AX.X, op=ALU.max, negate=True), r=[PK(sb[j])], w=[mk_])

        def S1(tg=tg):
            p = tg % 2
            sb = (2 * p, 2 * p + 1)
            m_, mk_ = msm[p], ("msm", p)
            ex_, exk = ex[p], ("ex", p)
            pb_, pbk = pbf[p], ("pbf", p)
            for h in range(4):
                P.act(lambda e, h=h: e.activation(out=ex_[:, h * 256:(h + 1) * 256],
                                                  in_=ps[sb[h // 2]][:, (h % 2) * 256:(h % 2) * 256 + 256],
                                                  func=AF.Exp, bias=m_[:, 12 + h:13 + h],
                                                  accum_out=m_[:, 4 + h:5 + h]),
                      r=[PK(sb[h // 2]), mk_], w=[exk, mk_])
            P.dve(lambda e: e.reciprocal(out=m_[:, 8:12], in_=m_[:, 4:8]), r=[mk_], w=[mk_])
            for h in range(4):
                P.dve(lambda e, h=h: e.tensor_scalar(out=pb_[:, h * 256:(h + 1) * 256],
                                                     in0=ex_[:, h * 256:(h + 1) * 256], scalar1=m_[:, 8 + h:9 + h],
                                                     scalar2=None, op0=ALU.mult), r=[exk, mk_], w=[pbk])

        def S2(tg=tg):
            p = tg % 2
            pb_, pbk = pbf[p], ("pbf", p)
            pt_, ptk = ptb[p], ("ptb", p)
            pv = psb(4)
            for c8 in range(8):
                P.pe(lambda e, c8=c8: e.transpose(out=pv[:, c8 * 128:(c8 + 1) * 128],
                                                  in_=pb_[:, c8 * 128:(c8 + 1) * 128], identity=idb[:]),
                     r=[pbk, "idb"], w=[PK(4)])
            P.act(lambda e: e.activation(out=pt_[:].rearrange("p a b -> p (a b)"), in_=pv[:, 0:1024], func=AF.Copy),
                  r=[PK(4)], w=[ptk])

        def S3(tg=tg):
            p = tg % 2
            lt = slice(tg * 128, (tg + 1) * 128)
            pt_, ptk = ptb[p], ("ptb", p)
            for h in range(4):
                ob = 5 + h // 2
                for hf in range(2):
                    cs = slice(((h % 2) * 2 + hf) * 128, ((h % 2) * 2 + hf + 1) * 128)
                    for mt in range(2):
                        P.pe(lambda e, ob=ob, cs=cs, h=h, hf=hf, mt=mt: e.matmul(
                            ps[ob][:, cs], lhsT=vmem[:, mt, h * 256 + hf * 128:h * 256 + (hf + 1) * 128],
                            rhs=pt_[:, h * 2 + mt, :], start=(mt == 0), stop=(mt == 1)), r=["vmem", ptk],
                            w=[PK(ob)])
            P.act(lambda e: e.activation(out=aoT[:, 0:4, lt], in_=ps[5][:, :].rearrange("p (a n) -> p a n", a=4),
                                         func=AF.Copy), r=[PK(5)], w=["aoT"])
            P.dve(lambda e: e.tensor_copy(out=aoT[:, 4:8, lt], in_=ps[6][:, :].rearrange("p (a n) -> p a n", a=4)),
                  r=[PK(6)], w=["aoT"])

        iters.append([S0, S1, S2, S3])
    V.run_skewed(iters)
    P.barrier()
    A.reset(mXe)
    prod = A.t([128, 1024], F32, "prod")
    qtok = A.t([16, 1024], BF16, "qtok")
    sel16b = A.t([16, 2048], BF16, "sel16b")
    P.pool(lambda e: e.tensor_copy(out=sel16b[:].rearrange("p (s m) -> p s m", s=16),
                                   in_=cst[0:16, C_OH16:C_OH16 + 16].unsqueeze(2).to_broadcast([16, 16, 128])),
           r=["cst"], w=["sel16b"])
    NKS, NVS = 8, 6
    Kt2 = [A.t([128, 1024], BF16, "Kt%d" % i) for i in range(NKS)]
    Vt2 = [A.t([128, 2, 1024], BF16, "Vt%d" % i) for i in range(NVS)]
    scS = A.t([128, 2, 4], F32, "scS")
    sm4 = A.t([4, 256], F32, "sm4")
    sm4s = A.t([4, 8], F32, "sm4s")
    pS = A.t([128, 2, 4], BF16, "pS")
    aoS = A.t([128, 8, NS], F32, "aoS")
    pv = psb(2)
    for c in range(8):
        P.pe(lambda e, c=c: e.transpose(out=pv[0:NS, c * 128:(c + 1) * 128], in_=qs_s[:, c, :],
                                        identity=idb[:]), r=["qs_s", "idb"], w=[PK(2)])
    P.act(lambda e: e.activation(out=qtok[:], in_=pv[0:NS, 0:1024], func=AF.Copy), r=[PK(2)], w=["qtok"])
    Vt3 = Vt2
    scS2 = [scS, A.t([128, 2, 4], F32, "scSb")]
    sm42 = [sm4, A.t([4, 256], F32, "sm4b")]
    sm4s2 = [sm4s, A.t([4, 8], F32, "sm4sb")]
    iters = []
    for s in range(NS):
        def S0(s=s):
            Vt = Vt3[s % NVS]
            vk = ("Vt", s % NVS)
            sc_, sck = scS2[s % 2], ("scS", s % 2)
            for mt in range(2):
                P.dma(Vt[:, mt, :], V.cmv[s, mt * 128:(mt + 1) * 128, :], w=[vk], tag="Vt%d" % (s % NVS), q="pool")
            qb = (3, 4) if s % 2 == 0 else (0, 1)
            for half in range(2):
                P.pe(lambda e, half=half: e.matmul(ps[qb[half]][:, :], lhsT=sel16b[:, s * 128:(s + 1) * 128],
                                                   rhs=qtok[:, half * 512:(half + 1) * 512], start=True, stop=True),
                     r=["qtok", "sel16b"], w=[PK(qb[half])])
            for mt in range(2):
                kti = s * 2 + mt
                Kt = Kt2[kti % NKS]
                kk = ("Kt", kti % NKS)
                P.dma(Kt[:, :], V.cmk[s, mt * 128:(mt + 1) * 128, :], w=[kk], tag="Kt%d" % (kti % NKS), q="pool")
                for half in range(2):
                    P.dve(lambda e, half=half, Kt=Kt: e.tensor_tensor(out=prod[:, half * 512:(half + 1) * 512],
                                                                      in0=ps[qb[half]][:, :],
                                                                      in1=Kt[:, half * 512:(half + 1) * 512],
                                                                      op=ALU.mult),
                          r=[PK(qb[half]), kk], w=["prod"])
                P.dve(lambda e, mt=mt: e.tensor_reduce(out=sc_[:, mt, :],
                                                       in_=prod[:].rearrange("p (h d) -> p h d", h=4),
                                                       axis=AX.X, op=ALU.add), r=["prod"], w=[sck])

        def S1(s=s):
            sc_, sck = scS2[s % 2], ("scS", s % 2)
            m4, m4k = sm42[s % 2], ("sm4", s % 2)
            m4s, m4sk = sm4s2[s % 2], ("sm4s", s % 2)
            for mt in range(2):
                P.pe(lambda e, mt=mt: e.transpose(out=ps[5][0:4, mt * 128:(mt + 1) * 128], in_=sc_[:, mt, :],
                                                  identity=ident_f), r=[sck, "cst"], w=[PK(5)])
            P.dve(lambda e: e.tensor_reduce(out=m4s[:, 1:2], in_=ps[5][0:4, 0:256], axis=AX.X, op=ALU.max,
                                            negate=True), r=[PK(5)], w=[m4sk])
            P.act(lambda e: e.activation(out=m4[:], in_=ps[5][0:4, 0:256], func=AF.Exp, bias=m4s[:, 1:2],
                                         accum_out=m4s[:, 2:3]), r=[PK(5), m4sk], w=[m4k, m4sk])
            P.dve(lambda e: e.reciprocal(out=m4s[:, 3:4], in_=m4s[:, 2:3]), r=[m4sk], w=[m4sk])
            P.dve(lambda e: e.tensor_scalar(out=m4[:], in0=m4[:], scalar1=m4s[:, 3:4], scalar2=None, op0=ALU.mult),
                  r=[m4k, m4sk], w=[m4k])

        def S2(s=s):
            Vt = Vt3[s % NVS]
            vk = ("Vt", s % NVS)
            m4, m4k = sm42[s % 2], ("sm4", s % 2)
            for mt in range(2):
                P.pe(lambda e, mt=mt: e.transpose(out=ps[6][:, mt * 4:(mt + 1) * 4],
                                                  in_=m4[:, mt * 128:(mt + 1) * 128], identity=ident_f[0:4, 0:4]),
                     r=[m4k, "cst"], w=[PK(6)])
            P.act(lambda e: e.activation(out=pS[:].rearrange("p a h -> p (a h)"), in_=ps[6][:, 0:8], func=AF.Copy),
                  r=[PK(6)], w=["pS"])
            for h in range(4):
                for hf in range(2):
                    c = h * 2 + hf
                    for mt in range(2):
                        P.pe(lambda e, c=c, mt=mt, h=h: e.matmul(ps[7][:, c:c + 1],
                                                                 lhsT=Vt[:, mt, c * 128:(c + 1) * 128],
                                                                 rhs=pS[:, mt, h:h + 1], start=(mt == 0),
                                                                 stop=(mt == 1)),
                             r=[vk, "pS"], w=[PK(7)])
            P.act(lambda e: e.activation(out=aoS[:, :, s], in_=ps[7][:, 0:8], func=AF.Copy), r=[PK(7)], w=["aoS"])

        iters.append([S0, S1, S2])
    V.run_skewed(iters)
    P.dve(lambda e: e.tensor_copy(out=aoT[:, :, T:TT], in_=aoS[:]), r=["aoS"], w=["aoT"])
    P.barrier()
    A.reset(mXe)
    wmo = A.t([128, 8, 512], BF16, "wmo")
    wmo2 = A.t([128, 8, 512], BF16, "wmo2")
    load_w(wmo, "wmo", V.w_mo, 0, 8, 0, 512)
    load_w(wmo2, "wmo2", V.w_mo, 0, 8, 512, 512)
    for t in range(NT + 1):
        n = 128 if t < NT else NS
        c0 = t * 128
        for half, wv in ((0, wmo), (1, wmo2)):
            b = (t % 2) * 2 + half
            for kc in range(8):
                P.pe(lambda e, b=b, kc=kc, wv=wv, n=n, c0=c0: e.matmul(ps[b][:n, :], lhsT=aoT[:, kc, c0:c0 + n],
                                                                      rhs=wv[:, kc, :], start=(kc == 0),
                                                                      stop=(kc == 7)), r=["aoT", "wmo", "wmo2"],
                     w=[PK(b)])
            P.dve(lambda e, b=b, t=t, n=n, half=half: e.tensor_tensor(
                out=x1[:n, t, half * 512:(half + 1) * 512], in0=ps[b][:n, :],
                in1=x1[:n, t, half * 512:(half + 1) * 512], op=ALU.add), r=[PK(b), ("x1", t)], w=[("x1", t)])

    stop_at(11)
    P.barrier()
    A.reset(mXe)
    TH = 1024
    AHh = A.child(V.off_hT, V.off_cT)
    hT4 = AHh.t([128, 8, TH + NS], BF16, "hT4")
    wg = [AHh.t([128, 8, 256], BF16, "wg%d" % i) for i in range(2)]
    wu = [AHh.t([128, 8, 256], BF16, "wu%d" % i) for i in range(2)]
    AHc = A.child(V.off_cT, mX)
    gF = AHc.t([128, 1024], F32, "gF")
    sgf = [AHc.t([128, 512], F32, "sgf%d" % i) for i in range(2)]
    fs = AHc.t([128, 8], F32, "fs")
    aT = A.t([128, 11, TH + NS], BF16, "aT")
    wdn = [A.t([128, 11, 512], BF16, "wdn%d" % i) for i in range(2)]
    fj = V.nrm_junk
    P.dma(gF[:], pr_d[:, PR_GF:PR_GF + 1024], w=["gF"], tag="c4")
    load_gam(PR_GFFN)
    gu_list = [(fh, gi) for _th in range(2) for fh in range(2) for gi in range(6)]

    def gu_load(idx):
        fh, gi = gu_list[idx]
        ncg = 256 if gi < 5 else 128
        c0 = fh * 1408 + gi * 256
        sl = idx % 2
        load_w(wg[sl], ("wg", sl), V.w_gate, 0, 8, c0, ncg)
        load_w(wu[sl], ("wu", sl), V.w_up, 0, 8, c0, ncg)

    fsl = [AHc.t([128, 8], F32, "fs%d" % i) for i in range(2)]
    fcnt = [0]

    def final_norm(t):
        n = 128 if t < NT else NS
        i_ = fcnt[0] % 2
        fcnt[0] += 1
        fs_ = fsl[i_]
        fk = ("fs", i_)
        P.act(lambda e: e.activation(out=fj[:n, :], in_=x1[:n, t, :], func=AF.Square, accum_out=fs_[:n, 0:1]),
              r=[("x1", t)], w=["njunk", fk])
        P.dve(lambda e: e.tensor_scalar(out=fs_[:n, 1:2], in0=fs_[:n, 0:1], scalar1=1.0 / 1024, scalar2=1e-6,
                                        op0=ALU.mult, op1=ALU.add), r=[fk], w=[fk])
        P.act(lambda e: e.activation(out=fs_[:n, 2:3], in_=fs_[:n, 1:2], func=AF.Sqrt), r=[fk], w=[fk])
        P.dve(lambda e: e.reciprocal(out=fs_[:n, 3:4], in_=fs_[:n, 2:3]), r=[fk], w=[fk])
        P.dve(lambda e: e.scalar_tensor_tensor(out=x1[:n, t, :], in0=x1[:n, t, :], scalar=fs_[:n, 3:4],
                                               in1=gF[:n, :], op0=ALU.mult, op1=ALU.mult),
              r=[("x1", t), fk, "gF"], w=[("x1", t)])
        dst_ = V.y_p[t * 128:(t + 1) * 128, :] if t < NT else V.y_s[:, :]
        P.dma(dst_, x1[:n, t, :], r=[("x1", t)], tag="out")

    gu_load(0)
    gidx = 0
    for th in range(2):
        tiles = list(range(th * 8, th * 8 + 8)) + ([NT] if th == 1 else [])
        V.norm_pass([(x1[:(128 if t < NT else NS), t, :], [("x1", t)], (128 if t < NT else NS), ti * 128, None)
                     for ti, t in enumerate(tiles)], hT4, "hT4")
        blks = [(0, 512), (512, 512)] + ([(1024, NS)] if th == 1 else [])
        for fh in range(2):
            for gi in range(6):
                cur = gidx
                gidx += 1
                if cur + 1 < len(gu_list):
                    gu_load(cur + 1)
                if gi == 2 or gi == 4:
                    hf_ = 0 if gi == 2 else 1
                    load_w(wdn[hf_], ("wdn", hf_), V.w_down, fh * 1408, 11, hf_ * 512, 512)
                ncg = 256 if gi < 5 else 128
                sl = cur % 2
                for cc in range(ncg // 128):
                    fc = gi * 2 + cc
                    for bi, (t0, n) in enumerate(blks):
                        bg, bu = (bi % 2) * 2, (bi % 2) * 2 + 1
                        mm8(bg, lambda kc: wg[sl][:, kc, cc * 128:(cc + 1) * 128],
                            lambda kc: hT4[:, kc, t0:t0 + n], n, [("hT4", t0 // 512), ("wg", sl)])
                        mm8(bu, lambda kc: wu[sl][:, kc, cc * 128:(cc + 1) * 128],
                            lambda kc: hT4[:, kc, t0:t0 + n], n, [("hT4", t0 // 512), ("wu", sl)])
                        sg = sgf[bi % 2]
                        P.act(lambda e, sg=sg, bg=bg, n=n: e.activation(out=sg[:, 0:n], in_=ps[bg][:, 0:n],
                                                                        func=AF.Silu), r=[PK(bg)],
                              w=[("sgf", bi % 2)])
                        P.dve(lambda e, sg=sg, bu=bu, n=n, fc=fc, t0=t0: e.tensor_tensor(
                            out=aT[:, fc, t0:t0 + n], in0=ps[bu][:, 0:n], in1=sg[:, 0:n], op=ALU.mult),
                            r=[PK(bu), ("sgf", bi % 2)], w=["aT"])
            for half in range(2):
                wslot = half
                wdk = ("wdn", wslot)
                for ti, t in enumerate(tiles):
                    n = 128 if t < NT else NS
                    b = 4 + (ti % 4)
                    for k in range(11):
                        P.pe(lambda e, b=b, k=k, ti=ti, n=n, wslot=wslot: e.matmul(
                            ps[b][:n, :], lhsT=aT[:, k, ti * 128:ti * 128 + n], rhs=wdn[wslot][:, k, :],
                            start=(k == 0), stop=(k == 10)), r=["aT", wdk], w=[PK(b)])
                    P.dve(lambda e, b=b, t=t, n=n, half=half: e.tensor_tensor(
                        out=x1[:n, t, half * 512:(half + 1) * 512], in0=ps[b][:n, :],
                        in1=x1[:n, t, half * 512:(half + 1) * 512], op=ALU.add), r=[PK(b), ("x1", t)],
                        w=[("x1", t)])
                    if fh == 1 and half == 1:
                        final_norm(t)
        continue
        for t in tiles:
            n = 128 if t < NT else NS
            P.act(lambda e, t=t, n=n: e.activation(out=fj[:n, :], in_=x1[:n, t, :], func=AF.Square,
                                                   accum_out=fs[:n, 0:1]), r=[("x1", t)], w=["njunk", "fs"])
            P.dve(lambda e, n=n: e.tensor_scalar(out=fs[:n, 1:2], in0=fs[:n, 0:1], scalar1=1.0 / 1024, scalar2=1e-6,
                                                 op0=ALU.mult, op1=ALU.add), r=["fs"], w=["fs"])
            P.act(lambda e, n=n: e.activation(out=fs[:n, 2:3], in_=fs[:n, 1:2], func=AF.Sqrt), r=["fs"], w=["fs"])
            P.dve(lambda e, n=n: e.reciprocal(out=fs[:n, 3:4], in_=fs[:n, 2:3]), r=["fs"], w=["fs"])
            P.dve(lambda e, t=t, n=n: e.scalar_tensor_tensor(out=x1[:n, t, :], in0=x1[:n, t, :], scalar=fs[:n, 3:4],
                                                             in1=gF[:n, :], op0=ALU.mult, op1=ALU.add if False else ALU.mult),
                  r=[("x1", t), "fs", "gF"], w=[("x1", t)])
            dst = V.y_p[t * 128:(t + 1) * 128, :] if t < NT else V.y_s[:, :]
            P.dma(dst, x1[:n, t, :], r=[("x1", t)], tag="out")

    P.emit(["out"])


_CACHE = {}


def kernel(**inputs):
    inp = {k: np.asarray(v) for k, v in inputs.items()}
    if "nc" not in _CACHE:
        _CACHE["nc"] = build_nc()[0]
    nc = _CACHE["nc"]
    cst = make_consts()
    pc, pr = make_params(inp)
    f = lambda a: np.ascontiguousarray(a, dtype=np.float32)
    shared = {
        "w_in": f(inp["w_in"][0]), "w_out": f(inp["w_out"][0]), "w_mq": f(inp["w_mq"][0]), "w_mk": f(inp["w_mk"][0]),
        "w_mv": f(inp["w_mv"][0]), "w_mo": f(inp["w_mo"][0]), "w_gate": f(inp["w_gate"][0]),
        "w_up": f(inp["w_up"][0]), "w_down": f(inp["w_down"][0]), "cst": cst, "pc": pc, "pr": pr,
    }
    in_maps = []
    for c in range(8):
        sl = slice(c * NS, (c + 1) * NS)
        m = dict(shared)
        m["x_p"] = f(inp["x_prompt"][c])
        m["x_s"] = f(inp["x_sample"][sl, 0, :])
        m["mem"] = f(inp["mem_prompt"][c])
        m["cconv"] = f(inp["cache_conv"][0, sl].reshape(NS * 30, 512))
        m["ssc"] = f(inp["state_short_conv"][0, sl].reshape(NS * 3, 1536))
        m["sdel"] = f(inp["state_delta"][0, sl].reshape(NS * 4, 128, 128))
        m["cmk"] = f(inp["cache_mem_k"][0, sl].reshape(NS, 256, 1024))
        m["cmv"] = f(inp["cache_mem_v"][0, sl].reshape(NS, 256, 1024))
        in_maps.append(m)
    res = run_bass_kernel_spmd(nc, in_maps, core_ids=list(range(8)))
    R = res.results
    cat = lambda k: np.stack([np.asarray(R[c][k]) for c in range(8)])
    y_p = cat("y_p")
    y_s = cat("y_s").reshape(128, 1, D)
    nconv_p = cat("nconv_p")[None]
    nsc_p = cat("nsc_p")[None]
    ndel_p = cat("ndel_p")[None]
    mk = cat("mk_p").reshape(1, 8, 256, 4, 256)
    mv = cat("mv_p").reshape(1, 8, 256, 4, 256)
    nconv_s = cat("nconv_s").reshape(1, 128, 30, 512)
    nsc_s = cat("nsc_s").reshape(1, 128, 3, 1536)
    ndel_s = cat("ndel_s").reshape(1, 128, 4, 128, 128)
    return tuple(np.ascontiguousarray(a, dtype=np.float32) for a in
                 (y_p, y_s, nconv_p, nsc_p, ndel_p, mk, mv, nconv_s, nsc_s, ndel_s))
```
